# Optimizing a Trainium2 kernel written in Bass

```python
import math
import jax, jax.numpy as jnp
from jax import lax
import numpy as np

D_MODEL = 1024
BATCH = 16
SEQ = 256
DEPTH = 4
DEC_BATCH = 8
DEC_SEQ = 1024
PAST_LEN = 512

GRID_W = 64
D_BRANCH = D_MODEL // 2
S5_GROUP = 16
S5_GROUPS = D_BRANCH // S5_GROUP
S5_STATE = 64
N_DIR = 2
DH = 64
N_HEADS = D_BRANCH // (2 * DH)
DV = 2 * DH
POOL_WINDOWS = (2, 4, 8, 16)
POOL_GROUP = D_BRANCH // len(POOL_WINDOWS)
D_FF = ((8 * D_MODEL // 3 + 255) // 256) * 256
N_MOD = 9
N_BRANCH = 3
IN_W = 5 * D_BRANCH + N_BRANCH * D_MODEL
SPLITS = [D_BRANCH, 2 * D_BRANCH, 3 * D_BRANCH, 4 * D_BRANCH, 5 * D_BRANCH]
ROPE_BASE = 10000.0
EPS = 1e-6
Q_BLOCK = 128
DENSE_MAX_KEYS = 2048

kernel_name = 'hybrid_s5_diffattn_pool_diffusion_step'


def rmsnorm(x, g):
    xf = x.astype(jnp.float32)
    y = xf * lax.rsqrt(jnp.mean(xf * xf, axis=-1, keepdims=True) + EPS)
    return (y * g.astype(jnp.float32)).astype(x.dtype)


def swiglu(h, w_in, w_out):
    a, b = jnp.split(h @ w_in, 2, axis=-1)
    return (jax.nn.silu(a) * b) @ w_out


def adaln(cvec, w_mod, b_mod):
    m = jax.nn.silu(cvec) @ w_mod + b_mod
    return m.reshape(cvec.shape[0], 1, N_MOD, D_MODEL)


def axial_rope_tables(L):
    rows = L // GRID_W
    row = jnp.repeat(jnp.arange(rows, dtype=jnp.float32), GRID_W)
    col = jnp.tile(jnp.arange(GRID_W, dtype=jnp.float32), rows)
    n_freq = DH // 4
    inv = ROPE_BASE ** (-jnp.arange(n_freq, dtype=jnp.float32) / n_freq)
    ar = row[:, None] * inv
    ac = col[:, None] * inv
    return jnp.cos(ar), jnp.sin(ar), jnp.cos(ac), jnp.sin(ac)


def apply_rope(x, tabs):
    cr, sr, cc, sc = [t[None, :, None, None, :].astype(x.dtype) for t in tabs]

    def rot(y, cs, sn):
        y1, y2 = jnp.split(y, 2, axis=-1)
        return jnp.concatenate([y1 * cs - y2 * sn, y2 * cs + y1 * sn], axis=-1)

    xr, xc = jnp.split(x, 2, axis=-1)
    return jnp.concatenate([rot(xr, cr, sr), rot(xc, cc, sc)], axis=-1)


def cmul(ar, ai, br, bi):
    return ar * br - ai * bi, ar * bi + ai * br


def scan_combine(e1, e2):
    a1r, a1i, b1r, b1i = e1
    a2r, a2i, b2r, b2i = e2
    ar, ai = cmul(a2r, a2i, a1r, a1i)
    br, bi = cmul(a2r, a2i, b1r, b1i)
    return ar, ai, br + b2r, bi + b2i


def s5_mixer(u, lam_re, lam_im, log_dt, b_re, b_im, c_re, c_im, d_skip, w_glu, h0):
    f32 = jnp.float32
    bsz, L, _ = u.shape
    uf = u.astype(f32).reshape(bsz, L, S5_GROUPS, S5_GROUP)
    y = uf * d_skip.astype(f32).reshape(S5_GROUPS, S5_GROUP)
    finals = []
    for d in range(N_DIR):
        lr = lam_re[d].astype(f32)
        li = lam_im[d].astype(f32)
        dt = jnp.exp(log_dt[d].astype(f32))[:, None]
        mag = jnp.exp(lr * dt)
        abr = mag * jnp.cos(li * dt)
        abi = mag * jnp.sin(li * dt)
        den = lr * lr + li * li
        nr = abr - 1.0
        kr = (nr * lr + abi * li) / den
        ki = (abi * lr - nr * li) / den
        br = b_re[d].astype(f32)
        bi = b_im[d].astype(f32)
        bbr = kr[..., None] * br - ki[..., None] * bi
        bbi = kr[..., None] * bi + ki[..., None] * br
        xr = jnp.einsum('blgc,gpc->blgp', uf, bbr)
        xi = jnp.einsum('blgc,gpc->blgp', uf, bbi)
        ar = jnp.broadcast_to(abr, xr.shape)
        ai = jnp.broadcast_to(abi, xi.shape)
        cum_r, cum_i, hr, hi = lax.associative_scan(scan_combine, (ar, ai, xr, xi), axis=1, reverse=(d == 1))
        if h0 is None:
            end = L - 1 if d == 0 else 0
            finals.append(jnp.stack([hr[:, end], hi[:, end]], axis=1))
        else:
            h0r = h0[:, d, 0].astype(f32)[:, None]
            h0i = h0[:, d, 1].astype(f32)[:, None]
            hr = hr + cum_r * h0r - cum_i * h0i
            hi = hi + cum_r * h0i + cum_i * h0r
        y = y + jnp.einsum('gcp,blgp->blgc', c_re[d].astype(f32), hr) - jnp.einsum('gcp,blgp->blgc', c_im[d].astype(f32), hi)
    y = jax.nn.gelu(y.reshape(bsz, L, D_BRANCH)).astype(u.dtype)
    y = y * jax.nn.sigmoid(y @ w_glu)
    final = jnp.stack(finals, axis=1).astype(u.dtype) if h0 is None else None
    return y, final


def diff_attention(q, keys, vals, lam, lam_init, g):
    bsz, Lq = q.shape[0], q.shape[1]
    scale = DH ** -0.5

    def attend(qb):
        s = jnp.einsum('bqhmd,bkhmd->bhmqk', qb, keys).astype(jnp.float32) * scale
        p = jax.nn.softmax(s, axis=-1)
        a = (p[:, :, 0] - lam * p[:, :, 1]).astype(vals.dtype)
        return jnp.einsum('bhqk,bkhe->bqhe', a, vals)

    if keys.shape[1] >= DENSE_MAX_KEYS:
        nb = Lq // Q_BLOCK
        qb = jnp.moveaxis(q.reshape(bsz, nb, Q_BLOCK, N_HEADS, 2, DH), 1, 0)
        o = lax.map(attend, qb)
        o = jnp.moveaxis(o, 0, 1).reshape(bsz, Lq, N_HEADS, DV)
    else:
        o = attend(q)
    o = rmsnorm(o, g) * (1.0 - lam_init)
    return o.reshape(bsz, Lq, N_HEADS * DV)


def pool_mixer(z, w_pool, scale):
    bsz, L, _ = z.shape
    zf = z.astype(jnp.float32).reshape(bsz, L, len(POOL_WINDOWS), POOL_GROUP)
    cs = jnp.concatenate([jnp.zeros_like(zf[:, :1]), jnp.cumsum(zf, axis=1)], axis=1)
    t = jnp.arange(L)
    outs = []
    for gi, w in enumerate(POOL_WINDOWS):
        lo = jnp.clip(t - w // 2, 0, L)
        hi = jnp.clip(t - w // 2 + w, 0, L)
        csg = cs[:, :, gi]
        mean = (csg[:, hi] - csg[:, lo]) / (hi - lo).astype(jnp.float32)[None, :, None]
        outs.append(mean - zf[:, :, gi])
    pooled = jnp.stack(outs, axis=2).astype(z.dtype)
    y = jnp.einsum('blgc,gcd->blgd', pooled, w_pool).reshape(bsz, L, D_BRANCH)
    return y * scale


def token_mixer(h, lp, layer_idx, ctx):
    bsz, L, _ = h.shape
    u, q, k, v, z, g = jnp.split(h @ lp['w_in'], SPLITS, axis=-1)
    q = q.reshape(bsz, L, N_HEADS, 2, DH)
    k = k.reshape(bsz, L, N_HEADS, 2, DH)
    v = v.reshape(bsz, L, N_HEADS, DV)
    f32 = jnp.float32
    lam_init = 0.8 - 0.6 * math.exp(-0.3 * layer_idx)
    lam = (jnp.exp(jnp.sum(lp['lam_q1'].astype(f32) * lp['lam_k1'].astype(f32)))
           - jnp.exp(jnp.sum(lp['lam_q2'].astype(f32) * lp['lam_k2'].astype(f32))) + lam_init)
    s5_args = (lp['lam_re'], lp['lam_im'], lp['log_dt'], lp['b_re'], lp['b_im'], lp['c_re'], lp['c_im'], lp['d_skip'], lp['w_glu'])
    if ctx is None:
        ya, s_fin = s5_mixer(u, *s5_args, None)
        yb = diff_attention(q, k, v, lam, lam_init, lp['attn_norm_g'])
        new = (k, v, s_fin)
    else:
        k_ctx, v_ctx, s_ctx = ctx
        tabs = axial_rope_tables(L)
        q = apply_rope(q, tabs)
        k = apply_rope(k, tabs)
        ya, _ = s5_mixer(u, *s5_args, s_ctx)
        yb = diff_attention(q, jnp.concatenate([k, k_ctx], axis=1), jnp.concatenate([v, v_ctx], axis=1),
                            lam, lam_init, lp['attn_norm_g'])
        new = None
    yc = pool_mixer(z, lp['w_pool'], lp['pool_scale'])
    ys = jnp.stack([ya, yb, yc], axis=2)
    gates = jax.nn.sigmoid(g.reshape(bsz, L, N_BRANCH, D_MODEL))
    merged = jnp.sum(gates * jnp.einsum('blnc,ncd->blnd', ys, lp['w_branch']), axis=2)
    return merged @ lp['w_out'], new


def trunk_layer(x, mod, lp, layer_idx, ctx):
    ng = lp['norm_g']
    n = rmsnorm(x, ng[0]) * (1.0 + mod[..., 1, :]) + mod[..., 0, :]
    x = x + 0.5 * mod[..., 2, :] * swiglu(n, lp['w_ffn_in'][0], lp['w_ffn_out'][0])
    n = rmsnorm(x, ng[1]) * (1.0 + mod[..., 4, :]) + mod[..., 3, :]
    y, new = token_mixer(n, lp, layer_idx, ctx)
    x = x + mod[..., 5, :] * y
    n = rmsnorm(x, ng[2]) * (1.0 + mod[..., 7, :]) + mod[..., 6, :]
    x = x + 0.5 * mod[..., 8, :] * swiglu(n, lp['w_ffn_in'][1], lp['w_ffn_out'][1])
    return x, new


def setup_inputs(seed: int = 0) -> dict:
    key = jax.random.key(seed)
    keys = list(jax.random.split(key, 40))
    f32 = jnp.float32

    def nrm(shape, s):
        return jax.random.normal(keys.pop(), shape, f32) * s

    G, P = S5_GROUPS, S5_STATE
    return {
        'x_prompt': nrm((BATCH, SEQ, D_MODEL), 1.0),
        'x_sample': nrm((DEC_BATCH, DEC_SEQ, D_MODEL), 1.0),
        'cache_k': nrm((DEC_BATCH, DEPTH, PAST_LEN, N_HEADS, 2, DH), 1.0),
        'cache_v': nrm((DEC_BATCH, DEPTH, PAST_LEN, N_HEADS, DV), 1.0),
        'state_ssm': nrm((DEC_BATCH, DEPTH, N_DIR, 2, G, P), 0.3),
        'c': nrm((DEC_BATCH, D_MODEL), 1.0),
        'c_ctx': nrm((D_MODEL,), 1.0),
        'norm_g': 1.0 + nrm((DEPTH, 3, D_MODEL), 0.02),
        'w_mod': nrm((DEPTH, D_MODEL, N_MOD * D_MODEL), 0.3 * D_MODEL ** -0.5),
        'b_mod': nrm((DEPTH, N_MOD * D_MODEL), 0.02),
        'w_ffn_in': nrm((DEPTH, 2, D_MODEL, 2 * D_FF), D_MODEL ** -0.5),
        'w_ffn_out': nrm((DEPTH, 2, D_FF, D_MODEL), D_FF ** -0.5),
        'w_in': nrm((DEPTH, D_MODEL, IN_W), D_MODEL ** -0.5),
        'ssm_lam_re': -0.5 + nrm((DEPTH, N_DIR, G, P), 0.01),
        'ssm_lam_im': jnp.pi * jnp.arange(P, dtype=f32) + nrm((DEPTH, N_DIR, G, P), 0.01),
        'ssm_log_dt': jax.random.uniform(keys.pop(), (DEPTH, N_DIR, G), f32, math.log(1e-3), math.log(1e-1)),
        'ssm_b_re': nrm((DEPTH, N_DIR, G, P, S5_GROUP), (2 * S5_GROUP) ** -0.5),
        'ssm_b_im': nrm((DEPTH, N_DIR, G, P, S5_GROUP), (2 * S5_GROUP) ** -0.5),
        'ssm_c_re': nrm((DEPTH, N_DIR, G, S5_GROUP, P), 0.5),
        'ssm_c_im': nrm((DEPTH, N_DIR, G, S5_GROUP, P), 0.5),
        'ssm_d': nrm((DEPTH, D_BRANCH), 0.5),
        'w_glu': nrm((DEPTH, D_BRANCH, D_BRANCH), D_BRANCH ** -0.5),
        'lam_q1': nrm((DEPTH, DH), 0.1),
        'lam_k1': nrm((DEPTH, DH), 0.1),
        'lam_q2': nrm((DEPTH, DH), 0.1),
        'lam_k2': nrm((DEPTH, DH), 0.1),
        'attn_norm_g': 1.0 + nrm((DEPTH, DV), 0.02),
        'w_pool': nrm((DEPTH, len(POOL_WINDOWS), POOL_GROUP, POOL_GROUP), POOL_GROUP ** -0.5),
        'pool_scale': 1.0 + nrm((DEPTH, D_BRANCH), 0.02),
        'w_branch': nrm((DEPTH, N_BRANCH, D_BRANCH, D_MODEL), D_BRANCH ** -0.5),
        'w_out': nrm((DEPTH, D_MODEL, D_MODEL), D_MODEL ** -0.5),
        'final_norm_g': 1.0 + nrm((D_MODEL,), 0.02),
    }


def reference(x_prompt, x_sample, cache_k, cache_v, state_ssm, c, c_ctx, norm_g, w_mod, b_mod, w_ffn_in, w_ffn_out,
              w_in, ssm_lam_re, ssm_lam_im, ssm_log_dt, ssm_b_re, ssm_b_im, ssm_c_re, ssm_c_im, ssm_d, w_glu,
              lam_q1, lam_k1, lam_q2, lam_k2, attn_norm_g, w_pool, pool_scale, w_branch, w_out, final_norm_g):
    xp = x_prompt
    xs = x_sample
    new_k, new_v, new_s = [], [], []
    for l in range(DEPTH):
        lp = dict(norm_g=norm_g[l], w_ffn_in=w_ffn_in[l], w_ffn_out=w_ffn_out[l], w_in=w_in[l],
                  lam_re=ssm_lam_re[l], lam_im=ssm_lam_im[l], log_dt=ssm_log_dt[l],
                  b_re=ssm_b_re[l], b_im=ssm_b_im[l], c_re=ssm_c_re[l], c_im=ssm_c_im[l],
                  d_skip=ssm_d[l], w_glu=w_glu[l],
                  lam_q1=lam_q1[l], lam_k1=lam_k1[l], lam_q2=lam_q2[l], lam_k2=lam_k2[l],
                  attn_norm_g=attn_norm_g[l], w_pool=w_pool[l], pool_scale=pool_scale[l],
                  w_branch=w_branch[l], w_out=w_out[l])
        mod_ctx = adaln(c_ctx[None, :], w_mod[l], b_mod[l])
        mod_lat = adaln(c, w_mod[l], b_mod[l])
        xp, (k_l, v_l, s_l) = trunk_layer(xp, mod_ctx, lp, l, None)
        xs, _ = trunk_layer(xs, mod_lat, lp, l, (cache_k[:, l], cache_v[:, l], state_ssm[:, l]))
        new_k.append(k_l)
        new_v.append(v_l)
        new_s.append(s_l)
    y_prompt = rmsnorm(xp, final_norm_g)
    y_sample = rmsnorm(xs, final_norm_g)
    new_cache_k = jnp.stack(new_k, axis=1)
    new_cache_v = jnp.stack(new_v, axis=1)
    new_state_ssm = jnp.stack(new_s, axis=1)
    return (y_prompt, y_sample, new_cache_k, new_cache_v, new_state_ssm)
```

```python
import math, os
import numpy as np
_VD = os.environ.get('VDBG', '')
import concourse.bass as bass
import concourse.mybir as mybir
import concourse.ap as apm
from concourse.bass_utils import run_bass_kernel_spmd

F32 = mybir.dt.float32
BF16 = mybir.dt.bfloat16
I32 = mybir.dt.int32
AF = mybir.ActivationFunctionType
ALU = mybir.AluOpType
AX = mybir.AxisListType
_ES = {str(F32): 4, str(BF16): 2, str(I32): 4}

D = 1024; NT = 1536; DFF = 2816; INW = 5632; DEPTH = 4
EPS = 1e-6
TWO_PI = 2.0 * math.pi
SIN_SC = 1.0 - 2e-4


class _Rec:
    __slots__ = ("eng", "sem", "val", "write", "p0", "p1", "f0", "f1")

    def __init__(self, eng, sem, val, write, box):
        self.eng = eng; self.sem = sem; self.val = val; self.write = write
        self.p0, self.p1, self.f0, self.f1 = box


def _box(ap):
    pat = ap.ap
    es = _ES[str(ap.dtype)]
    off = int(ap.offset)
    row = pat[0][0]
    if row == 0:
        p0 = 0; f = off
    else:
        p0 = off // row; f = off - p0 * row
    p1 = p0 + pat[0][1]
    lo = f; hi = f
    for st, cnt in pat[1:]:
        if st >= 0:
            hi += st * (cnt - 1)
        else:
            lo += st * (cnt - 1)
    return (ap.tensor.name, p0, p1, lo * es, (hi + 1) * es)


class K:
    def __init__(self, nc, n_dma_sems=32):
        self.nc = nc
        self.engs = {"pe": nc.tensor, "act": nc.scalar, "dve": nc.vector, "pool": nc.gpsimd, "sp": nc.sync}
        self.sem = {}; self.cnt = {}
        for e in ("pe", "act", "dve", "pool"):
            self.sem[e] = nc.semaphore("s_" + e).__enter__()
            self.cnt[e] = 0
        self.dsem = [nc.semaphore("d%d" % i).__enter__() for i in range(n_dma_sems)]
        self.dval = [0] * n_dma_sems
        self.dnext = 0
        self.known = {e: {} for e in self.engs}
        self.recs = {}
        self.out_waits = []
        self.n_inst = {e: 0 for e in self.engs}
        self.n_wait = 0

    def _need(self, eng, sem, val):
        if eng == "pe" and sem is self.sem["pe"]:
            return
        kn = self.known[eng]
        key = sem.name
        if kn.get(key, 0) >= val:
            return
        kn[key] = val
        self.engs[eng].wait_ge(sem, val)
        self.n_wait += 1

    def _tracked(self, ap):
        if ap is None or isinstance(ap, (int, float)):
            return False
        if str(ap.space) == "DRAM" and ap.tensor.name not in self.recs:
            return False
        return True

    def _deps(self, eng, reads, writes):
        for ap in reads:
            if not self._tracked(ap):
                continue
            name, p0, p1, f0, f1 = _box(ap)
            for r in self.recs.get(name, ()):
                if r.write and r.p0 < p1 and p0 < r.p1 and r.f0 < f1 and f0 < r.f1:
                    self._need(eng, r.sem, r.val)
        for ap in writes:
            if not self._tracked(ap):
                continue
            name, p0, p1, f0, f1 = _box(ap)
            for r in self.recs.get(name, ()):
                if r.p0 < p1 and p0 < r.p1 and r.f0 < f1 and f0 < r.f1:
                    self._need(eng, r.sem, r.val)

    def _record(self, eng, sem, val, reads, writes):
        for ap in writes:
            if not self._tracked(ap):
                continue
            name, p0, p1, f0, f1 = _box(ap)
            lst = self.recs.setdefault(name, [])
            lst[:] = [r for r in lst if not (p0 <= r.p0 and r.p1 <= p1 and f0 <= r.f0 and r.f1 <= f1)]
            lst.append(_Rec(eng, sem, val, True, (p0, p1, f0, f1)))
        for ap in reads:
            if not self._tracked(ap):
                continue
            name, p0, p1, f0, f1 = _box(ap)
            lst = self.recs.setdefault(name, [])
            lst[:] = [r for r in lst if not ((not r.write) and r.eng == eng and p0 <= r.p0 and r.p1 <= p1
                                              and f0 <= r.f0 and r.f1 <= f1)]
            lst.append(_Rec(eng, sem, val, False, (p0, p1, f0, f1)))

    def track_dram(self, t):
        self.recs.setdefault(t.name, [])

    def _fin(self, eng, inst, reads, writes, inc=True):
        self.n_inst[eng] += 1
        if inc:
            self.cnt[eng] += 1
            inst.then_inc(self.sem[eng], 1)
            self._record(eng, self.sem[eng], self.cnt[eng], reads, writes)
        else:
            self._record(eng, self.sem[eng], self.cnt[eng] + 1, reads, writes)

    def mm(self, out, lhsT, rhs, start=True, stop=True, **kw):
        self._deps("pe", [lhsT, rhs], [out])
        i = self.nc.tensor.matmul(out, lhsT, rhs, start=start, stop=stop, **kw)
        self._fin("pe", i, [lhsT, rhs], [out], inc=stop)

    def transpose(self, out, in_, ident):
        self._deps("pe", [in_, ident], [out])
        i = self.nc.tensor.transpose(out, in_, ident)
        self._fin("pe", i, [in_, ident], [out])

    def act(self, out, in_, func, scale=1.0, bias=None):
        rd = [in_] + [a for a in (scale, bias) if a is not None and not isinstance(a, (int, float))]
        self._deps("act", rd, [out])
        kw = {}
        if bias is not None:
            kw["bias"] = bias
        i = self.nc.scalar.activation(out=out, in_=in_, func=func, scale=scale, **kw)
        self._fin("act", i, rd, [out])

    def _ve(self, eng):
        return self.nc.vector if eng == "dve" else self.nc.gpsimd

    def tt(self, eng, out, in0, in1, op):
        self._deps(eng, [in0, in1], [out])
        i = self._ve(eng).tensor_tensor(out, in0, in1, op)
        self._fin(eng, i, [in0, in1], [out])

    def ts(self, eng, out, in0, s1, s2=None, op0=ALU.mult, op1=None):
        rd = [in0] + [a for a in (s1, s2) if a is not None and not isinstance(a, (int, float))]
        self._deps(eng, rd, [out])
        if op1 is not None:
            i = self._ve(eng).tensor_scalar(out, in0, s1, s2, op0, op1)
        else:
            i = self._ve(eng).tensor_scalar(out, in0, s1, None, op0)
        self._fin(eng, i, rd, [out])

    def stt(self, out, in0, scalar, in1, op0, op1):
        rd = [in0, in1] + ([scalar] if not isinstance(scalar, (int, float)) else [])
        self._deps("dve", rd, [out])
        i = self.nc.vector.scalar_tensor_tensor(out, in0, scalar, in1, op0, op1)
        self._fin("dve", i, rd, [out])

    def scan(self, out, d0, d1, initial):
        rd = [d0, d1] + ([initial] if not isinstance(initial, (int, float)) else [])
        self._deps("dve", rd, [out])
        i = self.nc.vector.tensor_tensor_scan(out, d0, d1, initial, ALU.mult, ALU.add)
        self._fin("dve", i, rd, [out])

    def copy(self, eng, out, in_):
        if eng == "act":
            return self.act(out, in_, AF.Copy)
        self._deps(eng, [in_], [out])
        i = self._ve(eng).tensor_copy(out, in_)
        self._fin(eng, i, [in_], [out])

    def memset(self, eng, ap, val):
        self._deps(eng, [], [ap])
        i = self._ve(eng).memset(ap, val)
        self._fin(eng, i, [], [ap])

    def recip(self, out, in_):
        self._deps("dve", [in_], [out])
        i = self.nc.vector.reciprocal(out, in_)
        self._fin("dve", i, [in_], [out])

    def dma(self, q, out, in_, is_output=False, **kw):
        s = self.dnext
        self.dnext = (self.dnext + 1) % len(self.dsem)
        sem = self.dsem[s]
        if self.dval[s] > 0:
            self._need(q, sem, self.dval[s])
        self._deps(q, [in_], [out])
        self.dval[s] += 16
        self.engs[q].dma_start(out=out, in_=in_, **kw).then_inc(sem, 16)
        self.n_inst[q] += 1
        self._record("dma%d" % s, sem, self.dval[s], [in_], [out])
        if is_output:
            self.out_waits.append((s, self.dval[s]))

    def finish(self):
        last = {}
        for s, v in self.out_waits:
            last[s] = max(last.get(s, 0), v)
        for s, v in last.items():
            self._need("sp", self.dsem[s], v)
        for e in ("pe", "act", "dve", "pool"):
            if self.cnt[e] > 0:
                self._need("sp", self.sem[e], self.cnt[e])


class Arena:
    def __init__(self, nc, nwords):
        self.t = nc.sbuf_tensor("arena", [128, nwords], F32).__enter__()
        self.n = nwords; self.top = 0; self.stack = []

    def alloc(self, free_shape, dt=F32):
        n = 1
        for s in free_shape:
            n *= s
        words = n if _ES[str(dt)] == 4 else (n + 1) // 2
        words = (words + 7) // 8 * 8
        off = self.top
        self.top += words
        assert self.top <= self.n, "arena overflow %d > %d" % (self.top, self.n)
        ap = self.t[:, off:off + words]
        if dt != F32:
            ap = ap.bitcast(dt)
        ap = ap[:, 0:n]
        if len(free_shape) == 2:
            ap = ap.rearrange("p (a b) -> p a b", a=free_shape[0])
        elif len(free_shape) == 3:
            ap = ap.rearrange("p (a b c) -> p a b c", a=free_shape[0], b=free_shape[1])
        elif len(free_shape) == 4:
            ap = ap.rearrange("p (a b c d) -> p a b c d", a=free_shape[0], b=free_shape[1], c=free_shape[2])
        return ap

    def mark(self):
        self.stack.append(self.top)

    def release(self):
        self.top = self.stack.pop()


def _row(ap):
    return ap.ap[0][0]


def rev(ap):
    pat = ap.ap
    assert len(pat) == 2
    st, n = pat[1]
    return apm.AP(ap.tensor, int(ap.offset) + (n - 1) * st, [list(pat[0]), [-st, n]])


class _Stop(Exception):
    pass


def build(nc, depth=DEPTH, dbg=False, stop=None):
    try:
        return _build(nc, depth, dbg, stop)
    except _Stop as e:
        e.args[0].finish()
        return e.args[0]


def _build(nc, depth, dbg, stop):
    k = K(nc)

    dbg_t = {}

    def stage(name, dumps=()):
        if stop != name:
            return
        off = 0
        ar.mark()
        scr = ar.alloc([2048])
        for ap in dumps:
            flat = ap
            if len(ap.shape) == 3:
                flat = ap.rearrange("p a b -> p (a b)")
            elif len(ap.shape) == 4:
                flat = ap.rearrange("p a b c -> p (a b c)")
            n = flat.shape[1]
            for c0 in range(0, n, 2048):
                c1 = min(n, c0 + 2048)
                k.copy("dve", scr[:, 0:c1 - c0], flat[:, c0:c1])
                k.dma("sp", dbg_t["d"].ap()[:, off + c0:off + c1], scr[:, 0:c1 - c0], is_output=True)
            off += n
        raise _Stop(k)

    ar = Arena(nc, 53000)
    psum = nc.psum_tensor("ps", [128, 8, 512], F32).__enter__()
    st_ps = {"i": 0}

    def ps(pool=(0, 1, 2, 3, 4, 5, 6, 7)):
        st_ps["i"] += 1
        return psum[:, pool[st_ps["i"] % len(pool)], :]

    def din(name, shape, dt=F32):
        return nc.dram_tensor(name, list(shape), dt, kind="ExternalInput")

    xin = din("xin", [NT, D]); ck = din("ck", [DEPTH, 512, 512]); cv = din("cv", [DEPTH, 512, 512])
    stt_in = din("st", [DEPTH, 2, 2, 2048]); cvec = din("cvec", [2, D])
    norm_g = din("norm_g", [DEPTH, 3, D]); w_mod = din("w_mod", [DEPTH, D, 9 * D]); b_mod = din("b_mod", [DEPTH, 9 * D])
    w_ffn_in = din("w_ffn_in", [DEPTH, 2, D, 2 * DFF]); w_ffn_out = din("w_ffn_out", [DEPTH, 2, DFF, D])
    w_in = din("w_in", [DEPTH, D, INW])
    lam_re = din("ssm_lam_re", [DEPTH, 2, 32, 64]); lam_im = din("ssm_lam_im", [DEPTH, 2, 32, 64])
    log_dt = din("ssm_log_dt", [DEPTH, 2, 32])
    b_re = din("ssm_b_re", [DEPTH, 2, 32, 64, 16]); b_im = din("ssm_b_im", [DEPTH, 2, 32, 64, 16])
    c_re = din("ssm_c_re", [DEPTH, 2, 32, 16, 64]); c_im = din("ssm_c_im", [DEPTH, 2, 32, 16, 64])
    ssm_d = din("ssm_d", [DEPTH, 512]); w_glu = din("w_glu", [DEPTH, 512, 512])
    lq1 = din("lam_q1", [DEPTH, 64]); lk1 = din("lam_k1", [DEPTH, 64])
    lq2 = din("lam_q2", [DEPTH, 64]); lk2 = din("lam_k2", [DEPTH, 64])
    attn_g = din("attn_norm_g", [DEPTH, 128]); w_pool = din("w_pool", [DEPTH, 4, 128, 128])
    pool_scale = din("pool_scale", [DEPTH, 512]); w_branch = din("w_branch", [DEPTH, 3, 512, D])
    w_out = din("w_out", [DEPTH, D, D]); final_g = din("final_norm_g", [D])
    c_ident = din("c_ident", [128, 128]); c_rope = din("c_rope", [128, 2, 1024]); c_pt = din("c_pt", [128, 128])
    c_iota = din("c_iota", [128, 1024]); c_mask = din("c_mask", [128, 2]); c_pool = din("c_pool", [128, 4, 2, 2, 8])

    def dout(name, shape):
        return nc.dram_tensor(name, list(shape), F32, kind="ExternalOutput")

    y_out = dout("y", [NT, D]); nk_out = dout("nk", [2, DEPTH, 256, 512]); nv_out = dout("nv", [2, DEPTH, 256, 512])
    ns_out = dout("ns", [2, DEPTH, 2, 2, 2048])
    if stop is not None:
        dbg_t["d"] = dout("dbg", [128, 32768])

    ident = ar.alloc([128]); ident_bf = ar.alloc([128], BF16); ones_bf = ar.alloc([128], BF16)
    ones_f = ar.alloc([128]); pt_bf = ar.alloc([128], BF16)
    rope = ar.alloc([2, 1024]); iota = ar.alloc([1024]); maskc = ar.alloc([2]); poolc = ar.alloc([4, 2, 2, 8])
    cst = ar.alloc([8])
    k.dma("sp", ident, c_ident.ap())
    k.dma("sp", rope, c_rope.ap())
    k.dma("sp", iota, c_iota.ap())
    k.dma("sp", maskc, c_mask.ap())
    k.dma("sp", poolc, c_pool.ap())
    tmp_pt = ar.alloc([128])
    k.dma("sp", tmp_pt, c_pt.ap())
    k.copy("dve", ident_bf, ident)
    k.copy("dve", pt_bf, tmp_pt)
    k.memset("dve", ones_bf, 1.0)
    k.memset("dve", ones_f, 1.0)
    k.memset("dve", cst[:, 0:1], EPS)
    k.memset("dve", cst[:, 1:2], 0.5 * math.pi * SIN_SC)
    k.memset("dve", cst[:, 2:3], 0.0)
    eps_c = cst[:, 0:1]; hpi_c = cst[:, 1:2]

    xT = ar.alloc([8, NT])
    nT = ar.alloc([8, NT], BF16)
    modT = ar.alloc([DEPTH, 72, 2])
    Amod = ar.alloc([DEPTH, 3, 8, 2])
    Gmod = ar.alloc([DEPTH, 3, 8, 2])
    normgT = ar.alloc([96])
    smallT = ar.alloc([64])
    lamc = ar.alloc([DEPTH, 2])
    gfac = ar.alloc([DEPTH])
    sc = ar.alloc([2, 8])

    ar.mark()
    stg = ar.alloc([128])
    k.dma("sp", stg[0:96, :], norm_g.ap().rearrange("l j (c p) -> (l j c) p", p=128))
    pt = ps()
    k.transpose(pt[:, 0:96], stg[0:96, :], ident[0:96, 0:96])
    k.copy("dve", normgT, pt[:, 0:96])
    stg2 = ar.alloc([128])
    k.dma("sp", stg2[0:16, :], ssm_d.ap().rearrange("l (c p) -> (l c) p", p=128))
    k.dma("sp", stg2[16:32, :], pool_scale.ap().rearrange("l (c p) -> (l c) p", p=128))
    k.dma("sp", stg2[32:36, :], attn_g.ap())
    k.dma("sp", stg2[36:44, :], final_g.ap().rearrange("(c p) -> c p", p=128))
    k.dma("sp", stg2[44:60, :], cvec.ap().rearrange("v (c p) -> (v c) p", p=128))
    pt = ps()
    k.transpose(pt[:, 0:60], stg2[0:60, :], ident[0:60, 0:60])
    k.copy("dve", smallT[:, 0:60], pt[:, 0:60])
    dskipT = smallT[:, 0:16]; pscT = smallT[:, 16:32]; attngT = smallT[:, 32:36]; fingT = smallT[:, 36:44]
    k.act(sc.rearrange("p v c -> p (v c)"), smallT[:, 44:60], AF.Silu)
    bmodT = ar.alloc([DEPTH, 72])
    for l in range(DEPTH):
        stg3 = ar.alloc([128])
        k.dma("sp", stg3[0:72, :], b_mod.ap()[l].rearrange("(c p) -> c p", p=128))
        pt = ps()
        k.transpose(pt[:, 0:72], stg3[0:72, :], ident[0:72, 0:72])
        k.copy("dve", bmodT[:, l, :], pt[:, 0:72])
    lv = ar.alloc([4, DEPTH])
    with nc.allow_non_contiguous_dma(reason="tiny"):
        for i, t in enumerate((lq1, lk1, lq2, lk2)):
            k.dma("sp", lv[0:64, i, :], t.ap().rearrange("l d -> d l"))
    lp = ar.alloc([2, DEPTH])
    k.tt("dve", lp[0:64, 0, :], lv[0:64, 0, :], lv[0:64, 1, :], ALU.mult)
    k.tt("dve", lp[0:64, 1, :], lv[0:64, 2, :], lv[0:64, 3, :], ALU.mult)
    pt = ps()
    k.mm(pt[:, 0:2 * DEPTH], ones_f[0:64, :], lp[0:64].rearrange("p a l -> p (a l)"))
    le = ar.alloc([2, DEPTH])
    k.act(le.rearrange("p a l -> p (a l)"), pt[:, 0:2 * DEPTH], AF.Exp)
    for l in range(DEPTH):
        li_ = 0.8 - 0.6 * math.exp(-0.3 * l)
        k.stt(lamc[:, l, 0:1], le[:, 0, l:l + 1], li_, le[:, 1, l:l + 1], ALU.add, ALU.subtract)
        k.ts("dve", lamc[:, l, 1:2], lamc[:, l, 0:1], -1.0)
        k.ts("dve", gfac[:, l:l + 1], attngT[:, l:l + 1], 1.0 - li_)

    wms = [ar.alloc([8, 256]) for _ in range(2)]
    for l in range(depth):
        mps = ps()
        for fc in range(36):
            wm = wms[fc % 2]
            k.dma("sp", wm, w_mod.ap()[l].rearrange("(c p) f -> p c f", p=128)[:, :, fc * 256:(fc + 1) * 256])
            for s in range(2):
                fb = fc * 2 + s
                for kc in range(8):
                    k.mm(mps[:, fb * 2:fb * 2 + 2], wm[:, kc, s * 128:(s + 1) * 128], sc[:, :, kc],
                         start=(kc == 0), stop=(kc == 7))
        for v in range(2):
            k.tt("dve", modT[:, l, :, v], mps[:, 0:144].rearrange("p (f v) -> p f v", v=2)[:, :, v], bmodT[:, l, :], ALU.add)
        for j in range(3):
            for v in range(2):
                k.stt(Amod[:, l, j, :, v], modT[:, l, (3 * j + 1) * 8:(3 * j + 2) * 8, v], 1.0,
                      normgT[:, l * 24 + j * 8:l * 24 + j * 8 + 8], ALU.add, ALU.mult)
                k.ts("dve", Gmod[:, l, j, :, v], modT[:, l, (3 * j + 2) * 8:(3 * j + 3) * 8, v],
                     1.0 if j == 1 else 0.5)
    ar.release()

    ar.mark()
    for tb in range(12):
        ar.mark()
        xs = ar.alloc([D])
        k.dma("sp", xs, xin.ap()[tb * 128:(tb + 1) * 128, :])
        for g4 in range(2):
            pt = ps()
            for c in range(4):
                kc = g4 * 4 + c
                k.transpose(pt[:, c * 128:(c + 1) * 128], xs[:, kc * 128:(kc + 1) * 128], ident)
            k.copy("act" if g4 else "dve", xT[:, g4 * 4:(g4 + 1) * 4, tb * 128:(tb + 1) * 128],
                   pt.rearrange("p (c t) -> p c t", c=4))
        ar.release()
    ar.release()

    stage('prologue', [modT, xT])

    def vsel(tt):
        return 0 if tt == 0 else 1

    def rmsnorm_mod(l, j, A_ap=None, shift_ap=None, out_fn=None):
        ar.mark()
        sq = ar.alloc([8, 512], BF16)
        rstd = ar.alloc([512]); tmp = ar.alloc([2, 512])
        for tt in range(3):
            tsl = slice(tt * 512, (tt + 1) * 512)
            for kc in range(8):
                k.act(sq[:, kc, :], xT[:, kc, tsl], AF.Square)
            ss = ps()
            for kc in range(8):
                k.mm(ss, ones_bf, sq[:, kc, :], start=(kc == 0), stop=(kc == 7))
            k.act(rstd, ss, AF.Sqrt, scale=1.0 / D, bias=eps_c)
            k.recip(rstd, rstd)
            v = vsel(tt)
            for kc in range(8):
                if j < 3:
                    a_col = Amod[:, l, j, kc, v:v + 1]
                    b_col = modT[:, l, 3 * j * 8 + kc, v:v + 1]
                    tb_ = tmp[:, kc % 2, :]
                    k.stt(tb_, xT[:, kc, tsl], a_col, rstd, ALU.mult, ALU.mult)
                    k.act(nT[:, kc, tsl], tb_, AF.Identity, bias=b_col)
                else:
                    k.stt(out_fn(kc, tt), xT[:, kc, tsl], fingT[:, kc:kc + 1], rstd, ALU.mult, ALU.mult)
        ar.release()

    def wload(src_ap, shape, dt=BF16):
        buf = ar.alloc(shape, dt)
        k.dma("pool" if dt == BF16 else "sp", buf, src_ap)
        return buf

    def ffn(l, i):
        rmsnorm_mod(l, 0 if i == 0 else 2)
        j = 0 if i == 0 else 2
        ar.mark()
        gT = ar.alloc([22, NT], BF16)
        sa = ar.alloc([2, 512])
        wv = w_ffn_in.ap()[l, i].rearrange("(c p) f -> p c f", p=128)
        wbufs = [ar.alloc([2, 8, 256], BF16) for _ in range(2)]
        for c in range(11):
            wb = wbufs[c % 2]
            k.dma("pool", wb[:, 0], wv[:, :, c * 256:(c + 1) * 256])
            k.dma("pool", wb[:, 1], wv[:, :, DFF + c * 256:DFF + (c + 1) * 256])
            for s in range(2):
                fb = 2 * c + s
                for tt in range(3):
                    tsl = slice(tt * 512, (tt + 1) * 512)
                    pa = ps(); pb = ps()
                    for kc in range(8):
                        k.mm(pa, wb[:, 0, kc, s * 128:(s + 1) * 128], nT[:, kc, tsl], start=(kc == 0), stop=(kc == 7))
                    for kc in range(8):
                        k.mm(pb, wb[:, 1, kc, s * 128:(s + 1) * 128], nT[:, kc, tsl], start=(kc == 0), stop=(kc == 7))
                    sab = sa[:, (fb * 3 + tt) % 2, :]
                    k.act(sab, pa, AF.Silu)
                    k.tt("dve", gT[:, fb, tsl], sab, pb, ALU.mult)
        wov = w_ffn_out.ap()[l, i].rearrange("(f p) d -> p f d", p=128)
        wobufs = [ar.alloc([22, 128], BF16) for _ in range(2)]
        for dc in range(8):
            wo = wobufs[dc % 2]
            k.dma("pool", wo, wov[:, :, dc * 128:(dc + 1) * 128])
            for tt in range(3):
                tsl = slice(tt * 512, (tt + 1) * 512)
                py = ps()
                for fb in range(22):
                    k.mm(py, wo[:, fb, :], gT[:, fb, tsl], start=(fb == 0), stop=(fb == 21))
                v = vsel(tt)
                k.stt(xT[:, dc, tsl], py, Gmod[:, l, j, dc, v:v + 1], xT[:, dc, tsl], ALU.mult, ALU.add)
        ar.release()

    def bc_last(ap2, n):
        pat = ap2.ap
        return apm.AP(ap2.tensor, int(ap2.offset), [list(pat[0]), list(pat[1]), [0, n]])

    def s5_branch(l, uT, yaT):
        ar.mark()
        lr = ar.alloc([2, 16]); li = ar.alloc([2, 16]); ldt = ar.alloc([2, 16])
        h0 = ar.alloc([2, 2, 16])
        with nc.allow_non_contiguous_dma(reason="ssm params"):
            for d in range(2):
                k.dma("sp", lr[:, d, :], lam_re.ap()[l, d].rearrange("(W h) p -> (h p) W", h=2))
                k.dma("sp", li[:, d, :], lam_im.ap()[l, d].rearrange("(W h) p -> (h p) W", h=2))
                for h in range(2):
                    src = apm.AP(log_dt, (l * 2 + d) * 32 + h, [[0, 64], [2, 16]])
                    k.dma("sp", ldt[64 * h:64 * h + 64, d, :], src)
                for r in range(2):
                    k.dma("sp", h0[:, d, r, :], stt_in.ap()[l, d, r].rearrange("(W hp) -> hp W", hp=128))
        dt_ = ar.alloc([2, 16]); mcol = ar.alloc([2, 16]); th = ar.alloc([2, 16])
        k.act(dt_, ldt, AF.Exp)
        a_ = ar.alloc([2, 16])
        k.tt("dve", a_, lr, dt_, ALU.mult)
        k.act(mcol, a_, AF.Exp)
        k.tt("dve", th, li, dt_, ALU.mult)
        qi_ = ar.alloc([2, 16], I32); rr = ar.alloc([2, 16]); sn = ar.alloc([2, 16]); cs = ar.alloc([2, 16])
        k.ts("dve", qi_, th, 1.0 / TWO_PI)
        k.stt(rr, qi_, -TWO_PI, th, ALU.mult, ALU.add)
        k.act(sn, rr, AF.Sin, scale=SIN_SC)
        qi2 = ar.alloc([2, 16], I32); rr2 = ar.alloc([2, 16])
        k.ts("dve", qi2, th, 1.0 / TWO_PI, 0.25, ALU.mult, ALU.add)
        k.stt(rr2, qi2, -TWO_PI, th, ALU.mult, ALU.add)
        k.act(cs, rr2, AF.Sin, scale=SIN_SC, bias=hpi_c)
        abr = ar.alloc([2, 16]); abi = ar.alloc([2, 16]); den = ar.alloc([2, 16]); t1 = ar.alloc([2, 16])
        t2_ = ar.alloc([2, 16]); den2 = ar.alloc([2, 16]); rden = ar.alloc([2, 16]); nr = ar.alloc([2, 16])
        kr = ar.alloc([2, 16]); ki = ar.alloc([2, 16]); kr0 = ar.alloc([2, 16]); ki0 = ar.alloc([2, 16])
        k.tt("dve", abr, mcol, cs, ALU.mult)
        k.tt("dve", abi, mcol, sn, ALU.mult)
        k.tt("dve", den, lr, lr, ALU.mult)
        k.tt("dve", t1, li, li, ALU.mult)
        k.tt("dve", den2, den, t1, ALU.add)
        k.recip(rden, den2)
        k.ts("dve", nr, abr, -1.0, None, ALU.add)
        k.tt("dve", kr0, nr, lr, ALU.mult)
        k.tt("dve", t2_, abi, li, ALU.mult)
        k.tt("dve", kr, kr0, t2_, ALU.add)
        k.tt("dve", kr, kr, rden, ALU.mult)
        k.tt("dve", ki0, abi, lr, ALU.mult)
        k.tt("dve", t1, nr, li, ALU.mult)
        k.tt("dve", ki, ki0, t1, ALU.subtract)
        k.tt("dve", ki, ki, rden, ALU.mult)
        BL = ar.alloc([2, 2, 4, 128], BF16)
        cns = ar.alloc([2, 2, 4, 64])
        for d in range(2):
            for r, csrc in enumerate((c_re, c_im)):
                k.dma("sp", cns[:, d, r], csrc.ap()[l, d].rearrange("(q g) c p -> (g c) q p", g=8))
        ar.mark()
        for d in range(2):
            braw = ar.alloc([16, 16]); iraw = ar.alloc([16, 16])
            with nc.allow_non_contiguous_dma(reason="ssm params"):
                k.dma("sp", braw, b_re.ap()[l, d].rearrange("(W h) p c -> (h p) W c", h=2))
                k.dma("sp", iraw, b_im.ap()[l, d].rearrange("(W h) p c -> (h p) W c", h=2))
            krb = bc_last(kr[:, d, :], 16); kib = bc_last(ki[:, d, :], 16)
            bb = ar.alloc([2, 16, 16]); tq = ar.alloc([16, 16]); tq2 = ar.alloc([16, 16])
            k.tt("dve", tq2, braw, krb, ALU.mult)
            k.tt("dve", tq, iraw, kib, ALU.mult)
            k.tt("dve", bb[:, 0], tq2, tq, ALU.subtract)
            tq3 = ar.alloc([16, 16]); tq4 = ar.alloc([16, 16])
            k.tt("dve", tq3, iraw, krb, ALU.mult)
            k.tt("dve", tq4, braw, kib, ALU.mult)
            k.tt("dve", bb[:, 1], tq3, tq4, ALU.add)
            for r in range(2):
                xall = ar.alloc([16, 2, 16])
                k.memset("dve", xall, 0.0)
                k.copy("dve", xall[0:64, :, 0, :], bb[0:64, r])
                k.copy("dve", xall[64:128, :, 1, :], bb[64:128, r])
                xf = xall.rearrange("p W h c -> p (W h c)")
                pt = ps((3, 4, 5, 6, 7))
                for q in range(4):
                    k.transpose(pt[:, q * 128:(q + 1) * 128], xf[:, q * 128:(q + 1) * 128], ident)
                k.copy("act", BL[:, d, r].rearrange("p q c -> p (q c)"), pt)
        ar.release()
        ar.mark()
        tabC = ar.alloc([1024]); tabS = ar.alloc([1024])
        xr = ar.alloc([NT]); xi = ar.alloc([NT]); gr = ar.alloc([NT]); gi = ar.alloc([NT])
        ang = xr[:, 0:1024]; qi = xi[:, 0:1024].bitcast(I32)
        tm = ar.alloc([4, 512]); hr = ar.alloc([NT], BF16); hi = ar.alloc([NT], BF16)
        h32 = ar.alloc([2, 512])
        NS = ar.alloc([2, 2, 2, 16])
        ygf = ar.alloc([512])
        CLq = ar.alloc([2, 2, 4, 128], BF16)
        zall = ar.alloc([2, 128])
        seqs = [(0, 256), (256, 256), (512, 1024)]

        def tabv(tab, d, tt):
            row = _row(tab); off = int(tab.offset)
            if d == 0:
                if tt == 0:
                    return apm.AP(tab.tensor, off, [[row, 128], [0, 2], [1, 256]])
                return tab[:, (tt - 1) * 512:tt * 512]
            if tt == 0:
                return apm.AP(tab.tensor, off + 255, [[row, 128], [0, 2], [-1, 256]])
            return apm.AP(tab.tensor, off + (1023 if tt == 1 else 511), [[row, 128], [-1, 512]])

        def v3(ap, tt):
            return ap.rearrange("p (s t) -> p s t", s=2) if tt == 0 else ap

        Ybanks = (0, 1, 2)
        XP = (3, 4, 5, 6, 7)
        for q in range(4):
            k.memset("dve", CLq.rearrange("p d r w c -> p (d r w c)"), 0.0)
            for d in range(2):
                for r in range(2):
                    zz = zall[:, (d * 2 + r) % 2]
                    sgn = 1.0 if r == 0 else -1.0
                    k.ts("dve", zz[:, 0:64], cns[:, d, r, q, :], maskc[:, 0:1], sgn, ALU.mult, ALU.mult)
                    k.ts("dve", zz[:, 64:128], cns[:, d, r, q, :], maskc[:, 1:2], sgn, ALU.mult, ALU.mult)
                    pt = ps(XP)
                    k.transpose(pt[:, 0:128], zz, ident)
                    for w in range(4):
                        k.copy("act", CLq[:, d, r, w, 32 * w:32 * w + 32], pt[:, 32 * w:32 * w + 32])
            for d in range(2):
                for w in range(4):
                    W = 4 * q + w
                    thc = th[:, d, W:W + 1]; mc = mcol[:, d, W:W + 1]
                    k.ts("pool", ang, iota, thc)
                    k.ts("dve", qi, ang, 1.0 / TWO_PI)
                    k.stt(gr[:, 0:1024], qi, -TWO_PI, ang, ALU.mult, ALU.add)
                    k.act(tabS, gr[:, 0:1024], AF.Sin, scale=SIN_SC)
                    k.ts("dve", qi, ang, 1.0 / TWO_PI, 0.25, ALU.mult, ALU.add)
                    k.stt(gi[:, 0:1024], qi, -TWO_PI, ang, ALU.mult, ALU.add)
                    k.act(tabC, gi[:, 0:1024], AF.Sin, scale=SIN_SC, bias=hpi_c)
                    for tt in range(3):
                        tsl = slice(tt * 512, (tt + 1) * 512)
                        XR = ps(XP); XI = ps(XP)
                        k.mm(XR, BL[32 * w:32 * w + 32, d, 0, q, :], uT[32 * w:32 * w + 32, q, tsl],
                             tile_position=(32 * w, 0))
                        k.mm(XI, BL[32 * w:32 * w + 32, d, 1, q, :], uT[32 * w:32 * w + 32, q, tsl],
                             tile_position=(32 * w, 0))
                        C = tabv(tabC, d, tt); S = tabv(tabS, d, tt)
                        k.tt("dve", v3(tm[:, 0], tt), v3(XR, tt), C, ALU.mult)
                        k.tt("dve", v3(tm[:, 1], tt), v3(XI, tt), S, ALU.mult)
                        k.tt("pool", xr[:, tsl], tm[:, 0], tm[:, 1], ALU.add)
                        k.tt("dve", v3(tm[:, 2], tt), v3(XI, tt), C, ALU.mult)
                        k.tt("dve", v3(tm[:, 3], tt), v3(XR, tt), S, ALU.mult)
                        k.tt("pool", xi[:, tsl], tm[:, 2], tm[:, 3], ALU.subtract)
                    for si, (o, L) in enumerate(seqs):
                        for (src, dst, r) in ((xr, gr, 0), (xi, gi, 1)):
                            init = h0[:, d, r, W:W + 1] if si == 2 else 0.0
                            a_in = src[:, o:o + L]; a_out = dst[:, o:o + L]
                            if d == 1:
                                a_in = rev(a_in); a_out = rev(a_out)
                            k.scan(a_out, mc.to_broadcast([128, L]), a_in, init)
                    for tt in range(3):
                        tsl = slice(tt * 512, (tt + 1) * 512)
                        C = tabv(tabC, d, tt); S = tabv(tabS, d, tt)
                        k.tt("pool", v3(tm[:, 0], tt), v3(gr[:, tsl], tt), C, ALU.mult)
                        k.tt("pool", v3(tm[:, 1], tt), v3(gi[:, tsl], tt), S, ALU.mult)
                        k.tt("pool", v3(tm[:, 2], tt), v3(gr[:, tsl], tt), S, ALU.mult)
                        k.tt("pool", v3(tm[:, 3], tt), v3(gi[:, tsl], tt), C, ALU.mult)
                        if tt == 0:
                            k.tt("dve", h32[:, 0], tm[:, 0], tm[:, 1], ALU.subtract)
                            k.tt("dve", h32[:, 1], tm[:, 2], tm[:, 3], ALU.add)
                            k.copy("act", hr[:, tsl], h32[:, 0])
                            k.copy("act", hi[:, tsl], h32[:, 1])
                            col = 255 if d == 0 else 0
                            for r in range(2):
                                srcc = apm.AP(h32.tensor, int(h32[:, r].offset) + col, [[_row(h32), 128], [256, 2]])
                                k.copy("act", NS[:, :, d, r, W], srcc)
                        else:
                            k.tt("dve", hr[:, tsl], tm[:, 0], tm[:, 1], ALU.subtract)
                            k.tt("dve", hi[:, tsl], tm[:, 2], tm[:, 3], ALU.add)
                        Y = psum[:, Ybanks[tt], :]
                        last = (d == 1 and w == 3)
                        k.mm(Y, CLq[:, d, 0, w, :], hr[:, tsl], start=(d == 0 and w == 0), stop=False)
                        k.mm(Y, CLq[:, d, 1, w, :], hi[:, tsl], start=False, stop=last)
            for tt in range(3):
                tsl = slice(tt * 512, (tt + 1) * 512)
                Y = psum[:, Ybanks[tt], :]
                k.stt(ygf, uT[:, q, tsl], dskipT[:, l * 4 + q:l * 4 + q + 1], Y, ALU.mult, ALU.add)
                k.act(yaT[:, q, tsl], ygf, AF.Gelu_apprx_tanh)
        pt = ps(XP)
        k.transpose(pt[:, 0:128], NS.rearrange("p s d r W -> p (s d r W)"), ident)
        nst = ar.alloc([128])
        k.copy("dve", nst, pt[:, 0:128])
        for s in range(2):
            k.dma("sp", ns_out.ap()[s, l].rearrange("d r (W hp) -> (d r W) hp", hp=128), nst[64 * s:64 * s + 64, :],
                  is_output=True)
        ar.release()
        wg = ar.alloc([4, 512], BF16)
        k.dma("pool", wg, w_glu.ap()[l].rearrange("(c p) f -> p c f", p=128))
        sg = ar.alloc([4, 512])
        for tt in range(3):
            tsl = slice(tt * 512, (tt + 1) * 512)
            for fb in range(4):
                pg = ps()
                for kc in range(4):
                    k.mm(pg, wg[:, kc, fb * 128:(fb + 1) * 128], yaT[:, kc, tsl], start=(kc == 0), stop=(kc == 3))
                k.act(sg[:, fb], pg, AF.Sigmoid)
            for fb in range(4):
                k.tt("dve", yaT[:, fb, tsl], yaT[:, fb, tsl], sg[:, fb], ALU.mult)
        ar.release()

    def attn_branch(l, qT, kT, Vt, ybT):
        ar.mark()
        E = ar.alloc([2, 512], BF16)
        rz = ar.alloc([2, 512]); o = ar.alloc([512]); t0 = ar.alloc([512]); sq = ar.alloc([512], BF16)
        rstd = ar.alloc([512])
        jobs = []
        for s in range(2):
            jobs.append((s * 256, 256, [(s * 256 + j * 128, s * 2 + j) for j in range(2)]))
        for qt in range(2):
            keys = [(512 + j * 128, 4 + j) for j in range(8)] + [(1536 + j * 128, 12 + j) for j in range(4)]
            jobs.append((512 + qt * 512, 512, keys))
        for h in range(4):
            for (qo, nq, keys) in jobs:
                O = [psum[:, 0, 0:nq], psum[:, 1, 0:nq]]
                Z = [psum[:, 2, 0:nq], psum[:, 3, 0:nq]]
                for m in range(2):
                    for ki_, (ko, vb) in enumerate(keys):
                        S = ps((4, 5, 6, 7))[:, 0:nq]
                        k.mm(S, kT[64 * m:64 * m + 64, h, ko:ko + 128], qT[64 * m:64 * m + 64, h, qo:qo + nq])
                        Eb = E[:, ki_ % 2, 0:nq]
                        k.act(Eb, S, AF.Exp, scale=0.125)
                        k.mm(O[m], Vt[:, vb, h * 128:(h + 1) * 128], Eb, start=(ki_ == 0), stop=(ki_ == len(keys) - 1))
                        k.mm(Z[m], ones_bf, Eb, start=(ki_ == 0), stop=(ki_ == len(keys) - 1))
                k.recip(rz[:, 0, 0:nq], Z[0])
                k.recip(rz[:, 1, 0:nq], Z[1])
                k.tt("dve", t0[:, 0:nq], O[0], rz[:, 0, 0:nq], ALU.mult)
                k.tt("dve", o[:, 0:nq], O[1], rz[:, 1, 0:nq], ALU.mult)
                k.stt(o[:, 0:nq], o[:, 0:nq], lamc[:, l, 1:2], t0[:, 0:nq], ALU.mult, ALU.add)
                k.act(sq[:, 0:nq], o[:, 0:nq], AF.Square)
                ssq = ps((4, 5, 6, 7))[:, 0:nq]
                k.mm(ssq, ones_bf, sq[:, 0:nq])
                k.act(rstd[:, 0:nq], ssq, AF.Sqrt, scale=1.0 / 128, bias=eps_c)
                k.recip(rstd[:, 0:nq], rstd[:, 0:nq])
                k.stt(ybT[:, h, qo:qo + nq], o[:, 0:nq], gfac[:, l:l + 1], rstd[:, 0:nq], ALU.mult, ALU.mult)
        ar.release()

    def pool_branch(l, zp, ycT):
        ar.mark()
        ZW = 1600
        offs = [16, 288, 560]
        wa = ar.alloc([ZW]); wb_ = ar.alloc([ZW]); pooled = ar.alloc([4, NT], BF16); pf = ar.alloc([ZW])
        k.memset("pool", wa, 0.0); k.memset("pool", wb_, 0.0)
        lo, hi_ = 8, ZW - 8
        for g, wdw in enumerate((2, 4, 8, 16)):
            z = zp[:, g, :]
            k.tt("pool", wa[:, lo:hi_], z[:, lo - 1:hi_ - 1], z[:, lo:hi_], ALU.add)
            cur, oth = wa, wb_
            sh = 1
            ww = 2
            while ww < wdw:
                k.tt("pool", oth[:, lo:hi_], cur[:, lo - sh:hi_ - sh], cur[:, lo + sh:hi_ + sh], ALU.add)
                cur, oth = oth, cur
                sh *= 2; ww *= 2
            k.stt(pf[:, lo:hi_], cur[:, lo:hi_], 1.0 / wdw, z[:, lo:hi_], ALU.mult, ALU.subtract)
            for si, (o, L) in enumerate(((16, 256), (288, 256), (560, 1024))):
                for e in range(2):
                    c0 = o if e == 0 else o + L - 8
                    k.tt("pool", pf[:, c0:c0 + 8], cur[:, c0:c0 + 8], poolc[:, g, 0 if si < 2 else 1, e, :], ALU.mult)
                    k.tt("pool", pf[:, c0:c0 + 8], pf[:, c0:c0 + 8], z[:, c0:c0 + 8], ALU.subtract)
            for si, (o, L) in enumerate(((16, 256), (288, 256), (560, 1024))):
                to = (0, 256, 512)[si]
                k.copy("act", pooled[:, g, to:to + L], pf[:, o:o + L])
        wp = ar.alloc([4, 128], BF16)
        k.dma("pool", wp, w_pool.ap()[l].rearrange("g c d -> c g d"))
        for g in range(4):
            for tt in range(3):
                tsl = slice(tt * 512, (tt + 1) * 512)
                pp = ps()
                k.mm(pp, wp[:, g, :], pooled[:, g, tsl])
                k.act(ycT[:, g, tsl], pp, AF.Identity, scale=pscT[:, l * 4 + g:l * 4 + g + 1])
        ar.release()

    def mixer(l):
        rmsnorm_mod(l, 1)
        ar.mark()
        yaT = ar.alloc([4, NT], BF16)
        wv = w_in.ap()[l].rearrange("(c p) f -> p c f", p=128)

        def wchunk(c):
            return wload(wv[:, :, c * 512:(c + 1) * 512], [8, 512])

        def proj_fm(wc, fb, tt):
            p = ps()
            for kc in range(8):
                k.mm(p, wc[:, kc, fb * 128:(fb + 1) * 128], nT[:, kc, tt * 512:(tt + 1) * 512],
                     start=(kc == 0), stop=(kc == 7))
            return p

        def proj_tm(wc, tb):
            p = ps()
            for kc in range(8):
                k.mm(p, nT[:, kc, tb * 128:(tb + 1) * 128], wc[:, kc, :], start=(kc == 0), stop=(kc == 7))
            return p

        ar.mark()
        uT = ar.alloc([4, NT], BF16)
        ar.mark()
        wc = wchunk(0)
        for fb in range(4):
            for tt in range(3):
                p = proj_fm(wc, fb, tt)
                k.copy("act" if (fb + tt) % 2 else "dve", uT[:, fb, tt * 512:(tt + 1) * 512], p)
        ar.release()
        s5_branch(l, uT, yaT)
        ar.release()
        stage('s5', [yaT])
        ybT = ar.alloc([4, NT], BF16); ycT = ar.alloc([4, NT], BF16)
        ar.mark()
        qT = ar.alloc([4, NT], BF16); kT = ar.alloc([4, 2048], BF16); Vt = ar.alloc([16, 512], BF16)
        qraw = ar.alloc([2, 512], BF16); tr = ar.alloc([2, 512]); stg_o = ar.alloc([2, 512])
        for which, dst in ((1, qT), (2, kT)):
            ar.mark()
            wc = wchunk(which)
            for fb in range(4):
                for tt in range(3):
                    tsl = slice(tt * 512, (tt + 1) * 512)
                    p = proj_fm(wc, fb, tt)
                    if tt == 0:
                        k.copy("act", dst[:, fb, tsl], p)
                    else:
                        qb = qraw[:, (fb + tt) % 2]
                        k.copy("act", qb, p)
                        pq = ps()
                        k.mm(pq, pt_bf, qb)
                        pos = slice((tt - 1) * 512, tt * 512)
                        k.tt("dve", tr[:, 0], qb, rope[:, 0, pos], ALU.mult)
                        k.tt("dve", tr[:, 1], pq, rope[:, 1, pos], ALU.mult)
                        k.tt("pool", dst[:, fb, tsl], tr[:, 0], tr[:, 1], ALU.add)
            if which == 2:
                for tb in range(4):
                    p = proj_tm(wc, tb)
                    st_ = stg_o[:, tb % 2]
                    k.copy("dve", st_, p)
                    k.dma("sp", nk_out.ap()[tb // 2, l, (tb % 2) * 128:(tb % 2) * 128 + 128, :], st_, is_output=True)
            ar.release()
        stage('qk', [qT])
        ar.mark()
        wc = wchunk(3)
        for tb in ((4, 5, 6, 7, 8, 9, 10, 11, 0, 1, 2, 3) if 'e' in _VD else range(4 if 'a' in _VD else 12)):
            if 'c' in _VD and tb < 4:
                continue
            p = proj_tm(wc, tb)
            if 'b' in _VD:
                k.copy("act", Vt[:, tb, :], p)
                continue
            if tb >= 4:
                k.copy("act", Vt[:, tb, :], p)
            else:
                st_ = stg_o[:, tb % 2]
                k.copy("dve", st_, p)
                k.copy("act", Vt[:, tb, :], st_)
                if 'd' in _VD:
                    continue
                k.dma("pool", nv_out.ap()[tb // 2, l, (tb % 2) * 128:(tb % 2) * 128 + 128, :], st_, is_output=True)
        stage('v', [ybT])
        k.dma("pool", Vt[:, 12:16, :], cv.ap()[l].rearrange("(b p) f -> p b f", p=128))
        ckb = ar.alloc([4, 512], BF16)
        k.dma("pool", ckb, ck.ap()[l].rearrange("(b p) f -> p b f", p=128))
        for h in range(4):
            ptb = ps().bitcast(BF16)
            for b in range(4):
                k.transpose(ptb[:, b * 128:(b + 1) * 128], ckb[:, b, h * 128:(h + 1) * 128], ident_bf)
            k.copy("dve", kT[:, h, 1536:2048], ptb[:, 0:512])
        ar.release()
        stage('cache', [ybT])
        attn_branch(l, qT, kT, Vt, ybT)
        ar.release()
        stage('attn', [ybT])
        ar.mark()
        zp = ar.alloc([4, 1600])
        k.memset("pool", zp, 0.0)
        ar.mark()
        wc = wchunk(4)
        for fb in range(4):
            p = proj_fm(wc, fb, 0)
            k.copy("act", zp[:, fb, 16:272], p[:, 0:256])
            k.copy("act", zp[:, fb, 288:544], p[:, 256:512])
            for tt in (1, 2):
                p = proj_fm(wc, fb, tt)
                k.copy("act", zp[:, fb, 560 + (tt - 1) * 512:560 + tt * 512], p)
        ar.release()
        pool_branch(l, zp, ycT)
        ar.release()
        stage('pool', [ycT])
        ar.mark()
        mT = ar.alloc([8, NT], BF16)
        ys = (yaT, ybT, ycT)
        gs = ar.alloc([2, 512]); acc = ar.alloc([512]); t2 = ar.alloc([512])
        wbv = w_branch.ap()[l].rearrange("n (c p) d -> p n c d", p=128)
        for dc in range(8):
            ar.mark()
            wg_ = ar.alloc([3, 8, 128], BF16); wb3 = ar.alloc([3, 4, 128], BF16)
            for n in range(3):
                k.dma("pool", wg_[:, n], wv[:, :, 2560 + n * 1024 + dc * 128:2560 + n * 1024 + (dc + 1) * 128])
            k.dma("pool", wb3, wbv[:, :, :, dc * 128:(dc + 1) * 128])
            for tt in range(3):
                tsl = slice(tt * 512, (tt + 1) * 512)
                for n in range(3):
                    pg = ps()
                    for kc in range(8):
                        k.mm(pg, wg_[:, n, kc, :], nT[:, kc, tsl], start=(kc == 0), stop=(kc == 7))
                    pb = ps()
                    for kc in range(4):
                        k.mm(pb, wb3[:, n, kc, :], ys[n][:, kc, tsl], start=(kc == 0), stop=(kc == 3))
                    gb = gs[:, n % 2]
                    k.act(gb, pg, AF.Sigmoid)
                    if n == 0:
                        k.tt("dve", acc, gb, pb, ALU.mult)
                    elif n == 1:
                        k.tt("dve", t2, gb, pb, ALU.mult)
                        k.tt("pool", acc, acc, t2, ALU.add)
                    else:
                        k.tt("dve", t2, gb, pb, ALU.mult)
                        k.tt("pool", mT[:, dc, tsl], acc, t2, ALU.add)
            ar.release()
        wov = w_out.ap()[l].rearrange("(c p) d -> p c d", p=128)
        for half in range(2):
            ar.mark()
            wo = wload(wov[:, :, half * 512:(half + 1) * 512], [8, 512])
            for s in range(4):
                dc = half * 4 + s
                for tt in range(3):
                    tsl = slice(tt * 512, (tt + 1) * 512)
                    py = ps()
                    for kc in range(8):
                        k.mm(py, wo[:, kc, s * 128:(s + 1) * 128], mT[:, kc, tsl], start=(kc == 0), stop=(kc == 7))
                    v = vsel(tt)
                    k.stt(xT[:, dc, tsl], py, Gmod[:, l, 1, dc, v:v + 1], xT[:, dc, tsl], ALU.mult, ALU.add)
            ar.release()
        ar.release()
        ar.release()

    for l in range(depth):
        ffn(l, 0)
        stage('ffn0', [xT])
        mixer(l)
        stage('mixer', [xT])
        ffn(l, 1)

    ar.mark()
    yT = ar.alloc([8, NT])
    rmsnorm_mod(0, 3, out_fn=lambda kc, tt: yT[:, kc, tt * 512:(tt + 1) * 512])
    obufs = [ar.alloc([D]) for _ in range(2)]
    for tb in range(12):
        ob = obufs[tb % 2]
        for g4 in range(2):
            pt = ps()
            for c in range(4):
                kc = g4 * 4 + c
                k.transpose(pt[:, c * 128:(c + 1) * 128], yT[:, kc, tb * 128:(tb + 1) * 128], ident)
            k.copy("act" if g4 else "dve", ob[:, g4 * 512:(g4 + 1) * 512], pt)
        k.dma("sp", y_out.ap()[tb * 128:(tb + 1) * 128, :], ob, is_output=True)
    ar.release()
    k.finish()
    return k


def _consts():
    c = {}
    c["c_ident"] = np.eye(128, dtype=np.float32)
    inv = (10000.0 ** (-np.arange(16, dtype=np.float32) / 16)).astype(np.float32)
    t = np.arange(1024)
    row = (t // 64).astype(np.float32); col = (t % 64).astype(np.float32)
    rope = np.zeros((128, 2, 1024), np.float32)
    pt = np.zeros((128, 128), np.float32)
    for m in range(2):
        for d in range(64):
            p = m * 64 + d
            pos = row if d < 32 else col
            ang = (pos * inv[d % 16]).astype(np.float32)
            rope[p, 0] = np.cos(ang); rope[p, 1] = np.sin(ang)
            dd = d % 32
            if dd < 16:
                pt[p + 16, p] = -1.0
            else:
                pt[p - 16, p] = 1.0
    c["c_rope"] = rope; c["c_pt"] = pt
    c["c_iota"] = np.broadcast_to(np.arange(1, 1025, dtype=np.float32), (128, 1024)).copy()
    mk = np.zeros((128, 2), np.float32)
    for p in range(128):
        g = p // 16
        mk[p, g % 2] = 1.0
    c["c_mask"] = mk
    pc = np.zeros((128, 4, 2, 2, 8), np.float32)
    for g, w in enumerate((2, 4, 8, 16)):
        for lt, L in enumerate((256, 1024)):
            for e in range(2):
                for j in range(8):
                    t_ = j if e == 0 else L - 8 + j
                    lo = min(max(t_ - w // 2, 0), L); hi = min(max(t_ - w // 2 + w, 0), L)
                    pc[:, g, lt, e, j] = 1.0 / (hi - lo)
    c["c_pool"] = pc
    return c


_W_NAMES = ["norm_g", "w_mod", "b_mod", "w_ffn_in", "w_ffn_out", "w_in", "ssm_lam_re", "ssm_lam_im", "ssm_log_dt",
            "ssm_b_re", "ssm_b_im", "ssm_c_re", "ssm_c_im", "ssm_d", "w_glu", "lam_q1", "lam_k1", "lam_q2", "lam_k2",
            "attn_norm_g", "w_pool", "pool_scale", "w_branch", "w_out", "final_norm_g"]


def kernel(**inp):
    nc = bass.Bass("TRN2", target_bir_lowering=False)
    build(nc, depth=int(os.environ.get('KDEPTH', DEPTH)), stop=os.environ.get('KSTOP') or None)
    consts = _consts()
    f = lambda a: np.ascontiguousarray(np.asarray(a, dtype=np.float32))
    xp = f(inp["x_prompt"]); xs = f(inp["x_sample"])
    ck = f(inp["cache_k"]); cv = f(inp["cache_v"]); st = f(inp["state_ssm"]); c = f(inp["c"]); cctx = f(inp["c_ctx"])
    wts = {n: f(inp[n]) for n in _W_NAMES}
    in_maps = []
    for i in range(8):
        m = dict(wts); m.update(consts)
        m["xin"] = np.concatenate([xp[2 * i], xp[2 * i + 1], xs[i]], axis=0)
        m["ck"] = ck[i].reshape(DEPTH, 512, 512)
        m["cv"] = cv[i].reshape(DEPTH, 512, 512)
        m["st"] = st[i].reshape(DEPTH, 2, 2, 2048)
        m["cvec"] = np.stack([cctx, c[i]], axis=0)
        in_maps.append(m)
    res = run_bass_kernel_spmd(nc, in_maps, core_ids=list(range(8)))
    R = res.results
    y_prompt = np.stack([R[i // 2]["y"][(i % 2) * 256:(i % 2) * 256 + 256] for i in range(16)], axis=0)
    y_sample = np.stack([R[i]["y"][512:1536] for i in range(8)], axis=0)
    nk = np.concatenate([R[i]["nk"] for i in range(8)], axis=0).reshape(16, DEPTH, 256, 4, 2, 64)
    nv = np.concatenate([R[i]["nv"] for i in range(8)], axis=0).reshape(16, DEPTH, 256, 4, 128)
    ns = np.concatenate([R[i]["ns"] for i in range(8)], axis=0).reshape(16, DEPTH, 2, 2, 32, 64)
    return (y_prompt.astype(np.float32), y_sample.astype(np.float32), nk.astype(np.float32), nv.astype(np.float32),
            ns.astype(np.float32))
```

```python
import math, os
import numpy as np
_VD = os.environ.get('VDBG', '')
import concourse.bass as bass
import concourse.mybir as mybir
import concourse.ap as apm
from concourse.bass_utils import run_bass_kernel_spmd

F32 = mybir.dt.float32
BF16 = mybir.dt.bfloat16
I32 = mybir.dt.int32
AF = mybir.ActivationFunctionType
ALU = mybir.AluOpType
AX = mybir.AxisListType
_ES = {str(F32): 4, str(BF16): 2, str(I32): 4}

D = 1024; NT = 1536; DFF = 2816; INW = 5632; DEPTH = 4
EPS = 1e-6
TWO_PI = 2.0 * math.pi
SIN_SC = 1.0 - 2e-4


class _Rec:
    __slots__ = ("eng", "sem", "val", "write", "p0", "p1", "f0", "f1")

    def __init__(self, eng, sem, val, write, box):
        self.eng = eng; self.sem = sem; self.val = val; self.write = write
        self.p0, self.p1, self.f0, self.f1 = box


def _box(ap):
    pat = ap.ap
    es = _ES[str(ap.dtype)]
    off = int(ap.offset)
    row = pat[0][0]
    if row == 0:
        p0 = 0; f = off
    else:
        p0 = off // row; f = off - p0 * row
    p1 = p0 + pat[0][1]
    lo = f; hi = f
    for st, cnt in pat[1:]:
        if st >= 0:
            hi += st * (cnt - 1)
        else:
            lo += st * (cnt - 1)
    return (ap.tensor.name, p0, p1, lo * es, (hi + 1) * es)


class K:
    def __init__(self, nc, n_dma_sems=32):
        self.nc = nc
        self.engs = {"pe": nc.tensor, "act": nc.scalar, "dve": nc.vector, "pool": nc.gpsimd, "sp": nc.sync}
        self.sem = {}; self.cnt = {}
        for e in ("pe", "act", "dve", "pool"):
            self.sem[e] = nc.semaphore("s_" + e).__enter__()
            self.cnt[e] = 0
        self.dsem = [nc.semaphore("d%d" % i).__enter__() for i in range(n_dma_sems)]
        self.dval = [0] * n_dma_sems
        self.dnext = 0
        self.known = {e: {} for e in self.engs}
        self.recs = {}
        self.out_waits = []
        self.n_inst = {e: 0 for e in self.engs}
        self.n_wait = 0

    def _need(self, eng, sem, val):
        if eng == "pe" and sem is self.sem["pe"]:
            return
        kn = self.known[eng]
        key = sem.name
        if kn.get(key, 0) >= val:
            return
        kn[key] = val
        self.engs[eng].wait_ge(sem, val)
        self.n_wait += 1

    def _tracked(self, ap):
        if ap is None or isinstance(ap, (int, float)):
            return False
        if str(ap.space) == "DRAM" and ap.tensor.name not in self.recs:
            return False
        return True

    def _deps(self, eng, reads, writes):
        for ap in reads:
            if not self._tracked(ap):
                continue
            name, p0, p1, f0, f1 = _box(ap)
            for r in self.recs.get(name, ()):
                if r.write and r.p0 < p1 and p0 < r.p1 and r.f0 < f1 and f0 < r.f1:
                    self._need(eng, r.sem, r.val)
        for ap in writes:
            if not self._tracked(ap):
                continue
            name, p0, p1, f0, f1 = _box(ap)
            for r in self.recs.get(name, ()):
                if r.p0 < p1 and p0 < r.p1 and r.f0 < f1 and f0 < r.f1:
                    self._need(eng, r.sem, r.val)

    def _record(self, eng, sem, val, reads, writes):
        for ap in writes:
            if not self._tracked(ap):
                continue
            name, p0, p1, f0, f1 = _box(ap)
            lst = self.recs.setdefault(name, [])
            lst[:] = [r for r in lst if not (p0 <= r.p0 and r.p1 <= p1 and f0 <= r.f0 and r.f1 <= f1)]
            lst.append(_Rec(eng, sem, val, True, (p0, p1, f0, f1)))
        for ap in reads:
            if not self._tracked(ap):
                continue
            name, p0, p1, f0, f1 = _box(ap)
            lst = self.recs.setdefault(name, [])
            lst[:] = [r for r in lst if not ((not r.write) and r.eng == eng and p0 <= r.p0 and r.p1 <= p1
                                              and f0 <= r.f0 and r.f1 <= f1)]
            lst.append(_Rec(eng, sem, val, False, (p0, p1, f0, f1)))

    def track_dram(self, t):
        self.recs.setdefault(t.name, [])

    def _fin(self, eng, inst, reads, writes, inc=True):
        self.n_inst[eng] += 1
        if inc:
            self.cnt[eng] += 1
            inst.then_inc(self.sem[eng], 1)
            self._record(eng, self.sem[eng], self.cnt[eng], reads, writes)
        else:
            self._record(eng, self.sem[eng], self.cnt[eng] + 1, reads, writes)

    def mm(self, out, lhsT, rhs, start=True, stop=True, **kw):
        self._deps("pe", [lhsT, rhs], [out])
        i = self.nc.tensor.matmul(out, lhsT, rhs, start=start, stop=stop, **kw)
        self._fin("pe", i, [lhsT, rhs], [out], inc=stop)

    def transpose(self, out, in_, ident):
        self._deps("pe", [in_, ident], [out])
        i = self.nc.tensor.transpose(out, in_, ident)
        self._fin("pe", i, [in_, ident], [out])

    def act(self, out, in_, func, scale=1.0, bias=None):
        rd = [in_] + [a for a in (scale, bias) if a is not None and not isinstance(a, (int, float))]
        self._deps("act", rd, [out])
        kw = {}
        if bias is not None:
            kw["bias"] = bias
        i = self.nc.scalar.activation(out=out, in_=in_, func=func, scale=scale, **kw)
        self._fin("act", i, rd, [out])

    def _ve(self, eng):
        return self.nc.vector if eng == "dve" else self.nc.gpsimd

    def tt(self, eng, out, in0, in1, op):
        self._deps(eng, [in0, in1], [out])
        i = self._ve(eng).tensor_tensor(out, in0, in1, op)
        self._fin(eng, i, [in0, in1], [out])

    def ts(self, eng, out, in0, s1, s2=None, op0=ALU.mult, op1=None):
        rd = [in0] + [a for a in (s1, s2) if a is not None and not isinstance(a, (int, float))]
        self._deps(eng, rd, [out])
        if op1 is not None:
            i = self._ve(eng).tensor_scalar(out, in0, s1, s2, op0, op1)
        else:
            i = self._ve(eng).tensor_scalar(out, in0, s1, None, op0)
        self._fin(eng, i, rd, [out])

    def stt(self, out, in0, scalar, in1, op0, op1):
        rd = [in0, in1] + ([scalar] if not isinstance(scalar, (int, float)) else [])
        self._deps("dve", rd, [out])
        i = self.nc.vector.scalar_tensor_tensor(out, in0, scalar, in1, op0, op1)
        self._fin("dve", i, rd, [out])

    def scan(self, out, d0, d1, initial):
        rd = [d0, d1] + ([initial] if not isinstance(initial, (int, float)) else [])
        self._deps("dve", rd, [out])
        i = self.nc.vector.tensor_tensor_scan(out, d0, d1, initial, ALU.mult, ALU.add)
        self._fin("dve", i, rd, [out])

    def copy(self, eng, out, in_):
        if eng == "act":
            return self.act(out, in_, AF.Copy)
        self._deps(eng, [in_], [out])
        i = self._ve(eng).tensor_copy(out, in_)
        self._fin(eng, i, [in_], [out])

    def memset(self, eng, ap, val):
        self._deps(eng, [], [ap])
        i = self._ve(eng).memset(ap, val)
        self._fin(eng, i, [], [ap])

    def recip(self, out, in_):
        self._deps("dve", [in_], [out])
        i = self.nc.vector.reciprocal(out, in_)
        self._fin("dve", i, [in_], [out])

    def dma(self, q, out, in_, is_output=False, **kw):
        s = self.dnext
        self.dnext = (self.dnext + 1) % len(self.dsem)
        sem = self.dsem[s]
        if self.dval[s] > 0:
            self._need(q, sem, self.dval[s])
        self._deps(q, [in_], [out])
        self.dval[s] += 16
        self.engs[q].dma_start(out=out, in_=in_, **kw).then_inc(sem, 16)
        self.n_inst[q] += 1
        self._record("dma%d" % s, sem, self.dval[s], [in_], [out])
        if is_output:
            self.out_waits.append((s, self.dval[s]))

    def finish(self):
        last = {}
        for s, v in self.out_waits:
            last[s] = max(last.get(s, 0), v)
        for s, v in last.items():
            self._need("sp", self.dsem[s], v)
        for e in ("pe", "act", "dve", "pool"):
            if self.cnt[e] > 0:
                self._need("sp", self.sem[e], self.cnt[e])


class Arena:
    def __init__(self, nc, nwords):
        self.t = nc.sbuf_tensor("arena", [128, nwords], F32).__enter__()
        self.n = nwords; self.top = 0; self.stack = []

    def alloc(self, free_shape, dt=F32):
        n = 1
        for s in free_shape:
            n *= s
        words = n if _ES[str(dt)] == 4 else (n + 1) // 2
        words = (words + 7) // 8 * 8
        off = self.top
        self.top += words
        assert self.top <= self.n, "arena overflow %d > %d" % (self.top, self.n)
        ap = self.t[:, off:off + words]
        if dt != F32:
            ap = ap.bitcast(dt)
        ap = ap[:, 0:n]
        if len(free_shape) == 2:
            ap = ap.rearrange("p (a b) -> p a b", a=free_shape[0])
        elif len(free_shape) == 3:
            ap = ap.rearrange("p (a b c) -> p a b c", a=free_shape[0], b=free_shape[1])
        elif len(free_shape) == 4:
            ap = ap.rearrange("p (a b c d) -> p a b c d", a=free_shape[0], b=free_shape[1], c=free_shape[2])
        return ap

    def mark(self):
        self.stack.append(self.top)

    def release(self):
        self.top = self.stack.pop()


def _row(ap):
    return ap.ap[0][0]


def rev(ap):
    pat = ap.ap
    assert len(pat) == 2
    st, n = pat[1]
    return apm.AP(ap.tensor, int(ap.offset) + (n - 1) * st, [list(pat[0]), [-st, n]])


class _Stop(Exception):
    pass


def build(nc, depth=DEPTH, dbg=False, stop=None):
    try:
        return _build(nc, depth, dbg, stop)
    except _Stop as e:
        e.args[0].finish()
        return e.args[0]


def _build(nc, depth, dbg, stop):
    k = K(nc)

    dbg_t = {}

    def stage(name, dumps=()):
        if stop != name:
            return
        off = 0
        ar.mark()
        scr = ar.alloc([2048])
        for ap in dumps:
            flat = ap
            if len(ap.shape) == 3:
                flat = ap.rearrange("p a b -> p (a b)")
            elif len(ap.shape) == 4:
                flat = ap.rearrange("p a b c -> p (a b c)")
            n = flat.shape[1]
            for c0 in range(0, n, 2048):
                c1 = min(n, c0 + 2048)
                k.copy("dve", scr[:, 0:c1 - c0], flat[:, c0:c1])
                k.dma("sp", dbg_t["d"].ap()[:, off + c0:off + c1], scr[:, 0:c1 - c0], is_output=True)
            off += n
        raise _Stop(k)

    ar = Arena(nc, 53000)
    psum = nc.psum_tensor("ps", [128, 8, 512], F32).__enter__()
    st_ps = {"i": 0}

    def ps(pool=(0, 1, 2, 3, 4, 5, 6, 7)):
        st_ps["i"] += 1
        return psum[:, pool[st_ps["i"] % len(pool)], :]

    def din(name, shape, dt=F32):
        return nc.dram_tensor(name, list(shape), dt, kind="ExternalInput")

    xin = din("xin", [NT, D]); ck = din("ck", [DEPTH, 512, 512]); cv = din("cv", [DEPTH, 512, 512])
    stt_in = din("st", [DEPTH, 2, 2, 2048]); cvec = din("cvec", [2, D])
    norm_g = din("norm_g", [DEPTH, 3, D]); w_mod = din("w_mod", [DEPTH, D, 9 * D]); b_mod = din("b_mod", [DEPTH, 9 * D])
    w_ffn_in = din("w_ffn_in", [DEPTH, 2, D, 2 * DFF]); w_ffn_out = din("w_ffn_out", [DEPTH, 2, DFF, D])
    w_in = din("w_in", [DEPTH, D, INW])
    lam_re = din("ssm_lam_re", [DEPTH, 2, 32, 64]); lam_im = din("ssm_lam_im", [DEPTH, 2, 32, 64])
    log_dt = din("ssm_log_dt", [DEPTH, 2, 32])
    b_re = din("ssm_b_re", [DEPTH, 2, 32, 64, 16]); b_im = din("ssm_b_im", [DEPTH, 2, 32, 64, 16])
    c_re = din("ssm_c_re", [DEPTH, 2, 32, 16, 64]); c_im = din("ssm_c_im", [DEPTH, 2, 32, 16, 64])
    ssm_d = din("ssm_d", [DEPTH, 512]); w_glu = din("w_glu", [DEPTH, 512, 512])
    lq1 = din("lam_q1", [DEPTH, 64]); lk1 = din("lam_k1", [DEPTH, 64])
    lq2 = din("lam_q2", [DEPTH, 64]); lk2 = din("lam_k2", [DEPTH, 64])
    attn_g = din("attn_norm_g", [DEPTH, 128]); w_pool = din("w_pool", [DEPTH, 4, 128, 128])
    pool_scale = din("pool_scale", [DEPTH, 512]); w_branch = din("w_branch", [DEPTH, 3, 512, D])
    w_out = din("w_out", [DEPTH, D, D]); final_g = din("final_norm_g", [D])
    c_ident = din("c_ident", [128, 128]); c_rope = din("c_rope", [128, 2, 1024]); c_pt = din("c_pt", [128, 128])
    c_iota = din("c_iota", [128, 1024]); c_mask = din("c_mask", [128, 2]); c_pool = din("c_pool", [128, 4, 2, 2, 8])

    def dout(name, shape):
        return nc.dram_tensor(name, list(shape), F32, kind="ExternalOutput")

    y_out = dout("y", [NT, D]); nk_out = dout("nk", [2, DEPTH, 256, 512]); nv_out = dout("nv", [2, DEPTH, 256, 512])
    ns_out = dout("ns", [2, DEPTH, 2, 2, 2048])
    if stop is not None:
        dbg_t["d"] = dout("dbg", [128, 32768])

    ident = ar.alloc([128]); ident_bf = ar.alloc([128], BF16); ones_bf = ar.alloc([128], BF16)
    ones_f = ar.alloc([128]); pt_bf = ar.alloc([128], BF16)
    rope = ar.alloc([2, 1024]); iota = ar.alloc([1024]); maskc = ar.alloc([2]); poolc = ar.alloc([4, 2, 2, 8])
    cst = ar.alloc([8])
    k.dma("sp", ident, c_ident.ap())
    k.dma("sp", rope, c_rope.ap())
    k.dma("sp", iota, c_iota.ap())
    k.dma("sp", maskc, c_mask.ap())
    k.dma("sp", poolc, c_pool.ap())
    tmp_pt = ar.alloc([128])
    k.dma("sp", tmp_pt, c_pt.ap())
    k.copy("dve", ident_bf, ident)
    k.copy("dve", pt_bf, tmp_pt)
    k.memset("dve", ones_bf, 1.0)
    k.memset("dve", ones_f, 1.0)
    k.memset("dve", cst[:, 0:1], EPS)
    k.memset("dve", cst[:, 1:2], 0.5 * math.pi * SIN_SC)
    k.memset("dve", cst[:, 2:3], 0.0)
    eps_c = cst[:, 0:1]; hpi_c = cst[:, 1:2]

    xT = ar.alloc([8, NT])
    nT = ar.alloc([8, NT], BF16)
    modT = ar.alloc([DEPTH, 72, 2])
    Amod = ar.alloc([DEPTH, 3, 8, 2])
    Gmod = ar.alloc([DEPTH, 3, 8, 2])
    normgT = ar.alloc([96])
    smallT = ar.alloc([64])
    lamc = ar.alloc([DEPTH, 2])
    gfac = ar.alloc([DEPTH])
    sc = ar.alloc([2, 8])

    ar.mark()
    stg = ar.alloc([128])
    k.dma("sp", stg[0:96, :], norm_g.ap().rearrange("l j (c p) -> (l j c) p", p=128))
    pt = ps()
    k.transpose(pt[:, 0:96], stg[0:96, :], ident[0:96, 0:96])
    k.copy("dve", normgT, pt[:, 0:96])
    stg2 = ar.alloc([128])
    k.dma("sp", stg2[0:16, :], ssm_d.ap().rearrange("l (c p) -> (l c) p", p=128))
    k.dma("sp", stg2[16:32, :], pool_scale.ap().rearrange("l (c p) -> (l c) p", p=128))
    k.dma("sp", stg2[32:36, :], attn_g.ap())
    k.dma("sp", stg2[36:44, :], final_g.ap().rearrange("(c p) -> c p", p=128))
    k.dma("sp", stg2[44:60, :], cvec.ap().rearrange("v (c p) -> (v c) p", p=128))
    pt = ps()
    k.transpose(pt[:, 0:60], stg2[0:60, :], ident[0:60, 0:60])
    k.copy("dve", smallT[:, 0:60], pt[:, 0:60])
    dskipT = smallT[:, 0:16]; pscT = smallT[:, 16:32]; attngT = smallT[:, 32:36]; fingT = smallT[:, 36:44]
    k.act(sc.rearrange("p v c -> p (v c)"), smallT[:, 44:60], AF.Silu)
    bmodT = ar.alloc([DEPTH, 72])
    for l in range(DEPTH):
        stg3 = ar.alloc([128])
        k.dma("sp", stg3[0:72, :], b_mod.ap()[l].rearrange("(c p) -> c p", p=128))
        pt = ps()
        k.transpose(pt[:, 0:72], stg3[0:72, :], ident[0:72, 0:72])
        k.copy("dve", bmodT[:, l, :], pt[:, 0:72])
    lv = ar.alloc([4, DEPTH])
    with nc.allow_non_contiguous_dma(reason="tiny"):
        for i, t in enumerate((lq1, lk1, lq2, lk2)):
            k.dma("sp", lv[0:64, i, :], t.ap().rearrange("l d -> d l"))
    lp = ar.alloc([2, DEPTH])
    k.tt("dve", lp[0:64, 0, :], lv[0:64, 0, :], lv[0:64, 1, :], ALU.mult)
    k.tt("dve", lp[0:64, 1, :], lv[0:64, 2, :], lv[0:64, 3, :], ALU.mult)
    pt = ps()
    k.mm(pt[:, 0:2 * DEPTH], ones_f[0:64, :], lp[0:64].rearrange("p a l -> p (a l)"))
    le = ar.alloc([2, DEPTH])
    k.act(le.rearrange("p a l -> p (a l)"), pt[:, 0:2 * DEPTH], AF.Exp)
    for l in range(DEPTH):
        li_ = 0.8 - 0.6 * math.exp(-0.3 * l)
        k.stt(lamc[:, l, 0:1], le[:, 0, l:l + 1], li_, le[:, 1, l:l + 1], ALU.add, ALU.subtract)
        k.ts("dve", lamc[:, l, 1:2], lamc[:, l, 0:1], -1.0)
        k.ts("dve", gfac[:, l:l + 1], attngT[:, l:l + 1], 1.0 - li_)

    wms = [ar.alloc([8, 256]) for _ in range(2)]
    for l in range(depth):
        mps = ps()
        for fc in range(36):
            wm = wms[fc % 2]
            k.dma("sp", wm, w_mod.ap()[l].rearrange("(c p) f -> p c f", p=128)[:, :, fc * 256:(fc + 1) * 256])
            for s in range(2):
                fb = fc * 2 + s
                for kc in range(8):
                    k.mm(mps[:, fb * 2:fb * 2 + 2], wm[:, kc, s * 128:(s + 1) * 128], sc[:, :, kc],
                         start=(kc == 0), stop=(kc == 7))
        for v in range(2):
            k.tt("dve", modT[:, l, :, v], mps[:, 0:144].rearrange("p (f v) -> p f v", v=2)[:, :, v], bmodT[:, l, :], ALU.add)
        for j in range(3):
            for v in range(2):
                k.stt(Amod[:, l, j, :, v], modT[:, l, (3 * j + 1) * 8:(3 * j + 2) * 8, v], 1.0,
                      normgT[:, l * 24 + j * 8:l * 24 + j * 8 + 8], ALU.add, ALU.mult)
                k.ts("dve", Gmod[:, l, j, :, v], modT[:, l, (3 * j + 2) * 8:(3 * j + 3) * 8, v],
                     1.0 if j == 1 else 0.5)
    ar.release()

    ar.mark()
    for tb in range(12):
        ar.mark()
        xs = ar.alloc([D])
        k.dma("sp", xs, xin.ap()[tb * 128:(tb + 1) * 128, :])
        for g4 in range(2):
            pt = ps()
            for c in range(4):
                kc = g4 * 4 + c
                k.transpose(pt[:, c * 128:(c + 1) * 128], xs[:, kc * 128:(kc + 1) * 128], ident)
            k.copy("act" if g4 else "dve", xT[:, g4 * 4:(g4 + 1) * 4, tb * 128:(tb + 1) * 128],
                   pt.rearrange("p (c t) -> p c t", c=4))
        ar.release()
    ar.release()

    stage('prologue', [modT, xT])

    def vsel(tt):
        return 0 if tt == 0 else 1

    def rmsnorm_mod(l, j, A_ap=None, shift_ap=None, out_fn=None):
        ar.mark()
        sq = ar.alloc([8, 512], BF16)
        rstd = ar.alloc([512]); tmp = ar.alloc([2, 512])
        for tt in range(3):
            tsl = slice(tt * 512, (tt + 1) * 512)
            for kc in range(8):
                k.act(sq[:, kc, :], xT[:, kc, tsl], AF.Square)
            ss = ps()
            for kc in range(8):
                k.mm(ss, ones_bf, sq[:, kc, :], start=(kc == 0), stop=(kc == 7))
            k.act(rstd, ss, AF.Sqrt, scale=1.0 / D, bias=eps_c)
            k.recip(rstd, rstd)
            v = vsel(tt)
            for kc in range(8):
                if j < 3:
                    a_col = Amod[:, l, j, kc, v:v + 1]
                    b_col = modT[:, l, 3 * j * 8 + kc, v:v + 1]
                    tb_ = tmp[:, kc % 2, :]
                    k.stt(tb_, xT[:, kc, tsl], a_col, rstd, ALU.mult, ALU.mult)
                    k.act(nT[:, kc, tsl], tb_, AF.Identity, bias=b_col)
                else:
                    k.stt(out_fn(kc, tt), xT[:, kc, tsl], fingT[:, kc:kc + 1], rstd, ALU.mult, ALU.mult)
        ar.release()

    def wload(src_ap, shape, dt=BF16):
        buf = ar.alloc(shape, dt)
        k.dma("pool" if dt == BF16 else "sp", buf, src_ap)
        return buf

    def ffn(l, i):
        rmsnorm_mod(l, 0 if i == 0 else 2)
        j = 0 if i == 0 else 2
        ar.mark()
        gT = ar.alloc([22, NT], BF16)
        sa = ar.alloc([2, 512])
        wv = w_ffn_in.ap()[l, i].rearrange("(c p) f -> p c f", p=128)
        wbufs = [ar.alloc([2, 8, 256], BF16) for _ in range(2)]
        for c in range(11):
            wb = wbufs[c % 2]
            k.dma("pool", wb[:, 0], wv[:, :, c * 256:(c + 1) * 256])
            k.dma("pool", wb[:, 1], wv[:, :, DFF + c * 256:DFF + (c + 1) * 256])
            for s in range(2):
                fb = 2 * c + s
                for tt in range(3):
                    tsl = slice(tt * 512, (tt + 1) * 512)
                    pa = ps(); pb = ps()
                    for kc in range(8):
                        k.mm(pa, wb[:, 0, kc, s * 128:(s + 1) * 128], nT[:, kc, tsl], start=(kc == 0), stop=(kc == 7))
                    for kc in range(8):
                        k.mm(pb, wb[:, 1, kc, s * 128:(s + 1) * 128], nT[:, kc, tsl], start=(kc == 0), stop=(kc == 7))
                    sab = sa[:, (fb * 3 + tt) % 2, :]
                    k.act(sab, pa, AF.Silu)
                    k.tt("dve", gT[:, fb, tsl], sab, pb, ALU.mult)
        wov = w_ffn_out.ap()[l, i].rearrange("(f p) d -> p f d", p=128)
        wobufs = [ar.alloc([22, 128], BF16) for _ in range(2)]
        for dc in range(8):
            wo = wobufs[dc % 2]
            k.dma("pool", wo, wov[:, :, dc * 128:(dc + 1) * 128])
            for tt in range(3):
                tsl = slice(tt * 512, (tt + 1) * 512)
                py = ps()
                for fb in range(22):
                    k.mm(py, wo[:, fb, :], gT[:, fb, tsl], start=(fb == 0), stop=(fb == 21))
                v = vsel(tt)
                k.stt(xT[:, dc, tsl], py, Gmod[:, l, j, dc, v:v + 1], xT[:, dc, tsl], ALU.mult, ALU.add)
        ar.release()

    def bc_last(ap2, n):
        pat = ap2.ap
        return apm.AP(ap2.tensor, int(ap2.offset), [list(pat[0]), list(pat[1]), [0, n]])

    def s5_branch(l, uT, yaT):
        ar.mark()
        lr = ar.alloc([2, 16]); li = ar.alloc([2, 16]); ldt = ar.alloc([2, 16])
        h0 = ar.alloc([2, 2, 16])
        with nc.allow_non_contiguous_dma(reason="ssm params"):
            for d in range(2):
                k.dma("sp", lr[:, d, :], lam_re.ap()[l, d].rearrange("(W h) p -> (h p) W", h=2))
                k.dma("sp", li[:, d, :], lam_im.ap()[l, d].rearrange("(W h) p -> (h p) W", h=2))
                for h in range(2):
                    src = apm.AP(log_dt, (l * 2 + d) * 32 + h, [[0, 64], [2, 16]])
                    k.dma("sp", ldt[64 * h:64 * h + 64, d, :], src)
                for r in range(2):
                    k.dma("sp", h0[:, d, r, :], stt_in.ap()[l, d, r].rearrange("(W hp) -> hp W", hp=128))
        dt_ = ar.alloc([2, 16]); mcol = ar.alloc([2, 16]); th = ar.alloc([2, 16])
        k.act(dt_, ldt, AF.Exp)
        a_ = ar.alloc([2, 16])
        k.tt("dve", a_, lr, dt_, ALU.mult)
        k.act(mcol, a_, AF.Exp)
        k.tt("dve", th, li, dt_, ALU.mult)
        qi_ = ar.alloc([2, 16], I32); rr = ar.alloc([2, 16]); sn = ar.alloc([2, 16]); cs = ar.alloc([2, 16])
        k.ts("dve", qi_, th, 1.0 / TWO_PI)
        k.stt(rr, qi_, -TWO_PI, th, ALU.mult, ALU.add)
        k.act(sn, rr, AF.Sin, scale=SIN_SC)
        qi2 = ar.alloc([2, 16], I32); rr2 = ar.alloc([2, 16])
        k.ts("dve", qi2, th, 1.0 / TWO_PI, 0.25, ALU.mult, ALU.add)
        k.stt(rr2, qi2, -TWO_PI, th, ALU.mult, ALU.add)
        k.act(cs, rr2, AF.Sin, scale=SIN_SC, bias=hpi_c)
        abr = ar.alloc([2, 16]); abi = ar.alloc([2, 16]); den = ar.alloc([2, 16]); t1 = ar.alloc([2, 16])
        t2_ = ar.alloc([2, 16]); den2 = ar.alloc([2, 16]); rden = ar.alloc([2, 16]); nr = ar.alloc([2, 16])
        kr = ar.alloc([2, 16]); ki = ar.alloc([2, 16]); kr0 = ar.alloc([2, 16]); ki0 = ar.alloc([2, 16])
        k.tt("dve", abr, mcol, cs, ALU.mult)
        k.tt("dve", abi, mcol, sn, ALU.mult)
        k.tt("dve", den, lr, lr, ALU.mult)
        k.tt("dve", t1, li, li, ALU.mult)
        k.tt("dve", den2, den, t1, ALU.add)
        k.recip(rden, den2)
        k.ts("dve", nr, abr, -1.0, None, ALU.add)
        k.tt("dve", kr0, nr, lr, ALU.mult)
        k.tt("dve", t2_, abi, li, ALU.mult)
        k.tt("dve", kr, kr0, t2_, ALU.add)
        k.tt("dve", kr, kr, rden, ALU.mult)
        k.tt("dve", ki0, abi, lr, ALU.mult)
        k.tt("dve", t1, nr, li, ALU.mult)
        k.tt("dve", ki, ki0, t1, ALU.subtract)
        k.tt("dve", ki, ki, rden, ALU.mult)
        BL = ar.alloc([2, 2, 4, 128], BF16)
        cns = ar.alloc([2, 2, 4, 64])
        for d in range(2):
            for r, csrc in enumerate((c_re, c_im)):
                k.dma("sp", cns[:, d, r], csrc.ap()[l, d].rearrange("(q g) c p -> (g c) q p", g=8))
        ar.mark()
        for d in range(2):
            braw = ar.alloc([16, 16]); iraw = ar.alloc([16, 16])
            with nc.allow_non_contiguous_dma(reason="ssm params"):
                k.dma("sp", braw, b_re.ap()[l, d].rearrange("(W h) p c -> (h p) W c", h=2))
                k.dma("sp", iraw, b_im.ap()[l, d].rearrange("(W h) p c -> (h p) W c", h=2))
            krb = bc_last(kr[:, d, :], 16); kib = bc_last(ki[:, d, :], 16)
            bb = ar.alloc([2, 16, 16]); tq = ar.alloc([16, 16]); tq2 = ar.alloc([16, 16])
            k.tt("dve", tq2, braw, krb, ALU.mult)
            k.tt("dve", tq, iraw, kib, ALU.mult)
            k.tt("dve", bb[:, 0], tq2, tq, ALU.subtract)
            tq3 = ar.alloc([16, 16]); tq4 = ar.alloc([16, 16])
            k.tt("dve", tq3, iraw, krb, ALU.mult)
            k.tt("dve", tq4, braw, kib, ALU.mult)
            k.tt("dve", bb[:, 1], tq3, tq4, ALU.add)
            for r in range(2):
                xall = ar.alloc([16, 2, 16])
                k.memset("dve", xall, 0.0)
                k.copy("dve", xall[0:64, :, 0, :], bb[0:64, r])
                k.copy("dve", xall[64:128, :, 1, :], bb[64:128, r])
                xf = xall.rearrange("p W h c -> p (W h c)")
                pt = ps((3, 4, 5, 6, 7))
                for q in range(4):
                    k.transpose(pt[:, q * 128:(q + 1) * 128], xf[:, q * 128:(q + 1) * 128], ident)
                k.copy("act", BL[:, d, r].rearrange("p q c -> p (q c)"), pt)
        ar.release()
        ar.mark()
        nTw = nT.rearrange("p a b -> p (a b)").bitcast(F32).rearrange("p (a b) -> p a b", a=4)
        sets = []
        for s_ in range(2):
            st = {}
            st["tabC"] = ar.alloc([1024]); st["tabS"] = ar.alloc([1024])
            st["xr"] = nTw[:, 2 * s_, :]; st["xi"] = nTw[:, 2 * s_ + 1, :]
            st["tm"] = ar.alloc([4, 512]); st["pr"] = ar.alloc([4, NT], BF16)
            st["h32"] = ar.alloc([2, 512])
            sets.append(st)
        NS = ar.alloc([2, 2, 2, 16])
        ygf = ar.alloc([512])
        CLqs = [ar.alloc([2, 3, 4, 128], BF16) for _ in range(2)]
        zall = ar.alloc([2, 128])
        seqs = [(0, 256), (256, 256), (512, 1024)]

        def tabv(tab, d, tt):
            row = _row(tab); off = int(tab.offset)
            if d == 0:
                if tt == 0:
                    return apm.AP(tab.tensor, off, [[row, 128], [0, 2], [1, 256]])
                return tab[:, (tt - 1) * 512:tt * 512]
            if tt == 0:
                return apm.AP(tab.tensor, off + 255, [[row, 128], [0, 2], [-1, 256]])
            return apm.AP(tab.tensor, off + (1023 if tt == 1 else 511), [[row, 128], [-1, 512]])

        def v3(ap, tt):
            return ap.rearrange("p (s t) -> p s t", s=2) if tt == 0 else ap

        Ybanks = (0, 1, 2)
        XP = (3, 4, 5, 6, 7)
        pairs = [(q, d, w) for q in range(4) for d in range(2) for w in range(4)]

        def build_CL(q):
            CLq = CLqs[q % 2]
            k.memset("pool", CLq.rearrange("p d r w c -> p (d r w c)"), 0.0)
            for d in range(2):
                for r in range(3):
                    zz = zall[:, (d * 3 + r) % 2]
                    sgn = 1.0 if r == 0 else -1.0
                    src = cns[:, d, 1 if r == 1 else 0, q, :]
                    k.ts("pool", zz[:, 0:64], src, maskc[:, 0:1], sgn, ALU.mult, ALU.mult)
                    k.ts("pool", zz[:, 64:128], src, maskc[:, 1:2], sgn, ALU.mult, ALU.mult)
                    pt = ps(XP)
                    k.transpose(pt[:, 0:128], zz, ident)
                    for w in range(4):
                        k.copy("act", CLq[:, d, r, w, 32 * w:32 * w + 32], pt[:, 32 * w:32 * w + 32])

        def stA(i):
            q, d, w = pairs[i]; st = sets[i % 2]; W = 4 * q + w
            thc = th[:, d, W:W + 1]
            ang = st["tm"][:, 0:2].rearrange("p a b -> p (a b)"); rb = st["tm"][:, 2:4].rearrange("p a b -> p (a b)")
            qi = st["h32"].rearrange("p a b -> p (a b)").bitcast(I32)
            k.act(ang, iota, AF.Identity, scale=thc)
            k.ts("dve", qi, ang, 1.0 / TWO_PI)
            k.stt(rb, qi, -TWO_PI, ang, ALU.mult, ALU.add)
            k.act(st["tabS"], rb, AF.Sin, scale=SIN_SC)
            k.ts("dve", qi, ang, 1.0 / TWO_PI, 0.25, ALU.mult, ALU.add)
            k.stt(rb, qi, -TWO_PI, ang, ALU.mult, ALU.add)
            k.act(st["tabC"], rb, AF.Sin, scale=SIN_SC, bias=hpi_c)

        def stB(i):
            q, d, w = pairs[i]; st = sets[i % 2]
            tm = st["tm"]
            for tt in range(3):
                tsl = slice(tt * 512, (tt + 1) * 512)
                XR = ps(XP); XI = ps(XP)
                k.mm(XR, BL[32 * w:32 * w + 32, d, 0, q, :], uT[32 * w:32 * w + 32, q, tsl], tile_position=(32 * w, 0))
                k.mm(XI, BL[32 * w:32 * w + 32, d, 1, q, :], uT[32 * w:32 * w + 32, q, tsl], tile_position=(32 * w, 0))
                C = tabv(st["tabC"], d, tt); S = tabv(st["tabS"], d, tt)
                k.tt("dve", v3(tm[:, 0], tt), v3(XR, tt), C, ALU.mult)
                k.tt("dve", v3(tm[:, 1], tt), v3(XI, tt), S, ALU.mult)
                k.tt("pool", st["xr"][:, tsl], tm[:, 0], tm[:, 1], ALU.add)
                k.tt("dve", v3(tm[:, 2], tt), v3(XI, tt), C, ALU.mult)
                k.tt("dve", v3(tm[:, 3], tt), v3(XR, tt), S, ALU.mult)
                k.tt("pool", st["xi"][:, tsl], tm[:, 2], tm[:, 3], ALU.subtract)

        def stC(i):
            q, d, w = pairs[i]; st = sets[i % 2]; W = 4 * q + w
            mc = mcol[:, d, W:W + 1]
            for si, (o, L) in enumerate(seqs):
                for (buf, r) in ((st["xr"], 0), (st["xi"], 1)):
                    init = h0[:, d, r, W:W + 1] if si == 2 else 0.0
                    a_ = buf[:, o:o + L]
                    if d == 1:
                        a_ = rev(a_)
                    k.scan(a_, mc.to_broadcast([128, L]), a_, init)

        def stD(i):
            q, d, w = pairs[i]; st = sets[i % 2]; W = 4 * q + w
            gr = st["xr"]; gi = st["xi"]; pr = st["pr"]
            CLq = CLqs[q % 2]
            col = 255 if d == 0 else 0
            grc = apm.AP(gr.tensor, int(gr.offset) + col, [[_row(gr), 128], [256, 2]])
            gic = apm.AP(gi.tensor, int(gi.offset) + col, [[_row(gi), 128], [256, 2]])
            c255 = st["tabC"][:, 255:256]; s255 = st["tabS"][:, 255:256]
            tn = st["h32"][:, 0, 0:4]
            k.ts("dve", tn[:, 0:2], gic, s255)
            k.stt(NS[:, :, d, 0, W], grc, c255, tn[:, 0:2], ALU.mult, ALU.subtract)
            k.ts("dve", tn[:, 2:4], gic, c255)
            k.stt(NS[:, :, d, 1, W], grc, s255, tn[:, 2:4], ALU.mult, ALU.add)
            for tt in range(3):
                tsl = slice(tt * 512, (tt + 1) * 512)
                C = tabv(st["tabC"], d, tt); S = tabv(st["tabS"], d, tt)
                k.tt("dve", v3(pr[:, 0, tsl], tt), v3(gr[:, tsl], tt), C, ALU.mult)
                k.tt("dve", v3(pr[:, 1, tsl], tt), v3(gi[:, tsl], tt), S, ALU.mult)
                k.tt("dve", v3(pr[:, 2, tsl], tt), v3(gr[:, tsl], tt), S, ALU.mult)
                k.tt("dve", v3(pr[:, 3, tsl], tt), v3(gi[:, tsl], tt), C, ALU.mult)
                Y = psum[:, Ybanks[tt], :]
                last = (d == 1 and w == 3)
                k.mm(Y, CLq[:, d, 0, w, :], pr[:, 0, tsl], start=(d == 0 and w == 0), stop=False)
                k.mm(Y, CLq[:, d, 2, w, :], pr[:, 1, tsl], start=False, stop=False)
                k.mm(Y, CLq[:, d, 1, w, :], pr[:, 2, tsl], start=False, stop=False)
                k.mm(Y, CLq[:, d, 1, w, :], pr[:, 3, tsl], start=False, stop=last)
            if d == 1 and w == 3:
                for tt in range(3):
                    tsl = slice(tt * 512, (tt + 1) * 512)
                    Y = psum[:, Ybanks[tt], :]
                    k.stt(ygf, uT[:, q, tsl], dskipT[:, l * 4 + q:l * 4 + q + 1], Y, ALU.mult, ALU.add)
                    k.act(yaT[:, q, tsl], ygf, AF.Gelu_apprx_tanh)

        build_CL(0)
        stA(0); stB(0)
        for i in range(32):
            if i + 1 < 32:
                if pairs[i + 1][1] == 0 and pairs[i + 1][2] == 0:
                    build_CL(pairs[i + 1][0])
                stA(i + 1)
                stB(i + 1)
            stC(i)
            stD(i)
        pt = ps(XP)
        k.transpose(pt[:, 0:128], NS.rearrange("p s d r W -> p (s d r W)"), ident)
        nst = ar.alloc([128])
        k.copy("dve", nst, pt[:, 0:128])
        for s in range(2):
            k.dma("sp", ns_out.ap()[s, l].rearrange("d r (W hp) -> (d r W) hp", hp=128), nst[64 * s:64 * s + 64, :],
                  is_output=True)
        ar.release()
        wg = ar.alloc([4, 512], BF16)
        k.dma("pool", wg, w_glu.ap()[l].rearrange("(c p) f -> p c f", p=128))
        sg = ar.alloc([4, 512])
        for tt in range(3):
            tsl = slice(tt * 512, (tt + 1) * 512)
            for fb in range(4):
                pg = ps()
                for kc in range(4):
                    k.mm(pg, wg[:, kc, fb * 128:(fb + 1) * 128], yaT[:, kc, tsl], start=(kc == 0), stop=(kc == 3))
                k.act(sg[:, fb], pg, AF.Sigmoid)
            for fb in range(4):
                k.tt("dve", yaT[:, fb, tsl], yaT[:, fb, tsl], sg[:, fb], ALU.mult)
        ar.release()

    def attn_branch(l, qT, kT, Vt, ybT):
        ar.mark()
        rz = ar.alloc([2, 512]); o = ar.alloc([512]); t0 = ar.alloc([512]); sq = ar.alloc([512], BF16)
        rstd = ar.alloc([512])
        jobs = []
        for s in range(2):
            jobs.append((s * 256, 256, [(s * 256 + j * 128, s * 2 + j) for j in range(2)]))
        for qt in range(2):
            keys = [(512 + j * 128, 4 + j) for j in range(8)] + [(1536 + j * 128, 12 + j) for j in range(4)]
            jobs.append((512 + qt * 512, 512, keys))
        E = ar.alloc([4, 512], BF16)
        for h in range(4):
            for (qo, nq, keys) in jobs:
                O = [psum[:, 0, 0:nq], psum[:, 1, 0:nq]]
                Z = [psum[:, 2, 0:nq], psum[:, 3, 0:nq]]
                items = [(m, ki_, ko, vb) for m in range(2) for ki_, (ko, vb) in enumerate(keys)]
                n_it = len(items); nk_ = len(keys)
                Sl = {}

                def emitS(j):
                    m, ki_, ko, vb = items[j]
                    S = ps((4, 5, 6, 7))[:, 0:nq]
                    k.mm(S, kT[64 * m:64 * m + 64, h, ko:ko + 128], qT[64 * m:64 * m + 64, h, qo:qo + nq])
                    Sl[j] = S

                emitS(0)
                if n_it > 1:
                    emitS(1)
                for j in range(n_it):
                    m, ki_, ko, vb = items[j]
                    Eb = E[:, j % 4, 0:nq]
                    k.act(Eb, Sl.pop(j), AF.Exp, scale=0.125)
                    if j + 2 < n_it:
                        emitS(j + 2)
                    k.mm(O[m], Vt[:, vb, h * 128:(h + 1) * 128], Eb, start=(ki_ == 0), stop=(ki_ == nk_ - 1))
                    k.mm(Z[m], ones_bf, Eb, start=(ki_ == 0), stop=(ki_ == nk_ - 1))
                k.recip(rz[:, 0, 0:nq], Z[0])
                k.recip(rz[:, 1, 0:nq], Z[1])
                k.tt("dve", t0[:, 0:nq], O[0], rz[:, 0, 0:nq], ALU.mult)
                k.tt("dve", o[:, 0:nq], O[1], rz[:, 1, 0:nq], ALU.mult)
                k.stt(o[:, 0:nq], o[:, 0:nq], lamc[:, l, 1:2], t0[:, 0:nq], ALU.mult, ALU.add)
                k.act(sq[:, 0:nq], o[:, 0:nq], AF.Square)
                ssq = ps((4, 5, 6, 7))[:, 0:nq]
                k.mm(ssq, ones_bf, sq[:, 0:nq])
                k.act(rstd[:, 0:nq], ssq, AF.Sqrt, scale=1.0 / 128, bias=eps_c)
                k.recip(rstd[:, 0:nq], rstd[:, 0:nq])
                k.stt(ybT[:, h, qo:qo + nq], o[:, 0:nq], gfac[:, l:l + 1], rstd[:, 0:nq], ALU.mult, ALU.mult)
        ar.release()

    def pool_branch(l, zp, ycT):
        ar.mark()
        ZW = 1600
        offs = [16, 288, 560]
        wa = ar.alloc([ZW]); wb_ = ar.alloc([ZW]); pooled = ar.alloc([4, NT], BF16); pf = ar.alloc([ZW])
        k.memset("pool", wa, 0.0); k.memset("pool", wb_, 0.0)
        lo, hi_ = 8, ZW - 8
        for g, wdw in enumerate((2, 4, 8, 16)):
            z = zp[:, g, :]
            k.tt("pool", wa[:, lo:hi_], z[:, lo - 1:hi_ - 1], z[:, lo:hi_], ALU.add)
            cur, oth = wa, wb_
            sh = 1
            ww = 2
            while ww < wdw:
                k.tt("pool", oth[:, lo:hi_], cur[:, lo - sh:hi_ - sh], cur[:, lo + sh:hi_ + sh], ALU.add)
                cur, oth = oth, cur
                sh *= 2; ww *= 2
            k.stt(pf[:, lo:hi_], cur[:, lo:hi_], 1.0 / wdw, z[:, lo:hi_], ALU.mult, ALU.subtract)
            for si, (o, L) in enumerate(((16, 256), (288, 256), (560, 1024))):
                for e in range(2):
                    c0 = o if e == 0 else o + L - 8
                    k.tt("pool", pf[:, c0:c0 + 8], cur[:, c0:c0 + 8], poolc[:, g, 0 if si < 2 else 1, e, :], ALU.mult)
                    k.tt("pool", pf[:, c0:c0 + 8], pf[:, c0:c0 + 8], z[:, c0:c0 + 8], ALU.subtract)
            for si, (o, L) in enumerate(((16, 256), (288, 256), (560, 1024))):
                to = (0, 256, 512)[si]
                k.copy("act", pooled[:, g, to:to + L], pf[:, o:o + L])
        wp = ar.alloc([4, 128], BF16)
        k.dma("pool", wp, w_pool.ap()[l].rearrange("g c d -> c g d"))
        for g in range(4):
            for tt in range(3):
                tsl = slice(tt * 512, (tt + 1) * 512)
                pp = ps()
                k.mm(pp, wp[:, g, :], pooled[:, g, tsl])
                k.act(ycT[:, g, tsl], pp, AF.Identity, scale=pscT[:, l * 4 + g:l * 4 + g + 1])
        ar.release()

    def mixer(l):
        rmsnorm_mod(l, 1)
        ar.mark()
        yaT = ar.alloc([4, NT], BF16)
        wv = w_in.ap()[l].rearrange("(c p) f -> p c f", p=128)

        def wchunk(c):
            return wload(wv[:, :, c * 512:(c + 1) * 512], [8, 512])

        def proj_fm(wc, fb, tt):
            p = ps()
            for kc in range(8):
                k.mm(p, wc[:, kc, fb * 128:(fb + 1) * 128], nT[:, kc, tt * 512:(tt + 1) * 512],
                     start=(kc == 0), stop=(kc == 7))
            return p

        def proj_tm(wc, tb):
            p = ps()
            for kc in range(8):
                k.mm(p, nT[:, kc, tb * 128:(tb + 1) * 128], wc[:, kc, :], start=(kc == 0), stop=(kc == 7))
            return p

        ar.mark()
        uT = yaT
        ar.mark()
        wc = wchunk(0)
        for fb in range(4):
            for tt in range(3):
                p = proj_fm(wc, fb, tt)
                k.copy("act" if (fb + tt) % 2 else "dve", uT[:, fb, tt * 512:(tt + 1) * 512], p)
        ar.release()
        s5_branch(l, uT, yaT)
        ar.release()
        rmsnorm_mod(l, 1)
        stage('s5', [yaT])
        ybT = ar.alloc([4, NT], BF16); ycT = ar.alloc([4, NT], BF16)
        ar.mark()
        qT = ar.alloc([4, NT], BF16); kT = ar.alloc([4, 2048], BF16); Vt = ar.alloc([16, 512], BF16)
        qraw = ar.alloc([2, 512], BF16); tr = ar.alloc([2, 512]); stg_o = ar.alloc([2, 512])
        for which, dst in ((1, qT), (2, kT)):
            ar.mark()
            wc = wchunk(which)
            for fb in range(4):
                for tt in range(3):
                    tsl = slice(tt * 512, (tt + 1) * 512)
                    p = proj_fm(wc, fb, tt)
                    if tt == 0:
                        k.copy("act", dst[:, fb, tsl], p)
                    else:
                        qb = qraw[:, (fb + tt) % 2]
                        k.copy("act", qb, p)
                        pq = ps()
                        k.mm(pq, pt_bf, qb)
                        pos = slice((tt - 1) * 512, tt * 512)
                        k.tt("dve", tr[:, 0], qb, rope[:, 0, pos], ALU.mult)
                        k.tt("dve", tr[:, 1], pq, rope[:, 1, pos], ALU.mult)
                        k.tt("pool", dst[:, fb, tsl], tr[:, 0], tr[:, 1], ALU.add)
            if which == 2:
                for tb in range(4):
                    p = proj_tm(wc, tb)
                    st_ = stg_o[:, tb % 2]
                    k.copy("dve", st_, p)
                    k.dma("sp", nk_out.ap()[tb // 2, l, (tb % 2) * 128:(tb % 2) * 128 + 128, :], st_, is_output=True)
            ar.release()
        stage('qk', [qT])
        ar.mark()
        wc = wchunk(3)
        for tb in ((4, 5, 6, 7, 8, 9, 10, 11, 0, 1, 2, 3) if 'e' in _VD else range(4 if 'a' in _VD else 12)):
            if 'c' in _VD and tb < 4:
                continue
            p = proj_tm(wc, tb)
            if 'b' in _VD:
                k.copy("act", Vt[:, tb, :], p)
                continue
            if tb >= 4:
                k.copy("act", Vt[:, tb, :], p)
            else:
                st_ = stg_o[:, tb % 2]
                k.copy("dve", st_, p)
                k.copy("act", Vt[:, tb, :], st_)
                if 'd' in _VD:
                    continue
                k.dma("pool", nv_out.ap()[tb // 2, l, (tb % 2) * 128:(tb % 2) * 128 + 128, :], st_, is_output=True)
        stage('v', [ybT])
        k.dma("pool", Vt[:, 12:16, :], cv.ap()[l].rearrange("(b p) f -> p b f", p=128))
        ckb = ar.alloc([4, 512], BF16)
        k.dma("pool", ckb, ck.ap()[l].rearrange("(b p) f -> p b f", p=128))
        for h in range(4):
            ptb = ps().bitcast(BF16)
            for b in range(4):
                k.transpose(ptb[:, b * 128:(b + 1) * 128], ckb[:, b, h * 128:(h + 1) * 128], ident_bf)
            k.copy("dve", kT[:, h, 1536:2048], ptb[:, 0:512])
        ar.release()
        stage('cache', [ybT])
        attn_branch(l, qT, kT, Vt, ybT)
        ar.release()
        stage('attn', [ybT])
        ar.mark()
        zp = ar.alloc([4, 1600])
        k.memset("pool", zp, 0.0)
        ar.mark()
        wc = wchunk(4)
        for fb in range(4):
            p = proj_fm(wc, fb, 0)
            k.copy("act", zp[:, fb, 16:272], p[:, 0:256])
            k.copy("act", zp[:, fb, 288:544], p[:, 256:512])
            for tt in (1, 2):
                p = proj_fm(wc, fb, tt)
                k.copy("act", zp[:, fb, 560 + (tt - 1) * 512:560 + tt * 512], p)
        ar.release()
        pool_branch(l, zp, ycT)
        ar.release()
        stage('pool', [ycT])
        ar.mark()
        mT = ar.alloc([8, NT], BF16)
        ys = (yaT, ybT, ycT)
        gs = ar.alloc([2, 512]); acc = ar.alloc([512]); t2 = ar.alloc([512])
        wbv = w_branch.ap()[l].rearrange("n (c p) d -> p n c d", p=128)
        mw = [(ar.alloc([3, 8, 128], BF16), ar.alloc([3, 4, 128], BF16)) for _ in range(2)]
        for dc in range(8):
            ar.mark()
            wg_, wb3 = mw[dc % 2]
            for n in range(3):
                k.dma("pool", wg_[:, n], wv[:, :, 2560 + n * 1024 + dc * 128:2560 + n * 1024 + (dc + 1) * 128])
            k.dma("pool", wb3, wbv[:, :, :, dc * 128:(dc + 1) * 128])
            for tt in range(3):
                tsl = slice(tt * 512, (tt + 1) * 512)
                for n in range(3):
                    pg = ps()
                    for kc in range(8):
                        k.mm(pg, wg_[:, n, kc, :], nT[:, kc, tsl], start=(kc == 0), stop=(kc == 7))
                    pb = ps()
                    for kc in range(4):
                        k.mm(pb, wb3[:, n, kc, :], ys[n][:, kc, tsl], start=(kc == 0), stop=(kc == 3))
                    gb = gs[:, n % 2]
                    k.act(gb, pg, AF.Sigmoid)
                    if n == 0:
                        k.tt("dve", acc, gb, pb, ALU.mult)
                    elif n == 1:
                        k.tt("dve", t2, gb, pb, ALU.mult)
                        k.tt("pool", acc, acc, t2, ALU.add)
                    else:
                        k.tt("dve", t2, gb, pb, ALU.mult)
                        k.tt("pool", mT[:, dc, tsl], acc, t2, ALU.add)
            ar.release()
        wov = w_out.ap()[l].rearrange("(c p) d -> p c d", p=128)
        for half in range(2):
            ar.mark()
            wo = wload(wov[:, :, half * 512:(half + 1) * 512], [8, 512])
            for s in range(4):
                dc = half * 4 + s
                for tt in range(3):
                    tsl = slice(tt * 512, (tt + 1) * 512)
                    py = ps()
                    for kc in range(8):
                        k.mm(py, wo[:, kc, s * 128:(s + 1) * 128], mT[:, kc, tsl], start=(kc == 0), stop=(kc == 7))
                    v = vsel(tt)
                    k.stt(xT[:, dc, tsl], py, Gmod[:, l, 1, dc, v:v + 1], xT[:, dc, tsl], ALU.mult, ALU.add)
            ar.release()
        ar.release()
        ar.release()

    for l in range(depth):
        ffn(l, 0)
        stage('ffn0', [xT])
        mixer(l)
        stage('mixer', [xT])
        ffn(l, 1)

    ar.mark()
    yT = ar.alloc([8, NT])
    rmsnorm_mod(0, 3, out_fn=lambda kc, tt: yT[:, kc, tt * 512:(tt + 1) * 512])
    obufs = [ar.alloc([D]) for _ in range(2)]
    for tb in range(12):
        ob = obufs[tb % 2]
        for g4 in range(2):
            pt = ps()
            for c in range(4):
                kc = g4 * 4 + c
                k.transpose(pt[:, c * 128:(c + 1) * 128], yT[:, kc, tb * 128:(tb + 1) * 128], ident)
            k.copy("act" if g4 else "dve", ob[:, g4 * 512:(g4 + 1) * 512], pt)
        k.dma("sp", y_out.ap()[tb * 128:(tb + 1) * 128, :], ob, is_output=True)
    ar.release()
    k.finish()
    return k


def _consts():
    c = {}
    c["c_ident"] = np.eye(128, dtype=np.float32)
    inv = (10000.0 ** (-np.arange(16, dtype=np.float32) / 16)).astype(np.float32)
    t = np.arange(1024)
    row = (t // 64).astype(np.float32); col = (t % 64).astype(np.float32)
    rope = np.zeros((128, 2, 1024), np.float32)
    pt = np.zeros((128, 128), np.float32)
    for m in range(2):
        for d in range(64):
            p = m * 64 + d
            pos = row if d < 32 else col
            ang = (pos * inv[d % 16]).astype(np.float32)
            rope[p, 0] = np.cos(ang); rope[p, 1] = np.sin(ang)
            dd = d % 32
            if dd < 16:
                pt[p + 16, p] = -1.0
            else:
                pt[p - 16, p] = 1.0
    c["c_rope"] = rope; c["c_pt"] = pt
    c["c_iota"] = np.broadcast_to(np.arange(1, 1025, dtype=np.float32), (128, 1024)).copy()
    mk = np.zeros((128, 2), np.float32)
    for p in range(128):
        g = p // 16
        mk[p, g % 2] = 1.0
    c["c_mask"] = mk
    pc = np.zeros((128, 4, 2, 2, 8), np.float32)
    for g, w in enumerate((2, 4, 8, 16)):
        for lt, L in enumerate((256, 1024)):
            for e in range(2):
                for j in range(8):
                    t_ = j if e == 0 else L - 8 + j
                    lo = min(max(t_ - w // 2, 0), L); hi = min(max(t_ - w // 2 + w, 0), L)
                    pc[:, g, lt, e, j] = 1.0 / (hi - lo)
    c["c_pool"] = pc
    return c


_W_NAMES = ["norm_g", "w_mod", "b_mod", "w_ffn_in", "w_ffn_out", "w_in", "ssm_lam_re", "ssm_lam_im", "ssm_log_dt",
            "ssm_b_re", "ssm_b_im", "ssm_c_re", "ssm_c_im", "ssm_d", "w_glu", "lam_q1", "lam_k1", "lam_q2", "lam_k2",
            "attn_norm_g", "w_pool", "pool_scale", "w_branch", "w_out", "final_norm_g"]


def kernel(**inp):
    nc = bass.Bass("TRN2", target_bir_lowering=False)
    build(nc, depth=int(os.environ.get('KDEPTH', DEPTH)), stop=os.environ.get('KSTOP') or None)
    consts = _consts()
    f = lambda a: np.ascontiguousarray(np.asarray(a, dtype=np.float32))
    xp = f(inp["x_prompt"]); xs = f(inp["x_sample"])
    ck = f(inp["cache_k"]); cv = f(inp["cache_v"]); st = f(inp["state_ssm"]); c = f(inp["c"]); cctx = f(inp["c_ctx"])
    wts = {n: f(inp[n]) for n in _W_NAMES}
    in_maps = []
    for i in range(8):
        m = dict(wts); m.update(consts)
        m["xin"] = np.concatenate([xp[2 * i], xp[2 * i + 1], xs[i]], axis=0)
        m["ck"] = ck[i].reshape(DEPTH, 512, 512)
        m["cv"] = cv[i].reshape(DEPTH, 512, 512)
        m["st"] = st[i].reshape(DEPTH, 2, 2, 2048)
        m["cvec"] = np.stack([cctx, c[i]], axis=0)
        in_maps.append(m)
    res = run_bass_kernel_spmd(nc, in_maps, core_ids=list(range(8)))
    R = res.results
    y_prompt = np.stack([R[i // 2]["y"][(i % 2) * 256:(i % 2) * 256 + 256] for i in range(16)], axis=0)
    y_sample = np.stack([R[i]["y"][512:1536] for i in range(8)], axis=0)
    nk = np.concatenate([R[i]["nk"] for i in range(8)], axis=0).reshape(16, DEPTH, 256, 4, 2, 64)
    nv = np.concatenate([R[i]["nv"] for i in range(8)], axis=0).reshape(16, DEPTH, 256, 4, 128)
    ns = np.concatenate([R[i]["ns"] for i in range(8)], axis=0).reshape(16, DEPTH, 2, 2, 32, 64)
    return (y_prompt.astype(np.float32), y_sample.astype(np.float32), nk.astype(np.float32), nv.astype(np.float32),
            ns.astype(np.float32))
```

```python
import math
import numpy as np
_VD = ''
import concourse.bass as bass
import concourse.mybir as mybir
import concourse.ap as apm
from concourse.bass_utils import run_bass_kernel_spmd

F32 = mybir.dt.float32
BF16 = mybir.dt.bfloat16
I32 = mybir.dt.int32
AF = mybir.ActivationFunctionType
ALU = mybir.AluOpType
AX = mybir.AxisListType
_ES = {str(F32): 4, str(BF16): 2, str(I32): 4}

D = 1024; NT = 1536; DFF = 2816; INW = 5632; DEPTH = 4
EPS = 1e-6
TWO_PI = 2.0 * math.pi
SIN_SC = 1.0 - 2e-4


class _Rec:
    __slots__ = ("eng", "sem", "val", "write", "p0", "p1", "f0", "f1")

    def __init__(self, eng, sem, val, write, box):
        self.eng = eng; self.sem = sem; self.val = val; self.write = write
        self.p0, self.p1, self.f0, self.f1 = box


def _box(ap):
    pat = ap.ap
    es = _ES[str(ap.dtype)]
    off = int(ap.offset)
    row = pat[0][0]
    if row == 0:
        p0 = 0; f = off
    else:
        p0 = off // row; f = off - p0 * row
    p1 = p0 + pat[0][1]
    lo = f; hi = f
    for st, cnt in pat[1:]:
        if st >= 0:
            hi += st * (cnt - 1)
        else:
            lo += st * (cnt - 1)
    return (ap.tensor.name, p0, p1, lo * es, (hi + 1) * es)


class K:
    def __init__(self, nc, n_dma_sems=32):
        self.nc = nc
        self.engs = {"pe": nc.tensor, "act": nc.scalar, "dve": nc.vector, "pool": nc.gpsimd, "sp": nc.sync}
        self.sem = {}; self.cnt = {}
        for e in ("pe", "act", "dve", "pool"):
            self.sem[e] = nc.semaphore("s_" + e).__enter__()
            self.cnt[e] = 0
        self.dsem = [nc.semaphore("d%d" % i).__enter__() for i in range(n_dma_sems)]
        self.dval = [0] * n_dma_sems
        self.dnext = 0
        self.known = {e: {} for e in self.engs}
        self.recs = {}
        self.out_waits = []
        self.n_inst = {e: 0 for e in self.engs}
        self.n_wait = 0

    def _need(self, eng, sem, val):
        if eng == "pe" and sem is self.sem["pe"]:
            return
        kn = self.known[eng]
        key = sem.name
        if kn.get(key, 0) >= val:
            return
        kn[key] = val
        self.engs[eng].wait_ge(sem, val)
        self.n_wait += 1

    def _tracked(self, ap):
        if ap is None or isinstance(ap, (int, float)):
            return False
        if str(ap.space) == "DRAM" and ap.tensor.name not in self.recs:
            return False
        return True

    def _deps(self, eng, reads, writes):
        for ap in reads:
            if not self._tracked(ap):
                continue
            name, p0, p1, f0, f1 = _box(ap)
            for r in self.recs.get(name, ()):
                if r.write and r.p0 < p1 and p0 < r.p1 and r.f0 < f1 and f0 < r.f1:
                    self._need(eng, r.sem, r.val)
        for ap in writes:
            if not self._tracked(ap):
                continue
            name, p0, p1, f0, f1 = _box(ap)
            for r in self.recs.get(name, ()):
                if r.p0 < p1 and p0 < r.p1 and r.f0 < f1 and f0 < r.f1:
                    self._need(eng, r.sem, r.val)

    def _record(self, eng, sem, val, reads, writes):
        for ap in writes:
            if not self._tracked(ap):
                continue
            name, p0, p1, f0, f1 = _box(ap)
            lst = self.recs.setdefault(name, [])
            lst[:] = [r for r in lst if not (p0 <= r.p0 and r.p1 <= p1 and f0 <= r.f0 and r.f1 <= f1)]
            lst.append(_Rec(eng, sem, val, True, (p0, p1, f0, f1)))
        for ap in reads:
            if not self._tracked(ap):
                continue
            name, p0, p1, f0, f1 = _box(ap)
            lst = self.recs.setdefault(name, [])
            lst[:] = [r for r in lst if not ((not r.write) and r.eng == eng and p0 <= r.p0 and r.p1 <= p1
                                              and f0 <= r.f0 and r.f1 <= f1)]
            lst.append(_Rec(eng, sem, val, False, (p0, p1, f0, f1)))

    def track_dram(self, t):
        self.recs.setdefault(t.name, [])

    def _fin(self, eng, inst, reads, writes, inc=True):
        self.n_inst[eng] += 1
        if inc:
            self.cnt[eng] += 1
            inst.then_inc(self.sem[eng], 1)
            self._record(eng, self.sem[eng], self.cnt[eng], reads, writes)
        else:
            self._record(eng, self.sem[eng], self.cnt[eng] + 1, reads, writes)

    def mm(self, out, lhsT, rhs, start=True, stop=True, **kw):
        self._deps("pe", [lhsT, rhs], [out])
        i = self.nc.tensor.matmul(out, lhsT, rhs, start=start, stop=stop, **kw)
        self._fin("pe", i, [lhsT, rhs], [out], inc=stop)

    def transpose(self, out, in_, ident):
        self._deps("pe", [in_, ident], [out])
        i = self.nc.tensor.transpose(out, in_, ident)
        self._fin("pe", i, [in_, ident], [out])

    def act(self, out, in_, func, scale=1.0, bias=None):
        rd = [in_] + [a for a in (scale, bias) if a is not None and not isinstance(a, (int, float))]
        self._deps("act", rd, [out])
        kw = {}
        if bias is not None:
            kw["bias"] = bias
        i = self.nc.scalar.activation(out=out, in_=in_, func=func, scale=scale, **kw)
        self._fin("act", i, rd, [out])

    def _ve(self, eng):
        return self.nc.vector if eng == "dve" else self.nc.gpsimd

    def tt(self, eng, out, in0, in1, op):
        self._deps(eng, [in0, in1], [out])
        i = self._ve(eng).tensor_tensor(out, in0, in1, op)
        self._fin(eng, i, [in0, in1], [out])

    def ts(self, eng, out, in0, s1, s2=None, op0=ALU.mult, op1=None):
        rd = [in0] + [a for a in (s1, s2) if a is not None and not isinstance(a, (int, float))]
        self._deps(eng, rd, [out])
        if op1 is not None:
            i = self._ve(eng).tensor_scalar(out, in0, s1, s2, op0, op1)
        else:
            i = self._ve(eng).tensor_scalar(out, in0, s1, None, op0)
        self._fin(eng, i, rd, [out])

    def stt(self, out, in0, scalar, in1, op0, op1):
        rd = [in0, in1] + ([scalar] if not isinstance(scalar, (int, float)) else [])
        self._deps("dve", rd, [out])
        i = self.nc.vector.scalar_tensor_tensor(out, in0, scalar, in1, op0, op1)
        self._fin("dve", i, rd, [out])

    def scan(self, out, d0, d1, initial):
        rd = [d0, d1] + ([initial] if not isinstance(initial, (int, float)) else [])
        self._deps("dve", rd, [out])
        i = self.nc.vector.tensor_tensor_scan(out, d0, d1, initial, ALU.mult, ALU.add)
        self._fin("dve", i, rd, [out])

    def copy(self, eng, out, in_):
        if eng == "act":
            return self.act(out, in_, AF.Copy)
        self._deps(eng, [in_], [out])
        i = self._ve(eng).tensor_copy(out, in_)
        self._fin(eng, i, [in_], [out])

    def memset(self, eng, ap, val):
        self._deps(eng, [], [ap])
        i = self._ve(eng).memset(ap, val)
        self._fin(eng, i, [], [ap])

    def recip(self, out, in_):
        self._deps("dve", [in_], [out])
        i = self.nc.vector.reciprocal(out, in_)
        self._fin("dve", i, [in_], [out])

    def dma(self, q, out, in_, is_output=False, **kw):
        s = self.dnext
        self.dnext = (self.dnext + 1) % len(self.dsem)
        sem = self.dsem[s]
        if self.dval[s] > 0:
            self._need(q, sem, self.dval[s])
        self._deps(q, [in_], [out])
        self.dval[s] += 16
        self.engs[q].dma_start(out=out, in_=in_, **kw).then_inc(sem, 16)
        self.n_inst[q] += 1
        self._record("dma%d" % s, sem, self.dval[s], [in_], [out])
        if is_output:
            self.out_waits.append((s, self.dval[s]))

    def finish(self):
        last = {}
        for s, v in self.out_waits:
            last[s] = max(last.get(s, 0), v)
        for s, v in last.items():
            self._need("sp", self.dsem[s], v)
        for e in ("pe", "act", "dve", "pool"):
            if self.cnt[e] > 0:
                self._need("sp", self.sem[e], self.cnt[e])


class Arena:
    def __init__(self, nc, nwords):
        self.t = nc.sbuf_tensor("arena", [128, nwords], F32).__enter__()
        self.n = nwords; self.top = 0; self.stack = []

    def alloc(self, free_shape, dt=F32):
        n = 1
        for s in free_shape:
            n *= s
        words = n if _ES[str(dt)] == 4 else (n + 1) // 2
        words = (words + 7) // 8 * 8
        off = self.top
        self.top += words
        assert self.top <= self.n, "arena overflow %d > %d" % (self.top, self.n)
        ap = self.t[:, off:off + words]
        if dt != F32:
            ap = ap.bitcast(dt)
        ap = ap[:, 0:n]
        if len(free_shape) == 2:
            ap = ap.rearrange("p (a b) -> p a b", a=free_shape[0])
        elif len(free_shape) == 3:
            ap = ap.rearrange("p (a b c) -> p a b c", a=free_shape[0], b=free_shape[1])
        elif len(free_shape) == 4:
            ap = ap.rearrange("p (a b c d) -> p a b c d", a=free_shape[0], b=free_shape[1], c=free_shape[2])
        return ap

    def mark(self):
        self.stack.append(self.top)

    def release(self):
        self.top = self.stack.pop()


def _row(ap):
    return ap.ap[0][0]


def rev(ap):
    pat = ap.ap
    assert len(pat) == 2
    st, n = pat[1]
    return apm.AP(ap.tensor, int(ap.offset) + (n - 1) * st, [list(pat[0]), [-st, n]])


class _Stop(Exception):
    pass


def build(nc, depth=DEPTH, dbg=False, stop=None):
    try:
        return _build(nc, depth, dbg, stop)
    except _Stop as e:
        e.args[0].finish()
        return e.args[0]


def _build(nc, depth, dbg, stop):
    k = K(nc)

    dbg_t = {}

    def stage(name, dumps=()):
        if stop != name:
            return
        off = 0
        ar.mark()
        scr = ar.alloc([2048])
        for ap in dumps:
            flat = ap
            if len(ap.shape) == 3:
                flat = ap.rearrange("p a b -> p (a b)")
            elif len(ap.shape) == 4:
                flat = ap.rearrange("p a b c -> p (a b c)")
            n = flat.shape[1]
            for c0 in range(0, n, 2048):
                c1 = min(n, c0 + 2048)
                k.copy("dve", scr[:, 0:c1 - c0], flat[:, c0:c1])
                k.dma("sp", dbg_t["d"].ap()[:, off + c0:off + c1], scr[:, 0:c1 - c0], is_output=True)
            off += n
        raise _Stop(k)

    ar = Arena(nc, 53000)
    psum = nc.psum_tensor("ps", [128, 8, 512], F32).__enter__()
    st_ps = {"i": 0}

    def ps(pool=(0, 1, 2, 3, 4, 5, 6, 7)):
        st_ps["i"] += 1
        return psum[:, pool[st_ps["i"] % len(pool)], :]

    def din(name, shape, dt=F32):
        return nc.dram_tensor(name, list(shape), dt, kind="ExternalInput")

    xin = din("xin", [NT, D]); ck = din("ck", [DEPTH, 512, 512]); cv = din("cv", [DEPTH, 512, 512])
    stt_in = din("st", [DEPTH, 2, 2, 2048]); cvec = din("cvec", [2, D])
    norm_g = din("norm_g", [DEPTH, 3, D]); w_mod = din("w_mod", [DEPTH, D, 9 * D]); b_mod = din("b_mod", [DEPTH, 9 * D])
    w_ffn_in = din("w_ffn_in", [DEPTH, 2, D, 2 * DFF]); w_ffn_out = din("w_ffn_out", [DEPTH, 2, DFF, D])
    w_in = din("w_in", [DEPTH, D, INW])
    lam_re = din("ssm_lam_re", [DEPTH, 2, 32, 64]); lam_im = din("ssm_lam_im", [DEPTH, 2, 32, 64])
    log_dt = din("ssm_log_dt", [DEPTH, 2, 32])
    b_re = din("ssm_b_re", [DEPTH, 2, 32, 64, 16]); b_im = din("ssm_b_im", [DEPTH, 2, 32, 64, 16])
    c_re = din("ssm_c_re", [DEPTH, 2, 32, 16, 64]); c_im = din("ssm_c_im", [DEPTH, 2, 32, 16, 64])
    ssm_d = din("ssm_d", [DEPTH, 512]); w_glu = din("w_glu", [DEPTH, 512, 512])
    lq1 = din("lam_q1", [DEPTH, 64]); lk1 = din("lam_k1", [DEPTH, 64])
    lq2 = din("lam_q2", [DEPTH, 64]); lk2 = din("lam_k2", [DEPTH, 64])
    attn_g = din("attn_norm_g", [DEPTH, 128]); w_pool = din("w_pool", [DEPTH, 4, 128, 128])
    pool_scale = din("pool_scale", [DEPTH, 512]); w_branch = din("w_branch", [DEPTH, 3, 512, D])
    w_out = din("w_out", [DEPTH, D, D]); final_g = din("final_norm_g", [D])
    c_ident = din("c_ident", [128, 128]); c_rope = din("c_rope", [128, 2, 1024]); c_pt = din("c_pt", [128, 128])
    c_iota = din("c_iota", [128, 1024]); c_mask = din("c_mask", [128, 2]); c_pool = din("c_pool", [128, 4, 2, 2, 8])

    def dout(name, shape):
        return nc.dram_tensor(name, list(shape), F32, kind="ExternalOutput")

    y_out = dout("y", [NT, D]); nk_out = dout("nk", [2, DEPTH, 256, 512]); nv_out = dout("nv", [2, DEPTH, 256, 512])
    ns_out = dout("ns", [2, DEPTH, 2, 2, 2048])
    if stop is not None:
        dbg_t["d"] = dout("dbg", [128, 32768])

    ident = ar.alloc([128]); ident_bf = ar.alloc([128], BF16); ones_bf = ar.alloc([128], BF16)
    ones_f = ar.alloc([128]); pt_bf = ar.alloc([128], BF16)
    rope = ar.alloc([2, 1024]); iota = ar.alloc([1024]); maskc = ar.alloc([2]); poolc = ar.alloc([4, 2, 2, 8])
    cst = ar.alloc([8])
    k.dma("sp", ident, c_ident.ap())
    k.dma("sp", rope, c_rope.ap())
    k.dma("sp", iota, c_iota.ap())
    k.dma("sp", maskc, c_mask.ap())
    k.dma("sp", poolc, c_pool.ap())
    tmp_pt = ar.alloc([128])
    k.dma("sp", tmp_pt, c_pt.ap())
    k.copy("dve", ident_bf, ident)
    k.copy("dve", pt_bf, tmp_pt)
    k.memset("dve", ones_bf, 1.0)
    k.memset("dve", ones_f, 1.0)
    k.memset("dve", cst[:, 0:1], EPS)
    k.memset("dve", cst[:, 1:2], 0.5 * math.pi * SIN_SC)
    k.memset("dve", cst[:, 2:3], 0.0)
    eps_c = cst[:, 0:1]; hpi_c = cst[:, 1:2]

    xT = ar.alloc([8, NT])
    nT = ar.alloc([8, NT], BF16)
    modT = ar.alloc([DEPTH, 72, 2])
    Amod = ar.alloc([DEPTH, 3, 8, 2])
    Gmod = ar.alloc([DEPTH, 3, 8, 2])
    normgT = ar.alloc([96])
    smallT = ar.alloc([64])
    lamc = ar.alloc([DEPTH, 2])
    gfac = ar.alloc([DEPTH])
    sc = ar.alloc([2, 8])

    ar.mark()
    stg = ar.alloc([128])
    k.dma("sp", stg[0:96, :], norm_g.ap().rearrange("l j (c p) -> (l j c) p", p=128))
    pt = ps()
    k.transpose(pt[:, 0:96], stg[0:96, :], ident[0:96, 0:96])
    k.copy("dve", normgT, pt[:, 0:96])
    stg2 = ar.alloc([128])
    k.dma("sp", stg2[0:16, :], ssm_d.ap().rearrange("l (c p) -> (l c) p", p=128))
    k.dma("sp", stg2[16:32, :], pool_scale.ap().rearrange("l (c p) -> (l c) p", p=128))
    k.dma("sp", stg2[32:36, :], attn_g.ap())
    k.dma("sp", stg2[36:44, :], final_g.ap().rearrange("(c p) -> c p", p=128))
    k.dma("sp", stg2[44:60, :], cvec.ap().rearrange("v (c p) -> (v c) p", p=128))
    pt = ps()
    k.transpose(pt[:, 0:60], stg2[0:60, :], ident[0:60, 0:60])
    k.copy("dve", smallT[:, 0:60], pt[:, 0:60])
    dskipT = smallT[:, 0:16]; pscT = smallT[:, 16:32]; attngT = smallT[:, 32:36]; fingT = smallT[:, 36:44]
    k.act(sc.rearrange("p v c -> p (v c)"), smallT[:, 44:60], AF.Silu)
    bmodT = ar.alloc([DEPTH, 72])
    for l in range(DEPTH):
        stg3 = ar.alloc([128])
        k.dma("sp", stg3[0:72, :], b_mod.ap()[l].rearrange("(c p) -> c p", p=128))
        pt = ps()
        k.transpose(pt[:, 0:72], stg3[0:72, :], ident[0:72, 0:72])
        k.copy("dve", bmodT[:, l, :], pt[:, 0:72])
    lv = ar.alloc([4, DEPTH])
    with nc.allow_non_contiguous_dma(reason="tiny"):
        for i, t in enumerate((lq1, lk1, lq2, lk2)):
            k.dma("sp", lv[0:64, i, :], t.ap().rearrange("l d -> d l"))
    lp = ar.alloc([2, DEPTH])
    k.tt("dve", lp[0:64, 0, :], lv[0:64, 0, :], lv[0:64, 1, :], ALU.mult)
    k.tt("dve", lp[0:64, 1, :], lv[0:64, 2, :], lv[0:64, 3, :], ALU.mult)
    pt = ps()
    k.mm(pt[:, 0:2 * DEPTH], ones_f[0:64, :], lp[0:64].rearrange("p a l -> p (a l)"))
    le = ar.alloc([2, DEPTH])
    k.act(le.rearrange("p a l -> p (a l)"), pt[:, 0:2 * DEPTH], AF.Exp)
    for l in range(DEPTH):
        li_ = 0.8 - 0.6 * math.exp(-0.3 * l)
        k.stt(lamc[:, l, 0:1], le[:, 0, l:l + 1], li_, le[:, 1, l:l + 1], ALU.add, ALU.subtract)
        k.ts("dve", lamc[:, l, 1:2], lamc[:, l, 0:1], -1.0)
        k.ts("dve", gfac[:, l:l + 1], attngT[:, l:l + 1], 1.0 - li_)

    wms = [ar.alloc([8, 256]) for _ in range(2)]
    for l in range(depth):
        mps = ps()
        for fc in range(36):
            wm = wms[fc % 2]
            k.dma("sp", wm, w_mod.ap()[l].rearrange("(c p) f -> p c f", p=128)[:, :, fc * 256:(fc + 1) * 256])
            for s in range(2):
                fb = fc * 2 + s
                for kc in range(8):
                    k.mm(mps[:, fb * 2:fb * 2 + 2], wm[:, kc, s * 128:(s + 1) * 128], sc[:, :, kc],
                         start=(kc == 0), stop=(kc == 7))
        for v in range(2):
            k.tt("dve", modT[:, l, :, v], mps[:, 0:144].rearrange("p (f v) -> p f v", v=2)[:, :, v], bmodT[:, l, :], ALU.add)
        for j in range(3):
            for v in range(2):
                k.stt(Amod[:, l, j, :, v], modT[:, l, (3 * j + 1) * 8:(3 * j + 2) * 8, v], 1.0,
                      normgT[:, l * 24 + j * 8:l * 24 + j * 8 + 8], ALU.add, ALU.mult)
                k.ts("dve", Gmod[:, l, j, :, v], modT[:, l, (3 * j + 2) * 8:(3 * j + 3) * 8, v],
                     1.0 if j == 1 else 0.5)
    ar.release()

    ar.mark()
    for tb in range(12):
        ar.mark()
        xs = ar.alloc([D])
        k.dma("sp", xs, xin.ap()[tb * 128:(tb + 1) * 128, :])
        for g4 in range(2):
            pt = ps()
            for c in range(4):
                kc = g4 * 4 + c
                k.transpose(pt[:, c * 128:(c + 1) * 128], xs[:, kc * 128:(kc + 1) * 128], ident)
            k.copy("act" if g4 else "dve", xT[:, g4 * 4:(g4 + 1) * 4, tb * 128:(tb + 1) * 128],
                   pt.rearrange("p (c t) -> p c t", c=4))
        ar.release()
    ar.release()

    stage('prologue', [modT, xT])

    def vsel(tt):
        return 0 if tt == 0 else 1

    def rmsnorm_mod(l, j, A_ap=None, shift_ap=None, out_fn=None):
        ar.mark()
        sq = ar.alloc([8, 512], BF16)
        rstd = ar.alloc([512]); tmp = ar.alloc([2, 512])
        for tt in range(3):
            tsl = slice(tt * 512, (tt + 1) * 512)
            for kc in range(8):
                k.act(sq[:, kc, :], xT[:, kc, tsl], AF.Square)
            ss = ps()
            for kc in range(8):
                k.mm(ss, ones_bf, sq[:, kc, :], start=(kc == 0), stop=(kc == 7))
            k.act(rstd, ss, AF.Sqrt, scale=1.0 / D, bias=eps_c)
            k.recip(rstd, rstd)
            v = vsel(tt)
            for kc in range(8):
                if j < 3:
                    a_col = Amod[:, l, j, kc, v:v + 1]
                    b_col = modT[:, l, 3 * j * 8 + kc, v:v + 1]
                    tb_ = tmp[:, kc % 2, :]
                    k.stt(tb_, xT[:, kc, tsl], a_col, rstd, ALU.mult, ALU.mult)
                    k.act(nT[:, kc, tsl], tb_, AF.Identity, bias=b_col)
                else:
                    k.stt(out_fn(kc, tt), xT[:, kc, tsl], fingT[:, kc:kc + 1], rstd, ALU.mult, ALU.mult)
        ar.release()

    def wload(src_ap, shape, dt=BF16):
        buf = ar.alloc(shape, dt)
        k.dma("pool" if dt == BF16 else "sp", buf, src_ap)
        return buf

    def ffn(l, i):
        rmsnorm_mod(l, 0 if i == 0 else 2)
        j = 0 if i == 0 else 2
        ar.mark()
        gT = ar.alloc([22, NT], BF16)
        sa = ar.alloc([2, 512])
        wv = w_ffn_in.ap()[l, i].rearrange("(c p) f -> p c f", p=128)
        wbufs = [ar.alloc([2, 8, 256], BF16) for _ in range(2)]
        for c in range(11):
            wb = wbufs[c % 2]
            k.dma("pool", wb[:, 0], wv[:, :, c * 256:(c + 1) * 256])
            k.dma("pool", wb[:, 1], wv[:, :, DFF + c * 256:DFF + (c + 1) * 256])
            for s in range(2):
                fb = 2 * c + s
                for tt in range(3):
                    tsl = slice(tt * 512, (tt + 1) * 512)
                    pa = ps(); pb = ps()
                    for kc in range(8):
                        k.mm(pa, wb[:, 0, kc, s * 128:(s + 1) * 128], nT[:, kc, tsl], start=(kc == 0), stop=(kc == 7))
                    for kc in range(8):
                        k.mm(pb, wb[:, 1, kc, s * 128:(s + 1) * 128], nT[:, kc, tsl], start=(kc == 0), stop=(kc == 7))
                    sab = sa[:, (fb * 3 + tt) % 2, :]
                    k.act(sab, pa, AF.Silu)
                    k.tt("dve", gT[:, fb, tsl], sab, pb, ALU.mult)
        wov = w_ffn_out.ap()[l, i].rearrange("(f p) d -> p f d", p=128)
        wobufs = [ar.alloc([22, 128], BF16) for _ in range(2)]
        for dc in range(8):
            wo = wobufs[dc % 2]
            k.dma("pool", wo, wov[:, :, dc * 128:(dc + 1) * 128])
            for tt in range(3):
                tsl = slice(tt * 512, (tt + 1) * 512)
                py = ps()
                for fb in range(22):
                    k.mm(py, wo[:, fb, :], gT[:, fb, tsl], start=(fb == 0), stop=(fb == 21))
                v = vsel(tt)
                k.stt(xT[:, dc, tsl], py, Gmod[:, l, j, dc, v:v + 1], xT[:, dc, tsl], ALU.mult, ALU.add)
        ar.release()

    def bc_last(ap2, n):
        pat = ap2.ap
        return apm.AP(ap2.tensor, int(ap2.offset), [list(pat[0]), list(pat[1]), [0, n]])

    def s5_branch(l, uT, yaT):
        ar.mark()
        lr = ar.alloc([2, 16]); li = ar.alloc([2, 16]); ldt = ar.alloc([2, 16])
        h0 = ar.alloc([2, 2, 16])
        with nc.allow_non_contiguous_dma(reason="ssm params"):
            for d in range(2):
                k.dma("sp", lr[:, d, :], lam_re.ap()[l, d].rearrange("(W h) p -> (h p) W", h=2))
                k.dma("sp", li[:, d, :], lam_im.ap()[l, d].rearrange("(W h) p -> (h p) W", h=2))
                for h in range(2):
                    src = apm.AP(log_dt, (l * 2 + d) * 32 + h, [[0, 64], [2, 16]])
                    k.dma("sp", ldt[64 * h:64 * h + 64, d, :], src)
                for r in range(2):
                    k.dma("sp", h0[:, d, r, :], stt_in.ap()[l, d, r].rearrange("(W hp) -> hp W", hp=128))
        dt_ = ar.alloc([2, 16]); mcol = ar.alloc([2, 16]); th = ar.alloc([2, 16])
        k.act(dt_, ldt, AF.Exp)
        a_ = ar.alloc([2, 16])
        k.tt("dve", a_, lr, dt_, ALU.mult)
        k.act(mcol, a_, AF.Exp)
        k.tt("dve", th, li, dt_, ALU.mult)
        qi_ = ar.alloc([2, 16], I32); rr = ar.alloc([2, 16]); sn = ar.alloc([2, 16]); cs = ar.alloc([2, 16])
        k.ts("dve", qi_, th, 1.0 / TWO_PI)
        k.stt(rr, qi_, -TWO_PI, th, ALU.mult, ALU.add)
        k.act(sn, rr, AF.Sin, scale=SIN_SC)
        qi2 = ar.alloc([2, 16], I32); rr2 = ar.alloc([2, 16])
        k.ts("dve", qi2, th, 1.0 / TWO_PI, 0.25, ALU.mult, ALU.add)
        k.stt(rr2, qi2, -TWO_PI, th, ALU.mult, ALU.add)
        k.act(cs, rr2, AF.Sin, scale=SIN_SC, bias=hpi_c)
        abr = ar.alloc([2, 16]); abi = ar.alloc([2, 16]); den = ar.alloc([2, 16]); t1 = ar.alloc([2, 16])
        t2_ = ar.alloc([2, 16]); den2 = ar.alloc([2, 16]); rden = ar.alloc([2, 16]); nr = ar.alloc([2, 16])
        kr = ar.alloc([2, 16]); ki = ar.alloc([2, 16]); kr0 = ar.alloc([2, 16]); ki0 = ar.alloc([2, 16])
        k.tt("dve", abr, mcol, cs, ALU.mult)
        k.tt("dve", abi, mcol, sn, ALU.mult)
        k.tt("dve", den, lr, lr, ALU.mult)
        k.tt("dve", t1, li, li, ALU.mult)
        k.tt("dve", den2, den, t1, ALU.add)
        k.recip(rden, den2)
        k.ts("dve", nr, abr, -1.0, None, ALU.add)
        k.tt("dve", kr0, nr, lr, ALU.mult)
        k.tt("dve", t2_, abi, li, ALU.mult)
        k.tt("dve", kr, kr0, t2_, ALU.add)
        k.tt("dve", kr, kr, rden, ALU.mult)
        k.tt("dve", ki0, abi, lr, ALU.mult)
        k.tt("dve", t1, nr, li, ALU.mult)
        k.tt("dve", ki, ki0, t1, ALU.subtract)
        k.tt("dve", ki, ki, rden, ALU.mult)
        BL = ar.alloc([2, 2, 4, 128], BF16)
        cns = ar.alloc([2, 2, 4, 64])
        for d in range(2):
            for r, csrc in enumerate((c_re, c_im)):
                k.dma("sp", cns[:, d, r], csrc.ap()[l, d].rearrange("(q g) c p -> (g c) q p", g=8))
        ar.mark()
        for d in range(2):
            braw = ar.alloc([16, 16]); iraw = ar.alloc([16, 16])
            with nc.allow_non_contiguous_dma(reason="ssm params"):
                k.dma("sp", braw, b_re.ap()[l, d].rearrange("(W h) p c -> (h p) W c", h=2))
                k.dma("sp", iraw, b_im.ap()[l, d].rearrange("(W h) p c -> (h p) W c", h=2))
            krb = bc_last(kr[:, d, :], 16); kib = bc_last(ki[:, d, :], 16)
            bb = ar.alloc([2, 16, 16]); tq = ar.alloc([16, 16]); tq2 = ar.alloc([16, 16])
            k.tt("dve", tq2, braw, krb, ALU.mult)
            k.tt("dve", tq, iraw, kib, ALU.mult)
            k.tt("dve", bb[:, 0], tq2, tq, ALU.subtract)
            tq3 = ar.alloc([16, 16]); tq4 = ar.alloc([16, 16])
            k.tt("dve", tq3, iraw, krb, ALU.mult)
            k.tt("dve", tq4, braw, kib, ALU.mult)
            k.tt("dve", bb[:, 1], tq3, tq4, ALU.add)
            for r in range(2):
                xall = ar.alloc([16, 2, 16])
                k.memset("dve", xall, 0.0)
                k.copy("dve", xall[0:64, :, 0, :], bb[0:64, r])
                k.copy("dve", xall[64:128, :, 1, :], bb[64:128, r])
                xf = xall.rearrange("p W h c -> p (W h c)")
                pt = ps((3, 4, 5, 6, 7))
                for q in range(4):
                    k.transpose(pt[:, q * 128:(q + 1) * 128], xf[:, q * 128:(q + 1) * 128], ident)
                k.copy("act", BL[:, d, r].rearrange("p q c -> p (q c)"), pt)
        ar.release()
        ar.mark()
        nTw = nT.rearrange("p a b -> p (a b)").bitcast(F32).rearrange("p (a b) -> p a b", a=4)
        sets = []
        for s_ in range(2):
            st = {}
            st["tabC"] = ar.alloc([1024]); st["tabS"] = ar.alloc([1024])
            st["xr"] = nTw[:, 2 * s_, :]; st["xi"] = nTw[:, 2 * s_ + 1, :]
            st["tm"] = ar.alloc([4, 512]); st["pr"] = ar.alloc([4, NT], BF16)
            st["h32"] = ar.alloc([2, 512])
            sets.append(st)
        NS = ar.alloc([2, 2, 2, 16])
        ygf = ar.alloc([512])
        CLqs = [ar.alloc([2, 3, 4, 128], BF16) for _ in range(2)]
        zall = ar.alloc([2, 128])
        seqs = [(0, 256), (256, 256), (512, 1024)]

        def tabv(tab, d, tt):
            row = _row(tab); off = int(tab.offset)
            if d == 0:
                if tt == 0:
                    return apm.AP(tab.tensor, off, [[row, 128], [0, 2], [1, 256]])
                return tab[:, (tt - 1) * 512:tt * 512]
            if tt == 0:
                return apm.AP(tab.tensor, off + 255, [[row, 128], [0, 2], [-1, 256]])
            return apm.AP(tab.tensor, off + (1023 if tt == 1 else 511), [[row, 128], [-1, 512]])

        def v3(ap, tt):
            return ap.rearrange("p (s t) -> p s t", s=2) if tt == 0 else ap

        Ybanks = (0, 1, 2)
        XP = (3, 4, 5, 6, 7)
        pairs = [(q, d, w) for q in range(4) for d in range(2) for w in range(4)]

        def build_CL(q):
            CLq = CLqs[q % 2]
            k.memset("pool", CLq.rearrange("p d r w c -> p (d r w c)"), 0.0)
            for d in range(2):
                for r in range(3):
                    zz = zall[:, (d * 3 + r) % 2]
                    sgn = 1.0 if r == 0 else -1.0
                    src = cns[:, d, 1 if r == 1 else 0, q, :]
                    k.ts("pool", zz[:, 0:64], src, maskc[:, 0:1], sgn, ALU.mult, ALU.mult)
                    k.ts("pool", zz[:, 64:128], src, maskc[:, 1:2], sgn, ALU.mult, ALU.mult)
                    pt = ps(XP)
                    k.transpose(pt[:, 0:128], zz, ident)
                    for w in range(4):
                        k.copy("act", CLq[:, d, r, w, 32 * w:32 * w + 32], pt[:, 32 * w:32 * w + 32])

        def stA(i):
            q, d, w = pairs[i]; st = sets[i % 2]; W = 4 * q + w
            thc = th[:, d, W:W + 1]
            ang = st["tm"][:, 0:2].rearrange("p a b -> p (a b)"); rb = st["tm"][:, 2:4].rearrange("p a b -> p (a b)")
            qi = st["h32"].rearrange("p a b -> p (a b)").bitcast(I32)
            k.act(ang, iota, AF.Identity, scale=thc)
            k.ts("dve", qi, ang, 1.0 / TWO_PI)
            k.stt(rb, qi, -TWO_PI, ang, ALU.mult, ALU.add)
            k.act(st["tabS"], rb, AF.Sin, scale=SIN_SC)
            k.ts("dve", qi, ang, 1.0 / TWO_PI, 0.25, ALU.mult, ALU.add)
            k.stt(rb, qi, -TWO_PI, ang, ALU.mult, ALU.add)
            k.act(st["tabC"], rb, AF.Sin, scale=SIN_SC, bias=hpi_c)

        def stB(i):
            q, d, w = pairs[i]; st = sets[i % 2]
            tm = st["tm"]
            for tt in range(3):
                tsl = slice(tt * 512, (tt + 1) * 512)
                XR = ps(XP); XI = ps(XP)
                k.mm(XR, BL[32 * w:32 * w + 32, d, 0, q, :], uT[32 * w:32 * w + 32, q, tsl], tile_position=(32 * w, 0))
                k.mm(XI, BL[32 * w:32 * w + 32, d, 1, q, :], uT[32 * w:32 * w + 32, q, tsl], tile_position=(32 * w, 0))
                C = tabv(st["tabC"], d, tt); S = tabv(st["tabS"], d, tt)
                k.tt("dve", v3(tm[:, 0], tt), v3(XR, tt), C, ALU.mult)
                k.tt("dve", v3(tm[:, 1], tt), v3(XI, tt), S, ALU.mult)
                k.tt("pool", st["xr"][:, tsl], tm[:, 0], tm[:, 1], ALU.add)
                k.tt("dve", v3(tm[:, 2], tt), v3(XI, tt), C, ALU.mult)
                k.tt("dve", v3(tm[:, 3], tt), v3(XR, tt), S, ALU.mult)
                k.tt("pool", st["xi"][:, tsl], tm[:, 2], tm[:, 3], ALU.subtract)

        def stC(i):
            q, d, w = pairs[i]; st = sets[i % 2]; W = 4 * q + w
            mc = mcol[:, d, W:W + 1]
            for si, (o, L) in enumerate(seqs):
                for (buf, r) in ((st["xr"], 0), (st["xi"], 1)):
                    init = h0[:, d, r, W:W + 1] if si == 2 else 0.0
                    a_ = buf[:, o:o + L]
                    if d == 1:
                        a_ = rev(a_)
                    k.scan(a_, mc.to_broadcast([128, L]), a_, init)

        def stD(i):
            q, d, w = pairs[i]; st = sets[i % 2]; W = 4 * q + w
            gr = st["xr"]; gi = st["xi"]; pr = st["pr"]
            CLq = CLqs[q % 2]
            col = 255 if d == 0 else 0
            grc = apm.AP(gr.tensor, int(gr.offset) + col, [[_row(gr), 128], [256, 2]])
            gic = apm.AP(gi.tensor, int(gi.offset) + col, [[_row(gi), 128], [256, 2]])
            c255 = st["tabC"][:, 255:256]; s255 = st["tabS"][:, 255:256]
            tn = st["h32"][:, 0, 0:4]
            k.ts("dve", tn[:, 0:2], gic, s255)
            k.stt(NS[:, :, d, 0, W], grc, c255, tn[:, 0:2], ALU.mult, ALU.subtract)
            k.ts("dve", tn[:, 2:4], gic, c255)
            k.stt(NS[:, :, d, 1, W], grc, s255, tn[:, 2:4], ALU.mult, ALU.add)
            for tt in range(3):
                tsl = slice(tt * 512, (tt + 1) * 512)
                C = tabv(st["tabC"], d, tt); S = tabv(st["tabS"], d, tt)
                k.tt("dve", v3(pr[:, 0, tsl], tt), v3(gr[:, tsl], tt), C, ALU.mult)
                k.tt("dve", v3(pr[:, 1, tsl], tt), v3(gi[:, tsl], tt), S, ALU.mult)
                k.tt("dve", v3(pr[:, 2, tsl], tt), v3(gr[:, tsl], tt), S, ALU.mult)
                k.tt("dve", v3(pr[:, 3, tsl], tt), v3(gi[:, tsl], tt), C, ALU.mult)
                Y = psum[:, Ybanks[tt], :]
                last = (d == 1 and w == 3)
                k.mm(Y, CLq[:, d, 0, w, :], pr[:, 0, tsl], start=(d == 0 and w == 0), stop=False)
                k.mm(Y, CLq[:, d, 2, w, :], pr[:, 1, tsl], start=False, stop=False)
                k.mm(Y, CLq[:, d, 1, w, :], pr[:, 2, tsl], start=False, stop=False)
                k.mm(Y, CLq[:, d, 1, w, :], pr[:, 3, tsl], start=False, stop=last)
            if d == 1 and w == 3:
                for tt in range(3):
                    tsl = slice(tt * 512, (tt + 1) * 512)
                    Y = psum[:, Ybanks[tt], :]
                    k.stt(ygf, uT[:, q, tsl], dskipT[:, l * 4 + q:l * 4 + q + 1], Y, ALU.mult, ALU.add)
                    k.act(yaT[:, q, tsl], ygf, AF.Gelu_apprx_tanh)

        build_CL(0)
        stA(0); stB(0)
        for i in range(32):
            if i + 1 < 32:
                if pairs[i + 1][1] == 0 and pairs[i + 1][2] == 0:
                    build_CL(pairs[i + 1][0])
                stA(i + 1)
                stB(i + 1)
            stC(i)
            stD(i)
        pt = ps(XP)
        k.transpose(pt[:, 0:128], NS.rearrange("p s d r W -> p (s d r W)"), ident)
        nst = ar.alloc([128])
        k.copy("dve", nst, pt[:, 0:128])
        for s in range(2):
            k.dma("sp", ns_out.ap()[s, l].rearrange("d r (W hp) -> (d r W) hp", hp=128), nst[64 * s:64 * s + 64, :],
                  is_output=True)
        ar.release()
        wg = ar.alloc([4, 512], BF16)
        k.dma("pool", wg, w_glu.ap()[l].rearrange("(c p) f -> p c f", p=128))
        sg = ar.alloc([4, 512])
        for tt in range(3):
            tsl = slice(tt * 512, (tt + 1) * 512)
            for fb in range(4):
                pg = ps()
                for kc in range(4):
                    k.mm(pg, wg[:, kc, fb * 128:(fb + 1) * 128], yaT[:, kc, tsl], start=(kc == 0), stop=(kc == 3))
                k.act(sg[:, fb], pg, AF.Sigmoid)
            for fb in range(4):
                k.tt("dve", yaT[:, fb, tsl], yaT[:, fb, tsl], sg[:, fb], ALU.mult)
        ar.release()

    def attn_branch(l, qT, kT, Vt, ybT):
        ar.mark()
        so = ar.alloc([2, 512]); sz = ar.alloc([2, 512])
        rz = ar.alloc([2, 512]); o = ar.alloc([512]); t0 = ar.alloc([512]); sq = ar.alloc([512], BF16)
        rstd = ar.alloc([512])
        jobs = []
        for s in range(2):
            jobs.append((s * 256, 256, [(s * 256 + j * 128, s * 2 + j) for j in range(2)]))
        for qt in range(2):
            keys = [(512 + j * 128, 4 + j) for j in range(8)] + [(1536 + j * 128, 12 + j) for j in range(4)]
            jobs.append((512 + qt * 512, 512, keys))
        E = ar.alloc([4, 512], BF16)
        pending = []
        for h in range(4):
            for (qo, nq, keys) in jobs:
                O = [psum[:, 0, 0:nq], psum[:, 1, 0:nq]]
                Z = [psum[:, 2, 0:nq], psum[:, 3, 0:nq]]
                items = [(m, ki_, ko, vb) for m in range(2) for ki_, (ko, vb) in enumerate(keys)]
                n_it = len(items); nk_ = len(keys)
                Sl = {}

                def emitS(j):
                    m, ki_, ko, vb = items[j]
                    S = ps((4, 5, 6, 7))[:, 0:nq]
                    k.mm(S, kT[64 * m:64 * m + 64, h, ko:ko + 128], qT[64 * m:64 * m + 64, h, qo:qo + nq])
                    Sl[j] = S

                emitS(0)
                if n_it > 1:
                    emitS(1)
                for j in range(n_it):
                    m, ki_, ko, vb = items[j]
                    Eb = E[:, j % 4, 0:nq]
                    if pending and j == min(10, n_it - 1):
                        pending.pop(0)()
                    k.act(Eb, Sl.pop(j), AF.Exp, scale=0.125)
                    if j + 2 < n_it:
                        emitS(j + 2)
                    k.mm(O[m], Vt[:, vb, h * 128:(h + 1) * 128], Eb, start=(ki_ == 0), stop=(ki_ == nk_ - 1))
                    k.mm(Z[m], ones_bf, Eb, start=(ki_ == 0), stop=(ki_ == nk_ - 1))
                for m in range(2):
                    k.copy("act", so[:, m, 0:nq], O[m])
                    k.copy("act", sz[:, m, 0:nq], Z[m])

                def epi(h=h, qo=qo, nq=nq):
                    k.recip(rz[:, 0, 0:nq], sz[:, 0, 0:nq])
                    k.recip(rz[:, 1, 0:nq], sz[:, 1, 0:nq])
                    k.tt("dve", t0[:, 0:nq], so[:, 0, 0:nq], rz[:, 0, 0:nq], ALU.mult)
                    k.tt("dve", o[:, 0:nq], so[:, 1, 0:nq], rz[:, 1, 0:nq], ALU.mult)
                    k.stt(o[:, 0:nq], o[:, 0:nq], lamc[:, l, 1:2], t0[:, 0:nq], ALU.mult, ALU.add)
                    k.tt("pool", sq[:, 0:nq], o[:, 0:nq], o[:, 0:nq], ALU.mult)
                    ssq = ps((4, 5, 6, 7))[:, 0:nq]
                    k.mm(ssq, ones_bf, sq[:, 0:nq])
                    k.act(rstd[:, 0:nq], ssq, AF.Sqrt, scale=1.0 / 128, bias=eps_c)
                    k.recip(rstd[:, 0:nq], rstd[:, 0:nq])
                    k.stt(ybT[:, h, qo:qo + nq], o[:, 0:nq], gfac[:, l:l + 1], rstd[:, 0:nq], ALU.mult, ALU.mult)
                pending.append(epi)
        while pending:
            pending.pop(0)()
        ar.release()

    def pool_branch(l, zp, ycT):
        ar.mark()
        ZW = 1600
        offs = [16, 288, 560]
        wa = ar.alloc([ZW]); wb_ = ar.alloc([ZW]); pooled = ar.alloc([4, NT], BF16); pf = ar.alloc([ZW])
        k.memset("pool", wa, 0.0); k.memset("pool", wb_, 0.0)
        lo, hi_ = 8, ZW - 8
        for g, wdw in enumerate((2, 4, 8, 16)):
            z = zp[:, g, :]
            k.tt("pool", wa[:, lo:hi_], z[:, lo - 1:hi_ - 1], z[:, lo:hi_], ALU.add)
            cur, oth = wa, wb_
            sh = 1
            ww = 2
            while ww < wdw:
                k.tt("pool", oth[:, lo:hi_], cur[:, lo - sh:hi_ - sh], cur[:, lo + sh:hi_ + sh], ALU.add)
                cur, oth = oth, cur
                sh *= 2; ww *= 2
            k.stt(pf[:, lo:hi_], cur[:, lo:hi_], 1.0 / wdw, z[:, lo:hi_], ALU.mult, ALU.subtract)
            for si, (o, L) in enumerate(((16, 256), (288, 256), (560, 1024))):
                for e in range(2):
                    c0 = o if e == 0 else o + L - 8
                    k.tt("pool", pf[:, c0:c0 + 8], cur[:, c0:c0 + 8], poolc[:, g, 0 if si < 2 else 1, e, :], ALU.mult)
                    k.tt("pool", pf[:, c0:c0 + 8], pf[:, c0:c0 + 8], z[:, c0:c0 + 8], ALU.subtract)
            for si, (o, L) in enumerate(((16, 256), (288, 256), (560, 1024))):
                to = (0, 256, 512)[si]
                k.copy("act", pooled[:, g, to:to + L], pf[:, o:o + L])
        wp = ar.alloc([4, 128], BF16)
        k.dma("pool", wp, w_pool.ap()[l].rearrange("g c d -> c g d"))
        for g in range(4):
            for tt in range(3):
                tsl = slice(tt * 512, (tt + 1) * 512)
                pp = ps()
                k.mm(pp, wp[:, g, :], pooled[:, g, tsl])
                k.act(ycT[:, g, tsl], pp, AF.Identity, scale=pscT[:, l * 4 + g:l * 4 + g + 1])
        ar.release()

    def mixer(l):
        rmsnorm_mod(l, 1)
        ar.mark()
        yaT = ar.alloc([4, NT], BF16)
        wv = w_in.ap()[l].rearrange("(c p) f -> p c f", p=128)

        def wchunk(c):
            return wload(wv[:, :, c * 512:(c + 1) * 512], [8, 512])

        def proj_fm(wc, fb, tt):
            p = ps()
            for kc in range(8):
                k.mm(p, wc[:, kc, fb * 128:(fb + 1) * 128], nT[:, kc, tt * 512:(tt + 1) * 512],
                     start=(kc == 0), stop=(kc == 7))
            return p

        def proj_tm(wc, tb):
            p = ps()
            for kc in range(8):
                k.mm(p, nT[:, kc, tb * 128:(tb + 1) * 128], wc[:, kc, :], start=(kc == 0), stop=(kc == 7))
            return p

        ar.mark()
        uT = yaT
        ar.mark()
        wc = wchunk(0)
        for fb in range(4):
            for tt in range(3):
                p = proj_fm(wc, fb, tt)
                k.copy("act" if (fb + tt) % 2 else "dve", uT[:, fb, tt * 512:(tt + 1) * 512], p)
        ar.release()
        s5_branch(l, uT, yaT)
        ar.release()
        rmsnorm_mod(l, 1)
        stage('s5', [yaT])
        ybT = ar.alloc([4, NT], BF16); ycT = ar.alloc([4, NT], BF16)
        ar.mark()
        qT = ar.alloc([4, NT], BF16); kT = ar.alloc([4, 2048], BF16); Vt = ar.alloc([16, 512], BF16)
        qraw = ar.alloc([2, 512], BF16); tr = ar.alloc([2, 512]); stg_o = ar.alloc([2, 512])
        for which, dst in ((1, qT), (2, kT)):
            ar.mark()
            wc = wchunk(which)
            for fb in range(4):
                for tt in range(3):
                    tsl = slice(tt * 512, (tt + 1) * 512)
                    p = proj_fm(wc, fb, tt)
                    if tt == 0:
                        k.copy("act", dst[:, fb, tsl], p)
                    else:
                        qb = qraw[:, (fb + tt) % 2]
                        k.copy("act", qb, p)
                        pq = ps()
                        k.mm(pq, pt_bf, qb)
                        pos = slice((tt - 1) * 512, tt * 512)
                        k.tt("dve", tr[:, 0], qb, rope[:, 0, pos], ALU.mult)
                        k.tt("dve", tr[:, 1], pq, rope[:, 1, pos], ALU.mult)
                        k.tt("pool", dst[:, fb, tsl], tr[:, 0], tr[:, 1], ALU.add)
            if which == 2:
                for tb in range(4):
                    p = proj_tm(wc, tb)
                    st_ = stg_o[:, tb % 2]
                    k.copy("dve", st_, p)
                    k.dma("sp", nk_out.ap()[tb // 2, l, (tb % 2) * 128:(tb % 2) * 128 + 128, :], st_, is_output=True)
            ar.release()
        stage('qk', [qT])
        ar.mark()
        wc = wchunk(3)
        for tb in ((4, 5, 6, 7, 8, 9, 10, 11, 0, 1, 2, 3) if 'e' in _VD else range(4 if 'a' in _VD else 12)):
            if 'c' in _VD and tb < 4:
                continue
            p = proj_tm(wc, tb)
            if 'b' in _VD:
                k.copy("act", Vt[:, tb, :], p)
                continue
            if tb >= 4:
                k.copy("act", Vt[:, tb, :], p)
            else:
                st_ = stg_o[:, tb % 2]
                k.copy("dve", st_, p)
                k.copy("act", Vt[:, tb, :], st_)
                if 'd' in _VD:
                    continue
                k.dma("pool", nv_out.ap()[tb // 2, l, (tb % 2) * 128:(tb % 2) * 128 + 128, :], st_, is_output=True)
        stage('v', [ybT])
        k.dma("pool", Vt[:, 12:16, :], cv.ap()[l].rearrange("(b p) f -> p b f", p=128))
        ckb = ar.alloc([4, 512], BF16)
        k.dma("pool", ckb, ck.ap()[l].rearrange("(b p) f -> p b f", p=128))
        for h in range(4):
            ptb = ps().bitcast(BF16)
            for b in range(4):
                k.transpose(ptb[:, b * 128:(b + 1) * 128], ckb[:, b, h * 128:(h + 1) * 128], ident_bf)
            k.copy("dve", kT[:, h, 1536:2048], ptb[:, 0:512])
        ar.release()
        stage('cache', [ybT])
        attn_branch(l, qT, kT, Vt, ybT)
        ar.release()
        stage('attn', [ybT])
        ar.mark()
        zp = ar.alloc([4, 1600])
        k.memset("pool", zp, 0.0)
        ar.mark()
        wc = wchunk(4)
        for fb in range(4):
            p = proj_fm(wc, fb, 0)
            k.copy("act", zp[:, fb, 16:272], p[:, 0:256])
            k.copy("act", zp[:, fb, 288:544], p[:, 256:512])
            for tt in (1, 2):
                p = proj_fm(wc, fb, tt)
                k.copy("act", zp[:, fb, 560 + (tt - 1) * 512:560 + tt * 512], p)
        ar.release()
        pool_branch(l, zp, ycT)
        ar.release()
        stage('pool', [ycT])
        ar.mark()
        mT = ar.alloc([8, NT], BF16)
        ys = (yaT, ybT, ycT)
        gs = ar.alloc([2, 512]); acc = ar.alloc([512]); t2 = ar.alloc([512])
        wbv = w_branch.ap()[l].rearrange("n (c p) d -> p n c d", p=128)
        mw = [(ar.alloc([3, 8, 128], BF16), ar.alloc([3, 4, 128], BF16)) for _ in range(2)]
        for dc in range(8):
            ar.mark()
            wg_, wb3 = mw[dc % 2]
            for n in range(3):
                k.dma("pool", wg_[:, n], wv[:, :, 2560 + n * 1024 + dc * 128:2560 + n * 1024 + (dc + 1) * 128])
            k.dma("pool", wb3, wbv[:, :, :, dc * 128:(dc + 1) * 128])
            for tt in range(3):
                tsl = slice(tt * 512, (tt + 1) * 512)
                for n in range(3):
                    pg = ps()
                    for kc in range(8):
                        k.mm(pg, wg_[:, n, kc, :], nT[:, kc, tsl], start=(kc == 0), stop=(kc == 7))
                    pb = ps()
                    for kc in range(4):
                        k.mm(pb, wb3[:, n, kc, :], ys[n][:, kc, tsl], start=(kc == 0), stop=(kc == 3))
                    gb = gs[:, n % 2]
                    k.act(gb, pg, AF.Sigmoid)
                    if n == 0:
                        k.tt("dve", acc, gb, pb, ALU.mult)
                    elif n == 1:
                        k.tt("dve", t2, gb, pb, ALU.mult)
                        k.tt("pool", acc, acc, t2, ALU.add)
                    else:
                        k.tt("dve", t2, gb, pb, ALU.mult)
                        k.tt("pool", mT[:, dc, tsl], acc, t2, ALU.add)
            ar.release()
        wov = w_out.ap()[l].rearrange("(c p) d -> p c d", p=128)
        for half in range(2):
            ar.mark()
            wo = wload(wov[:, :, half * 512:(half + 1) * 512], [8, 512])
            for s in range(4):
                dc = half * 4 + s
                for tt in range(3):
                    tsl = slice(tt * 512, (tt + 1) * 512)
                    py = ps()
                    for kc in range(8):
                        k.mm(py, wo[:, kc, s * 128:(s + 1) * 128], mT[:, kc, tsl], start=(kc == 0), stop=(kc == 7))
                    v = vsel(tt)
                    k.stt(xT[:, dc, tsl], py, Gmod[:, l, 1, dc, v:v + 1], xT[:, dc, tsl], ALU.mult, ALU.add)
            ar.release()
        ar.release()
        ar.release()

    for l in range(depth):
        ffn(l, 0)
        stage('ffn0', [xT])
        mixer(l)
        stage('mixer', [xT])
        ffn(l, 1)

    ar.mark()
    yT = ar.alloc([8, NT])
    rmsnorm_mod(0, 3, out_fn=lambda kc, tt: yT[:, kc, tt * 512:(tt + 1) * 512])
    obufs = [ar.alloc([D]) for _ in range(2)]
    for tb in range(12):
        ob = obufs[tb % 2]
        for g4 in range(2):
            pt = ps()
            for c in range(4):
                kc = g4 * 4 + c
                k.transpose(pt[:, c * 128:(c + 1) * 128], yT[:, kc, tb * 128:(tb + 1) * 128], ident)
            k.copy("act" if g4 else "dve", ob[:, g4 * 512:(g4 + 1) * 512], pt)
        k.dma("sp", y_out.ap()[tb * 128:(tb + 1) * 128, :], ob, is_output=True)
    ar.release()
    k.finish()
    return k


def _consts():
    c = {}
    c["c_ident"] = np.eye(128, dtype=np.float32)
    inv = (10000.0 ** (-np.arange(16, dtype=np.float32) / 16)).astype(np.float32)
    t = np.arange(1024)
    row = (t // 64).astype(np.float32); col = (t % 64).astype(np.float32)
    rope = np.zeros((128, 2, 1024), np.float32)
    pt = np.zeros((128, 128), np.float32)
    for m in range(2):
        for d in range(64):
            p = m * 64 + d
            pos = row if d < 32 else col
            ang = (pos * inv[d % 16]).astype(np.float32)
            rope[p, 0] = np.cos(ang); rope[p, 1] = np.sin(ang)
            dd = d % 32
            if dd < 16:
                pt[p + 16, p] = -1.0
            else:
                pt[p - 16, p] = 1.0
    c["c_rope"] = rope; c["c_pt"] = pt
    c["c_iota"] = np.broadcast_to(np.arange(1, 1025, dtype=np.float32), (128, 1024)).copy()
    mk = np.zeros((128, 2), np.float32)
    for p in range(128):
        g = p // 16
        mk[p, g % 2] = 1.0
    c["c_mask"] = mk
    pc = np.zeros((128, 4, 2, 2, 8), np.float32)
    for g, w in enumerate((2, 4, 8, 16)):
        for lt, L in enumerate((256, 1024)):
            for e in range(2):
                for j in range(8):
                    t_ = j if e == 0 else L - 8 + j
                    lo = min(max(t_ - w // 2, 0), L); hi = min(max(t_ - w // 2 + w, 0), L)
                    pc[:, g, lt, e, j] = 1.0 / (hi - lo)
    c["c_pool"] = pc
    return c


_W_NAMES = ["norm_g", "w_mod", "b_mod", "w_ffn_in", "w_ffn_out", "w_in", "ssm_lam_re", "ssm_lam_im", "ssm_log_dt",
            "ssm_b_re", "ssm_b_im", "ssm_c_re", "ssm_c_im", "ssm_d", "w_glu", "lam_q1", "lam_k1", "lam_q2", "lam_k2",
            "attn_norm_g", "w_pool", "pool_scale", "w_branch", "w_out", "final_norm_g"]


def kernel(**inp):
    nc = bass.Bass("TRN2", target_bir_lowering=False)
    build(nc)
    consts = _consts()
    f = lambda a: np.ascontiguousarray(np.asarray(a, dtype=np.float32))
    xp = f(inp["x_prompt"]); xs = f(inp["x_sample"])
    ck = f(inp["cache_k"]); cv = f(inp["cache_v"]); st = f(inp["state_ssm"]); c = f(inp["c"]); cctx = f(inp["c_ctx"])
    wts = {n: f(inp[n]) for n in _W_NAMES}
    in_maps = []
    for i in range(8):
        m = dict(wts); m.update(consts)
        m["xin"] = np.concatenate([xp[2 * i], xp[2 * i + 1], xs[i]], axis=0)
        m["ck"] = ck[i].reshape(DEPTH, 512, 512)
        m["cv"] = cv[i].reshape(DEPTH, 512, 512)
        m["st"] = st[i].reshape(DEPTH, 2, 2, 2048)
        m["cvec"] = np.stack([cctx, c[i]], axis=0)
        in_maps.append(m)
    res = run_bass_kernel_spmd(nc, in_maps, core_ids=list(range(8)))
    R = res.results
    y_prompt = np.stack([R[i // 2]["y"][(i % 2) * 256:(i % 2) * 256 + 256] for i in range(16)], axis=0)
    y_sample = np.stack([R[i]["y"][512:1536] for i in range(8)], axis=0)
    nk = np.concatenate([R[i]["nk"] for i in range(8)], axis=0).reshape(16, DEPTH, 256, 4, 2, 64)
    nv = np.concatenate([R[i]["nv"] for i in range(8)], axis=0).reshape(16, DEPTH, 256, 4, 128)
    ns = np.concatenate([R[i]["ns"] for i in range(8)], axis=0).reshape(16, DEPTH, 2, 2, 32, 64)
    return (y_prompt.astype(np.float32), y_sample.astype(np.float32), nk.astype(np.float32), nv.astype(np.float32),
            ns.astype(np.float32))
```

```python
import math
import numpy as np
_VD = ''
import concourse.bass as bass
import concourse.mybir as mybir
import concourse.ap as apm
from concourse.bass_utils import run_bass_kernel_spmd

F32 = mybir.dt.float32
BF16 = mybir.dt.bfloat16
I32 = mybir.dt.int32
AF = mybir.ActivationFunctionType
ALU = mybir.AluOpType
AX = mybir.AxisListType
_ES = {str(F32): 4, str(BF16): 2, str(I32): 4}

D = 1024; NT = 1536; DFF = 2816; INW = 5632; DEPTH = 4
EPS = 1e-6
TWO_PI = 2.0 * math.pi
SIN_SC = 1.0 - 2e-4


class _Rec:
    __slots__ = ("eng", "sem", "val", "write", "p0", "p1", "f0", "f1")

    def __init__(self, eng, sem, val, write, box):
        self.eng = eng; self.sem = sem; self.val = val; self.write = write
        self.p0, self.p1, self.f0, self.f1 = box


def _box(ap):
    pat = ap.ap
    es = _ES[str(ap.dtype)]
    off = int(ap.offset)
    row = pat[0][0]
    if row == 0:
        p0 = 0; f = off
    else:
        p0 = off // row; f = off - p0 * row
    p1 = p0 + pat[0][1]
    lo = f; hi = f
    for st, cnt in pat[1:]:
        if st >= 0:
            hi += st * (cnt - 1)
        else:
            lo += st * (cnt - 1)
    return (ap.tensor.name, p0, p1, lo * es, (hi + 1) * es)


class K:
    def __init__(self, nc, n_dma_sems=32):
        self.nc = nc
        self.engs = {"pe": nc.tensor, "act": nc.scalar, "dve": nc.vector, "pool": nc.gpsimd, "sp": nc.sync}
        self.sem = {}; self.cnt = {}
        for e in ("pe", "act", "dve", "pool"):
            self.sem[e] = nc.semaphore("s_" + e).__enter__()
            self.cnt[e] = 0
        self.dsem = [nc.semaphore("d%d" % i).__enter__() for i in range(n_dma_sems)]
        self.dval = [0] * n_dma_sems
        self.dnext = 0
        self.known = {e: {} for e in self.engs}
        self.recs = {}
        self.out_waits = []
        self.n_inst = {e: 0 for e in self.engs}
        self.n_wait = 0

    def _need(self, eng, sem, val):
        if eng == "pe" and sem is self.sem["pe"]:
            return
        kn = self.known[eng]
        key = sem.name
        if kn.get(key, 0) >= val:
            return
        kn[key] = val
        self.engs[eng].wait_ge(sem, val)
        self.n_wait += 1

    def _tracked(self, ap):
        if ap is None or isinstance(ap, (int, float)):
            return False
        if str(ap.space) == "DRAM" and ap.tensor.name not in self.recs:
            return False
        return True

    def _deps(self, eng, reads, writes):
        for ap in reads:
            if not self._tracked(ap):
                continue
            name, p0, p1, f0, f1 = _box(ap)
            for r in self.recs.get(name, ()):
                if r.write and r.p0 < p1 and p0 < r.p1 and r.f0 < f1 and f0 < r.f1:
                    self._need(eng, r.sem, r.val)
        for ap in writes:
            if not self._tracked(ap):
                continue
            name, p0, p1, f0, f1 = _box(ap)
            for r in self.recs.get(name, ()):
                if r.p0 < p1 and p0 < r.p1 and r.f0 < f1 and f0 < r.f1:
                    self._need(eng, r.sem, r.val)

    def _record(self, eng, sem, val, reads, writes):
        for ap in writes:
            if not self._tracked(ap):
                continue
            name, p0, p1, f0, f1 = _box(ap)
            lst = self.recs.setdefault(name, [])
            lst[:] = [r for r in lst if not (p0 <= r.p0 and r.p1 <= p1 and f0 <= r.f0 and r.f1 <= f1)]
            lst.append(_Rec(eng, sem, val, True, (p0, p1, f0, f1)))
        for ap in reads:
            if not self._tracked(ap):
                continue
            name, p0, p1, f0, f1 = _box(ap)
            lst = self.recs.setdefault(name, [])
            lst[:] = [r for r in lst if not ((not r.write) and r.eng == eng and p0 <= r.p0 and r.p1 <= p1
                                              and f0 <= r.f0 and r.f1 <= f1)]
            lst.append(_Rec(eng, sem, val, False, (p0, p1, f0, f1)))

    def track_dram(self, t):
        self.recs.setdefault(t.name, [])

    def _fin(self, eng, inst, reads, writes, inc=True):
        self.n_inst[eng] += 1
        if inc:
            self.cnt[eng] += 1
            inst.then_inc(self.sem[eng], 1)
            self._record(eng, self.sem[eng], self.cnt[eng], reads, writes)
        else:
            self._record(eng, self.sem[eng], self.cnt[eng] + 1, reads, writes)

    def mm(self, out, lhsT, rhs, start=True, stop=True, **kw):
        self._deps("pe", [lhsT, rhs], [out])
        i = self.nc.tensor.matmul(out, lhsT, rhs, start=start, stop=stop, **kw)
        self._fin("pe", i, [lhsT, rhs], [out], inc=stop)

    def transpose(self, out, in_, ident):
        self._deps("pe", [in_, ident], [out])
        i = self.nc.tensor.transpose(out, in_, ident)
        self._fin("pe", i, [in_, ident], [out])

    def act(self, out, in_, func, scale=1.0, bias=None):
        rd = [in_] + [a for a in (scale, bias) if a is not None and not isinstance(a, (int, float))]
        self._deps("act", rd, [out])
        kw = {}
        if bias is not None:
            kw["bias"] = bias
        i = self.nc.scalar.activation(out=out, in_=in_, func=func, scale=scale, **kw)
        self._fin("act", i, rd, [out])

    def _ve(self, eng):
        return self.nc.vector if eng == "dve" else self.nc.gpsimd

    def tt(self, eng, out, in0, in1, op):
        self._deps(eng, [in0, in1], [out])
        i = self._ve(eng).tensor_tensor(out, in0, in1, op)
        self._fin(eng, i, [in0, in1], [out])

    def ts(self, eng, out, in0, s1, s2=None, op0=ALU.mult, op1=None):
        rd = [in0] + [a for a in (s1, s2) if a is not None and not isinstance(a, (int, float))]
        self._deps(eng, rd, [out])
        if op1 is not None:
            i = self._ve(eng).tensor_scalar(out, in0, s1, s2, op0, op1)
        else:
            i = self._ve(eng).tensor_scalar(out, in0, s1, None, op0)
        self._fin(eng, i, rd, [out])

    def stt(self, out, in0, scalar, in1, op0, op1):
        rd = [in0, in1] + ([scalar] if not isinstance(scalar, (int, float)) else [])
        self._deps("dve", rd, [out])
        i = self.nc.vector.scalar_tensor_tensor(out, in0, scalar, in1, op0, op1)
        self._fin("dve", i, rd, [out])

    def scan(self, out, d0, d1, initial):
        rd = [d0, d1] + ([initial] if not isinstance(initial, (int, float)) else [])
        self._deps("dve", rd, [out])
        i = self.nc.vector.tensor_tensor_scan(out, d0, d1, initial, ALU.mult, ALU.add)
        self._fin("dve", i, rd, [out])

    def copy(self, eng, out, in_):
        if eng == "act":
            return self.act(out, in_, AF.Copy)
        self._deps(eng, [in_], [out])
        i = self._ve(eng).tensor_copy(out, in_)
        self._fin(eng, i, [in_], [out])

    def memset(self, eng, ap, val):
        self._deps(eng, [], [ap])
        i = self._ve(eng).memset(ap, val)
        self._fin(eng, i, [], [ap])

    def recip(self, out, in_):
        self._deps("dve", [in_], [out])
        i = self.nc.vector.reciprocal(out, in_)
        self._fin("dve", i, [in_], [out])

    def dma(self, q, out, in_, is_output=False, **kw):
        s = self.dnext
        self.dnext = (self.dnext + 1) % len(self.dsem)
        sem = self.dsem[s]
        if self.dval[s] > 0:
            self._need(q, sem, self.dval[s])
        self._deps(q, [in_], [out])
        self.dval[s] += 16
        self.engs[q].dma_start(out=out, in_=in_, **kw).then_inc(sem, 16)
        self.n_inst[q] += 1
        self._record("dma%d" % s, sem, self.dval[s], [in_], [out])
        if is_output:
            self.out_waits.append((s, self.dval[s]))

    def finish(self):
        last = {}
        for s, v in self.out_waits:
            last[s] = max(last.get(s, 0), v)
        for s, v in last.items():
            self._need("sp", self.dsem[s], v)
        for e in ("pe", "act", "dve", "pool"):
            if self.cnt[e] > 0:
                self._need("sp", self.sem[e], self.cnt[e])


class Arena:
    def __init__(self, nc, nwords):
        self.t = nc.sbuf_tensor("arena", [128, nwords], F32).__enter__()
        self.n = nwords; self.top = 0; self.stack = []

    def alloc(self, free_shape, dt=F32):
        n = 1
        for s in free_shape:
            n *= s
        words = n if _ES[str(dt)] == 4 else (n + 1) // 2
        words = (words + 7) // 8 * 8
        off = self.top
        self.top += words
        assert self.top <= self.n, "arena overflow %d > %d" % (self.top, self.n)
        ap = self.t[:, off:off + words]
        if dt != F32:
            ap = ap.bitcast(dt)
        ap = ap[:, 0:n]
        if len(free_shape) == 2:
            ap = ap.rearrange("p (a b) -> p a b", a=free_shape[0])
        elif len(free_shape) == 3:
            ap = ap.rearrange("p (a b c) -> p a b c", a=free_shape[0], b=free_shape[1])
        elif len(free_shape) == 4:
            ap = ap.rearrange("p (a b c d) -> p a b c d", a=free_shape[0], b=free_shape[1], c=free_shape[2])
        return ap

    def mark(self):
        self.stack.append(self.top)

    def release(self):
        self.top = self.stack.pop()


def _row(ap):
    return ap.ap[0][0]


def rev(ap):
    pat = ap.ap
    assert len(pat) == 2
    st, n = pat[1]
    return apm.AP(ap.tensor, int(ap.offset) + (n - 1) * st, [list(pat[0]), [-st, n]])


class _Stop(Exception):
    pass


def build(nc, depth=DEPTH, dbg=False, stop=None):
    try:
        return _build(nc, depth, dbg, stop)
    except _Stop as e:
        e.args[0].finish()
        return e.args[0]


def _build(nc, depth, dbg, stop):
    k = K(nc)

    dbg_t = {}

    def stage(name, dumps=()):
        if stop != name:
            return
        off = 0
        ar.mark()
        scr = ar.alloc([2048])
        for ap in dumps:
            flat = ap
            if len(ap.shape) == 3:
                flat = ap.rearrange("p a b -> p (a b)")
            elif len(ap.shape) == 4:
                flat = ap.rearrange("p a b c -> p (a b c)")
            n = flat.shape[1]
            for c0 in range(0, n, 2048):
                c1 = min(n, c0 + 2048)
                k.copy("dve", scr[:, 0:c1 - c0], flat[:, c0:c1])
                k.dma("sp", dbg_t["d"].ap()[:, off + c0:off + c1], scr[:, 0:c1 - c0], is_output=True)
            off += n
        raise _Stop(k)

    ar = Arena(nc, 53000)
    psum = nc.psum_tensor("ps", [128, 8, 512], F32).__enter__()
    st_ps = {"i": 0}

    def ps(pool=(0, 1, 2, 3, 4, 5, 6, 7)):
        st_ps["i"] += 1
        return psum[:, pool[st_ps["i"] % len(pool)], :]

    def din(name, shape, dt=F32):
        return nc.dram_tensor(name, list(shape), dt, kind="ExternalInput")

    xin = din("xin", [NT, D]); ck = din("ck", [DEPTH, 512, 512]); cv = din("cv", [DEPTH, 512, 512])
    stt_in = din("st", [DEPTH, 2, 2, 2048]); cvec = din("cvec", [2, D])
    norm_g = din("norm_g", [DEPTH, 3, D]); w_mod = din("w_mod", [DEPTH, D, 9 * D]); b_mod = din("b_mod", [DEPTH, 9 * D])
    w_ffn_in = din("w_ffn_in", [DEPTH, 2, D, 2 * DFF]); w_ffn_out = din("w_ffn_out", [DEPTH, 2, DFF, D])
    w_in = din("w_in", [DEPTH, D, INW])
    lam_re = din("ssm_lam_re", [DEPTH, 2, 32, 64]); lam_im = din("ssm_lam_im", [DEPTH, 2, 32, 64])
    log_dt = din("ssm_log_dt", [DEPTH, 2, 32])
    b_re = din("ssm_b_re", [DEPTH, 2, 32, 64, 16]); b_im = din("ssm_b_im", [DEPTH, 2, 32, 64, 16])
    c_re = din("ssm_c_re", [DEPTH, 2, 32, 16, 64]); c_im = din("ssm_c_im", [DEPTH, 2, 32, 16, 64])
    ssm_d = din("ssm_d", [DEPTH, 512]); w_glu = din("w_glu", [DEPTH, 512, 512])
    lq1 = din("lam_q1", [DEPTH, 64]); lk1 = din("lam_k1", [DEPTH, 64])
    lq2 = din("lam_q2", [DEPTH, 64]); lk2 = din("lam_k2", [DEPTH, 64])
    attn_g = din("attn_norm_g", [DEPTH, 128]); w_pool = din("w_pool", [DEPTH, 4, 128, 128])
    pool_scale = din("pool_scale", [DEPTH, 512]); w_branch = din("w_branch", [DEPTH, 3, 512, D])
    w_out = din("w_out", [DEPTH, D, D]); final_g = din("final_norm_g", [D])
    c_ident = din("c_ident", [128, 128]); c_rope = din("c_rope", [128, 2, 1024]); c_pt = din("c_pt", [128, 128])
    c_iota = din("c_iota", [128, 1024]); c_mask = din("c_mask", [128, 2]); c_pool = din("c_pool", [128, 4, 2, 2, 8])

    def dout(name, shape):
        return nc.dram_tensor(name, list(shape), F32, kind="ExternalOutput")

    y_out = dout("y", [NT, D]); nk_out = dout("nk", [2, DEPTH, 256, 512]); nv_out = dout("nv", [2, DEPTH, 256, 512])
    ns_out = dout("ns", [2, DEPTH, 2, 2, 2048])
    if stop is not None:
        dbg_t["d"] = dout("dbg", [128, 32768])

    ident = ar.alloc([128]); ident_bf = ar.alloc([128], BF16); ones_bf = ar.alloc([128], BF16)
    ones_f = ar.alloc([128]); pt_bf = ar.alloc([128], BF16)
    rope = ar.alloc([2, 1024]); iota = ar.alloc([1024]); maskc = ar.alloc([2]); poolc = ar.alloc([4, 2, 2, 8])
    cst = ar.alloc([8])
    k.dma("sp", ident, c_ident.ap())
    k.dma("sp", rope, c_rope.ap())
    k.dma("sp", iota, c_iota.ap())
    k.dma("sp", maskc, c_mask.ap())
    k.dma("sp", poolc, c_pool.ap())
    tmp_pt = ar.alloc([128])
    k.dma("sp", tmp_pt, c_pt.ap())
    k.copy("dve", ident_bf, ident)
    k.copy("dve", pt_bf, tmp_pt)
    k.memset("dve", ones_bf, 1.0)
    k.memset("dve", ones_f, 1.0)
    k.memset("dve", cst[:, 0:1], EPS)
    k.memset("dve", cst[:, 1:2], 0.5 * math.pi * SIN_SC)
    k.memset("dve", cst[:, 2:3], 0.0)
    eps_c = cst[:, 0:1]; hpi_c = cst[:, 1:2]

    xT = ar.alloc([8, NT])
    nT = ar.alloc([8, NT], BF16)
    modT = ar.alloc([DEPTH, 72, 2])
    Amod = ar.alloc([DEPTH, 3, 8, 2])
    Gmod = ar.alloc([DEPTH, 3, 8, 2])
    normgT = ar.alloc([96])
    smallT = ar.alloc([64])
    lamc = ar.alloc([DEPTH, 2])
    gfac = ar.alloc([DEPTH])
    sc = ar.alloc([2, 8])

    ar.mark()
    stg = ar.alloc([128])
    k.dma("sp", stg[0:96, :], norm_g.ap().rearrange("l j (c p) -> (l j c) p", p=128))
    pt = ps()
    k.transpose(pt[:, 0:96], stg[0:96, :], ident[0:96, 0:96])
    k.copy("dve", normgT, pt[:, 0:96])
    stg2 = ar.alloc([128])
    k.dma("sp", stg2[0:16, :], ssm_d.ap().rearrange("l (c p) -> (l c) p", p=128))
    k.dma("sp", stg2[16:32, :], pool_scale.ap().rearrange("l (c p) -> (l c) p", p=128))
    k.dma("sp", stg2[32:36, :], attn_g.ap())
    k.dma("sp", stg2[36:44, :], final_g.ap().rearrange("(c p) -> c p", p=128))
    k.dma("sp", stg2[44:60, :], cvec.ap().rearrange("v (c p) -> (v c) p", p=128))
    pt = ps()
    k.transpose(pt[:, 0:60], stg2[0:60, :], ident[0:60, 0:60])
    k.copy("dve", smallT[:, 0:60], pt[:, 0:60])
    dskipT = smallT[:, 0:16]; pscT = smallT[:, 16:32]; attngT = smallT[:, 32:36]; fingT = smallT[:, 36:44]
    k.act(sc.rearrange("p v c -> p (v c)"), smallT[:, 44:60], AF.Silu)
    bmodT = ar.alloc([DEPTH, 72])
    for l in range(DEPTH):
        stg3 = ar.alloc([128])
        k.dma("sp", stg3[0:72, :], b_mod.ap()[l].rearrange("(c p) -> c p", p=128))
        pt = ps()
        k.transpose(pt[:, 0:72], stg3[0:72, :], ident[0:72, 0:72])
        k.copy("dve", bmodT[:, l, :], pt[:, 0:72])
    lv = ar.alloc([4, DEPTH])
    with nc.allow_non_contiguous_dma(reason="tiny"):
        for i, t in enumerate((lq1, lk1, lq2, lk2)):
            k.dma("sp", lv[0:64, i, :], t.ap().rearrange("l d -> d l"))
    lp = ar.alloc([2, DEPTH])
    k.tt("dve", lp[0:64, 0, :], lv[0:64, 0, :], lv[0:64, 1, :], ALU.mult)
    k.tt("dve", lp[0:64, 1, :], lv[0:64, 2, :], lv[0:64, 3, :], ALU.mult)
    pt = ps()
    k.mm(pt[:, 0:2 * DEPTH], ones_f[0:64, :], lp[0:64].rearrange("p a l -> p (a l)"))
    le = ar.alloc([2, DEPTH])
    k.act(le.rearrange("p a l -> p (a l)"), pt[:, 0:2 * DEPTH], AF.Exp)
    for l in range(DEPTH):
        li_ = 0.8 - 0.6 * math.exp(-0.3 * l)
        k.stt(lamc[:, l, 0:1], le[:, 0, l:l + 1], li_, le[:, 1, l:l + 1], ALU.add, ALU.subtract)
        k.ts("dve", lamc[:, l, 1:2], lamc[:, l, 0:1], -1.0)
        k.ts("dve", gfac[:, l:l + 1], attngT[:, l:l + 1], 1.0 - li_)

    wms = [ar.alloc([8, 256]) for _ in range(2)]
    for l in range(depth):
        mps = ps()
        for fc in range(36):
            wm = wms[fc % 2]
            k.dma("sp", wm, w_mod.ap()[l].rearrange("(c p) f -> p c f", p=128)[:, :, fc * 256:(fc + 1) * 256])
            for s in range(2):
                fb = fc * 2 + s
                for kc in range(8):
                    k.mm(mps[:, fb * 2:fb * 2 + 2], wm[:, kc, s * 128:(s + 1) * 128], sc[:, :, kc],
                         start=(kc == 0), stop=(kc == 7))
        for v in range(2):
            k.tt("dve", modT[:, l, :, v], mps[:, 0:144].rearrange("p (f v) -> p f v", v=2)[:, :, v], bmodT[:, l, :], ALU.add)
        for j in range(3):
            for v in range(2):
                k.stt(Amod[:, l, j, :, v], modT[:, l, (3 * j + 1) * 8:(3 * j + 2) * 8, v], 1.0,
                      normgT[:, l * 24 + j * 8:l * 24 + j * 8 + 8], ALU.add, ALU.mult)
                k.ts("dve", Gmod[:, l, j, :, v], modT[:, l, (3 * j + 2) * 8:(3 * j + 3) * 8, v],
                     1.0 if j == 1 else 0.5)
    ar.release()

    ar.mark()
    for tb in range(12):
        ar.mark()
        xs = ar.alloc([D])
        k.dma("sp", xs, xin.ap()[tb * 128:(tb + 1) * 128, :])
        for g4 in range(2):
            pt = ps()
            for c in range(4):
                kc = g4 * 4 + c
                k.transpose(pt[:, c * 128:(c + 1) * 128], xs[:, kc * 128:(kc + 1) * 128], ident)
            k.copy("act" if g4 else "dve", xT[:, g4 * 4:(g4 + 1) * 4, tb * 128:(tb + 1) * 128],
                   pt.rearrange("p (c t) -> p c t", c=4))
        ar.release()
    ar.release()

    stage('prologue', [modT, xT])

    def vsel(tt):
        return 0 if tt == 0 else 1

    def rmsnorm_mod(l, j, A_ap=None, shift_ap=None, out_fn=None):
        ar.mark()
        sq = ar.alloc([8, 512], BF16)
        rstd = ar.alloc([512]); tmp = ar.alloc([2, 512])
        for tt in range(3):
            tsl = slice(tt * 512, (tt + 1) * 512)
            for kc in range(8):
                k.act(sq[:, kc, :], xT[:, kc, tsl], AF.Square)
            ss = ps()
            for kc in range(8):
                k.mm(ss, ones_bf, sq[:, kc, :], start=(kc == 0), stop=(kc == 7))
            k.act(rstd, ss, AF.Sqrt, scale=1.0 / D, bias=eps_c)
            k.recip(rstd, rstd)
            v = vsel(tt)
            for kc in range(8):
                if j < 3:
                    a_col = Amod[:, l, j, kc, v:v + 1]
                    b_col = modT[:, l, 3 * j * 8 + kc, v:v + 1]
                    tb_ = tmp[:, kc % 2, :]
                    k.stt(tb_, xT[:, kc, tsl], a_col, rstd, ALU.mult, ALU.mult)
                    k.act(nT[:, kc, tsl], tb_, AF.Identity, bias=b_col)
                else:
                    k.stt(out_fn(kc, tt), xT[:, kc, tsl], fingT[:, kc:kc + 1], rstd, ALU.mult, ALU.mult)
        ar.release()

    def wload(src_ap, shape, dt=BF16):
        buf = ar.alloc(shape, dt)
        k.dma("pool" if dt == BF16 else "sp", buf, src_ap)
        return buf

    def ffn(l, i):
        rmsnorm_mod(l, 0 if i == 0 else 2)
        j = 0 if i == 0 else 2
        ar.mark()
        gT = ar.alloc([22, NT], BF16)
        sa = ar.alloc([2, 512])
        wv = w_ffn_in.ap()[l, i].rearrange("(c p) f -> p c f", p=128)
        wbufs = [ar.alloc([2, 8, 256], BF16) for _ in range(2)]
        for c in range(11):
            wb = wbufs[c % 2]
            k.dma("pool", wb[:, 0], wv[:, :, c * 256:(c + 1) * 256])
            k.dma("pool", wb[:, 1], wv[:, :, DFF + c * 256:DFF + (c + 1) * 256])
            for s in range(2):
                fb = 2 * c + s
                for tt in range(3):
                    tsl = slice(tt * 512, (tt + 1) * 512)
                    pa = ps(); pb = ps()
                    for kc in range(8):
                        k.mm(pa, wb[:, 0, kc, s * 128:(s + 1) * 128], nT[:, kc, tsl], start=(kc == 0), stop=(kc == 7))
                    for kc in range(8):
                        k.mm(pb, wb[:, 1, kc, s * 128:(s + 1) * 128], nT[:, kc, tsl], start=(kc == 0), stop=(kc == 7))
                    sab = sa[:, (fb * 3 + tt) % 2, :]
                    k.act(sab, pa, AF.Silu)
                    k.tt("dve", gT[:, fb, tsl], sab, pb, ALU.mult)
        wov = w_ffn_out.ap()[l, i].rearrange("(f p) d -> p f d", p=128)
        wobufs = [ar.alloc([22, 128], BF16) for _ in range(2)]
        for dc in range(8):
            wo = wobufs[dc % 2]
            k.dma("pool", wo, wov[:, :, dc * 128:(dc + 1) * 128])
            for tt in range(3):
                tsl = slice(tt * 512, (tt + 1) * 512)
                py = ps()
                for fb in range(22):
                    k.mm(py, wo[:, fb, :], gT[:, fb, tsl], start=(fb == 0), stop=(fb == 21))
                v = vsel(tt)
                k.stt(xT[:, dc, tsl], py, Gmod[:, l, j, dc, v:v + 1], xT[:, dc, tsl], ALU.mult, ALU.add)
        ar.release()

    def bc_last(ap2, n):
        pat = ap2.ap
        return apm.AP(ap2.tensor, int(ap2.offset), [list(pat[0]), list(pat[1]), [0, n]])

    def s5_branch(l, uT, yaT):
        ar.mark()
        lr = ar.alloc([2, 16]); li = ar.alloc([2, 16]); ldt = ar.alloc([2, 16])
        h0 = ar.alloc([2, 2, 16])
        with nc.allow_non_contiguous_dma(reason="ssm params"):
            for d in range(2):
                k.dma("sp", lr[:, d, :], lam_re.ap()[l, d].rearrange("(W h) p -> (h p) W", h=2))
                k.dma("sp", li[:, d, :], lam_im.ap()[l, d].rearrange("(W h) p -> (h p) W", h=2))
                for h in range(2):
                    src = apm.AP(log_dt, (l * 2 + d) * 32 + h, [[0, 64], [2, 16]])
                    k.dma("sp", ldt[64 * h:64 * h + 64, d, :], src)
                for r in range(2):
                    k.dma("sp", h0[:, d, r, :], stt_in.ap()[l, d, r].rearrange("(W hp) -> hp W", hp=128))
        dt_ = ar.alloc([2, 16]); mcol = ar.alloc([2, 16]); th = ar.alloc([2, 16])
        k.act(dt_, ldt, AF.Exp)
        a_ = ar.alloc([2, 16])
        k.tt("dve", a_, lr, dt_, ALU.mult)
        k.act(mcol, a_, AF.Exp)
        k.tt("dve", th, li, dt_, ALU.mult)
        qi_ = ar.alloc([2, 16], I32); rr = ar.alloc([2, 16]); sn = ar.alloc([2, 16]); cs = ar.alloc([2, 16])
        k.ts("dve", qi_, th, 1.0 / TWO_PI)
        k.stt(rr, qi_, -TWO_PI, th, ALU.mult, ALU.add)
        k.act(sn, rr, AF.Sin, scale=SIN_SC)
        qi2 = ar.alloc([2, 16], I32); rr2 = ar.alloc([2, 16])
        k.ts("dve", qi2, th, 1.0 / TWO_PI, 0.25, ALU.mult, ALU.add)
        k.stt(rr2, qi2, -TWO_PI, th, ALU.mult, ALU.add)
        k.act(cs, rr2, AF.Sin, scale=SIN_SC, bias=hpi_c)
        abr = ar.alloc([2, 16]); abi = ar.alloc([2, 16]); den = ar.alloc([2, 16]); t1 = ar.alloc([2, 16])
        t2_ = ar.alloc([2, 16]); den2 = ar.alloc([2, 16]); rden = ar.alloc([2, 16]); nr = ar.alloc([2, 16])
        kr = ar.alloc([2, 16]); ki = ar.alloc([2, 16]); kr0 = ar.alloc([2, 16]); ki0 = ar.alloc([2, 16])
        k.tt("dve", abr, mcol, cs, ALU.mult)
        k.tt("dve", abi, mcol, sn, ALU.mult)
        k.tt("dve", den, lr, lr, ALU.mult)
        k.tt("dve", t1, li, li, ALU.mult)
        k.tt("dve", den2, den, t1, ALU.add)
        k.recip(rden, den2)
        k.ts("dve", nr, abr, -1.0, None, ALU.add)
        k.tt("dve", kr0, nr, lr, ALU.mult)
        k.tt("dve", t2_, abi, li, ALU.mult)
        k.tt("dve", kr, kr0, t2_, ALU.add)
        k.tt("dve", kr, kr, rden, ALU.mult)
        k.tt("dve", ki0, abi, lr, ALU.mult)
        k.tt("dve", t1, nr, li, ALU.mult)
        k.tt("dve", ki, ki0, t1, ALU.subtract)
        k.tt("dve", ki, ki, rden, ALU.mult)
        BL = ar.alloc([2, 2, 4, 128], BF16)
        cns = ar.alloc([2, 2, 4, 64])
        for d in range(2):
            for r, csrc in enumerate((c_re, c_im)):
                k.dma("sp", cns[:, d, r], csrc.ap()[l, d].rearrange("(q g) c p -> (g c) q p", g=8))
        ar.mark()
        for d in range(2):
            braw = ar.alloc([16, 16]); iraw = ar.alloc([16, 16])
            with nc.allow_non_contiguous_dma(reason="ssm params"):
                k.dma("sp", braw, b_re.ap()[l, d].rearrange("(W h) p c -> (h p) W c", h=2))
                k.dma("sp", iraw, b_im.ap()[l, d].rearrange("(W h) p c -> (h p) W c", h=2))
            krb = bc_last(kr[:, d, :], 16); kib = bc_last(ki[:, d, :], 16)
            bb = ar.alloc([2, 16, 16]); tq = ar.alloc([16, 16]); tq2 = ar.alloc([16, 16])
            k.tt("dve", tq2, braw, krb, ALU.mult)
            k.tt("dve", tq, iraw, kib, ALU.mult)
            k.tt("dve", bb[:, 0], tq2, tq, ALU.subtract)
            tq3 = ar.alloc([16, 16]); tq4 = ar.alloc([16, 16])
            k.tt("dve", tq3, iraw, krb, ALU.mult)
            k.tt("dve", tq4, braw, kib, ALU.mult)
            k.tt("dve", bb[:, 1], tq3, tq4, ALU.add)
            for r in range(2):
                xall = ar.alloc([16, 2, 16])
                k.memset("dve", xall, 0.0)
                k.copy("dve", xall[0:64, :, 0, :], bb[0:64, r])
                k.copy("dve", xall[64:128, :, 1, :], bb[64:128, r])
                xf = xall.rearrange("p W h c -> p (W h c)")
                pt = ps((3, 4, 5, 6, 7))
                for q in range(4):
                    k.transpose(pt[:, q * 128:(q + 1) * 128], xf[:, q * 128:(q + 1) * 128], ident)
                k.copy("act", BL[:, d, r].rearrange("p q c -> p (q c)"), pt)
        ar.release()
        ar.mark()
        nTw = nT.rearrange("p a b -> p (a b)").bitcast(F32).rearrange("p (a b) -> p a b", a=4)
        sets = []
        for s_ in range(2):
            st = {}
            st["tabC"] = ar.alloc([1024]); st["tabS"] = ar.alloc([1024])
            st["xr"] = nTw[:, 2 * s_, :]; st["xi"] = nTw[:, 2 * s_ + 1, :]
            st["tm"] = ar.alloc([4, 512]); st["pr"] = ar.alloc([4, NT], BF16)
            st["h32"] = ar.alloc([2, 512])
            sets.append(st)
        NS = ar.alloc([2, 2, 2, 16])
        ygf = ar.alloc([512])
        CLqs = [ar.alloc([2, 3, 4, 128], BF16) for _ in range(2)]
        zall = ar.alloc([2, 128])
        seqs = [(0, 256), (256, 256), (512, 1024)]

        def tabv(tab, d, tt):
            row = _row(tab); off = int(tab.offset)
            if d == 0:
                if tt == 0:
                    return apm.AP(tab.tensor, off, [[row, 128], [0, 2], [1, 256]])
                return tab[:, (tt - 1) * 512:tt * 512]
            if tt == 0:
                return apm.AP(tab.tensor, off + 255, [[row, 128], [0, 2], [-1, 256]])
            return apm.AP(tab.tensor, off + (1023 if tt == 1 else 511), [[row, 128], [-1, 512]])

        def v3(ap, tt):
            return ap.rearrange("p (s t) -> p s t", s=2) if tt == 0 else ap

        Ybanks = (0, 1, 2)
        XP = (3, 4, 5, 6, 7)
        pairs = [(q, d, w) for q in range(4) for d in range(2) for w in range(4)]

        def build_CL(q):
            CLq = CLqs[q % 2]
            k.memset("pool", CLq.rearrange("p d r w c -> p (d r w c)"), 0.0)
            for d in range(2):
                for r in range(3):
                    zz = zall[:, (d * 3 + r) % 2]
                    sgn = 1.0 if r == 0 else -1.0
                    src = cns[:, d, 1 if r == 1 else 0, q, :]
                    k.ts("pool", zz[:, 0:64], src, maskc[:, 0:1], sgn, ALU.mult, ALU.mult)
                    k.ts("pool", zz[:, 64:128], src, maskc[:, 1:2], sgn, ALU.mult, ALU.mult)
                    pt = ps(XP)
                    k.transpose(pt[:, 0:128], zz, ident)
                    for w in range(4):
                        k.copy("act", CLq[:, d, r, w, 32 * w:32 * w + 32], pt[:, 32 * w:32 * w + 32])

        def stA(i):
            q, d, w = pairs[i]; st = sets[i % 2]; W = 4 * q + w
            thc = th[:, d, W:W + 1]
            ang = st["tm"][:, 0:2].rearrange("p a b -> p (a b)"); rb = st["tm"][:, 2:4].rearrange("p a b -> p (a b)")
            qi = st["h32"].rearrange("p a b -> p (a b)").bitcast(I32)
            k.act(ang, iota, AF.Identity, scale=thc)
            k.ts("dve", qi, ang, 1.0 / TWO_PI)
            k.stt(rb, qi, -TWO_PI, ang, ALU.mult, ALU.add)
            k.act(st["tabS"], rb, AF.Sin, scale=SIN_SC)
            k.ts("dve", qi, ang, 1.0 / TWO_PI, 0.25, ALU.mult, ALU.add)
            k.stt(rb, qi, -TWO_PI, ang, ALU.mult, ALU.add)
            k.act(st["tabC"], rb, AF.Sin, scale=SIN_SC, bias=hpi_c)

        def stB(i):
            q, d, w = pairs[i]; st = sets[i % 2]
            tm = st["tm"]
            for tt in range(3):
                tsl = slice(tt * 512, (tt + 1) * 512)
                XR = ps(XP); XI = ps(XP)
                k.mm(XR, BL[32 * w:32 * w + 32, d, 0, q, :], uT[32 * w:32 * w + 32, q, tsl], tile_position=(32 * w, 0))
                k.mm(XI, BL[32 * w:32 * w + 32, d, 1, q, :], uT[32 * w:32 * w + 32, q, tsl], tile_position=(32 * w, 0))
                C = tabv(st["tabC"], d, tt); S = tabv(st["tabS"], d, tt)
                k.tt("dve", v3(tm[:, 0], tt), v3(XR, tt), C, ALU.mult)
                k.tt("dve", v3(tm[:, 1], tt), v3(XI, tt), S, ALU.mult)
                k.tt("pool", st["xr"][:, tsl], tm[:, 0], tm[:, 1], ALU.add)
                k.tt("dve", v3(tm[:, 2], tt), v3(XI, tt), C, ALU.mult)
                k.tt("dve", v3(tm[:, 3], tt), v3(XR, tt), S, ALU.mult)
                k.tt("pool", st["xi"][:, tsl], tm[:, 2], tm[:, 3], ALU.subtract)

        def stC(i):
            q, d, w = pairs[i]; st = sets[i % 2]; W = 4 * q + w
            mc = mcol[:, d, W:W + 1]
            for si, (o, L) in enumerate(seqs):
                for (buf, r) in ((st["xr"], 0), (st["xi"], 1)):
                    init = h0[:, d, r, W:W + 1] if si == 2 else 0.0
                    a_ = buf[:, o:o + L]
                    if d == 1:
                        a_ = rev(a_)
                    k.scan(a_, mc.to_broadcast([128, L]), a_, init)

        def stD(i):
            q, d, w = pairs[i]; st = sets[i % 2]; W = 4 * q + w
            gr = st["xr"]; gi = st["xi"]; pr = st["pr"]
            CLq = CLqs[q % 2]
            col = 255 if d == 0 else 0
            grc = apm.AP(gr.tensor, int(gr.offset) + col, [[_row(gr), 128], [256, 2]])
            gic = apm.AP(gi.tensor, int(gi.offset) + col, [[_row(gi), 128], [256, 2]])
            c255 = st["tabC"][:, 255:256]; s255 = st["tabS"][:, 255:256]
            tn = st["h32"][:, 0, 0:4]
            k.ts("dve", tn[:, 0:2], gic, s255)
            k.stt(NS[:, :, d, 0, W], grc, c255, tn[:, 0:2], ALU.mult, ALU.subtract)
            k.ts("dve", tn[:, 2:4], gic, c255)
            k.stt(NS[:, :, d, 1, W], grc, s255, tn[:, 2:4], ALU.mult, ALU.add)
            for tt in range(3):
                tsl = slice(tt * 512, (tt + 1) * 512)
                C = tabv(st["tabC"], d, tt); S = tabv(st["tabS"], d, tt)
                k.tt("dve", v3(pr[:, 0, tsl], tt), v3(gr[:, tsl], tt), C, ALU.mult)
                k.tt("dve", v3(pr[:, 1, tsl], tt), v3(gi[:, tsl], tt), S, ALU.mult)
                k.tt("dve", v3(pr[:, 2, tsl], tt), v3(gr[:, tsl], tt), S, ALU.mult)
                k.tt("dve", v3(pr[:, 3, tsl], tt), v3(gi[:, tsl], tt), C, ALU.mult)
                Y = psum[:, Ybanks[tt], :]
                last = (d == 1 and w == 3)
                k.mm(Y, CLq[:, d, 0, w, :], pr[:, 0, tsl], start=(d == 0 and w == 0), stop=False)
                k.mm(Y, CLq[:, d, 2, w, :], pr[:, 1, tsl], start=False, stop=False)
                k.mm(Y, CLq[:, d, 1, w, :], pr[:, 2, tsl], start=False, stop=False)
                k.mm(Y, CLq[:, d, 1, w, :], pr[:, 3, tsl], start=False, stop=last)
            if d == 1 and w == 3:
                for tt in range(3):
                    tsl = slice(tt * 512, (tt + 1) * 512)
                    Y = psum[:, Ybanks[tt], :]
                    k.stt(ygf, uT[:, q, tsl], dskipT[:, l * 4 + q:l * 4 + q + 1], Y, ALU.mult, ALU.add)
                    k.act(yaT[:, q, tsl], ygf, AF.Gelu_apprx_tanh)

        build_CL(0)
        stA(0); stB(0)
        for i in range(32):
            if i + 1 < 32:
                if pairs[i + 1][1] == 0 and pairs[i + 1][2] == 0:
                    build_CL(pairs[i + 1][0])
                stA(i + 1)
                stB(i + 1)
            stC(i)
            stD(i)
        pt = ps(XP)
        k.transpose(pt[:, 0:128], NS.rearrange("p s d r W -> p (s d r W)"), ident)
        nst = ar.alloc([128])
        k.copy("dve", nst, pt[:, 0:128])
        for s in range(2):
            k.dma("pool", ns_out.ap()[s, l].rearrange("d r (W hp) -> (d r W) hp", hp=128), nst[64 * s:64 * s + 64, :],
                  is_output=True)
        ar.release()
        wg = ar.alloc([4, 512], BF16)
        k.dma("pool", wg, w_glu.ap()[l].rearrange("(c p) f -> p c f", p=128))
        sg = ar.alloc([4, 512])
        for tt in range(3):
            tsl = slice(tt * 512, (tt + 1) * 512)
            for fb in range(4):
                pg = ps()
                for kc in range(4):
                    k.mm(pg, wg[:, kc, fb * 128:(fb + 1) * 128], yaT[:, kc, tsl], start=(kc == 0), stop=(kc == 3))
                k.act(sg[:, fb], pg, AF.Sigmoid)
            for fb in range(4):
                k.tt("dve", yaT[:, fb, tsl], yaT[:, fb, tsl], sg[:, fb], ALU.mult)
        ar.release()

    def attn_branch(l, qT, kT, Vt, ybT):
        ar.mark()
        so = ar.alloc([2, 512]); sz = ar.alloc([2, 512])
        rz = ar.alloc([2, 512]); o = ar.alloc([512]); t0 = ar.alloc([512]); sq = ar.alloc([512], BF16)
        rstd = ar.alloc([512])
        jobs = []
        for s in range(2):
            jobs.append((s * 256, 256, [(s * 256 + j * 128, s * 2 + j) for j in range(2)]))
        for qt in range(2):
            keys = [(512 + j * 128, 4 + j) for j in range(8)] + [(1536 + j * 128, 12 + j) for j in range(4)]
            jobs.append((512 + qt * 512, 512, keys))
        E = ar.alloc([4, 512], BF16)
        pending = []
        for h in range(4):
            for (qo, nq, keys) in jobs:
                O = [psum[:, 0, 0:nq], psum[:, 1, 0:nq]]
                Z = [psum[:, 2, 0:nq], psum[:, 3, 0:nq]]
                items = [(m, ki_, ko, vb) for m in range(2) for ki_, (ko, vb) in enumerate(keys)]
                n_it = len(items); nk_ = len(keys)
                Sl = {}

                def emitS(j):
                    m, ki_, ko, vb = items[j]
                    S = ps((4, 5, 6, 7))[:, 0:nq]
                    k.mm(S, kT[64 * m:64 * m + 64, h, ko:ko + 128], qT[64 * m:64 * m + 64, h, qo:qo + nq])
                    Sl[j] = S

                emitS(0)
                if n_it > 1:
                    emitS(1)
                for j in range(n_it):
                    m, ki_, ko, vb = items[j]
                    Eb = E[:, j % 4, 0:nq]
                    if pending and j == min(10, n_it - 1):
                        pending.pop(0)()
                    k.act(Eb, Sl.pop(j), AF.Exp, scale=0.125)
                    if j + 2 < n_it:
                        emitS(j + 2)
                    k.mm(O[m], Vt[:, vb, h * 128:(h + 1) * 128], Eb, start=(ki_ == 0), stop=(ki_ == nk_ - 1))
                    k.mm(Z[m], ones_bf, Eb, start=(ki_ == 0), stop=(ki_ == nk_ - 1))
                for m in range(2):
                    k.copy("act", so[:, m, 0:nq], O[m])
                    k.copy("act", sz[:, m, 0:nq], Z[m])

                def epi(h=h, qo=qo, nq=nq):
                    k.recip(rz[:, 0, 0:nq], sz[:, 0, 0:nq])
                    k.recip(rz[:, 1, 0:nq], sz[:, 1, 0:nq])
                    k.tt("dve", t0[:, 0:nq], so[:, 0, 0:nq], rz[:, 0, 0:nq], ALU.mult)
                    k.tt("dve", o[:, 0:nq], so[:, 1, 0:nq], rz[:, 1, 0:nq], ALU.mult)
                    k.stt(o[:, 0:nq], o[:, 0:nq], lamc[:, l, 1:2], t0[:, 0:nq], ALU.mult, ALU.add)
                    k.tt("pool", sq[:, 0:nq], o[:, 0:nq], o[:, 0:nq], ALU.mult)
                    ssq = ps((4, 5, 6, 7))[:, 0:nq]
                    k.mm(ssq, ones_bf, sq[:, 0:nq])
                    k.act(rstd[:, 0:nq], ssq, AF.Sqrt, scale=1.0 / 128, bias=eps_c)
                    k.recip(rstd[:, 0:nq], rstd[:, 0:nq])
                    k.stt(ybT[:, h, qo:qo + nq], o[:, 0:nq], gfac[:, l:l + 1], rstd[:, 0:nq], ALU.mult, ALU.mult)
                pending.append(epi)
        while pending:
            pending.pop(0)()
        ar.release()

    def pool_branch(l, zp, ycT):
        ar.mark()
        ZW = 1600
        offs = [16, 288, 560]
        wa = ar.alloc([ZW]); wb_ = ar.alloc([ZW]); pooled = ar.alloc([4, NT], BF16); pf = ar.alloc([ZW])
        k.memset("pool", wa, 0.0); k.memset("pool", wb_, 0.0)
        lo, hi_ = 8, ZW - 8
        for g, wdw in enumerate((2, 4, 8, 16)):
            z = zp[:, g, :]
            k.tt("pool", wa[:, lo:hi_], z[:, lo - 1:hi_ - 1], z[:, lo:hi_], ALU.add)
            cur, oth = wa, wb_
            sh = 1
            ww = 2
            while ww < wdw:
                k.tt("pool", oth[:, lo:hi_], cur[:, lo - sh:hi_ - sh], cur[:, lo + sh:hi_ + sh], ALU.add)
                cur, oth = oth, cur
                sh *= 2; ww *= 2
            k.stt(pf[:, lo:hi_], cur[:, lo:hi_], 1.0 / wdw, z[:, lo:hi_], ALU.mult, ALU.subtract)
            for si, (o, L) in enumerate(((16, 256), (288, 256), (560, 1024))):
                for e in range(2):
                    c0 = o if e == 0 else o + L - 8
                    k.tt("pool", pf[:, c0:c0 + 8], cur[:, c0:c0 + 8], poolc[:, g, 0 if si < 2 else 1, e, :], ALU.mult)
                    k.tt("pool", pf[:, c0:c0 + 8], pf[:, c0:c0 + 8], z[:, c0:c0 + 8], ALU.subtract)
            for si, (o, L) in enumerate(((16, 256), (288, 256), (560, 1024))):
                to = (0, 256, 512)[si]
                k.copy("act", pooled[:, g, to:to + L], pf[:, o:o + L])
        wp = ar.alloc([4, 128], BF16)
        k.dma("pool", wp, w_pool.ap()[l].rearrange("g c d -> c g d"))
        for g in range(4):
            for tt in range(3):
                tsl = slice(tt * 512, (tt + 1) * 512)
                pp = ps()
                k.mm(pp, wp[:, g, :], pooled[:, g, tsl])
                k.act(ycT[:, g, tsl], pp, AF.Identity, scale=pscT[:, l * 4 + g:l * 4 + g + 1])
        ar.release()

    def mixer(l):
        rmsnorm_mod(l, 1)
        ar.mark()
        yaT = ar.alloc([4, NT], BF16)
        wv = w_in.ap()[l].rearrange("(c p) f -> p c f", p=128)

        def wchunk(c):
            return wload(wv[:, :, c * 512:(c + 1) * 512], [8, 512])

        def proj_fm(wc, fb, tt):
            p = ps()
            for kc in range(8):
                k.mm(p, wc[:, kc, fb * 128:(fb + 1) * 128], nT[:, kc, tt * 512:(tt + 1) * 512],
                     start=(kc == 0), stop=(kc == 7))
            return p

        def proj_tm(wc, tb):
            p = ps()
            for kc in range(8):
                k.mm(p, nT[:, kc, tb * 128:(tb + 1) * 128], wc[:, kc, :], start=(kc == 0), stop=(kc == 7))
            return p

        ar.mark()
        uT = yaT
        ar.mark()
        wc = wchunk(0)
        for fb in range(4):
            for tt in range(3):
                p = proj_fm(wc, fb, tt)
                k.copy("act" if (fb + tt) % 2 else "dve", uT[:, fb, tt * 512:(tt + 1) * 512], p)
        ar.release()
        s5_branch(l, uT, yaT)
        ar.release()
        rmsnorm_mod(l, 1)
        stage('s5', [yaT])
        ybT = ar.alloc([4, NT], BF16); ycT = ar.alloc([4, NT], BF16)
        ar.mark()
        qT = ar.alloc([4, NT], BF16); kT = ar.alloc([4, 2048], BF16); Vt = ar.alloc([16, 512], BF16)
        qraw = ar.alloc([2, 512], BF16); tr = ar.alloc([2, 512]); stg_o = ar.alloc([2, 512])
        for which, dst in ((1, qT), (2, kT)):
            ar.mark()
            wc = wchunk(which)
            for fb in range(4):
                for tt in range(3):
                    tsl = slice(tt * 512, (tt + 1) * 512)
                    p = proj_fm(wc, fb, tt)
                    if tt == 0:
                        k.copy("act", dst[:, fb, tsl], p)
                    else:
                        qb = qraw[:, (fb + tt) % 2]
                        k.copy("act", qb, p)
                        pq = ps()
                        k.mm(pq, pt_bf, qb)
                        pos = slice((tt - 1) * 512, tt * 512)
                        k.tt("dve", tr[:, 0], qb, rope[:, 0, pos], ALU.mult)
                        k.tt("dve", tr[:, 1], pq, rope[:, 1, pos], ALU.mult)
                        k.tt("pool", dst[:, fb, tsl], tr[:, 0], tr[:, 1], ALU.add)
            if which == 2:
                for tb in range(4):
                    p = proj_tm(wc, tb)
                    st_ = stg_o[:, tb % 2]
                    k.copy("dve", st_, p)
                    k.dma("pool", nk_out.ap()[tb // 2, l, (tb % 2) * 128:(tb % 2) * 128 + 128, :], st_, is_output=True)
            ar.release()
        stage('qk', [qT])
        ar.mark()
        wc = wchunk(3)
        for tb in ((4, 5, 6, 7, 8, 9, 10, 11, 0, 1, 2, 3) if 'e' in _VD else range(4 if 'a' in _VD else 12)):
            if 'c' in _VD and tb < 4:
                continue
            p = proj_tm(wc, tb)
            if 'b' in _VD:
                k.copy("act", Vt[:, tb, :], p)
                continue
            if tb >= 4:
                k.copy("act", Vt[:, tb, :], p)
            else:
                st_ = stg_o[:, tb % 2]
                k.copy("dve", st_, p)
                k.copy("act", Vt[:, tb, :], st_)
                if 'd' in _VD:
                    continue
                k.dma("pool", nv_out.ap()[tb // 2, l, (tb % 2) * 128:(tb % 2) * 128 + 128, :], st_, is_output=True)
        stage('v', [ybT])
        k.dma("pool", Vt[:, 12:16, :], cv.ap()[l].rearrange("(b p) f -> p b f", p=128))
        ckb = ar.alloc([4, 512], BF16)
        k.dma("pool", ckb, ck.ap()[l].rearrange("(b p) f -> p b f", p=128))
        for h in range(4):
            ptb = ps().bitcast(BF16)
            for b in range(4):
                k.transpose(ptb[:, b * 128:(b + 1) * 128], ckb[:, b, h * 128:(h + 1) * 128], ident_bf)
            k.copy("dve", kT[:, h, 1536:2048], ptb[:, 0:512])
        ar.release()
        stage('cache', [ybT])
        attn_branch(l, qT, kT, Vt, ybT)
        ar.release()
        stage('attn', [ybT])
        ar.mark()
        zp = ar.alloc([4, 1600])
        k.memset("pool", zp, 0.0)
        ar.mark()
        wc = wchunk(4)
        for fb in range(4):
            p = proj_fm(wc, fb, 0)
            k.copy("act", zp[:, fb, 16:272], p[:, 0:256])
            k.copy("act", zp[:, fb, 288:544], p[:, 256:512])
            for tt in (1, 2):
                p = proj_fm(wc, fb, tt)
                k.copy("act", zp[:, fb, 560 + (tt - 1) * 512:560 + tt * 512], p)
        ar.release()
        pool_branch(l, zp, ycT)
        ar.release()
        stage('pool', [ycT])
        ar.mark()
        mT = ar.alloc([8, NT], BF16)
        ys = (yaT, ybT, ycT)
        gs = ar.alloc([2, 512]); acc = ar.alloc([512]); t2 = ar.alloc([512])
        wbv = w_branch.ap()[l].rearrange("n (c p) d -> p n c d", p=128)
        mw = [(ar.alloc([3, 8, 128], BF16), ar.alloc([3, 4, 128], BF16)) for _ in range(2)]
        for dc in range(8):
            ar.mark()
            wg_, wb3 = mw[dc % 2]
            for n in range(3):
                k.dma("pool", wg_[:, n], wv[:, :, 2560 + n * 1024 + dc * 128:2560 + n * 1024 + (dc + 1) * 128])
            k.dma("pool", wb3, wbv[:, :, :, dc * 128:(dc + 1) * 128])
            for tt in range(3):
                tsl = slice(tt * 512, (tt + 1) * 512)
                for n in range(3):
                    pg = ps()
                    for kc in range(8):
                        k.mm(pg, wg_[:, n, kc, :], nT[:, kc, tsl], start=(kc == 0), stop=(kc == 7))
                    pb = ps()
                    for kc in range(4):
                        k.mm(pb, wb3[:, n, kc, :], ys[n][:, kc, tsl], start=(kc == 0), stop=(kc == 3))
                    gb = gs[:, n % 2]
                    k.act(gb, pg, AF.Sigmoid)
                    if n == 0:
                        k.tt("dve", acc, gb, pb, ALU.mult)
                    elif n == 1:
                        k.tt("dve", t2, gb, pb, ALU.mult)
                        k.tt("pool", acc, acc, t2, ALU.add)
                    else:
                        k.tt("dve", t2, gb, pb, ALU.mult)
                        k.tt("pool", mT[:, dc, tsl], acc, t2, ALU.add)
            ar.release()
        wov = w_out.ap()[l].rearrange("(c p) d -> p c d", p=128)
        for half in range(2):
            ar.mark()
            wo = wload(wov[:, :, half * 512:(half + 1) * 512], [8, 512])
            for s in range(4):
                dc = half * 4 + s
                for tt in range(3):
                    tsl = slice(tt * 512, (tt + 1) * 512)
                    py = ps()
                    for kc in range(8):
                        k.mm(py, wo[:, kc, s * 128:(s + 1) * 128], mT[:, kc, tsl], start=(kc == 0), stop=(kc == 7))
                    v = vsel(tt)
                    k.stt(xT[:, dc, tsl], py, Gmod[:, l, 1, dc, v:v + 1], xT[:, dc, tsl], ALU.mult, ALU.add)
            ar.release()
        ar.release()
        ar.release()

    for l in range(depth):
        ffn(l, 0)
        stage('ffn0', [xT])
        mixer(l)
        stage('mixer', [xT])
        ffn(l, 1)

    ar.mark()
    yT = ar.alloc([8, NT])
    rmsnorm_mod(0, 3, out_fn=lambda kc, tt: yT[:, kc, tt * 512:(tt + 1) * 512])
    obufs = [ar.alloc([D]) for _ in range(2)]
    for tb in range(12):
        ob = obufs[tb % 2]
        for g4 in range(2):
            pt = ps()
            for c in range(4):
                kc = g4 * 4 + c
                k.transpose(pt[:, c * 128:(c + 1) * 128], yT[:, kc, tb * 128:(tb + 1) * 128], ident)
            k.copy("act" if g4 else "dve", ob[:, g4 * 512:(g4 + 1) * 512], pt)
        k.dma("sp", y_out.ap()[tb * 128:(tb + 1) * 128, :], ob, is_output=True)
    ar.release()
    k.finish()
    return k


def _consts():
    c = {}
    c["c_ident"] = np.eye(128, dtype=np.float32)
    inv = (10000.0 ** (-np.arange(16, dtype=np.float32) / 16)).astype(np.float32)
    t = np.arange(1024)
    row = (t // 64).astype(np.float32); col = (t % 64).astype(np.float32)
    rope = np.zeros((128, 2, 1024), np.float32)
    pt = np.zeros((128, 128), np.float32)
    for m in range(2):
        for d in range(64):
            p = m * 64 + d
            pos = row if d < 32 else col
            ang = (pos * inv[d % 16]).astype(np.float32)
            rope[p, 0] = np.cos(ang); rope[p, 1] = np.sin(ang)
            dd = d % 32
            if dd < 16:
                pt[p + 16, p] = -1.0
            else:
                pt[p - 16, p] = 1.0
    c["c_rope"] = rope; c["c_pt"] = pt
    c["c_iota"] = np.broadcast_to(np.arange(1, 1025, dtype=np.float32), (128, 1024)).copy()
    mk = np.zeros((128, 2), np.float32)
    for p in range(128):
        g = p // 16
        mk[p, g % 2] = 1.0
    c["c_mask"] = mk
    pc = np.zeros((128, 4, 2, 2, 8), np.float32)
    for g, w in enumerate((2, 4, 8, 16)):
        for lt, L in enumerate((256, 1024)):
            for e in range(2):
                for j in range(8):
                    t_ = j if e == 0 else L - 8 + j
                    lo = min(max(t_ - w // 2, 0), L); hi = min(max(t_ - w // 2 + w, 0), L)
                    pc[:, g, lt, e, j] = 1.0 / (hi - lo)
    c["c_pool"] = pc
    return c


_W_NAMES = ["norm_g", "w_mod", "b_mod", "w_ffn_in", "w_ffn_out", "w_in", "ssm_lam_re", "ssm_lam_im", "ssm_log_dt",
            "ssm_b_re", "ssm_b_im", "ssm_c_re", "ssm_c_im", "ssm_d", "w_glu", "lam_q1", "lam_k1", "lam_q2", "lam_k2",
            "attn_norm_g", "w_pool", "pool_scale", "w_branch", "w_out", "final_norm_g"]


def kernel(**inp):
    nc = bass.Bass("TRN2", target_bir_lowering=False)
    build(nc)
    consts = _consts()
    f = lambda a: np.ascontiguousarray(np.asarray(a, dtype=np.float32))
    xp = f(inp["x_prompt"]); xs = f(inp["x_sample"])
    ck = f(inp["cache_k"]); cv = f(inp["cache_v"]); st = f(inp["state_ssm"]); c = f(inp["c"]); cctx = f(inp["c_ctx"])
    wts = {n: f(inp[n]) for n in _W_NAMES}
    in_maps = []
    for i in range(8):
        m = dict(wts); m.update(consts)
        m["xin"] = np.concatenate([xp[2 * i], xp[2 * i + 1], xs[i]], axis=0)
        m["ck"] = ck[i].reshape(DEPTH, 512, 512)
        m["cv"] = cv[i].reshape(DEPTH, 512, 512)
        m["st"] = st[i].reshape(DEPTH, 2, 2, 2048)
        m["cvec"] = np.stack([cctx, c[i]], axis=0)
        in_maps.append(m)
    res = run_bass_kernel_spmd(nc, in_maps, core_ids=list(range(8)))
    R = res.results
    y_prompt = np.stack([R[i // 2]["y"][(i % 2) * 256:(i % 2) * 256 + 256] for i in range(16)], axis=0)
    y_sample = np.stack([R[i]["y"][512:1536] for i in range(8)], axis=0)
    nk = np.concatenate([R[i]["nk"] for i in range(8)], axis=0).reshape(16, DEPTH, 256, 4, 2, 64)
    nv = np.concatenate([R[i]["nv"] for i in range(8)], axis=0).reshape(16, DEPTH, 256, 4, 128)
    ns = np.concatenate([R[i]["ns"] for i in range(8)], axis=0).reshape(16, DEPTH, 2, 2, 32, 64)
    return (y_prompt.astype(np.float32), y_sample.astype(np.float32), nk.astype(np.float32), nv.astype(np.float32),
            ns.astype(np.float32))
```

```python
import math
import numpy as np
_VD = ''
import concourse.bass as bass
import concourse.mybir as mybir
import concourse.ap as apm
from concourse.bass_utils import run_bass_kernel_spmd

F32 = mybir.dt.float32
BF16 = mybir.dt.bfloat16
I32 = mybir.dt.int32
AF = mybir.ActivationFunctionType
ALU = mybir.AluOpType
AX = mybir.AxisListType
_ES = {str(F32): 4, str(BF16): 2, str(I32): 4}

D = 1024; NT = 1536; DFF = 2816; INW = 5632; DEPTH = 4
EPS = 1e-6
TWO_PI = 2.0 * math.pi
SIN_SC = 1.0 - 2e-4


class _Rec:
    __slots__ = ("eng", "sem", "val", "write", "p0", "p1", "f0", "f1")

    def __init__(self, eng, sem, val, write, box):
        self.eng = eng; self.sem = sem; self.val = val; self.write = write
        self.p0, self.p1, self.f0, self.f1 = box


def _box(ap):
    pat = ap.ap
    es = _ES[str(ap.dtype)]
    off = int(ap.offset)
    row = pat[0][0]
    if row == 0:
        p0 = 0; f = off
    else:
        p0 = off // row; f = off - p0 * row
    p1 = p0 + pat[0][1]
    lo = f; hi = f
    for st, cnt in pat[1:]:
        if st >= 0:
            hi += st * (cnt - 1)
        else:
            lo += st * (cnt - 1)
    return (ap.tensor.name, p0, p1, lo * es, (hi + 1) * es)


class K:
    def __init__(self, nc, n_dma_sems=32):
        self.nc = nc
        self.engs = {"pe": nc.tensor, "act": nc.scalar, "dve": nc.vector, "pool": nc.gpsimd, "sp": nc.sync}
        self.sem = {}; self.cnt = {}
        for e in ("pe", "act", "dve", "pool"):
            self.sem[e] = nc.semaphore("s_" + e).__enter__()
            self.cnt[e] = 0
        self.dsem = [nc.semaphore("d%d" % i).__enter__() for i in range(n_dma_sems)]
        self.dval = [0] * n_dma_sems
        self.dnext = 0
        self.dnext_sw = 0
        self.known = {e: {} for e in self.engs}
        self.recs = {}
        self.out_waits = []
        self.n_inst = {e: 0 for e in self.engs}
        self.n_wait = 0

    def _need(self, eng, sem, val):
        if eng == "pe" and sem is self.sem["pe"]:
            return
        kn = self.known[eng]
        key = sem.name
        if kn.get(key, 0) >= val:
            return
        kn[key] = val
        self.engs[eng].wait_ge(sem, val)
        self.n_wait += 1

    def _tracked(self, ap):
        if ap is None or isinstance(ap, (int, float)):
            return False
        if str(ap.space) == "DRAM" and ap.tensor.name not in self.recs:
            return False
        return True

    def _deps(self, eng, reads, writes):
        for ap in reads:
            if not self._tracked(ap):
                continue
            name, p0, p1, f0, f1 = _box(ap)
            for r in self.recs.get(name, ()):
                if r.write and r.p0 < p1 and p0 < r.p1 and r.f0 < f1 and f0 < r.f1:
                    self._need(eng, r.sem, r.val)
        for ap in writes:
            if not self._tracked(ap):
                continue
            name, p0, p1, f0, f1 = _box(ap)
            for r in self.recs.get(name, ()):
                if r.p0 < p1 and p0 < r.p1 and r.f0 < f1 and f0 < r.f1:
                    self._need(eng, r.sem, r.val)

    def _record(self, eng, sem, val, reads, writes):
        for ap in writes:
            if not self._tracked(ap):
                continue
            name, p0, p1, f0, f1 = _box(ap)
            lst = self.recs.setdefault(name, [])
            lst[:] = [r for r in lst if not (p0 <= r.p0 and r.p1 <= p1 and f0 <= r.f0 and r.f1 <= f1)]
            lst.append(_Rec(eng, sem, val, True, (p0, p1, f0, f1)))
        for ap in reads:
            if not self._tracked(ap):
                continue
            name, p0, p1, f0, f1 = _box(ap)
            lst = self.recs.setdefault(name, [])
            lst[:] = [r for r in lst if not ((not r.write) and r.eng == eng and p0 <= r.p0 and r.p1 <= p1
                                              and f0 <= r.f0 and r.f1 <= f1)]
            lst.append(_Rec(eng, sem, val, False, (p0, p1, f0, f1)))

    def track_dram(self, t):
        self.recs.setdefault(t.name, [])

    def _fin(self, eng, inst, reads, writes, inc=True):
        self.n_inst[eng] += 1
        if inc:
            self.cnt[eng] += 1
            inst.then_inc(self.sem[eng], 1)
            self._record(eng, self.sem[eng], self.cnt[eng], reads, writes)
        else:
            self._record(eng, self.sem[eng], self.cnt[eng] + 1, reads, writes)

    def mm(self, out, lhsT, rhs, start=True, stop=True, **kw):
        self._deps("pe", [lhsT, rhs], [out])
        i = self.nc.tensor.matmul(out, lhsT, rhs, start=start, stop=stop, **kw)
        self._fin("pe", i, [lhsT, rhs], [out], inc=stop)

    def transpose(self, out, in_, ident):
        self._deps("pe", [in_, ident], [out])
        i = self.nc.tensor.transpose(out, in_, ident)
        self._fin("pe", i, [in_, ident], [out])

    def act(self, out, in_, func, scale=1.0, bias=None):
        rd = [in_] + [a for a in (scale, bias) if a is not None and not isinstance(a, (int, float))]
        self._deps("act", rd, [out])
        kw = {}
        if bias is not None:
            kw["bias"] = bias
        i = self.nc.scalar.activation(out=out, in_=in_, func=func, scale=scale, **kw)
        self._fin("act", i, rd, [out])

    def _ve(self, eng):
        return self.nc.vector if eng == "dve" else self.nc.gpsimd

    def tt(self, eng, out, in0, in1, op):
        self._deps(eng, [in0, in1], [out])
        i = self._ve(eng).tensor_tensor(out, in0, in1, op)
        self._fin(eng, i, [in0, in1], [out])

    def ts(self, eng, out, in0, s1, s2=None, op0=ALU.mult, op1=None):
        rd = [in0] + [a for a in (s1, s2) if a is not None and not isinstance(a, (int, float))]
        self._deps(eng, rd, [out])
        if op1 is not None:
            i = self._ve(eng).tensor_scalar(out, in0, s1, s2, op0, op1)
        else:
            i = self._ve(eng).tensor_scalar(out, in0, s1, None, op0)
        self._fin(eng, i, rd, [out])

    def stt(self, out, in0, scalar, in1, op0, op1):
        rd = [in0, in1] + ([scalar] if not isinstance(scalar, (int, float)) else [])
        self._deps("dve", rd, [out])
        i = self.nc.vector.scalar_tensor_tensor(out, in0, scalar, in1, op0, op1)
        self._fin("dve", i, rd, [out])

    def scan(self, out, d0, d1, initial):
        rd = [d0, d1] + ([initial] if not isinstance(initial, (int, float)) else [])
        self._deps("dve", rd, [out])
        i = self.nc.vector.tensor_tensor_scan(out, d0, d1, initial, ALU.mult, ALU.add)
        self._fin("dve", i, rd, [out])

    def copy(self, eng, out, in_):
        if eng == "act":
            return self.act(out, in_, AF.Copy)
        self._deps(eng, [in_], [out])
        i = self._ve(eng).tensor_copy(out, in_)
        self._fin(eng, i, [in_], [out])

    def memset(self, eng, ap, val):
        self._deps(eng, [], [ap])
        i = self._ve(eng).memset(ap, val)
        self._fin(eng, i, [], [ap])

    def recip(self, out, in_):
        self._deps("dve", [in_], [out])
        i = self.nc.vector.reciprocal(out, in_)
        self._fin("dve", i, [in_], [out])

    def dma(self, q, out, in_, is_output=False, **kw):
        half = len(self.dsem) // 2
        if q == "pool":
            s = half + self.dnext_sw
            self.dnext_sw = (self.dnext_sw + 1) % half
        else:
            s = self.dnext
            self.dnext = (self.dnext + 1) % half
        sem = self.dsem[s]
        if self.dval[s] > 0:
            self._need(q, sem, self.dval[s])
        self._deps(q, [in_], [out])
        self.dval[s] += 16
        self.engs[q].dma_start(out=out, in_=in_, **kw).then_inc(sem, 16)
        self.n_inst[q] += 1
        self._record("dma%d" % s, sem, self.dval[s], [in_], [out])
        if is_output:
            self.out_waits.append((s, self.dval[s]))

    def finish(self):
        last = {}
        for s, v in self.out_waits:
            last[s] = max(last.get(s, 0), v)
        for s, v in last.items():
            self._need("sp", self.dsem[s], v)
        for e in ("pe", "act", "dve", "pool"):
            if self.cnt[e] > 0:
                self._need("sp", self.sem[e], self.cnt[e])


class Arena:
    def __init__(self, nc, nwords):
        self.t = nc.sbuf_tensor("arena", [128, nwords], F32).__enter__()
        self.n = nwords; self.top = 0; self.stack = []

    def alloc(self, free_shape, dt=F32):
        n = 1
        for s in free_shape:
            n *= s
        words = n if _ES[str(dt)] == 4 else (n + 1) // 2
        words = (words + 7) // 8 * 8
        off = self.top
        self.top += words
        assert self.top <= self.n, "arena overflow %d > %d" % (self.top, self.n)
        ap = self.t[:, off:off + words]
        if dt != F32:
            ap = ap.bitcast(dt)
        ap = ap[:, 0:n]
        if len(free_shape) == 2:
            ap = ap.rearrange("p (a b) -> p a b", a=free_shape[0])
        elif len(free_shape) == 3:
            ap = ap.rearrange("p (a b c) -> p a b c", a=free_shape[0], b=free_shape[1])
        elif len(free_shape) == 4:
            ap = ap.rearrange("p (a b c d) -> p a b c d", a=free_shape[0], b=free_shape[1], c=free_shape[2])
        return ap

    def mark(self):
        self.stack.append(self.top)

    def release(self):
        self.top = self.stack.pop()


def _row(ap):
    return ap.ap[0][0]


def rev(ap):
    pat = ap.ap
    assert len(pat) == 2
    st, n = pat[1]
    return apm.AP(ap.tensor, int(ap.offset) + (n - 1) * st, [list(pat[0]), [-st, n]])


class _Stop(Exception):
    pass


def build(nc, depth=DEPTH, dbg=False, stop=None):
    try:
        return _build(nc, depth, dbg, stop)
    except _Stop as e:
        e.args[0].finish()
        return e.args[0]


def _build(nc, depth, dbg, stop):
    k = K(nc)

    dbg_t = {}

    def stage(name, dumps=()):
        if stop != name:
            return
        off = 0
        ar.mark()
        scr = ar.alloc([2048])
        for ap in dumps:
            flat = ap
            if len(ap.shape) == 3:
                flat = ap.rearrange("p a b -> p (a b)")
            elif len(ap.shape) == 4:
                flat = ap.rearrange("p a b c -> p (a b c)")
            n = flat.shape[1]
            for c0 in range(0, n, 2048):
                c1 = min(n, c0 + 2048)
                k.copy("dve", scr[:, 0:c1 - c0], flat[:, c0:c1])
                k.dma("sp", dbg_t["d"].ap()[:, off + c0:off + c1], scr[:, 0:c1 - c0], is_output=True)
            off += n
        raise _Stop(k)

    ar = Arena(nc, 53000)
    psum = nc.psum_tensor("ps", [128, 8, 512], F32).__enter__()
    st_ps = {"i": 0}

    def ps(pool=(0, 1, 2, 3, 4, 5, 6, 7)):
        st_ps["i"] += 1
        return psum[:, pool[st_ps["i"] % len(pool)], :]

    def din(name, shape, dt=F32):
        return nc.dram_tensor(name, list(shape), dt, kind="ExternalInput")

    xin = din("xin", [NT, D]); ck = din("ck", [DEPTH, 512, 512]); cv = din("cv", [DEPTH, 512, 512])
    stt_in = din("st", [DEPTH, 2, 2, 2048]); cvec = din("cvec", [2, D])
    norm_g = din("norm_g", [DEPTH, 3, D]); w_mod = din("w_mod", [DEPTH, D, 9 * D]); b_mod = din("b_mod", [DEPTH, 9 * D])
    w_ffn_in = din("w_ffn_in", [DEPTH, 2, D, 2 * DFF]); w_ffn_out = din("w_ffn_out", [DEPTH, 2, DFF, D])
    w_in = din("w_in", [DEPTH, D, INW])
    lam_re = din("ssm_lam_re", [DEPTH, 2, 32, 64]); lam_im = din("ssm_lam_im", [DEPTH, 2, 32, 64])
    log_dt = din("ssm_log_dt", [DEPTH, 2, 32])
    b_re = din("ssm_b_re", [DEPTH, 2, 32, 64, 16]); b_im = din("ssm_b_im", [DEPTH, 2, 32, 64, 16])
    c_re = din("ssm_c_re", [DEPTH, 2, 32, 16, 64]); c_im = din("ssm_c_im", [DEPTH, 2, 32, 16, 64])
    ssm_d = din("ssm_d", [DEPTH, 512]); w_glu = din("w_glu", [DEPTH, 512, 512])
    lq1 = din("lam_q1", [DEPTH, 64]); lk1 = din("lam_k1", [DEPTH, 64])
    lq2 = din("lam_q2", [DEPTH, 64]); lk2 = din("lam_k2", [DEPTH, 64])
    attn_g = din("attn_norm_g", [DEPTH, 128]); w_pool = din("w_pool", [DEPTH, 4, 128, 128])
    pool_scale = din("pool_scale", [DEPTH, 512]); w_branch = din("w_branch", [DEPTH, 3, 512, D])
    w_out = din("w_out", [DEPTH, D, D]); final_g = din("final_norm_g", [D])
    c_ident = din("c_ident", [128, 128]); c_rope = din("c_rope", [128, 2, 1024]); c_pt = din("c_pt", [128, 128])
    c_iota = din("c_iota", [128, 1024]); c_mask = din("c_mask", [128, 2]); c_pool = din("c_pool", [128, 4, 2, 2, 8])

    def dout(name, shape):
        return nc.dram_tensor(name, list(shape), F32, kind="ExternalOutput")

    y_out = dout("y", [NT, D]); nk_out = dout("nk", [2, DEPTH, 256, 512]); nv_out = dout("nv", [2, DEPTH, 256, 512])
    ns_out = dout("ns", [2, DEPTH, 2, 2, 2048])
    if stop is not None:
        dbg_t["d"] = dout("dbg", [128, 32768])

    ident = ar.alloc([128]); ident_bf = ar.alloc([128], BF16); ones_bf = ar.alloc([128], BF16)
    ones_f = ar.alloc([128]); pt_bf = ar.alloc([128], BF16)
    rope = ar.alloc([2, 1024]); iota = ar.alloc([1024]); maskc = ar.alloc([2]); poolc = ar.alloc([4, 2, 2, 8])
    cst = ar.alloc([8])
    k.dma("sp", ident, c_ident.ap())
    k.dma("sp", rope, c_rope.ap())
    k.dma("sp", iota, c_iota.ap())
    k.dma("sp", maskc, c_mask.ap())
    k.dma("sp", poolc, c_pool.ap())
    tmp_pt = ar.alloc([128])
    k.dma("sp", tmp_pt, c_pt.ap())
    k.copy("dve", ident_bf, ident)
    k.copy("dve", pt_bf, tmp_pt)
    k.memset("dve", ones_bf, 1.0)
    k.memset("dve", ones_f, 1.0)
    k.memset("dve", cst[:, 0:1], EPS)
    k.memset("dve", cst[:, 1:2], 0.5 * math.pi * SIN_SC)
    k.memset("dve", cst[:, 2:3], 0.0)
    eps_c = cst[:, 0:1]; hpi_c = cst[:, 1:2]

    xT = ar.alloc([8, NT])
    nT = ar.alloc([8, NT], BF16)
    modT = ar.alloc([DEPTH, 72, 2])
    Amod = ar.alloc([DEPTH, 3, 8, 2])
    Gmod = ar.alloc([DEPTH, 3, 8, 2])
    normgT = ar.alloc([96])
    smallT = ar.alloc([64])
    lamc = ar.alloc([DEPTH, 2])
    gfac = ar.alloc([DEPTH])
    sc = ar.alloc([2, 8])

    ar.mark()
    stg = ar.alloc([128])
    k.dma("sp", stg[0:96, :], norm_g.ap().rearrange("l j (c p) -> (l j c) p", p=128))
    pt = ps()
    k.transpose(pt[:, 0:96], stg[0:96, :], ident[0:96, 0:96])
    k.copy("dve", normgT, pt[:, 0:96])
    stg2 = ar.alloc([128])
    k.dma("sp", stg2[0:16, :], ssm_d.ap().rearrange("l (c p) -> (l c) p", p=128))
    k.dma("sp", stg2[16:32, :], pool_scale.ap().rearrange("l (c p) -> (l c) p", p=128))
    k.dma("sp", stg2[32:36, :], attn_g.ap())
    k.dma("sp", stg2[36:44, :], final_g.ap().rearrange("(c p) -> c p", p=128))
    k.dma("sp", stg2[44:60, :], cvec.ap().rearrange("v (c p) -> (v c) p", p=128))
    pt = ps()
    k.transpose(pt[:, 0:60], stg2[0:60, :], ident[0:60, 0:60])
    k.copy("dve", smallT[:, 0:60], pt[:, 0:60])
    dskipT = smallT[:, 0:16]; pscT = smallT[:, 16:32]; attngT = smallT[:, 32:36]; fingT = smallT[:, 36:44]
    k.act(sc.rearrange("p v c -> p (v c)"), smallT[:, 44:60], AF.Silu)
    bmodT = ar.alloc([DEPTH, 72])
    for l in range(DEPTH):
        stg3 = ar.alloc([128])
        k.dma("sp", stg3[0:72, :], b_mod.ap()[l].rearrange("(c p) -> c p", p=128))
        pt = ps()
        k.transpose(pt[:, 0:72], stg3[0:72, :], ident[0:72, 0:72])
        k.copy("dve", bmodT[:, l, :], pt[:, 0:72])
    lv = ar.alloc([4, DEPTH])
    with nc.allow_non_contiguous_dma(reason="tiny"):
        for i, t in enumerate((lq1, lk1, lq2, lk2)):
            k.dma("sp", lv[0:64, i, :], t.ap().rearrange("l d -> d l"))
    lp = ar.alloc([2, DEPTH])
    k.tt("dve", lp[0:64, 0, :], lv[0:64, 0, :], lv[0:64, 1, :], ALU.mult)
    k.tt("dve", lp[0:64, 1, :], lv[0:64, 2, :], lv[0:64, 3, :], ALU.mult)
    pt = ps()
    k.mm(pt[:, 0:2 * DEPTH], ones_f[0:64, :], lp[0:64].rearrange("p a l -> p (a l)"))
    le = ar.alloc([2, DEPTH])
    k.act(le.rearrange("p a l -> p (a l)"), pt[:, 0:2 * DEPTH], AF.Exp)
    for l in range(DEPTH):
        li_ = 0.8 - 0.6 * math.exp(-0.3 * l)
        k.stt(lamc[:, l, 0:1], le[:, 0, l:l + 1], li_, le[:, 1, l:l + 1], ALU.add, ALU.subtract)
        k.ts("dve", lamc[:, l, 1:2], lamc[:, l, 0:1], -1.0)
        k.ts("dve", gfac[:, l:l + 1], attngT[:, l:l + 1], 1.0 - li_)

    wms = [ar.alloc([8, 256]) for _ in range(2)]
    for l in range(depth):
        mps = ps()
        for fc in range(36):
            wm = wms[fc % 2]
            k.dma("sp", wm, w_mod.ap()[l].rearrange("(c p) f -> p c f", p=128)[:, :, fc * 256:(fc + 1) * 256])
            for s in range(2):
                fb = fc * 2 + s
                for kc in range(8):
                    k.mm(mps[:, fb * 2:fb * 2 + 2], wm[:, kc, s * 128:(s + 1) * 128], sc[:, :, kc],
                         start=(kc == 0), stop=(kc == 7))
        for v in range(2):
            k.tt("dve", modT[:, l, :, v], mps[:, 0:144].rearrange("p (f v) -> p f v", v=2)[:, :, v], bmodT[:, l, :], ALU.add)
        for j in range(3):
            for v in range(2):
                k.stt(Amod[:, l, j, :, v], modT[:, l, (3 * j + 1) * 8:(3 * j + 2) * 8, v], 1.0,
                      normgT[:, l * 24 + j * 8:l * 24 + j * 8 + 8], ALU.add, ALU.mult)
                k.ts("dve", Gmod[:, l, j, :, v], modT[:, l, (3 * j + 2) * 8:(3 * j + 3) * 8, v],
                     1.0 if j == 1 else 0.5)
    ar.release()

    ar.mark()
    for tb in range(12):
        ar.mark()
        xs = ar.alloc([D])
        k.dma("sp", xs, xin.ap()[tb * 128:(tb + 1) * 128, :])
        for g4 in range(2):
            pt = ps()
            for c in range(4):
                kc = g4 * 4 + c
                k.transpose(pt[:, c * 128:(c + 1) * 128], xs[:, kc * 128:(kc + 1) * 128], ident)
            k.copy("act" if g4 else "dve", xT[:, g4 * 4:(g4 + 1) * 4, tb * 128:(tb + 1) * 128],
                   pt.rearrange("p (c t) -> p c t", c=4))
        ar.release()
    ar.release()

    stage('prologue', [modT, xT])

    def vsel(tt):
        return 0 if tt == 0 else 1

    def rmsnorm_mod(l, j, A_ap=None, shift_ap=None, out_fn=None):
        ar.mark()
        sq = ar.alloc([8, 512], BF16)
        rstd = ar.alloc([512]); tmp = ar.alloc([2, 512])
        for tt in range(3):
            tsl = slice(tt * 512, (tt + 1) * 512)
            for kc in range(8):
                k.act(sq[:, kc, :], xT[:, kc, tsl], AF.Square)
            ss = ps()
            for kc in range(8):
                k.mm(ss, ones_bf, sq[:, kc, :], start=(kc == 0), stop=(kc == 7))
            k.act(rstd, ss, AF.Sqrt, scale=1.0 / D, bias=eps_c)
            k.recip(rstd, rstd)
            v = vsel(tt)
            for kc in range(8):
                if j < 3:
                    a_col = Amod[:, l, j, kc, v:v + 1]
                    b_col = modT[:, l, 3 * j * 8 + kc, v:v + 1]
                    tb_ = tmp[:, kc % 2, :]
                    k.stt(tb_, xT[:, kc, tsl], a_col, rstd, ALU.mult, ALU.mult)
                    k.act(nT[:, kc, tsl], tb_, AF.Identity, bias=b_col)
                else:
                    k.stt(out_fn(kc, tt), xT[:, kc, tsl], fingT[:, kc:kc + 1], rstd, ALU.mult, ALU.mult)
        ar.release()

    def wload(src_ap, shape, dt=BF16):
        buf = ar.alloc(shape, dt)
        k.dma("pool" if dt == BF16 else "sp", buf, src_ap)
        return buf

    def ffn(l, i):
        rmsnorm_mod(l, 0 if i == 0 else 2)
        j = 0 if i == 0 else 2
        ar.mark()
        gT = ar.alloc([22, NT], BF16)
        sa = ar.alloc([2, 512])
        wv = w_ffn_in.ap()[l, i].rearrange("(c p) f -> p c f", p=128)
        wbufs = [ar.alloc([2, 8, 256], BF16) for _ in range(2)]
        for c in range(11):
            wb = wbufs[c % 2]
            k.dma("pool", wb[:, 0], wv[:, :, c * 256:(c + 1) * 256])
            k.dma("pool", wb[:, 1], wv[:, :, DFF + c * 256:DFF + (c + 1) * 256])
            for s in range(2):
                fb = 2 * c + s
                for tt in range(3):
                    tsl = slice(tt * 512, (tt + 1) * 512)
                    pa = ps(); pb = ps()
                    for kc in range(8):
                        k.mm(pa, wb[:, 0, kc, s * 128:(s + 1) * 128], nT[:, kc, tsl], start=(kc == 0), stop=(kc == 7))
                    for kc in range(8):
                        k.mm(pb, wb[:, 1, kc, s * 128:(s + 1) * 128], nT[:, kc, tsl], start=(kc == 0), stop=(kc == 7))
                    sab = sa[:, (fb * 3 + tt) % 2, :]
                    k.act(sab, pa, AF.Silu)
                    k.tt("dve", gT[:, fb, tsl], sab, pb, ALU.mult)
        wov = w_ffn_out.ap()[l, i].rearrange("(f p) d -> p f d", p=128)
        wobufs = [ar.alloc([22, 128], BF16) for _ in range(2)]
        for dc in range(8):
            wo = wobufs[dc % 2]
            k.dma("pool", wo, wov[:, :, dc * 128:(dc + 1) * 128])
            for tt in range(3):
                tsl = slice(tt * 512, (tt + 1) * 512)
                py = ps()
                for fb in range(22):
                    k.mm(py, wo[:, fb, :], gT[:, fb, tsl], start=(fb == 0), stop=(fb == 21))
                v = vsel(tt)
                k.stt(xT[:, dc, tsl], py, Gmod[:, l, j, dc, v:v + 1], xT[:, dc, tsl], ALU.mult, ALU.add)
        ar.release()

    def bc_last(ap2, n):
        pat = ap2.ap
        return apm.AP(ap2.tensor, int(ap2.offset), [list(pat[0]), list(pat[1]), [0, n]])

    def s5_branch(l, uT, yaT):
        ar.mark()
        lr = ar.alloc([2, 16]); li = ar.alloc([2, 16]); ldt = ar.alloc([2, 16])
        h0 = ar.alloc([2, 2, 16])
        with nc.allow_non_contiguous_dma(reason="ssm params"):
            for d in range(2):
                k.dma("sp", lr[:, d, :], lam_re.ap()[l, d].rearrange("(W h) p -> (h p) W", h=2))
                k.dma("sp", li[:, d, :], lam_im.ap()[l, d].rearrange("(W h) p -> (h p) W", h=2))
                for h in range(2):
                    src = apm.AP(log_dt, (l * 2 + d) * 32 + h, [[0, 64], [2, 16]])
                    k.dma("sp", ldt[64 * h:64 * h + 64, d, :], src)
                for r in range(2):
                    k.dma("sp", h0[:, d, r, :], stt_in.ap()[l, d, r].rearrange("(W hp) -> hp W", hp=128))
        dt_ = ar.alloc([2, 16]); mcol = ar.alloc([2, 16]); th = ar.alloc([2, 16])
        k.act(dt_, ldt, AF.Exp)
        a_ = ar.alloc([2, 16])
        k.tt("dve", a_, lr, dt_, ALU.mult)
        k.act(mcol, a_, AF.Exp)
        k.tt("dve", th, li, dt_, ALU.mult)
        qi_ = ar.alloc([2, 16], I32); rr = ar.alloc([2, 16]); sn = ar.alloc([2, 16]); cs = ar.alloc([2, 16])
        k.ts("dve", qi_, th, 1.0 / TWO_PI)
        k.stt(rr, qi_, -TWO_PI, th, ALU.mult, ALU.add)
        k.act(sn, rr, AF.Sin, scale=SIN_SC)
        qi2 = ar.alloc([2, 16], I32); rr2 = ar.alloc([2, 16])
        k.ts("dve", qi2, th, 1.0 / TWO_PI, 0.25, ALU.mult, ALU.add)
        k.stt(rr2, qi2, -TWO_PI, th, ALU.mult, ALU.add)
        k.act(cs, rr2, AF.Sin, scale=SIN_SC, bias=hpi_c)
        abr = ar.alloc([2, 16]); abi = ar.alloc([2, 16]); den = ar.alloc([2, 16]); t1 = ar.alloc([2, 16])
        t2_ = ar.alloc([2, 16]); den2 = ar.alloc([2, 16]); rden = ar.alloc([2, 16]); nr = ar.alloc([2, 16])
        kr = ar.alloc([2, 16]); ki = ar.alloc([2, 16]); kr0 = ar.alloc([2, 16]); ki0 = ar.alloc([2, 16])
        k.tt("dve", abr, mcol, cs, ALU.mult)
        k.tt("dve", abi, mcol, sn, ALU.mult)
        k.tt("dve", den, lr, lr, ALU.mult)
        k.tt("dve", t1, li, li, ALU.mult)
        k.tt("dve", den2, den, t1, ALU.add)
        k.recip(rden, den2)
        k.ts("dve", nr, abr, -1.0, None, ALU.add)
        k.tt("dve", kr0, nr, lr, ALU.mult)
        k.tt("dve", t2_, abi, li, ALU.mult)
        k.tt("dve", kr, kr0, t2_, ALU.add)
        k.tt("dve", kr, kr, rden, ALU.mult)
        k.tt("dve", ki0, abi, lr, ALU.mult)
        k.tt("dve", t1, nr, li, ALU.mult)
        k.tt("dve", ki, ki0, t1, ALU.subtract)
        k.tt("dve", ki, ki, rden, ALU.mult)
        BL = ar.alloc([2, 2, 4, 128], BF16)
        cns = ar.alloc([2, 2, 4, 64])
        for d in range(2):
            for r, csrc in enumerate((c_re, c_im)):
                k.dma("sp", cns[:, d, r], csrc.ap()[l, d].rearrange("(q g) c p -> (g c) q p", g=8))
        ar.mark()
        for d in range(2):
            braw = ar.alloc([16, 16]); iraw = ar.alloc([16, 16])
            with nc.allow_non_contiguous_dma(reason="ssm params"):
                k.dma("sp", braw, b_re.ap()[l, d].rearrange("(W h) p c -> (h p) W c", h=2))
                k.dma("sp", iraw, b_im.ap()[l, d].rearrange("(W h) p c -> (h p) W c", h=2))
            krb = bc_last(kr[:, d, :], 16); kib = bc_last(ki[:, d, :], 16)
            bb = ar.alloc([2, 16, 16]); tq = ar.alloc([16, 16]); tq2 = ar.alloc([16, 16])
            k.tt("dve", tq2, braw, krb, ALU.mult)
            k.tt("dve", tq, iraw, kib, ALU.mult)
            k.tt("dve", bb[:, 0], tq2, tq, ALU.subtract)
            tq3 = ar.alloc([16, 16]); tq4 = ar.alloc([16, 16])
            k.tt("dve", tq3, iraw, krb, ALU.mult)
            k.tt("dve", tq4, braw, kib, ALU.mult)
            k.tt("dve", bb[:, 1], tq3, tq4, ALU.add)
            for r in range(2):
                xall = ar.alloc([16, 2, 16])
                k.memset("dve", xall, 0.0)
                k.copy("dve", xall[0:64, :, 0, :], bb[0:64, r])
                k.copy("dve", xall[64:128, :, 1, :], bb[64:128, r])
                xf = xall.rearrange("p W h c -> p (W h c)")
                pt = ps((3, 4, 5, 6, 7))
                for q in range(4):
                    k.transpose(pt[:, q * 128:(q + 1) * 128], xf[:, q * 128:(q + 1) * 128], ident)
                k.copy("act", BL[:, d, r].rearrange("p q c -> p (q c)"), pt)
        ar.release()
        ar.mark()
        nTw = nT.rearrange("p a b -> p (a b)").bitcast(F32).rearrange("p (a b) -> p a b", a=4)
        sets = []
        for s_ in range(2):
            st = {}
            st["tabC"] = ar.alloc([1024]); st["tabS"] = ar.alloc([1024])
            st["xr"] = nTw[:, 2 * s_, :]; st["xi"] = nTw[:, 2 * s_ + 1, :]
            st["tm"] = ar.alloc([4, 512]); st["pr"] = ar.alloc([4, NT], BF16)
            st["h32"] = ar.alloc([2, 512])
            sets.append(st)
        NS = ar.alloc([2, 2, 2, 16])
        ygf = ar.alloc([512])
        CLqs = [ar.alloc([2, 3, 4, 128], BF16) for _ in range(2)]
        zall = ar.alloc([2, 128])
        seqs = [(0, 256), (256, 256), (512, 1024)]

        def tabv(tab, d, tt):
            row = _row(tab); off = int(tab.offset)
            if d == 0:
                if tt == 0:
                    return apm.AP(tab.tensor, off, [[row, 128], [0, 2], [1, 256]])
                return tab[:, (tt - 1) * 512:tt * 512]
            if tt == 0:
                return apm.AP(tab.tensor, off + 255, [[row, 128], [0, 2], [-1, 256]])
            return apm.AP(tab.tensor, off + (1023 if tt == 1 else 511), [[row, 128], [-1, 512]])

        def v3(ap, tt):
            return ap.rearrange("p (s t) -> p s t", s=2) if tt == 0 else ap

        Ybanks = (0, 1, 2)
        XP = (3, 4, 5, 6, 7)
        pairs = [(q, d, w) for q in range(4) for d in range(2) for w in range(4)]

        def build_CL(q):
            CLq = CLqs[q % 2]
            k.memset("pool", CLq.rearrange("p d r w c -> p (d r w c)"), 0.0)
            for d in range(2):
                for r in range(3):
                    zz = zall[:, (d * 3 + r) % 2]
                    sgn = 1.0 if r == 0 else -1.0
                    src = cns[:, d, 1 if r == 1 else 0, q, :]
                    k.ts("pool", zz[:, 0:64], src, maskc[:, 0:1], sgn, ALU.mult, ALU.mult)
                    k.ts("pool", zz[:, 64:128], src, maskc[:, 1:2], sgn, ALU.mult, ALU.mult)
                    pt = ps(XP)
                    k.transpose(pt[:, 0:128], zz, ident)
                    for w in range(4):
                        k.copy("act", CLq[:, d, r, w, 32 * w:32 * w + 32], pt[:, 32 * w:32 * w + 32])

        def stA(i):
            q, d, w = pairs[i]; st = sets[i % 2]; W = 4 * q + w
            thc = th[:, d, W:W + 1]
            ang = st["tm"][:, 0:2].rearrange("p a b -> p (a b)"); rb = st["tm"][:, 2:4].rearrange("p a b -> p (a b)")
            qi = st["h32"].rearrange("p a b -> p (a b)").bitcast(I32)
            k.act(ang, iota, AF.Identity, scale=thc)
            k.ts("dve", qi, ang, 1.0 / TWO_PI)
            k.stt(rb, qi, -TWO_PI, ang, ALU.mult, ALU.add)
            k.act(st["tabS"], rb, AF.Sin, scale=SIN_SC)
            k.ts("dve", qi, ang, 1.0 / TWO_PI, 0.25, ALU.mult, ALU.add)
            k.stt(rb, qi, -TWO_PI, ang, ALU.mult, ALU.add)
            k.act(st["tabC"], rb, AF.Sin, scale=SIN_SC, bias=hpi_c)

        def stB(i):
            q, d, w = pairs[i]; st = sets[i % 2]
            tm = st["tm"]
            for tt in range(3):
                tsl = slice(tt * 512, (tt + 1) * 512)
                XR = ps(XP); XI = ps(XP)
                k.mm(XR, BL[32 * w:32 * w + 32, d, 0, q, :], uT[32 * w:32 * w + 32, q, tsl], tile_position=(32 * w, 0))
                k.mm(XI, BL[32 * w:32 * w + 32, d, 1, q, :], uT[32 * w:32 * w + 32, q, tsl], tile_position=(32 * w, 0))
                C = tabv(st["tabC"], d, tt); S = tabv(st["tabS"], d, tt)
                k.tt("dve", v3(tm[:, 0], tt), v3(XR, tt), C, ALU.mult)
                k.tt("dve", v3(tm[:, 1], tt), v3(XI, tt), S, ALU.mult)
                k.tt("pool", st["xr"][:, tsl], tm[:, 0], tm[:, 1], ALU.add)
                k.tt("dve", v3(tm[:, 2], tt), v3(XI, tt), C, ALU.mult)
                k.tt("dve", v3(tm[:, 3], tt), v3(XR, tt), S, ALU.mult)
                k.tt("pool", st["xi"][:, tsl], tm[:, 2], tm[:, 3], ALU.subtract)

        def stC(i):
            q, d, w = pairs[i]; st = sets[i % 2]; W = 4 * q + w
            mc = mcol[:, d, W:W + 1]
            for si, (o, L) in enumerate(seqs):
                for (buf, r) in ((st["xr"], 0), (st["xi"], 1)):
                    init = h0[:, d, r, W:W + 1] if si == 2 else 0.0
                    a_ = buf[:, o:o + L]
                    if d == 1:
                        a_ = rev(a_)
                    k.scan(a_, mc.to_broadcast([128, L]), a_, init)

        def stD(i):
            q, d, w = pairs[i]; st = sets[i % 2]; W = 4 * q + w
            gr = st["xr"]; gi = st["xi"]; pr = st["pr"]
            CLq = CLqs[q % 2]
            col = 255 if d == 0 else 0
            grc = apm.AP(gr.tensor, int(gr.offset) + col, [[_row(gr), 128], [256, 2]])
            gic = apm.AP(gi.tensor, int(gi.offset) + col, [[_row(gi), 128], [256, 2]])
            c255 = st["tabC"][:, 255:256]; s255 = st["tabS"][:, 255:256]
            tn = st["h32"][:, 0, 0:4]
            k.ts("dve", tn[:, 0:2], gic, s255)
            k.stt(NS[:, :, d, 0, W], grc, c255, tn[:, 0:2], ALU.mult, ALU.subtract)
            k.ts("dve", tn[:, 2:4], gic, c255)
            k.stt(NS[:, :, d, 1, W], grc, s255, tn[:, 2:4], ALU.mult, ALU.add)
            for tt in range(3):
                tsl = slice(tt * 512, (tt + 1) * 512)
                C = tabv(st["tabC"], d, tt); S = tabv(st["tabS"], d, tt)
                k.tt("dve", v3(pr[:, 0, tsl], tt), v3(gr[:, tsl], tt), C, ALU.mult)
                k.tt("dve", v3(pr[:, 1, tsl], tt), v3(gi[:, tsl], tt), S, ALU.mult)
                k.tt("dve", v3(pr[:, 2, tsl], tt), v3(gr[:, tsl], tt), S, ALU.mult)
                k.tt("dve", v3(pr[:, 3, tsl], tt), v3(gi[:, tsl], tt), C, ALU.mult)
                Y = psum[:, Ybanks[tt], :]
                last = (d == 1 and w == 3)
                k.mm(Y, CLq[:, d, 0, w, :], pr[:, 0, tsl], start=(d == 0 and w == 0), stop=False)
                k.mm(Y, CLq[:, d, 2, w, :], pr[:, 1, tsl], start=False, stop=False)
                k.mm(Y, CLq[:, d, 1, w, :], pr[:, 2, tsl], start=False, stop=False)
                k.mm(Y, CLq[:, d, 1, w, :], pr[:, 3, tsl], start=False, stop=last)
            if d == 1 and w == 3:
                for tt in range(3):
                    tsl = slice(tt * 512, (tt + 1) * 512)
                    Y = psum[:, Ybanks[tt], :]
                    k.stt(ygf, uT[:, q, tsl], dskipT[:, l * 4 + q:l * 4 + q + 1], Y, ALU.mult, ALU.add)
                    k.act(yaT[:, q, tsl], ygf, AF.Gelu_apprx_tanh)

        build_CL(0)
        stA(0); stB(0)
        for i in range(32):
            if i + 1 < 32:
                if pairs[i + 1][1] == 0 and pairs[i + 1][2] == 0:
                    build_CL(pairs[i + 1][0])
                stA(i + 1)
                stB(i + 1)
            stC(i)
            stD(i)
        pt = ps(XP)
        k.transpose(pt[:, 0:128], NS.rearrange("p s d r W -> p (s d r W)"), ident)
        nst = ar.alloc([128])
        k.copy("dve", nst, pt[:, 0:128])
        for s in range(2):
            k.dma("pool", ns_out.ap()[s, l].rearrange("d r (W hp) -> (d r W) hp", hp=128), nst[64 * s:64 * s + 64, :],
                  is_output=True)
        ar.release()
        wg = ar.alloc([4, 512], BF16)
        k.dma("pool", wg, w_glu.ap()[l].rearrange("(c p) f -> p c f", p=128))
        sg = ar.alloc([4, 512])
        for tt in range(3):
            tsl = slice(tt * 512, (tt + 1) * 512)
            for fb in range(4):
                pg = ps()
                for kc in range(4):
                    k.mm(pg, wg[:, kc, fb * 128:(fb + 1) * 128], yaT[:, kc, tsl], start=(kc == 0), stop=(kc == 3))
                k.act(sg[:, fb], pg, AF.Sigmoid)
            for fb in range(4):
                k.tt("dve", yaT[:, fb, tsl], yaT[:, fb, tsl], sg[:, fb], ALU.mult)
        ar.release()

    def attn_branch(l, qT, kT, Vt, ybT):
        ar.mark()
        so = ar.alloc([2, 512]); sz = ar.alloc([2, 512])
        rz = ar.alloc([2, 512]); o = ar.alloc([512]); t0 = ar.alloc([512]); sq = ar.alloc([512], BF16)
        rstd = ar.alloc([512])
        jobs = []
        for s in range(2):
            jobs.append((s * 256, 256, [(s * 256 + j * 128, s * 2 + j) for j in range(2)]))
        for qt in range(2):
            keys = [(512 + j * 128, 4 + j) for j in range(8)] + [(1536 + j * 128, 12 + j) for j in range(4)]
            jobs.append((512 + qt * 512, 512, keys))
        E = ar.alloc([4, 512], BF16)
        pending = []
        for h in range(4):
            for (qo, nq, keys) in jobs:
                O = [psum[:, 0, 0:nq], psum[:, 1, 0:nq]]
                Z = [psum[:, 2, 0:nq], psum[:, 3, 0:nq]]
                items = [(m, ki_, ko, vb) for m in range(2) for ki_, (ko, vb) in enumerate(keys)]
                n_it = len(items); nk_ = len(keys)
                Sl = {}

                def emitS(j):
                    m, ki_, ko, vb = items[j]
                    S = ps((4, 5, 6, 7))[:, 0:nq]
                    k.mm(S, kT[64 * m:64 * m + 64, h, ko:ko + 128], qT[64 * m:64 * m + 64, h, qo:qo + nq])
                    Sl[j] = S

                emitS(0)
                if n_it > 1:
                    emitS(1)
                for j in range(n_it):
                    m, ki_, ko, vb = items[j]
                    Eb = E[:, j % 4, 0:nq]
                    if pending and j == min(10, n_it - 1):
                        pending.pop(0)()
                    k.act(Eb, Sl.pop(j), AF.Exp, scale=0.125)
                    if j + 2 < n_it:
                        emitS(j + 2)
                    k.mm(O[m], Vt[:, vb, h * 128:(h + 1) * 128], Eb, start=(ki_ == 0), stop=(ki_ == nk_ - 1))
                    k.mm(Z[m], ones_bf, Eb, start=(ki_ == 0), stop=(ki_ == nk_ - 1))
                for m in range(2):
                    k.copy("act", so[:, m, 0:nq], O[m])
                    k.copy("act", sz[:, m, 0:nq], Z[m])

                def epi(h=h, qo=qo, nq=nq):
                    k.recip(rz[:, 0, 0:nq], sz[:, 0, 0:nq])
                    k.recip(rz[:, 1, 0:nq], sz[:, 1, 0:nq])
                    k.tt("dve", t0[:, 0:nq], so[:, 0, 0:nq], rz[:, 0, 0:nq], ALU.mult)
                    k.tt("dve", o[:, 0:nq], so[:, 1, 0:nq], rz[:, 1, 0:nq], ALU.mult)
                    k.stt(o[:, 0:nq], o[:, 0:nq], lamc[:, l, 1:2], t0[:, 0:nq], ALU.mult, ALU.add)
                    k.tt("pool", sq[:, 0:nq], o[:, 0:nq], o[:, 0:nq], ALU.mult)
                    ssq = ps((4, 5, 6, 7))[:, 0:nq]
                    k.mm(ssq, ones_bf, sq[:, 0:nq])
                    k.act(rstd[:, 0:nq], ssq, AF.Sqrt, scale=1.0 / 128, bias=eps_c)
                    k.recip(rstd[:, 0:nq], rstd[:, 0:nq])
                    k.stt(ybT[:, h, qo:qo + nq], o[:, 0:nq], gfac[:, l:l + 1], rstd[:, 0:nq], ALU.mult, ALU.mult)
                pending.append(epi)
        while pending:
            pending.pop(0)()
        ar.release()

    def pool_branch(l, zp, ycT):
        ar.mark()
        ZW = 1600
        offs = [16, 288, 560]
        wa = ar.alloc([ZW]); wb_ = ar.alloc([ZW]); pooled = ar.alloc([4, NT], BF16); pf = ar.alloc([ZW])
        k.memset("pool", wa, 0.0); k.memset("pool", wb_, 0.0)
        lo, hi_ = 8, ZW - 8
        for g, wdw in enumerate((2, 4, 8, 16)):
            z = zp[:, g, :]
            k.tt("pool", wa[:, lo:hi_], z[:, lo - 1:hi_ - 1], z[:, lo:hi_], ALU.add)
            cur, oth = wa, wb_
            sh = 1
            ww = 2
            while ww < wdw:
                k.tt("pool", oth[:, lo:hi_], cur[:, lo - sh:hi_ - sh], cur[:, lo + sh:hi_ + sh], ALU.add)
                cur, oth = oth, cur
                sh *= 2; ww *= 2
            k.stt(pf[:, lo:hi_], cur[:, lo:hi_], 1.0 / wdw, z[:, lo:hi_], ALU.mult, ALU.subtract)
            for si, (o, L) in enumerate(((16, 256), (288, 256), (560, 1024))):
                for e in range(2):
                    c0 = o if e == 0 else o + L - 8
                    k.tt("pool", pf[:, c0:c0 + 8], cur[:, c0:c0 + 8], poolc[:, g, 0 if si < 2 else 1, e, :], ALU.mult)
                    k.tt("pool", pf[:, c0:c0 + 8], pf[:, c0:c0 + 8], z[:, c0:c0 + 8], ALU.subtract)
            for si, (o, L) in enumerate(((16, 256), (288, 256), (560, 1024))):
                to = (0, 256, 512)[si]
                k.copy("act", pooled[:, g, to:to + L], pf[:, o:o + L])
        wp = ar.alloc([4, 128], BF16)
        k.dma("pool", wp, w_pool.ap()[l].rearrange("g c d -> c g d"))
        for g in range(4):
            for tt in range(3):
                tsl = slice(tt * 512, (tt + 1) * 512)
                pp = ps()
                k.mm(pp, wp[:, g, :], pooled[:, g, tsl])
                k.act(ycT[:, g, tsl], pp, AF.Identity, scale=pscT[:, l * 4 + g:l * 4 + g + 1])
        ar.release()

    def mixer(l):
        rmsnorm_mod(l, 1)
        ar.mark()
        yaT = ar.alloc([4, NT], BF16)
        wv = w_in.ap()[l].rearrange("(c p) f -> p c f", p=128)

        def wchunk(c):
            return wload(wv[:, :, c * 512:(c + 1) * 512], [8, 512])

        def proj_fm(wc, fb, tt):
            p = ps()
            for kc in range(8):
                k.mm(p, wc[:, kc, fb * 128:(fb + 1) * 128], nT[:, kc, tt * 512:(tt + 1) * 512],
                     start=(kc == 0), stop=(kc == 7))
            return p

        def proj_tm(wc, tb):
            p = ps()
            for kc in range(8):
                k.mm(p, nT[:, kc, tb * 128:(tb + 1) * 128], wc[:, kc, :], start=(kc == 0), stop=(kc == 7))
            return p

        ar.mark()
        uT = yaT
        ar.mark()
        wc = wchunk(0)
        for fb in range(4):
            for tt in range(3):
                p = proj_fm(wc, fb, tt)
                k.copy("act" if (fb + tt) % 2 else "dve", uT[:, fb, tt * 512:(tt + 1) * 512], p)
        ar.release()
        s5_branch(l, uT, yaT)
        ar.release()
        rmsnorm_mod(l, 1)
        stage('s5', [yaT])
        ybT = ar.alloc([4, NT], BF16); ycT = ar.alloc([4, NT], BF16)
        ar.mark()
        qT = ar.alloc([4, NT], BF16); kT = ar.alloc([4, 2048], BF16); Vt = ar.alloc([16, 512], BF16)
        qraw = ar.alloc([2, 512], BF16); tr = ar.alloc([2, 512]); stg_o = ar.alloc([2, 512])
        for which, dst in ((1, qT), (2, kT)):
            ar.mark()
            wc = wchunk(which)
            for fb in range(4):
                for tt in range(3):
                    tsl = slice(tt * 512, (tt + 1) * 512)
                    p = proj_fm(wc, fb, tt)
                    if tt == 0:
                        k.copy("act", dst[:, fb, tsl], p)
                    else:
                        qb = qraw[:, (fb + tt) % 2]
                        k.copy("act", qb, p)
                        pq = ps()
                        k.mm(pq, pt_bf, qb)
                        pos = slice((tt - 1) * 512, tt * 512)
                        k.tt("dve", tr[:, 0], qb, rope[:, 0, pos], ALU.mult)
                        k.tt("dve", tr[:, 1], pq, rope[:, 1, pos], ALU.mult)
                        k.tt("pool", dst[:, fb, tsl], tr[:, 0], tr[:, 1], ALU.add)
            if which == 2:
                for tb in range(4):
                    p = proj_tm(wc, tb)
                    st_ = stg_o[:, tb % 2]
                    k.copy("dve", st_, p)
                    k.dma("pool", nk_out.ap()[tb // 2, l, (tb % 2) * 128:(tb % 2) * 128 + 128, :], st_, is_output=True)
            ar.release()
        stage('qk', [qT])
        ar.mark()
        wc = wchunk(3)
        for tb in ((4, 5, 6, 7, 8, 9, 10, 11, 0, 1, 2, 3) if 'e' in _VD else range(4 if 'a' in _VD else 12)):
            if 'c' in _VD and tb < 4:
                continue
            p = proj_tm(wc, tb)
            if 'b' in _VD:
                k.copy("act", Vt[:, tb, :], p)
                continue
            if tb >= 4:
                k.copy("act", Vt[:, tb, :], p)
            else:
                st_ = stg_o[:, tb % 2]
                k.copy("dve", st_, p)
                k.copy("act", Vt[:, tb, :], st_)
                if 'd' in _VD:
                    continue
                k.dma("pool", nv_out.ap()[tb // 2, l, (tb % 2) * 128:(tb % 2) * 128 + 128, :], st_, is_output=True)
        stage('v', [ybT])
        k.dma("pool", Vt[:, 12:16, :], cv.ap()[l].rearrange("(b p) f -> p b f", p=128))
        ckb = ar.alloc([4, 512], BF16)
        k.dma("pool", ckb, ck.ap()[l].rearrange("(b p) f -> p b f", p=128))
        for h in range(4):
            ptb = ps().bitcast(BF16)
            for b in range(4):
                k.transpose(ptb[:, b * 128:(b + 1) * 128], ckb[:, b, h * 128:(h + 1) * 128], ident_bf)
            k.copy("dve", kT[:, h, 1536:2048], ptb[:, 0:512])
        ar.release()
        stage('cache', [ybT])
        attn_branch(l, qT, kT, Vt, ybT)
        ar.release()
        stage('attn', [ybT])
        ar.mark()
        zp = ar.alloc([4, 1600])
        k.memset("pool", zp, 0.0)
        ar.mark()
        wc = wchunk(4)
        for fb in range(4):
            p = proj_fm(wc, fb, 0)
            k.copy("act", zp[:, fb, 16:272], p[:, 0:256])
            k.copy("act", zp[:, fb, 288:544], p[:, 256:512])
            for tt in (1, 2):
                p = proj_fm(wc, fb, tt)
                k.copy("act", zp[:, fb, 560 + (tt - 1) * 512:560 + tt * 512], p)
        ar.release()
        pool_branch(l, zp, ycT)
        ar.release()
        stage('pool', [ycT])
        ar.mark()
        mT = ar.alloc([8, NT], BF16)
        ys = (yaT, ybT, ycT)
        gs = ar.alloc([2, 512]); acc = ar.alloc([512]); t2 = ar.alloc([512])
        wbv = w_branch.ap()[l].rearrange("n (c p) d -> p n c d", p=128)
        mw = [(ar.alloc([3, 8, 128], BF16), ar.alloc([3, 4, 128], BF16)) for _ in range(2)]
        for dc in range(8):
            ar.mark()
            wg_, wb3 = mw[dc % 2]
            for n in range(3):
                k.dma("pool", wg_[:, n], wv[:, :, 2560 + n * 1024 + dc * 128:2560 + n * 1024 + (dc + 1) * 128])
            k.dma("pool", wb3, wbv[:, :, :, dc * 128:(dc + 1) * 128])
            for tt in range(3):
                tsl = slice(tt * 512, (tt + 1) * 512)
                for n in range(3):
                    pg = ps()
                    for kc in range(8):
                        k.mm(pg, wg_[:, n, kc, :], nT[:, kc, tsl], start=(kc == 0), stop=(kc == 7))
                    pb = ps()
                    for kc in range(4):
                        k.mm(pb, wb3[:, n, kc, :], ys[n][:, kc, tsl], start=(kc == 0), stop=(kc == 3))
                    gb = gs[:, n % 2]
                    k.act(gb, pg, AF.Sigmoid)
                    if n == 0:
                        k.tt("dve", acc, gb, pb, ALU.mult)
                    elif n == 1:
                        k.tt("dve", t2, gb, pb, ALU.mult)
                        k.tt("pool", acc, acc, t2, ALU.add)
                    else:
                        k.tt("dve", t2, gb, pb, ALU.mult)
                        k.tt("pool", mT[:, dc, tsl], acc, t2, ALU.add)
            ar.release()
        wov = w_out.ap()[l].rearrange("(c p) d -> p c d", p=128)
        for half in range(2):
            ar.mark()
            wo = wload(wov[:, :, half * 512:(half + 1) * 512], [8, 512])
            for s in range(4):
                dc = half * 4 + s
                for tt in range(3):
                    tsl = slice(tt * 512, (tt + 1) * 512)
                    py = ps()
                    for kc in range(8):
                        k.mm(py, wo[:, kc, s * 128:(s + 1) * 128], mT[:, kc, tsl], start=(kc == 0), stop=(kc == 7))
                    v = vsel(tt)
                    k.stt(xT[:, dc, tsl], py, Gmod[:, l, 1, dc, v:v + 1], xT[:, dc, tsl], ALU.mult, ALU.add)
            ar.release()
        ar.release()
        ar.release()

    for l in range(depth):
        ffn(l, 0)
        stage('ffn0', [xT])
        mixer(l)
        stage('mixer', [xT])
        ffn(l, 1)

    ar.mark()
    yT = ar.alloc([8, NT])
    rmsnorm_mod(0, 3, out_fn=lambda kc, tt: yT[:, kc, tt * 512:(tt + 1) * 512])
    obufs = [ar.alloc([D]) for _ in range(2)]
    for tb in range(12):
        ob = obufs[tb % 2]
        for g4 in range(2):
            pt = ps()
            for c in range(4):
                kc = g4 * 4 + c
                k.transpose(pt[:, c * 128:(c + 1) * 128], yT[:, kc, tb * 128:(tb + 1) * 128], ident)
            k.copy("act" if g4 else "dve", ob[:, g4 * 512:(g4 + 1) * 512], pt)
        k.dma("sp", y_out.ap()[tb * 128:(tb + 1) * 128, :], ob, is_output=True)
    ar.release()
    k.finish()
    return k


def _consts():
    c = {}
    c["c_ident"] = np.eye(128, dtype=np.float32)
    inv = (10000.0 ** (-np.arange(16, dtype=np.float32) / 16)).astype(np.float32)
    t = np.arange(1024)
    row = (t // 64).astype(np.float32); col = (t % 64).astype(np.float32)
    rope = np.zeros((128, 2, 1024), np.float32)
    pt = np.zeros((128, 128), np.float32)
    for m in range(2):
        for d in range(64):
            p = m * 64 + d
            pos = row if d < 32 else col
            ang = (pos * inv[d % 16]).astype(np.float32)
            rope[p, 0] = np.cos(ang); rope[p, 1] = np.sin(ang)
            dd = d % 32
            if dd < 16:
                pt[p + 16, p] = -1.0
            else:
                pt[p - 16, p] = 1.0
    c["c_rope"] = rope; c["c_pt"] = pt
    c["c_iota"] = np.broadcast_to(np.arange(1, 1025, dtype=np.float32), (128, 1024)).copy()
    mk = np.zeros((128, 2), np.float32)
    for p in range(128):
        g = p // 16
        mk[p, g % 2] = 1.0
    c["c_mask"] = mk
    pc = np.zeros((128, 4, 2, 2, 8), np.float32)
    for g, w in enumerate((2, 4, 8, 16)):
        for lt, L in enumerate((256, 1024)):
            for e in range(2):
                for j in range(8):
                    t_ = j if e == 0 else L - 8 + j
                    lo = min(max(t_ - w // 2, 0), L); hi = min(max(t_ - w // 2 + w, 0), L)
                    pc[:, g, lt, e, j] = 1.0 / (hi - lo)
    c["c_pool"] = pc
    return c


_W_NAMES = ["norm_g", "w_mod", "b_mod", "w_ffn_in", "w_ffn_out", "w_in", "ssm_lam_re", "ssm_lam_im", "ssm_log_dt",
            "ssm_b_re", "ssm_b_im", "ssm_c_re", "ssm_c_im", "ssm_d", "w_glu", "lam_q1", "lam_k1", "lam_q2", "lam_k2",
            "attn_norm_g", "w_pool", "pool_scale", "w_branch", "w_out", "final_norm_g"]


def kernel(**inp):
    nc = bass.Bass("TRN2", target_bir_lowering=False)
    build(nc)
    consts = _consts()
    f = lambda a: np.ascontiguousarray(np.asarray(a, dtype=np.float32))
    xp = f(inp["x_prompt"]); xs = f(inp["x_sample"])
    ck = f(inp["cache_k"]); cv = f(inp["cache_v"]); st = f(inp["state_ssm"]); c = f(inp["c"]); cctx = f(inp["c_ctx"])
    wts = {n: f(inp[n]) for n in _W_NAMES}
    in_maps = []
    for i in range(8):
        m = dict(wts); m.update(consts)
        m["xin"] = np.concatenate([xp[2 * i], xp[2 * i + 1], xs[i]], axis=0)
        m["ck"] = ck[i].reshape(DEPTH, 512, 512)
        m["cv"] = cv[i].reshape(DEPTH, 512, 512)
        m["st"] = st[i].reshape(DEPTH, 2, 2, 2048)
        m["cvec"] = np.stack([cctx, c[i]], axis=0)
        in_maps.append(m)
    res = run_bass_kernel_spmd(nc, in_maps, core_ids=list(range(8)))
    R = res.results
    y_prompt = np.stack([R[i // 2]["y"][(i % 2) * 256:(i % 2) * 256 + 256] for i in range(16)], axis=0)
    y_sample = np.stack([R[i]["y"][512:1536] for i in range(8)], axis=0)
    nk = np.concatenate([R[i]["nk"] for i in range(8)], axis=0).reshape(16, DEPTH, 256, 4, 2, 64)
    nv = np.concatenate([R[i]["nv"] for i in range(8)], axis=0).reshape(16, DEPTH, 256, 4, 128)
    ns = np.concatenate([R[i]["ns"] for i in range(8)], axis=0).reshape(16, DEPTH, 2, 2, 32, 64)
    return (y_prompt.astype(np.float32), y_sample.astype(np.float32), nk.astype(np.float32), nv.astype(np.float32),
            ns.astype(np.float32))
```

```python
import math
import numpy as np
_VD = ''
import concourse.bass as bass
import concourse.mybir as mybir
import concourse.ap as apm
from concourse.bass_utils import run_bass_kernel_spmd

F32 = mybir.dt.float32
BF16 = mybir.dt.bfloat16
I32 = mybir.dt.int32
AF = mybir.ActivationFunctionType
ALU = mybir.AluOpType
AX = mybir.AxisListType
_ES = {str(F32): 4, str(BF16): 2, str(I32): 4}

D = 1024; NT = 1536; DFF = 2816; INW = 5632; DEPTH = 4
EPS = 1e-6
TWO_PI = 2.0 * math.pi
SIN_SC = 1.0 - 2e-4


class _Rec:
    __slots__ = ("eng", "sem", "val", "write", "p0", "p1", "f0", "f1")

    def __init__(self, eng, sem, val, write, box):
        self.eng = eng; self.sem = sem; self.val = val; self.write = write
        self.p0, self.p1, self.f0, self.f1 = box


def _box(ap):
    pat = ap.ap
    es = _ES[str(ap.dtype)]
    off = int(ap.offset)
    row = pat[0][0]
    if row == 0:
        p0 = 0; f = off
    else:
        p0 = off // row; f = off - p0 * row
    p1 = p0 + pat[0][1]
    lo = f; hi = f
    for st, cnt in pat[1:]:
        if st >= 0:
            hi += st * (cnt - 1)
        else:
            lo += st * (cnt - 1)
    return (ap.tensor.name, p0, p1, lo * es, (hi + 1) * es)


class K:
    def __init__(self, nc, n_dma_sems=32):
        self.nc = nc
        self.engs = {"pe": nc.tensor, "act": nc.scalar, "dve": nc.vector, "pool": nc.gpsimd, "sp": nc.sync}
        self.sem = {}; self.cnt = {}
        for e in ("pe", "act", "dve", "pool"):
            self.sem[e] = nc.semaphore("s_" + e).__enter__()
            self.cnt[e] = 0
        self.dsem = [nc.semaphore("d%d" % i).__enter__() for i in range(n_dma_sems)]
        self.dval = [0] * n_dma_sems
        self.dnext = 0
        self.dnext_sw = 0
        self.known = {e: {} for e in self.engs}
        self.recs = {}
        self.out_waits = []
        self.n_inst = {e: 0 for e in self.engs}
        self.n_wait = 0

    def _need(self, eng, sem, val):
        if eng == "pe" and sem is self.sem["pe"]:
            return
        kn = self.known[eng]
        key = sem.name
        if kn.get(key, 0) >= val:
            return
        kn[key] = val
        self.engs[eng].wait_ge(sem, val)
        self.n_wait += 1

    def _tracked(self, ap):
        if ap is None or isinstance(ap, (int, float)):
            return False
        if str(ap.space) == "DRAM" and ap.tensor.name not in self.recs:
            return False
        return True

    def _deps(self, eng, reads, writes):
        for ap in reads:
            if not self._tracked(ap):
                continue
            name, p0, p1, f0, f1 = _box(ap)
            for r in self.recs.get(name, ()):
                if r.write and r.p0 < p1 and p0 < r.p1 and r.f0 < f1 and f0 < r.f1:
                    self._need(eng, r.sem, r.val)
        for ap in writes:
            if not self._tracked(ap):
                continue
            name, p0, p1, f0, f1 = _box(ap)
            for r in self.recs.get(name, ()):
                if r.p0 < p1 and p0 < r.p1 and r.f0 < f1 and f0 < r.f1:
                    self._need(eng, r.sem, r.val)

    def _record(self, eng, sem, val, reads, writes):
        for ap in writes:
            if not self._tracked(ap):
                continue
            name, p0, p1, f0, f1 = _box(ap)
            lst = self.recs.setdefault(name, [])
            lst[:] = [r for r in lst if not (p0 <= r.p0 and r.p1 <= p1 and f0 <= r.f0 and r.f1 <= f1)]
            lst.append(_Rec(eng, sem, val, True, (p0, p1, f0, f1)))
        for ap in reads:
            if not self._tracked(ap):
                continue
            name, p0, p1, f0, f1 = _box(ap)
            lst = self.recs.setdefault(name, [])
            lst[:] = [r for r in lst if not ((not r.write) and r.eng == eng and p0 <= r.p0 and r.p1 <= p1
                                              and f0 <= r.f0 and r.f1 <= f1)]
            lst.append(_Rec(eng, sem, val, False, (p0, p1, f0, f1)))

    def track_dram(self, t):
        self.recs.setdefault(t.name, [])

    def _fin(self, eng, inst, reads, writes, inc=True):
        self.n_inst[eng] += 1
        if inc:
            self.cnt[eng] += 1
            inst.then_inc(self.sem[eng], 1)
            self._record(eng, self.sem[eng], self.cnt[eng], reads, writes)
        else:
            self._record(eng, self.sem[eng], self.cnt[eng] + 1, reads, writes)

    def mm(self, out, lhsT, rhs, start=True, stop=True, **kw):
        self._deps("pe", [lhsT, rhs], [out])
        i = self.nc.tensor.matmul(out, lhsT, rhs, start=start, stop=stop, **kw)
        self._fin("pe", i, [lhsT, rhs], [out], inc=stop)

    def transpose(self, out, in_, ident):
        self._deps("pe", [in_, ident], [out])
        i = self.nc.tensor.transpose(out, in_, ident)
        self._fin("pe", i, [in_, ident], [out])

    def act(self, out, in_, func, scale=1.0, bias=None):
        rd = [in_] + [a for a in (scale, bias) if a is not None and not isinstance(a, (int, float))]
        self._deps("act", rd, [out])
        kw = {}
        if bias is not None:
            kw["bias"] = bias
        i = self.nc.scalar.activation(out=out, in_=in_, func=func, scale=scale, **kw)
        self._fin("act", i, rd, [out])

    def _ve(self, eng):
        return self.nc.vector if eng == "dve" else self.nc.gpsimd

    def tt(self, eng, out, in0, in1, op):
        self._deps(eng, [in0, in1], [out])
        i = self._ve(eng).tensor_tensor(out, in0, in1, op)
        self._fin(eng, i, [in0, in1], [out])

    def ts(self, eng, out, in0, s1, s2=None, op0=ALU.mult, op1=None):
        rd = [in0] + [a for a in (s1, s2) if a is not None and not isinstance(a, (int, float))]
        self._deps(eng, rd, [out])
        if op1 is not None:
            i = self._ve(eng).tensor_scalar(out, in0, s1, s2, op0, op1)
        else:
            i = self._ve(eng).tensor_scalar(out, in0, s1, None, op0)
        self._fin(eng, i, rd, [out])

    def stt(self, out, in0, scalar, in1, op0, op1):
        rd = [in0, in1] + ([scalar] if not isinstance(scalar, (int, float)) else [])
        self._deps("dve", rd, [out])
        i = self.nc.vector.scalar_tensor_tensor(out, in0, scalar, in1, op0, op1)
        self._fin("dve", i, rd, [out])

    def scan(self, out, d0, d1, initial):
        rd = [d0, d1] + ([initial] if not isinstance(initial, (int, float)) else [])
        self._deps("dve", rd, [out])
        i = self.nc.vector.tensor_tensor_scan(out, d0, d1, initial, ALU.mult, ALU.add)
        self._fin("dve", i, rd, [out])

    def copy(self, eng, out, in_):
        if eng == "act":
            return self.act(out, in_, AF.Copy)
        self._deps(eng, [in_], [out])
        i = self._ve(eng).tensor_copy(out, in_)
        self._fin(eng, i, [in_], [out])

    def memset(self, eng, ap, val):
        self._deps(eng, [], [ap])
        i = self._ve(eng).memset(ap, val)
        self._fin(eng, i, [], [ap])

    def recip(self, out, in_):
        self._deps("dve", [in_], [out])
        i = self.nc.vector.reciprocal(out, in_)
        self._fin("dve", i, [in_], [out])

    def dma(self, q, out, in_, is_output=False, **kw):
        half = len(self.dsem) // 2
        if q == "pool":
            s = half + self.dnext_sw
            self.dnext_sw = (self.dnext_sw + 1) % half
        else:
            s = self.dnext
            self.dnext = (self.dnext + 1) % half
        sem = self.dsem[s]
        if self.dval[s] > 0:
            self._need(q, sem, self.dval[s])
        self._deps(q, [in_], [out])
        self.dval[s] += 16
        self.engs[q].dma_start(out=out, in_=in_, **kw).then_inc(sem, 16)
        self.n_inst[q] += 1
        self._record("dma%d" % s, sem, self.dval[s], [in_], [out])
        if is_output:
            self.out_waits.append((s, self.dval[s]))

    def finish(self):
        last = {}
        for s, v in self.out_waits:
            last[s] = max(last.get(s, 0), v)
        for s, v in last.items():
            self._need("sp", self.dsem[s], v)
        for e in ("pe", "act", "dve", "pool"):
            if self.cnt[e] > 0:
                self._need("sp", self.sem[e], self.cnt[e])


class Arena:
    def __init__(self, nc, nwords):
        self.t = nc.sbuf_tensor("arena", [128, nwords], F32).__enter__()
        self.n = nwords; self.top = 0; self.stack = []

    def alloc(self, free_shape, dt=F32):
        n = 1
        for s in free_shape:
            n *= s
        words = n if _ES[str(dt)] == 4 else (n + 1) // 2
        words = (words + 7) // 8 * 8
        off = self.top
        self.top += words
        assert self.top <= self.n, "arena overflow %d > %d" % (self.top, self.n)
        ap = self.t[:, off:off + words]
        if dt != F32:
            ap = ap.bitcast(dt)
        ap = ap[:, 0:n]
        if len(free_shape) == 2:
            ap = ap.rearrange("p (a b) -> p a b", a=free_shape[0])
        elif len(free_shape) == 3:
            ap = ap.rearrange("p (a b c) -> p a b c", a=free_shape[0], b=free_shape[1])
        elif len(free_shape) == 4:
            ap = ap.rearrange("p (a b c d) -> p a b c d", a=free_shape[0], b=free_shape[1], c=free_shape[2])
        return ap

    def mark(self):
        self.stack.append(self.top)

    def release(self):
        self.top = self.stack.pop()


def _row(ap):
    return ap.ap[0][0]


def rev(ap):
    pat = ap.ap
    assert len(pat) == 2
    st, n = pat[1]
    return apm.AP(ap.tensor, int(ap.offset) + (n - 1) * st, [list(pat[0]), [-st, n]])


class _Stop(Exception):
    pass


def build(nc, depth=DEPTH, dbg=False, stop=None):
    try:
        return _build(nc, depth, dbg, stop)
    except _Stop as e:
        e.args[0].finish()
        return e.args[0]


def _build(nc, depth, dbg, stop):
    k = K(nc)

    dbg_t = {}

    def stage(name, dumps=()):
        if stop != name:
            return
        off = 0
        ar.mark()
        scr = ar.alloc([2048])
        for ap in dumps:
            flat = ap
            if len(ap.shape) == 3:
                flat = ap.rearrange("p a b -> p (a b)")
            elif len(ap.shape) == 4:
                flat = ap.rearrange("p a b c -> p (a b c)")
            n = flat.shape[1]
            for c0 in range(0, n, 2048):
                c1 = min(n, c0 + 2048)
                k.copy("dve", scr[:, 0:c1 - c0], flat[:, c0:c1])
                k.dma("sp", dbg_t["d"].ap()[:, off + c0:off + c1], scr[:, 0:c1 - c0], is_output=True)
            off += n
        raise _Stop(k)

    ar = Arena(nc, 53000)
    psum = nc.psum_tensor("ps", [128, 8, 512], F32).__enter__()
    st_ps = {"i": 0}

    def ps(pool=(0, 1, 2, 3, 4, 5, 6, 7)):
        st_ps["i"] += 1
        return psum[:, pool[st_ps["i"] % len(pool)], :]

    def din(name, shape, dt=F32):
        return nc.dram_tensor(name, list(shape), dt, kind="ExternalInput")

    xin = din("xin", [NT, D]); ck = din("ck", [DEPTH, 512, 512]); cv = din("cv", [DEPTH, 512, 512])
    stt_in = din("st", [DEPTH, 2, 2, 2048]); cvec = din("cvec", [2, D])
    norm_g = din("norm_g", [DEPTH, 3, D]); w_mod = din("w_mod", [DEPTH, D, 9 * D]); b_mod = din("b_mod", [DEPTH, 9 * D])
    w_ffn_in = din("w_ffn_in", [DEPTH, 2, D, 2 * DFF]); w_ffn_out = din("w_ffn_out", [DEPTH, 2, DFF, D])
    w_in = din("w_in", [DEPTH, D, INW])
    lam_re = din("ssm_lam_re", [DEPTH, 2, 32, 64]); lam_im = din("ssm_lam_im", [DEPTH, 2, 32, 64])
    log_dt = din("ssm_log_dt", [DEPTH, 2, 32])
    b_re = din("ssm_b_re", [DEPTH, 2, 32, 64, 16]); b_im = din("ssm_b_im", [DEPTH, 2, 32, 64, 16])
    c_re = din("ssm_c_re", [DEPTH, 2, 32, 16, 64]); c_im = din("ssm_c_im", [DEPTH, 2, 32, 16, 64])
    ssm_d = din("ssm_d", [DEPTH, 512]); w_glu = din("w_glu", [DEPTH, 512, 512])
    lq1 = din("lam_q1", [DEPTH, 64]); lk1 = din("lam_k1", [DEPTH, 64])
    lq2 = din("lam_q2", [DEPTH, 64]); lk2 = din("lam_k2", [DEPTH, 64])
    attn_g = din("attn_norm_g", [DEPTH, 128]); w_pool = din("w_pool", [DEPTH, 4, 128, 128])
    pool_scale = din("pool_scale", [DEPTH, 512]); w_branch = din("w_branch", [DEPTH, 3, 512, D])
    w_out = din("w_out", [DEPTH, D, D]); final_g = din("final_norm_g", [D])
    c_ident = din("c_ident", [128, 128]); c_rope = din("c_rope", [128, 2, 1024]); c_pt = din("c_pt", [128, 128])
    c_iota = din("c_iota", [128, 1024]); c_mask = din("c_mask", [128, 2]); c_pool = din("c_pool", [128, 4, 2, 2, 8])

    def dout(name, shape):
        return nc.dram_tensor(name, list(shape), F32, kind="ExternalOutput")

    y_out = dout("y", [NT, D]); nk_out = dout("nk", [2, DEPTH, 256, 512]); nv_out = dout("nv", [2, DEPTH, 256, 512])
    ns_out = dout("ns", [2, DEPTH, 2, 2, 2048])
    if stop is not None:
        dbg_t["d"] = dout("dbg", [128, 32768])

    ident = ar.alloc([128]); ident_bf = ar.alloc([128], BF16); ones_bf = ar.alloc([128], BF16)
    ones_f = ar.alloc([128]); pt_bf = ar.alloc([128], BF16)
    rope = ar.alloc([2, 1024]); iota = ar.alloc([1024]); maskc = ar.alloc([2]); poolc = ar.alloc([4, 2, 2, 8])
    cst = ar.alloc([8])
    k.dma("sp", ident, c_ident.ap())
    k.dma("sp", rope, c_rope.ap())
    k.dma("sp", iota, c_iota.ap())
    k.dma("sp", maskc, c_mask.ap())
    k.dma("sp", poolc, c_pool.ap())
    tmp_pt = ar.alloc([128])
    k.dma("sp", tmp_pt, c_pt.ap())
    k.copy("dve", ident_bf, ident)
    k.copy("dve", pt_bf, tmp_pt)
    k.memset("dve", ones_bf, 1.0)
    k.memset("dve", ones_f, 1.0)
    k.memset("dve", cst[:, 0:1], EPS)
    k.memset("dve", cst[:, 1:2], 0.5 * math.pi * SIN_SC)
    k.memset("dve", cst[:, 2:3], 0.0)
    eps_c = cst[:, 0:1]; hpi_c = cst[:, 1:2]

    xT = ar.alloc([8, NT])
    nT = ar.alloc([8, NT], BF16)
    modT = ar.alloc([DEPTH, 72, 2])
    Amod = ar.alloc([DEPTH, 3, 8, 2])
    Gmod = ar.alloc([DEPTH, 3, 8, 2])
    normgT = ar.alloc([96])
    smallT = ar.alloc([64])
    lamc = ar.alloc([DEPTH, 2])
    gfac = ar.alloc([DEPTH])
    sc = ar.alloc([2, 8])
    bmodT = ar.alloc([DEPTH, 72])

    ar.mark()
    stg = ar.alloc([128])
    k.dma("sp", stg[0:96, :], norm_g.ap().rearrange("l j (c p) -> (l j c) p", p=128))
    pt = ps()
    k.transpose(pt[:, 0:96], stg[0:96, :], ident[0:96, 0:96])
    k.copy("dve", normgT, pt[:, 0:96])
    stg2 = ar.alloc([128])
    k.dma("sp", stg2[0:16, :], ssm_d.ap().rearrange("l (c p) -> (l c) p", p=128))
    k.dma("sp", stg2[16:32, :], pool_scale.ap().rearrange("l (c p) -> (l c) p", p=128))
    k.dma("sp", stg2[32:36, :], attn_g.ap())
    k.dma("sp", stg2[36:44, :], final_g.ap().rearrange("(c p) -> c p", p=128))
    k.dma("sp", stg2[44:60, :], cvec.ap().rearrange("v (c p) -> (v c) p", p=128))
    pt = ps()
    k.transpose(pt[:, 0:60], stg2[0:60, :], ident[0:60, 0:60])
    k.copy("dve", smallT[:, 0:60], pt[:, 0:60])
    dskipT = smallT[:, 0:16]; pscT = smallT[:, 16:32]; attngT = smallT[:, 32:36]; fingT = smallT[:, 36:44]
    k.act(sc.rearrange("p v c -> p (v c)"), smallT[:, 44:60], AF.Silu)
    for l in range(DEPTH):
        stg3 = ar.alloc([128])
        k.dma("sp", stg3[0:72, :], b_mod.ap()[l].rearrange("(c p) -> c p", p=128))
        pt = ps()
        k.transpose(pt[:, 0:72], stg3[0:72, :], ident[0:72, 0:72])
        k.copy("dve", bmodT[:, l, :], pt[:, 0:72])
    lv = ar.alloc([4, DEPTH])
    with nc.allow_non_contiguous_dma(reason="tiny"):
        for i, t in enumerate((lq1, lk1, lq2, lk2)):
            k.dma("sp", lv[0:64, i, :], t.ap().rearrange("l d -> d l"))
    lp = ar.alloc([2, DEPTH])
    k.tt("dve", lp[0:64, 0, :], lv[0:64, 0, :], lv[0:64, 1, :], ALU.mult)
    k.tt("dve", lp[0:64, 1, :], lv[0:64, 2, :], lv[0:64, 3, :], ALU.mult)
    pt = ps()
    k.mm(pt[:, 0:2 * DEPTH], ones_f[0:64, :], lp[0:64].rearrange("p a l -> p (a l)"))
    le = ar.alloc([2, DEPTH])
    k.act(le.rearrange("p a l -> p (a l)"), pt[:, 0:2 * DEPTH], AF.Exp)
    for l in range(DEPTH):
        li_ = 0.8 - 0.6 * math.exp(-0.3 * l)
        k.stt(lamc[:, l, 0:1], le[:, 0, l:l + 1], li_, le[:, 1, l:l + 1], ALU.add, ALU.subtract)
        k.ts("dve", lamc[:, l, 1:2], lamc[:, l, 0:1], -1.0)
        k.ts("dve", gfac[:, l:l + 1], attngT[:, l:l + 1], 1.0 - li_)

    def adaln_fin(l, mps):
        for v in range(2):
            k.tt("dve", modT[:, l, :, v], mps[:, 0:144].rearrange("p (f v) -> p f v", v=2)[:, :, v], bmodT[:, l, :], ALU.add)
        for j in range(3):
            for v in range(2):
                k.stt(Amod[:, l, j, :, v], modT[:, l, (3 * j + 1) * 8:(3 * j + 2) * 8, v], 1.0,
                      normgT[:, l * 24 + j * 8:l * 24 + j * 8 + 8], ALU.add, ALU.mult)
                k.ts("dve", Gmod[:, l, j, :, v], modT[:, l, (3 * j + 2) * 8:(3 * j + 3) * 8, v],
                     1.0 if j == 1 else 0.5)

    def adaln_dma(l, fb, wm):
        k.dma("sp", wm, w_mod.ap()[l].rearrange("(c p) f -> p c f", p=128)[:, :, fb * 128:(fb + 1) * 128])

    def adaln_mm(l, fb, wm, mps):
        for kc in range(8):
            k.mm(mps[:, fb * 2:fb * 2 + 2], wm[:, kc, :], sc[:, :, kc], start=(kc == 0), stop=(kc == 7))

    wms = [ar.alloc([8, 256]) for _ in range(2)]
    for l in range(1):
        mps = ps()
        for fc in range(36):
            wm = wms[fc % 2]
            k.dma("sp", wm, w_mod.ap()[l].rearrange("(c p) f -> p c f", p=128)[:, :, fc * 256:(fc + 1) * 256])
            for s_ in range(2):
                fb = fc * 2 + s_
                for kc in range(8):
                    k.mm(mps[:, fb * 2:fb * 2 + 2], wm[:, kc, s_ * 128:(s_ + 1) * 128], sc[:, :, kc],
                         start=(kc == 0), stop=(kc == 7))
        adaln_fin(l, mps)
    ar.release()

    ar.mark()
    for tb in range(12):
        ar.mark()
        xs = ar.alloc([D])
        k.dma("sp", xs, xin.ap()[tb * 128:(tb + 1) * 128, :])
        for g4 in range(2):
            pt = ps()
            for c in range(4):
                kc = g4 * 4 + c
                k.transpose(pt[:, c * 128:(c + 1) * 128], xs[:, kc * 128:(kc + 1) * 128], ident)
            k.copy("act" if g4 else "dve", xT[:, g4 * 4:(g4 + 1) * 4, tb * 128:(tb + 1) * 128],
                   pt.rearrange("p (c t) -> p c t", c=4))
        ar.release()
    ar.release()

    stage('prologue', [modT, xT])

    def vsel(tt):
        return 0 if tt == 0 else 1

    def rmsnorm_mod(l, j, A_ap=None, shift_ap=None, out_fn=None):
        ar.mark()
        sq = ar.alloc([8, 512], BF16)
        rstd = ar.alloc([512]); tmp = ar.alloc([2, 512])
        for tt in range(3):
            tsl = slice(tt * 512, (tt + 1) * 512)
            for kc in range(8):
                k.act(sq[:, kc, :], xT[:, kc, tsl], AF.Square)
            ss = ps()
            for kc in range(8):
                k.mm(ss, ones_bf, sq[:, kc, :], start=(kc == 0), stop=(kc == 7))
            k.act(rstd, ss, AF.Sqrt, scale=1.0 / D, bias=eps_c)
            k.recip(rstd, rstd)
            v = vsel(tt)
            for kc in range(8):
                if j < 3:
                    a_col = Amod[:, l, j, kc, v:v + 1]
                    b_col = modT[:, l, 3 * j * 8 + kc, v:v + 1]
                    tb_ = tmp[:, kc % 2, :]
                    k.stt(tb_, xT[:, kc, tsl], a_col, rstd, ALU.mult, ALU.mult)
                    k.act(nT[:, kc, tsl], tb_, AF.Identity, bias=b_col)
                else:
                    k.stt(out_fn(kc, tt), xT[:, kc, tsl], fingT[:, kc:kc + 1], rstd, ALU.mult, ALU.mult)
        ar.release()

    def wload(src_ap, shape, dt=BF16):
        buf = ar.alloc(shape, dt)
        k.dma("pool" if dt == BF16 else "sp", buf, src_ap)
        return buf

    def ffn(l, i):
        rmsnorm_mod(l, 0 if i == 0 else 2)
        j = 0 if i == 0 else 2
        ar.mark()
        gT = ar.alloc([22, NT], BF16)
        sa = ar.alloc([2, 512])
        wv = w_ffn_in.ap()[l, i].rearrange("(c p) f -> p c f", p=128)
        wbufs = [ar.alloc([2, 8, 256], BF16) for _ in range(2)]
        for c in range(11):
            wb = wbufs[c % 2]
            k.dma("pool", wb[:, 0], wv[:, :, c * 256:(c + 1) * 256])
            k.dma("pool", wb[:, 1], wv[:, :, DFF + c * 256:DFF + (c + 1) * 256])
            for s in range(2):
                fb = 2 * c + s
                for tt in range(3):
                    tsl = slice(tt * 512, (tt + 1) * 512)
                    pa = ps(); pb = ps()
                    for kc in range(8):
                        k.mm(pa, wb[:, 0, kc, s * 128:(s + 1) * 128], nT[:, kc, tsl], start=(kc == 0), stop=(kc == 7))
                    for kc in range(8):
                        k.mm(pb, wb[:, 1, kc, s * 128:(s + 1) * 128], nT[:, kc, tsl], start=(kc == 0), stop=(kc == 7))
                    sab = sa[:, (fb * 3 + tt) % 2, :]
                    k.act(sab, pa, AF.Silu)
                    k.tt("dve", gT[:, fb, tsl], sab, pb, ALU.mult)
        wov = w_ffn_out.ap()[l, i].rearrange("(f p) d -> p f d", p=128)
        wobufs = [ar.alloc([22, 128], BF16) for _ in range(2)]
        for dc in range(8):
            wo = wobufs[dc % 2]
            k.dma("pool", wo, wov[:, :, dc * 128:(dc + 1) * 128])
            for tt in range(3):
                tsl = slice(tt * 512, (tt + 1) * 512)
                py = ps()
                for fb in range(22):
                    k.mm(py, wo[:, fb, :], gT[:, fb, tsl], start=(fb == 0), stop=(fb == 21))
                v = vsel(tt)
                k.stt(xT[:, dc, tsl], py, Gmod[:, l, j, dc, v:v + 1], xT[:, dc, tsl], ALU.mult, ALU.add)
        ar.release()

    def bc_last(ap2, n):
        pat = ap2.ap
        return apm.AP(ap2.tensor, int(ap2.offset), [list(pat[0]), list(pat[1]), [0, n]])

    def s5_branch(l, uT, yaT):
        ar.mark()
        lr = ar.alloc([2, 16]); li = ar.alloc([2, 16]); ldt = ar.alloc([2, 16])
        h0 = ar.alloc([2, 2, 16])
        with nc.allow_non_contiguous_dma(reason="ssm params"):
            for d in range(2):
                k.dma("sp", lr[:, d, :], lam_re.ap()[l, d].rearrange("(W h) p -> (h p) W", h=2))
                k.dma("sp", li[:, d, :], lam_im.ap()[l, d].rearrange("(W h) p -> (h p) W", h=2))
                for h in range(2):
                    src = apm.AP(log_dt, (l * 2 + d) * 32 + h, [[0, 64], [2, 16]])
                    k.dma("sp", ldt[64 * h:64 * h + 64, d, :], src)
                for r in range(2):
                    k.dma("sp", h0[:, d, r, :], stt_in.ap()[l, d, r].rearrange("(W hp) -> hp W", hp=128))
        dt_ = ar.alloc([2, 16]); mcol = ar.alloc([2, 16]); th = ar.alloc([2, 16])
        k.act(dt_, ldt, AF.Exp)
        a_ = ar.alloc([2, 16])
        k.tt("dve", a_, lr, dt_, ALU.mult)
        k.act(mcol, a_, AF.Exp)
        k.tt("dve", th, li, dt_, ALU.mult)
        qi_ = ar.alloc([2, 16], I32); rr = ar.alloc([2, 16]); sn = ar.alloc([2, 16]); cs = ar.alloc([2, 16])
        k.ts("dve", qi_, th, 1.0 / TWO_PI)
        k.stt(rr, qi_, -TWO_PI, th, ALU.mult, ALU.add)
        k.act(sn, rr, AF.Sin, scale=SIN_SC)
        qi2 = ar.alloc([2, 16], I32); rr2 = ar.alloc([2, 16])
        k.ts("dve", qi2, th, 1.0 / TWO_PI, 0.25, ALU.mult, ALU.add)
        k.stt(rr2, qi2, -TWO_PI, th, ALU.mult, ALU.add)
        k.act(cs, rr2, AF.Sin, scale=SIN_SC, bias=hpi_c)
        abr = ar.alloc([2, 16]); abi = ar.alloc([2, 16]); den = ar.alloc([2, 16]); t1 = ar.alloc([2, 16])
        t2_ = ar.alloc([2, 16]); den2 = ar.alloc([2, 16]); rden = ar.alloc([2, 16]); nr = ar.alloc([2, 16])
        kr = ar.alloc([2, 16]); ki = ar.alloc([2, 16]); kr0 = ar.alloc([2, 16]); ki0 = ar.alloc([2, 16])
        k.tt("dve", abr, mcol, cs, ALU.mult)
        k.tt("dve", abi, mcol, sn, ALU.mult)
        k.tt("dve", den, lr, lr, ALU.mult)
        k.tt("dve", t1, li, li, ALU.mult)
        k.tt("dve", den2, den, t1, ALU.add)
        k.recip(rden, den2)
        k.ts("dve", nr, abr, -1.0, None, ALU.add)
        k.tt("dve", kr0, nr, lr, ALU.mult)
        k.tt("dve", t2_, abi, li, ALU.mult)
        k.tt("dve", kr, kr0, t2_, ALU.add)
        k.tt("dve", kr, kr, rden, ALU.mult)
        k.tt("dve", ki0, abi, lr, ALU.mult)
        k.tt("dve", t1, nr, li, ALU.mult)
        k.tt("dve", ki, ki0, t1, ALU.subtract)
        k.tt("dve", ki, ki, rden, ALU.mult)
        BL = ar.alloc([2, 2, 4, 128], BF16)
        cns = ar.alloc([2, 2, 4, 64])
        for d in range(2):
            for r, csrc in enumerate((c_re, c_im)):
                k.dma("sp", cns[:, d, r], csrc.ap()[l, d].rearrange("(q g) c p -> (g c) q p", g=8))
        ar.mark()
        for d in range(2):
            braw = ar.alloc([16, 16]); iraw = ar.alloc([16, 16])
            with nc.allow_non_contiguous_dma(reason="ssm params"):
                k.dma("sp", braw, b_re.ap()[l, d].rearrange("(W h) p c -> (h p) W c", h=2))
                k.dma("sp", iraw, b_im.ap()[l, d].rearrange("(W h) p c -> (h p) W c", h=2))
            krb = bc_last(kr[:, d, :], 16); kib = bc_last(ki[:, d, :], 16)
            bb = ar.alloc([2, 16, 16]); tq = ar.alloc([16, 16]); tq2 = ar.alloc([16, 16])
            k.tt("dve", tq2, braw, krb, ALU.mult)
            k.tt("dve", tq, iraw, kib, ALU.mult)
            k.tt("dve", bb[:, 0], tq2, tq, ALU.subtract)
            tq3 = ar.alloc([16, 16]); tq4 = ar.alloc([16, 16])
            k.tt("dve", tq3, iraw, krb, ALU.mult)
            k.tt("dve", tq4, braw, kib, ALU.mult)
            k.tt("dve", bb[:, 1], tq3, tq4, ALU.add)
            for r in range(2):
                xall = ar.alloc([16, 2, 16])
                k.memset("dve", xall, 0.0)
                k.copy("dve", xall[0:64, :, 0, :], bb[0:64, r])
                k.copy("dve", xall[64:128, :, 1, :], bb[64:128, r])
                xf = xall.rearrange("p W h c -> p (W h c)")
                pt = ps((3, 4, 5, 6, 7))
                for q in range(4):
                    k.transpose(pt[:, q * 128:(q + 1) * 128], xf[:, q * 128:(q + 1) * 128], ident)
                k.copy("act", BL[:, d, r].rearrange("p q c -> p (q c)"), pt)
        ar.release()
        ar.mark()
        nTw = nT.rearrange("p a b -> p (a b)").bitcast(F32).rearrange("p (a b) -> p a b", a=4)
        sets = []
        for s_ in range(2):
            st = {}
            st["tabC"] = ar.alloc([1024]); st["tabS"] = ar.alloc([1024])
            st["xr"] = nTw[:, 2 * s_, :]; st["xi"] = nTw[:, 2 * s_ + 1, :]
            st["tm"] = ar.alloc([4, 512]); st["pr"] = ar.alloc([4, NT], BF16)
            st["h32"] = ar.alloc([2, 512])
            sets.append(st)
        NS = ar.alloc([2, 2, 2, 16])
        ygf = ar.alloc([512])
        CLqs = [ar.alloc([2, 3, 4, 128], BF16) for _ in range(2)]
        zall = ar.alloc([2, 128])
        seqs = [(0, 256), (256, 256), (512, 1024)]

        def tabv(tab, d, tt):
            row = _row(tab); off = int(tab.offset)
            if d == 0:
                if tt == 0:
                    return apm.AP(tab.tensor, off, [[row, 128], [0, 2], [1, 256]])
                return tab[:, (tt - 1) * 512:tt * 512]
            if tt == 0:
                return apm.AP(tab.tensor, off + 255, [[row, 128], [0, 2], [-1, 256]])
            return apm.AP(tab.tensor, off + (1023 if tt == 1 else 511), [[row, 128], [-1, 512]])

        def v3(ap, tt):
            return ap.rearrange("p (s t) -> p s t", s=2) if tt == 0 else ap

        Ybanks = (0, 1, 2)
        nxt = l + 1 if l + 1 < depth else None
        XP = (3, 4, 5, 6, 7) if nxt is None else (3, 4, 5, 6)
        mps_n = psum[:, 7, :]
        wma = [ar.alloc([8, 128]) for _ in range(2)] if nxt is not None else None
        pairs = [(q, d, w) for q in range(4) for d in range(2) for w in range(4)]

        def build_CL(q):
            CLq = CLqs[q % 2]
            k.memset("pool", CLq.rearrange("p d r w c -> p (d r w c)"), 0.0)
            for d in range(2):
                for r in range(3):
                    zz = zall[:, (d * 3 + r) % 2]
                    sgn = 1.0 if r == 0 else -1.0
                    src = cns[:, d, 1 if r == 1 else 0, q, :]
                    k.ts("pool", zz[:, 0:64], src, maskc[:, 0:1], sgn, ALU.mult, ALU.mult)
                    k.ts("pool", zz[:, 64:128], src, maskc[:, 1:2], sgn, ALU.mult, ALU.mult)
                    pt = ps(XP)
                    k.transpose(pt[:, 0:128], zz, ident)
                    for w in range(4):
                        k.copy("act", CLq[:, d, r, w, 32 * w:32 * w + 32], pt[:, 32 * w:32 * w + 32])

        def stA(i):
            q, d, w = pairs[i]; st = sets[i % 2]; W = 4 * q + w
            thc = th[:, d, W:W + 1]
            ang = st["tm"][:, 0:2].rearrange("p a b -> p (a b)"); rb = st["tm"][:, 2:4].rearrange("p a b -> p (a b)")
            qi = st["h32"].rearrange("p a b -> p (a b)").bitcast(I32)
            k.act(ang, iota, AF.Identity, scale=thc)
            k.ts("dve", qi, ang, 1.0 / TWO_PI)
            k.stt(rb, qi, -TWO_PI, ang, ALU.mult, ALU.add)
            k.act(st["tabS"], rb, AF.Sin, scale=SIN_SC)
            k.ts("dve", qi, ang, 1.0 / TWO_PI, 0.25, ALU.mult, ALU.add)
            k.stt(rb, qi, -TWO_PI, ang, ALU.mult, ALU.add)
            k.act(st["tabC"], rb, AF.Sin, scale=SIN_SC, bias=hpi_c)

        def stB(i):
            q, d, w = pairs[i]; st = sets[i % 2]
            tm = st["tm"]
            for tt in range(3):
                tsl = slice(tt * 512, (tt + 1) * 512)
                XR = ps(XP); XI = ps(XP)
                k.mm(XR, BL[32 * w:32 * w + 32, d, 0, q, :], uT[32 * w:32 * w + 32, q, tsl], tile_position=(32 * w, 0))
                k.mm(XI, BL[32 * w:32 * w + 32, d, 1, q, :], uT[32 * w:32 * w + 32, q, tsl], tile_position=(32 * w, 0))
                C = tabv(st["tabC"], d, tt); S = tabv(st["tabS"], d, tt)
                k.tt("dve", v3(tm[:, 0], tt), v3(XR, tt), C, ALU.mult)
                k.tt("dve", v3(tm[:, 1], tt), v3(XI, tt), S, ALU.mult)
                k.tt("pool", st["xr"][:, tsl], tm[:, 0], tm[:, 1], ALU.add)
                k.tt("dve", v3(tm[:, 2], tt), v3(XI, tt), C, ALU.mult)
                k.tt("dve", v3(tm[:, 3], tt), v3(XR, tt), S, ALU.mult)
                k.tt("pool", st["xi"][:, tsl], tm[:, 2], tm[:, 3], ALU.subtract)

        def stC(i):
            q, d, w = pairs[i]; st = sets[i % 2]; W = 4 * q + w
            mc = mcol[:, d, W:W + 1]
            for si, (o, L) in enumerate(seqs):
                for (buf, r) in ((st["xr"], 0), (st["xi"], 1)):
                    init = h0[:, d, r, W:W + 1] if si == 2 else 0.0
                    a_ = buf[:, o:o + L]
                    if d == 1:
                        a_ = rev(a_)
                    k.scan(a_, mc.to_broadcast([128, L]), a_, init)

        def stD(i):
            q, d, w = pairs[i]; st = sets[i % 2]; W = 4 * q + w
            gr = st["xr"]; gi = st["xi"]; pr = st["pr"]
            CLq = CLqs[q % 2]
            col = 255 if d == 0 else 0
            grc = apm.AP(gr.tensor, int(gr.offset) + col, [[_row(gr), 128], [256, 2]])
            gic = apm.AP(gi.tensor, int(gi.offset) + col, [[_row(gi), 128], [256, 2]])
            c255 = st["tabC"][:, 255:256]; s255 = st["tabS"][:, 255:256]
            tn = st["h32"][:, 0, 0:4]
            k.ts("dve", tn[:, 0:2], gic, s255)
            k.stt(NS[:, :, d, 0, W], grc, c255, tn[:, 0:2], ALU.mult, ALU.subtract)
            k.ts("dve", tn[:, 2:4], gic, c255)
            k.stt(NS[:, :, d, 1, W], grc, s255, tn[:, 2:4], ALU.mult, ALU.add)
            for tt in range(3):
                tsl = slice(tt * 512, (tt + 1) * 512)
                C = tabv(st["tabC"], d, tt); S = tabv(st["tabS"], d, tt)
                k.tt("dve", v3(pr[:, 0, tsl], tt), v3(gr[:, tsl], tt), C, ALU.mult)
                k.tt("dve", v3(pr[:, 1, tsl], tt), v3(gi[:, tsl], tt), S, ALU.mult)
                k.tt("dve", v3(pr[:, 2, tsl], tt), v3(gr[:, tsl], tt), S, ALU.mult)
                k.tt("dve", v3(pr[:, 3, tsl], tt), v3(gi[:, tsl], tt), C, ALU.mult)
                Y = psum[:, Ybanks[tt], :]
                last = (d == 1 and w == 3)
                k.mm(Y, CLq[:, d, 0, w, :], pr[:, 0, tsl], start=(d == 0 and w == 0), stop=False)
                k.mm(Y, CLq[:, d, 2, w, :], pr[:, 1, tsl], start=False, stop=False)
                k.mm(Y, CLq[:, d, 1, w, :], pr[:, 2, tsl], start=False, stop=False)
                k.mm(Y, CLq[:, d, 1, w, :], pr[:, 3, tsl], start=False, stop=last)
            if d == 1 and w == 3:
                for tt in range(3):
                    tsl = slice(tt * 512, (tt + 1) * 512)
                    Y = psum[:, Ybanks[tt], :]
                    k.stt(ygf, uT[:, q, tsl], dskipT[:, l * 4 + q:l * 4 + q + 1], Y, ALU.mult, ALU.add)
                    k.act(yaT[:, q, tsl], ygf, AF.Gelu_apprx_tanh)

        build_CL(0)
        stA(0); stB(0)
        if nxt is not None:
            adaln_dma(nxt, 0, wma[0]); adaln_dma(nxt, 1, wma[1])
        for i in range(32):
            if i + 1 < 32:
                if pairs[i + 1][1] == 0 and pairs[i + 1][2] == 0:
                    build_CL(pairs[i + 1][0])
                stA(i + 1)
                stB(i + 1)
            stC(i)
            stD(i)
            if nxt is not None:
                for fb in range(i * 72 // 32, (i + 1) * 72 // 32):
                    adaln_mm(nxt, fb, wma[fb % 2], mps_n)
                    if fb + 2 < 72:
                        adaln_dma(nxt, fb + 2, wma[fb % 2])
        if nxt is not None:
            adaln_fin(nxt, mps_n)
        pt = ps(XP)
        k.transpose(pt[:, 0:128], NS.rearrange("p s d r W -> p (s d r W)"), ident)
        nst = ar.alloc([128])
        k.copy("dve", nst, pt[:, 0:128])
        for s in range(2):
            k.dma("pool", ns_out.ap()[s, l].rearrange("d r (W hp) -> (d r W) hp", hp=128), nst[64 * s:64 * s + 64, :],
                  is_output=True)
        ar.release()
        wg = ar.alloc([4, 512], BF16)
        k.dma("pool", wg, w_glu.ap()[l].rearrange("(c p) f -> p c f", p=128))
        sg = ar.alloc([4, 512])
        for tt in range(3):
            tsl = slice(tt * 512, (tt + 1) * 512)
            for fb in range(4):
                pg = ps()
                for kc in range(4):
                    k.mm(pg, wg[:, kc, fb * 128:(fb + 1) * 128], yaT[:, kc, tsl], start=(kc == 0), stop=(kc == 3))
                k.act(sg[:, fb], pg, AF.Sigmoid)
            for fb in range(4):
                k.tt("dve", yaT[:, fb, tsl], yaT[:, fb, tsl], sg[:, fb], ALU.mult)
        ar.release()

    def attn_branch(l, qT, kT, Vt, ybT):
        ar.mark()
        so = ar.alloc([2, 512]); sz = ar.alloc([2, 512])
        rz = ar.alloc([2, 512]); o = ar.alloc([512]); t0 = ar.alloc([512]); sq = ar.alloc([512], BF16)
        rstd = ar.alloc([512])
        jobs = []
        for s in range(2):
            jobs.append((s * 256, 256, [(s * 256 + j * 128, s * 2 + j) for j in range(2)]))
        for qt in range(2):
            keys = [(512 + j * 128, 4 + j) for j in range(8)] + [(1536 + j * 128, 12 + j) for j in range(4)]
            jobs.append((512 + qt * 512, 512, keys))
        E = ar.alloc([4, 512], BF16)
        pending = []
        for h in range(4):
            for (qo, nq, keys) in jobs:
                O = [psum[:, 0, 0:nq], psum[:, 1, 0:nq]]
                Z = [psum[:, 2, 0:nq], psum[:, 3, 0:nq]]
                items = [(m, ki_, ko, vb) for m in range(2) for ki_, (ko, vb) in enumerate(keys)]
                n_it = len(items); nk_ = len(keys)
                Sl = {}

                def emitS(j):
                    m, ki_, ko, vb = items[j]
                    S = ps((4, 5, 6, 7))[:, 0:nq]
                    k.mm(S, kT[64 * m:64 * m + 64, h, ko:ko + 128], qT[64 * m:64 * m + 64, h, qo:qo + nq])
                    Sl[j] = S

                emitS(0)
                if n_it > 1:
                    emitS(1)
                for j in range(n_it):
                    m, ki_, ko, vb = items[j]
                    Eb = E[:, j % 4, 0:nq]
                    if pending and j == min(10, n_it - 1):
                        pending.pop(0)()
                    k.act(Eb, Sl.pop(j), AF.Exp, scale=0.125)
                    if j + 2 < n_it:
                        emitS(j + 2)
                    k.mm(O[m], Vt[:, vb, h * 128:(h + 1) * 128], Eb, start=(ki_ == 0), stop=(ki_ == nk_ - 1))
                    k.mm(Z[m], ones_bf, Eb, start=(ki_ == 0), stop=(ki_ == nk_ - 1))
                for m in range(2):
                    k.copy("act", so[:, m, 0:nq], O[m])
                    k.copy("act", sz[:, m, 0:nq], Z[m])

                def epi(h=h, qo=qo, nq=nq):
                    k.recip(rz[:, 0, 0:nq], sz[:, 0, 0:nq])
                    k.recip(rz[:, 1, 0:nq], sz[:, 1, 0:nq])
                    k.tt("dve", t0[:, 0:nq], so[:, 0, 0:nq], rz[:, 0, 0:nq], ALU.mult)
                    k.tt("dve", o[:, 0:nq], so[:, 1, 0:nq], rz[:, 1, 0:nq], ALU.mult)
                    k.stt(o[:, 0:nq], o[:, 0:nq], lamc[:, l, 1:2], t0[:, 0:nq], ALU.mult, ALU.add)
                    k.tt("pool", sq[:, 0:nq], o[:, 0:nq], o[:, 0:nq], ALU.mult)
                    ssq = ps((4, 5, 6, 7))[:, 0:nq]
                    k.mm(ssq, ones_bf, sq[:, 0:nq])
                    k.act(rstd[:, 0:nq], ssq, AF.Sqrt, scale=1.0 / 128, bias=eps_c)
                    k.recip(rstd[:, 0:nq], rstd[:, 0:nq])
                    k.stt(ybT[:, h, qo:qo + nq], o[:, 0:nq], gfac[:, l:l + 1], rstd[:, 0:nq], ALU.mult, ALU.mult)
                pending.append(epi)
        while pending:
            pending.pop(0)()
        ar.release()

    def pool_branch(l, zp, ycT):
        ar.mark()
        ZW = 1600
        offs = [16, 288, 560]
        wa = ar.alloc([ZW]); wb_ = ar.alloc([ZW]); pooled = ar.alloc([4, NT], BF16); pf = ar.alloc([ZW])
        k.memset("dve", wa, 0.0); k.memset("dve", wb_, 0.0)
        lo, hi_ = 8, ZW - 8
        for g, wdw in enumerate((2, 4, 8, 16)):
            z = zp[:, g, :]
            k.tt("dve", wa[:, lo:hi_], z[:, lo - 1:hi_ - 1], z[:, lo:hi_], ALU.add)
            cur, oth = wa, wb_
            sh = 1
            ww = 2
            while ww < wdw:
                k.tt("dve", oth[:, lo:hi_], cur[:, lo - sh:hi_ - sh], cur[:, lo + sh:hi_ + sh], ALU.add)
                cur, oth = oth, cur
                sh *= 2; ww *= 2
            k.stt(pf[:, lo:hi_], cur[:, lo:hi_], 1.0 / wdw, z[:, lo:hi_], ALU.mult, ALU.subtract)
            for si, (o, L) in enumerate(((16, 256), (288, 256), (560, 1024))):
                for e in range(2):
                    c0 = o if e == 0 else o + L - 8
                    k.tt("dve", pf[:, c0:c0 + 8], cur[:, c0:c0 + 8], poolc[:, g, 0 if si < 2 else 1, e, :], ALU.mult)
                    k.tt("dve", pf[:, c0:c0 + 8], pf[:, c0:c0 + 8], z[:, c0:c0 + 8], ALU.subtract)
            for si, (o, L) in enumerate(((16, 256), (288, 256), (560, 1024))):
                to = (0, 256, 512)[si]
                k.copy("act", pooled[:, g, to:to + L], pf[:, o:o + L])
        wp = ar.alloc([4, 128], BF16)
        k.dma("pool", wp, w_pool.ap()[l].rearrange("g c d -> c g d"))
        for g in range(4):
            for tt in range(3):
                tsl = slice(tt * 512, (tt + 1) * 512)
                pp = ps()
                k.mm(pp, wp[:, g, :], pooled[:, g, tsl])
                k.act(ycT[:, g, tsl], pp, AF.Identity, scale=pscT[:, l * 4 + g:l * 4 + g + 1])
        ar.release()

    def mixer(l):
        rmsnorm_mod(l, 1)
        ar.mark()
        yaT = ar.alloc([4, NT], BF16)
        wv = w_in.ap()[l].rearrange("(c p) f -> p c f", p=128)

        def wchunk(c):
            return wload(wv[:, :, c * 512:(c + 1) * 512], [8, 512])

        def proj_fm(wc, fb, tt):
            p = ps()
            for kc in range(8):
                k.mm(p, wc[:, kc, fb * 128:(fb + 1) * 128], nT[:, kc, tt * 512:(tt + 1) * 512],
                     start=(kc == 0), stop=(kc == 7))
            return p

        def proj_tm(wc, tb):
            p = ps()
            for kc in range(8):
                k.mm(p, nT[:, kc, tb * 128:(tb + 1) * 128], wc[:, kc, :], start=(kc == 0), stop=(kc == 7))
            return p

        ar.mark()
        uT = yaT
        ar.mark()
        wc = wchunk(0)
        for fb in range(4):
            for tt in range(3):
                p = proj_fm(wc, fb, tt)
                k.copy("act" if (fb + tt) % 2 else "dve", uT[:, fb, tt * 512:(tt + 1) * 512], p)
        ar.release()
        s5_branch(l, uT, yaT)
        ar.release()
        rmsnorm_mod(l, 1)
        stage('s5', [yaT])
        ybT = ar.alloc([4, NT], BF16); ycT = ar.alloc([4, NT], BF16)
        ar.mark()
        qT = ar.alloc([4, NT], BF16); kT = ar.alloc([4, 2048], BF16); Vt = ar.alloc([16, 512], BF16)
        qraw = ar.alloc([2, 512], BF16); tr = ar.alloc([2, 512]); stg_o = ar.alloc([2, 512])
        for which, dst in ((1, qT), (2, kT)):
            ar.mark()
            wc = wchunk(which)
            for fb in range(4):
                for tt in range(3):
                    tsl = slice(tt * 512, (tt + 1) * 512)
                    p = proj_fm(wc, fb, tt)
                    if tt == 0:
                        k.copy("act", dst[:, fb, tsl], p)
                    else:
                        qb = qraw[:, (fb + tt) % 2]
                        k.copy("act", qb, p)
                        pq = ps()
                        k.mm(pq, pt_bf, qb)
                        pos = slice((tt - 1) * 512, tt * 512)
                        k.tt("dve", tr[:, 0], qb, rope[:, 0, pos], ALU.mult)
                        k.tt("dve", tr[:, 1], pq, rope[:, 1, pos], ALU.mult)
                        k.tt("pool", dst[:, fb, tsl], tr[:, 0], tr[:, 1], ALU.add)
            if which == 2:
                for tb in range(4):
                    p = proj_tm(wc, tb)
                    st_ = stg_o[:, tb % 2]
                    k.copy("dve", st_, p)
                    k.dma("pool", nk_out.ap()[tb // 2, l, (tb % 2) * 128:(tb % 2) * 128 + 128, :], st_, is_output=True)
            ar.release()
        stage('qk', [qT])
        ar.mark()
        wc = wchunk(3)
        for tb in ((4, 5, 6, 7, 8, 9, 10, 11, 0, 1, 2, 3) if 'e' in _VD else range(4 if 'a' in _VD else 12)):
            if 'c' in _VD and tb < 4:
                continue
            p = proj_tm(wc, tb)
            if 'b' in _VD:
                k.copy("act", Vt[:, tb, :], p)
                continue
            if tb >= 4:
                k.copy("act", Vt[:, tb, :], p)
            else:
                st_ = stg_o[:, tb % 2]
                k.copy("dve", st_, p)
                k.copy("act", Vt[:, tb, :], st_)
                if 'd' in _VD:
                    continue
                k.dma("pool", nv_out.ap()[tb // 2, l, (tb % 2) * 128:(tb % 2) * 128 + 128, :], st_, is_output=True)
        stage('v', [ybT])
        k.dma("pool", Vt[:, 12:16, :], cv.ap()[l].rearrange("(b p) f -> p b f", p=128))
        ckb = ar.alloc([4, 512], BF16)
        k.dma("pool", ckb, ck.ap()[l].rearrange("(b p) f -> p b f", p=128))
        for h in range(4):
            ptb = ps().bitcast(BF16)
            for b in range(4):
                k.transpose(ptb[:, b * 128:(b + 1) * 128], ckb[:, b, h * 128:(h + 1) * 128], ident_bf)
            k.copy("dve", kT[:, h, 1536:2048], ptb[:, 0:512])
        ar.release()
        stage('cache', [ybT])
        attn_branch(l, qT, kT, Vt, ybT)
        ar.release()
        stage('attn', [ybT])
        ar.mark()
        zp = ar.alloc([4, 1600])
        k.memset("pool", zp, 0.0)
        ar.mark()
        wc = wchunk(4)
        for fb in range(4):
            p = proj_fm(wc, fb, 0)
            k.copy("act", zp[:, fb, 16:272], p[:, 0:256])
            k.copy("act", zp[:, fb, 288:544], p[:, 256:512])
            for tt in (1, 2):
                p = proj_fm(wc, fb, tt)
                k.copy("act", zp[:, fb, 560 + (tt - 1) * 512:560 + tt * 512], p)
        ar.release()
        pool_branch(l, zp, ycT)
        ar.release()
        stage('pool', [ycT])
        ar.mark()
        mT = ar.alloc([8, NT], BF16)
        ys = (yaT, ybT, ycT)
        gs = ar.alloc([2, 512]); acc = ar.alloc([512]); t2 = ar.alloc([512])
        wbv = w_branch.ap()[l].rearrange("n (c p) d -> p n c d", p=128)
        mw = [(ar.alloc([3, 8, 128], BF16), ar.alloc([3, 4, 128], BF16)) for _ in range(2)]
        for dc in range(8):
            ar.mark()
            wg_, wb3 = mw[dc % 2]
            for n in range(3):
                k.dma("pool", wg_[:, n], wv[:, :, 2560 + n * 1024 + dc * 128:2560 + n * 1024 + (dc + 1) * 128])
            k.dma("pool", wb3, wbv[:, :, :, dc * 128:(dc + 1) * 128])
            for tt in range(3):
                tsl = slice(tt * 512, (tt + 1) * 512)
                for n in range(3):
                    pg = ps()
                    for kc in range(8):
                        k.mm(pg, wg_[:, n, kc, :], nT[:, kc, tsl], start=(kc == 0), stop=(kc == 7))
                    pb = ps()
                    for kc in range(4):
                        k.mm(pb, wb3[:, n, kc, :], ys[n][:, kc, tsl], start=(kc == 0), stop=(kc == 3))
                    gb = gs[:, n % 2]
                    k.act(gb, pg, AF.Sigmoid)
                    if n == 0:
                        k.tt("dve", acc, gb, pb, ALU.mult)
                    elif n == 1:
                        k.tt("dve", t2, gb, pb, ALU.mult)
                        k.tt("pool", acc, acc, t2, ALU.add)
                    else:
                        k.tt("dve", t2, gb, pb, ALU.mult)
                        k.tt("pool", mT[:, dc, tsl], acc, t2, ALU.add)
            ar.release()
        wov = w_out.ap()[l].rearrange("(c p) d -> p c d", p=128)
        for half in range(2):
            ar.mark()
            wo = wload(wov[:, :, half * 512:(half + 1) * 512], [8, 512])
            for s in range(4):
                dc = half * 4 + s
                for tt in range(3):
                    tsl = slice(tt * 512, (tt + 1) * 512)
                    py = ps()
                    for kc in range(8):
                        k.mm(py, wo[:, kc, s * 128:(s + 1) * 128], mT[:, kc, tsl], start=(kc == 0), stop=(kc == 7))
                    v = vsel(tt)
                    k.stt(xT[:, dc, tsl], py, Gmod[:, l, 1, dc, v:v + 1], xT[:, dc, tsl], ALU.mult, ALU.add)
            ar.release()
        ar.release()
        ar.release()

    for l in range(depth):
        ffn(l, 0)
        stage('ffn0', [xT])
        mixer(l)
        stage('mixer', [xT])
        ffn(l, 1)

    ar.mark()
    yT = ar.alloc([8, NT])
    rmsnorm_mod(0, 3, out_fn=lambda kc, tt: yT[:, kc, tt * 512:(tt + 1) * 512])
    obufs = [ar.alloc([D]) for _ in range(2)]
    for tb in range(12):
        ob = obufs[tb % 2]
        for g4 in range(2):
            pt = ps()
            for c in range(4):
                kc = g4 * 4 + c
                k.transpose(pt[:, c * 128:(c + 1) * 128], yT[:, kc, tb * 128:(tb + 1) * 128], ident)
            k.copy("act" if g4 else "dve", ob[:, g4 * 512:(g4 + 1) * 512], pt)
        k.dma("sp", y_out.ap()[tb * 128:(tb + 1) * 128, :], ob, is_output=True)
    ar.release()
    k.finish()
    return k


def _consts():
    c = {}
    c["c_ident"] = np.eye(128, dtype=np.float32)
    inv = (10000.0 ** (-np.arange(16, dtype=np.float32) / 16)).astype(np.float32)
    t = np.arange(1024)
    row = (t // 64).astype(np.float32); col = (t % 64).astype(np.float32)
    rope = np.zeros((128, 2, 1024), np.float32)
    pt = np.zeros((128, 128), np.float32)
    for m in range(2):
        for d in range(64):
            p = m * 64 + d
            pos = row if d < 32 else col
            ang = (pos * inv[d % 16]).astype(np.float32)
            rope[p, 0] = np.cos(ang); rope[p, 1] = np.sin(ang)
            dd = d % 32
            if dd < 16:
                pt[p + 16, p] = -1.0
            else:
                pt[p - 16, p] = 1.0
    c["c_rope"] = rope; c["c_pt"] = pt
    c["c_iota"] = np.broadcast_to(np.arange(1, 1025, dtype=np.float32), (128, 1024)).copy()
    mk = np.zeros((128, 2), np.float32)
    for p in range(128):
        g = p // 16
        mk[p, g % 2] = 1.0
    c["c_mask"] = mk
    pc = np.zeros((128, 4, 2, 2, 8), np.float32)
    for g, w in enumerate((2, 4, 8, 16)):
        for lt, L in enumerate((256, 1024)):
            for e in range(2):
                for j in range(8):
                    t_ = j if e == 0 else L - 8 + j
                    lo = min(max(t_ - w // 2, 0), L); hi = min(max(t_ - w // 2 + w, 0), L)
                    pc[:, g, lt, e, j] = 1.0 / (hi - lo)
    c["c_pool"] = pc
    return c


_W_NAMES = ["norm_g", "w_mod", "b_mod", "w_ffn_in", "w_ffn_out", "w_in", "ssm_lam_re", "ssm_lam_im", "ssm_log_dt",
            "ssm_b_re", "ssm_b_im", "ssm_c_re", "ssm_c_im", "ssm_d", "w_glu", "lam_q1", "lam_k1", "lam_q2", "lam_k2",
            "attn_norm_g", "w_pool", "pool_scale", "w_branch", "w_out", "final_norm_g"]


def kernel(**inp):
    nc = bass.Bass("TRN2", target_bir_lowering=False)
    build(nc)
    consts = _consts()
    f = lambda a: np.ascontiguousarray(np.asarray(a, dtype=np.float32))
    xp = f(inp["x_prompt"]); xs = f(inp["x_sample"])
    ck = f(inp["cache_k"]); cv = f(inp["cache_v"]); st = f(inp["state_ssm"]); c = f(inp["c"]); cctx = f(inp["c_ctx"])
    wts = {n: f(inp[n]) for n in _W_NAMES}
    in_maps = []
    for i in range(8):
        m = dict(wts); m.update(consts)
        m["xin"] = np.concatenate([xp[2 * i], xp[2 * i + 1], xs[i]], axis=0)
        m["ck"] = ck[i].reshape(DEPTH, 512, 512)
        m["cv"] = cv[i].reshape(DEPTH, 512, 512)
        m["st"] = st[i].reshape(DEPTH, 2, 2, 2048)
        m["cvec"] = np.stack([cctx, c[i]], axis=0)
        in_maps.append(m)
    res = run_bass_kernel_spmd(nc, in_maps, core_ids=list(range(8)))
    R = res.results
    y_prompt = np.stack([R[i // 2]["y"][(i % 2) * 256:(i % 2) * 256 + 256] for i in range(16)], axis=0)
    y_sample = np.stack([R[i]["y"][512:1536] for i in range(8)], axis=0)
    nk = np.concatenate([R[i]["nk"] for i in range(8)], axis=0).reshape(16, DEPTH, 256, 4, 2, 64)
    nv = np.concatenate([R[i]["nv"] for i in range(8)], axis=0).reshape(16, DEPTH, 256, 4, 128)
    ns = np.concatenate([R[i]["ns"] for i in range(8)], axis=0).reshape(16, DEPTH, 2, 2, 32, 64)
    return (y_prompt.astype(np.float32), y_sample.astype(np.float32), nk.astype(np.float32), nv.astype(np.float32),
            ns.astype(np.float32))
```

```python
import math
import numpy as np
_VD = ''
import concourse.bass as bass
import concourse.mybir as mybir
import concourse.ap as apm
from concourse.bass_utils import run_bass_kernel_spmd

F32 = mybir.dt.float32
BF16 = mybir.dt.bfloat16
I32 = mybir.dt.int32
AF = mybir.ActivationFunctionType
ALU = mybir.AluOpType
AX = mybir.AxisListType
_ES = {str(F32): 4, str(BF16): 2, str(I32): 4}

D = 1024; NT = 1536; DFF = 2816; INW = 5632; DEPTH = 4
EPS = 1e-6
TWO_PI = 2.0 * math.pi
SIN_SC = 1.0 - 2e-4
PI_LO = 3.141592


class _Rec:
    __slots__ = ("eng", "sem", "val", "write", "p0", "p1", "f0", "f1")

    def __init__(self, eng, sem, val, write, box):
        self.eng = eng; self.sem = sem; self.val = val; self.write = write
        self.p0, self.p1, self.f0, self.f1 = box


def _box(ap):
    pat = ap.ap
    es = _ES[str(ap.dtype)]
    off = int(ap.offset)
    row = pat[0][0]
    if row == 0:
        p0 = 0; f = off
    else:
        p0 = off // row; f = off - p0 * row
    p1 = p0 + pat[0][1]
    lo = f; hi = f
    for st, cnt in pat[1:]:
        if st >= 0:
            hi += st * (cnt - 1)
        else:
            lo += st * (cnt - 1)
    return (ap.tensor.name, p0, p1, lo * es, (hi + 1) * es)


class K:
    def __init__(self, nc, n_dma_sems=32):
        self.nc = nc
        self.engs = {"pe": nc.tensor, "act": nc.scalar, "dve": nc.vector, "pool": nc.gpsimd, "sp": nc.sync}
        self.sem = {}; self.cnt = {}
        for e in ("pe", "act", "dve", "pool"):
            self.sem[e] = nc.semaphore("s_" + e).__enter__()
            self.cnt[e] = 0
        self.dsem = [nc.semaphore("d%d" % i).__enter__() for i in range(n_dma_sems)]
        self.dval = [0] * n_dma_sems
        self.dnext = 0
        self.dnext_sw = 0
        self.known = {e: {} for e in self.engs}
        self.recs = {}
        self.out_waits = []
        self.n_inst = {e: 0 for e in self.engs}
        self.n_wait = 0

    def _need(self, eng, sem, val):
        if eng == "pe" and sem is self.sem["pe"]:
            return
        kn = self.known[eng]
        key = sem.name
        if kn.get(key, 0) >= val:
            return
        kn[key] = val
        self.engs[eng].wait_ge(sem, val)
        self.n_wait += 1

    def _tracked(self, ap):
        if ap is None or isinstance(ap, (int, float)):
            return False
        if str(ap.space) == "DRAM" and ap.tensor.name not in self.recs:
            return False
        return True

    def _deps(self, eng, reads, writes):
        for ap in reads:
            if not self._tracked(ap):
                continue
            name, p0, p1, f0, f1 = _box(ap)
            for r in self.recs.get(name, ()):
                if r.write and r.p0 < p1 and p0 < r.p1 and r.f0 < f1 and f0 < r.f1:
                    self._need(eng, r.sem, r.val)
        for ap in writes:
            if not self._tracked(ap):
                continue
            name, p0, p1, f0, f1 = _box(ap)
            for r in self.recs.get(name, ()):
                if r.p0 < p1 and p0 < r.p1 and r.f0 < f1 and f0 < r.f1:
                    self._need(eng, r.sem, r.val)

    def _record(self, eng, sem, val, reads, writes):
        for ap in writes:
            if not self._tracked(ap):
                continue
            name, p0, p1, f0, f1 = _box(ap)
            lst = self.recs.setdefault(name, [])
            lst[:] = [r for r in lst if not (p0 <= r.p0 and r.p1 <= p1 and f0 <= r.f0 and r.f1 <= f1)]
            lst.append(_Rec(eng, sem, val, True, (p0, p1, f0, f1)))
        for ap in reads:
            if not self._tracked(ap):
                continue
            name, p0, p1, f0, f1 = _box(ap)
            lst = self.recs.setdefault(name, [])
            lst[:] = [r for r in lst if not ((not r.write) and r.eng == eng and p0 <= r.p0 and r.p1 <= p1
                                              and f0 <= r.f0 and r.f1 <= f1)]
            lst.append(_Rec(eng, sem, val, False, (p0, p1, f0, f1)))

    def track_dram(self, t):
        self.recs.setdefault(t.name, [])

    def _fin(self, eng, inst, reads, writes, inc=True):
        self.n_inst[eng] += 1
        if inc:
            self.cnt[eng] += 1
            inst.then_inc(self.sem[eng], 1)
            self._record(eng, self.sem[eng], self.cnt[eng], reads, writes)
        else:
            self._record(eng, self.sem[eng], self.cnt[eng] + 1, reads, writes)

    def mm(self, out, lhsT, rhs, start=True, stop=True, **kw):
        self._deps("pe", [lhsT, rhs], [out])
        i = self.nc.tensor.matmul(out, lhsT, rhs, start=start, stop=stop, **kw)
        self._fin("pe", i, [lhsT, rhs], [out], inc=stop)

    def transpose(self, out, in_, ident):
        self._deps("pe", [in_, ident], [out])
        i = self.nc.tensor.transpose(out, in_, ident)
        self._fin("pe", i, [in_, ident], [out])

    def act(self, out, in_, func, scale=1.0, bias=None):
        rd = [in_] + [a for a in (scale, bias) if a is not None and not isinstance(a, (int, float))]
        self._deps("act", rd, [out])
        kw = {}
        if bias is not None:
            kw["bias"] = bias
        i = self.nc.scalar.activation(out=out, in_=in_, func=func, scale=scale, **kw)
        self._fin("act", i, rd, [out])

    def _ve(self, eng):
        return self.nc.vector if eng == "dve" else self.nc.gpsimd

    def tt(self, eng, out, in0, in1, op):
        self._deps(eng, [in0, in1], [out])
        i = self._ve(eng).tensor_tensor(out, in0, in1, op)
        self._fin(eng, i, [in0, in1], [out])

    def ts(self, eng, out, in0, s1, s2=None, op0=ALU.mult, op1=None):
        rd = [in0] + [a for a in (s1, s2) if a is not None and not isinstance(a, (int, float))]
        self._deps(eng, rd, [out])
        if op1 is not None:
            i = self._ve(eng).tensor_scalar(out, in0, s1, s2, op0, op1)
        else:
            i = self._ve(eng).tensor_scalar(out, in0, s1, None, op0)
        self._fin(eng, i, rd, [out])

    def stt(self, out, in0, scalar, in1, op0, op1):
        rd = [in0, in1] + ([scalar] if not isinstance(scalar, (int, float)) else [])
        self._deps("dve", rd, [out])
        i = self.nc.vector.scalar_tensor_tensor(out, in0, scalar, in1, op0, op1)
        self._fin("dve", i, rd, [out])

    def scan(self, out, d0, d1, initial):
        rd = [d0, d1] + ([initial] if not isinstance(initial, (int, float)) else [])
        self._deps("dve", rd, [out])
        i = self.nc.vector.tensor_tensor_scan(out, d0, d1, initial, ALU.mult, ALU.add)
        self._fin("dve", i, rd, [out])

    def copy(self, eng, out, in_):
        if eng == "act":
            return self.act(out, in_, AF.Copy)
        self._deps(eng, [in_], [out])
        i = self._ve(eng).tensor_copy(out, in_)
        self._fin(eng, i, [in_], [out])

    def memset(self, eng, ap, val):
        self._deps(eng, [], [ap])
        i = self._ve(eng).memset(ap, val)
        self._fin(eng, i, [], [ap])

    def recip(self, out, in_):
        self._deps("dve", [in_], [out])
        i = self.nc.vector.reciprocal(out, in_)
        self._fin("dve", i, [in_], [out])

    def dma(self, q, out, in_, is_output=False, **kw):
        half = len(self.dsem) // 2
        if q == "pool":
            s = half + self.dnext_sw
            self.dnext_sw = (self.dnext_sw + 1) % half
        else:
            s = self.dnext
            self.dnext = (self.dnext + 1) % half
        sem = self.dsem[s]
        if self.dval[s] > 0:
            self._need(q, sem, self.dval[s])
        self._deps(q, [in_], [out])
        self.dval[s] += 16
        self.engs[q].dma_start(out=out, in_=in_, **kw).then_inc(sem, 16)
        self.n_inst[q] += 1
        self._record("dma%d" % s, sem, self.dval[s], [in_], [out])
        if is_output:
            self.out_waits.append((s, self.dval[s]))

    def finish(self):
        last = {}
        for s, v in self.out_waits:
            last[s] = max(last.get(s, 0), v)
        for s, v in last.items():
            self._need("sp", self.dsem[s], v)
        for e in ("pe", "act", "dve", "pool"):
            if self.cnt[e] > 0:
                self._need("sp", self.sem[e], self.cnt[e])


class Arena:
    def __init__(self, nc, nwords):
        self.t = nc.sbuf_tensor("arena", [128, nwords], F32).__enter__()
        self.n = nwords; self.top = 0; self.stack = []

    def alloc(self, free_shape, dt=F32):
        n = 1
        for s in free_shape:
            n *= s
        words = n if _ES[str(dt)] == 4 else (n + 1) // 2
        words = (words + 7) // 8 * 8
        off = self.top
        self.top += words
        assert self.top <= self.n, "arena overflow %d > %d" % (self.top, self.n)
        ap = self.t[:, off:off + words]
        if dt != F32:
            ap = ap.bitcast(dt)
        ap = ap[:, 0:n]
        if len(free_shape) == 2:
            ap = ap.rearrange("p (a b) -> p a b", a=free_shape[0])
        elif len(free_shape) == 3:
            ap = ap.rearrange("p (a b c) -> p a b c", a=free_shape[0], b=free_shape[1])
        elif len(free_shape) == 4:
            ap = ap.rearrange("p (a b c d) -> p a b c d", a=free_shape[0], b=free_shape[1], c=free_shape[2])
        return ap

    def mark(self):
        self.stack.append(self.top)

    def release(self):
        self.top = self.stack.pop()


def _row(ap):
    return ap.ap[0][0]


def rev(ap):
    pat = ap.ap
    assert len(pat) == 2
    st, n = pat[1]
    return apm.AP(ap.tensor, int(ap.offset) + (n - 1) * st, [list(pat[0]), [-st, n]])


class _Stop(Exception):
    pass


def build(nc, depth=DEPTH, dbg=False, stop=None):
    try:
        return _build(nc, depth, dbg, stop)
    except _Stop as e:
        e.args[0].finish()
        return e.args[0]


def _build(nc, depth, dbg, stop):
    k = K(nc)

    dbg_t = {}

    def stage(name, dumps=()):
        if stop != name:
            return
        off = 0
        ar.mark()
        scr = ar.alloc([2048])
        for ap in dumps:
            flat = ap
            if len(ap.shape) == 3:
                flat = ap.rearrange("p a b -> p (a b)")
            elif len(ap.shape) == 4:
                flat = ap.rearrange("p a b c -> p (a b c)")
            n = flat.shape[1]
            for c0 in range(0, n, 2048):
                c1 = min(n, c0 + 2048)
                k.copy("dve", scr[:, 0:c1 - c0], flat[:, c0:c1])
                k.dma("sp", dbg_t["d"].ap()[:, off + c0:off + c1], scr[:, 0:c1 - c0], is_output=True)
            off += n
        raise _Stop(k)

    ar = Arena(nc, 53000)
    psum = nc.psum_tensor("ps", [128, 8, 512], F32).__enter__()
    st_ps = {"i": 0}

    def ps(pool=(0, 1, 2, 3, 4, 5, 6, 7)):
        st_ps["i"] += 1
        return psum[:, pool[st_ps["i"] % len(pool)], :]

    def din(name, shape, dt=F32):
        return nc.dram_tensor(name, list(shape), dt, kind="ExternalInput")

    xin = din("xin", [NT, D]); ck = din("ck", [DEPTH, 512, 512]); cv = din("cv", [DEPTH, 512, 512])
    stt_in = din("st", [DEPTH, 2, 2, 2048]); cvec = din("cvec", [2, D])
    norm_g = din("norm_g", [DEPTH, 3, D]); w_mod = din("w_mod", [DEPTH, D, 9 * D]); b_mod = din("b_mod", [DEPTH, 9 * D])
    w_ffn_in = din("w_ffn_in", [DEPTH, 2, D, 2 * DFF]); w_ffn_out = din("w_ffn_out", [DEPTH, 2, DFF, D])
    w_in = din("w_in", [DEPTH, D, INW])
    lam_re = din("ssm_lam_re", [DEPTH, 2, 32, 64]); lam_im = din("ssm_lam_im", [DEPTH, 2, 32, 64])
    log_dt = din("ssm_log_dt", [DEPTH, 2, 32])
    b_re = din("ssm_b_re", [DEPTH, 2, 32, 64, 16]); b_im = din("ssm_b_im", [DEPTH, 2, 32, 64, 16])
    c_re = din("ssm_c_re", [DEPTH, 2, 32, 16, 64]); c_im = din("ssm_c_im", [DEPTH, 2, 32, 16, 64])
    ssm_d = din("ssm_d", [DEPTH, 512]); w_glu = din("w_glu", [DEPTH, 512, 512])
    lq1 = din("lam_q1", [DEPTH, 64]); lk1 = din("lam_k1", [DEPTH, 64])
    lq2 = din("lam_q2", [DEPTH, 64]); lk2 = din("lam_k2", [DEPTH, 64])
    attn_g = din("attn_norm_g", [DEPTH, 128]); w_pool = din("w_pool", [DEPTH, 4, 128, 128])
    pool_scale = din("pool_scale", [DEPTH, 512]); w_branch = din("w_branch", [DEPTH, 3, 512, D])
    w_out = din("w_out", [DEPTH, D, D]); final_g = din("final_norm_g", [D])
    c_ident = din("c_ident", [128, 128]); c_rope = din("c_rope", [128, 2, 1024]); c_pt = din("c_pt", [128, 128])
    c_iota = din("c_iota", [128, 1024]); c_mask = din("c_mask", [128, 2]); c_pool = din("c_pool", [128, 4, 2, 2, 8])

    def dout(name, shape):
        return nc.dram_tensor(name, list(shape), F32, kind="ExternalOutput")

    y_out = dout("y", [NT, D]); nk_out = dout("nk", [2, DEPTH, 256, 512]); nv_out = dout("nv", [2, DEPTH, 256, 512])
    ns_out = dout("ns", [2, DEPTH, 2, 2, 2048])
    if stop is not None:
        dbg_t["d"] = dout("dbg", [128, 32768])

    ident = ar.alloc([128]); ident_bf = ar.alloc([128], BF16); ones_bf = ar.alloc([128], BF16)
    ones_f = ar.alloc([128]); pt_bf = ar.alloc([128], BF16)
    rope = ar.alloc([2, 1024]); iota = ar.alloc([1024]); maskc = ar.alloc([2]); poolc = ar.alloc([4, 2, 2, 8])
    cst = ar.alloc([8])
    k.dma("sp", ident, c_ident.ap())
    k.dma("sp", rope, c_rope.ap())
    k.dma("sp", iota, c_iota.ap())
    k.dma("sp", maskc, c_mask.ap())
    k.dma("sp", poolc, c_pool.ap())
    tmp_pt = ar.alloc([128])
    k.dma("sp", tmp_pt, c_pt.ap())
    k.copy("dve", ident_bf, ident)
    k.copy("dve", pt_bf, tmp_pt)
    k.memset("dve", ones_bf, 1.0)
    k.memset("dve", ones_f, 1.0)
    k.memset("dve", cst[:, 0:1], EPS)
    k.memset("dve", cst[:, 1:2], 0.5 * math.pi * SIN_SC)
    k.memset("dve", cst[:, 2:3], 0.0)
    k.memset("dve", cst[:, 3:4], 0.5 * math.pi)
    hpi_x = cst[:, 3:4]
    eps_c = cst[:, 0:1]; hpi_c = cst[:, 1:2]

    xT = ar.alloc([8, NT])
    nT = ar.alloc([8, NT], BF16)
    modT = ar.alloc([DEPTH, 72, 2])
    Amod = ar.alloc([DEPTH, 3, 8, 2])
    Gmod = ar.alloc([DEPTH, 3, 8, 2])
    normgT = ar.alloc([96])
    smallT = ar.alloc([64])
    lamc = ar.alloc([DEPTH, 2])
    gfac = ar.alloc([DEPTH])
    sc = ar.alloc([2, 8])
    bmodT = ar.alloc([DEPTH, 72])

    ar.mark()
    stg = ar.alloc([128])
    k.dma("sp", stg[0:96, :], norm_g.ap().rearrange("l j (c p) -> (l j c) p", p=128))
    pt = ps()
    k.transpose(pt[:, 0:96], stg[0:96, :], ident[0:96, 0:96])
    k.copy("dve", normgT, pt[:, 0:96])
    stg2 = ar.alloc([128])
    k.dma("sp", stg2[0:16, :], ssm_d.ap().rearrange("l (c p) -> (l c) p", p=128))
    k.dma("sp", stg2[16:32, :], pool_scale.ap().rearrange("l (c p) -> (l c) p", p=128))
    k.dma("sp", stg2[32:36, :], attn_g.ap())
    k.dma("sp", stg2[36:44, :], final_g.ap().rearrange("(c p) -> c p", p=128))
    k.dma("sp", stg2[44:60, :], cvec.ap().rearrange("v (c p) -> (v c) p", p=128))
    pt = ps()
    k.transpose(pt[:, 0:60], stg2[0:60, :], ident[0:60, 0:60])
    k.copy("dve", smallT[:, 0:60], pt[:, 0:60])
    dskipT = smallT[:, 0:16]; pscT = smallT[:, 16:32]; attngT = smallT[:, 32:36]; fingT = smallT[:, 36:44]
    k.act(sc.rearrange("p v c -> p (v c)"), smallT[:, 44:60], AF.Silu)
    for l in range(DEPTH):
        stg3 = ar.alloc([128])
        k.dma("sp", stg3[0:72, :], b_mod.ap()[l].rearrange("(c p) -> c p", p=128))
        pt = ps()
        k.transpose(pt[:, 0:72], stg3[0:72, :], ident[0:72, 0:72])
        k.copy("dve", bmodT[:, l, :], pt[:, 0:72])
    lv = ar.alloc([4, DEPTH])
    with nc.allow_non_contiguous_dma(reason="tiny"):
        for i, t in enumerate((lq1, lk1, lq2, lk2)):
            k.dma("sp", lv[0:64, i, :], t.ap().rearrange("l d -> d l"))
    lp = ar.alloc([2, DEPTH])
    k.tt("dve", lp[0:64, 0, :], lv[0:64, 0, :], lv[0:64, 1, :], ALU.mult)
    k.tt("dve", lp[0:64, 1, :], lv[0:64, 2, :], lv[0:64, 3, :], ALU.mult)
    pt = ps()
    k.mm(pt[:, 0:2 * DEPTH], ones_f[0:64, :], lp[0:64].rearrange("p a l -> p (a l)"))
    le = ar.alloc([2, DEPTH])
    k.act(le.rearrange("p a l -> p (a l)"), pt[:, 0:2 * DEPTH], AF.Exp)
    for l in range(DEPTH):
        li_ = 0.8 - 0.6 * math.exp(-0.3 * l)
        k.stt(lamc[:, l, 0:1], le[:, 0, l:l + 1], li_, le[:, 1, l:l + 1], ALU.add, ALU.subtract)
        k.ts("dve", lamc[:, l, 1:2], lamc[:, l, 0:1], -1.0)
        k.ts("dve", gfac[:, l:l + 1], attngT[:, l:l + 1], 1.0 - li_)

    def adaln_fin(l, mps):
        for v in range(2):
            k.tt("dve", modT[:, l, :, v], mps[:, 0:144].rearrange("p (f v) -> p f v", v=2)[:, :, v], bmodT[:, l, :], ALU.add)
        for j in range(3):
            for v in range(2):
                k.stt(Amod[:, l, j, :, v], modT[:, l, (3 * j + 1) * 8:(3 * j + 2) * 8, v], 1.0,
                      normgT[:, l * 24 + j * 8:l * 24 + j * 8 + 8], ALU.add, ALU.mult)
                k.ts("dve", Gmod[:, l, j, :, v], modT[:, l, (3 * j + 2) * 8:(3 * j + 3) * 8, v],
                     1.0 if j == 1 else 0.5)

    def adaln_dma(l, fb, wm):
        k.dma("sp", wm, w_mod.ap()[l].rearrange("(c p) f -> p c f", p=128)[:, :, fb * 128:(fb + 1) * 128])

    def adaln_mm(l, fb, wm, mps):
        for kc in range(8):
            k.mm(mps[:, fb * 2:fb * 2 + 2], wm[:, kc, :], sc[:, :, kc], start=(kc == 0), stop=(kc == 7))

    wms = [ar.alloc([8, 256]) for _ in range(2)]
    for l in range(1):
        mps = ps()
        for fc in range(36):
            wm = wms[fc % 2]
            k.dma("sp", wm, w_mod.ap()[l].rearrange("(c p) f -> p c f", p=128)[:, :, fc * 256:(fc + 1) * 256])
            for s_ in range(2):
                fb = fc * 2 + s_
                for kc in range(8):
                    k.mm(mps[:, fb * 2:fb * 2 + 2], wm[:, kc, s_ * 128:(s_ + 1) * 128], sc[:, :, kc],
                         start=(kc == 0), stop=(kc == 7))
        adaln_fin(l, mps)
    ar.release()

    ar.mark()
    for tb in range(12):
        ar.mark()
        xs = ar.alloc([D])
        k.dma("sp", xs, xin.ap()[tb * 128:(tb + 1) * 128, :])
        for g4 in range(2):
            pt = ps()
            for c in range(4):
                kc = g4 * 4 + c
                k.transpose(pt[:, c * 128:(c + 1) * 128], xs[:, kc * 128:(kc + 1) * 128], ident)
            k.copy("act" if g4 else "dve", xT[:, g4 * 4:(g4 + 1) * 4, tb * 128:(tb + 1) * 128],
                   pt.rearrange("p (c t) -> p c t", c=4))
        ar.release()
    ar.release()

    stage('prologue', [modT, xT])

    def vsel(tt):
        return 0 if tt == 0 else 1

    def rmsnorm_mod(l, j, A_ap=None, shift_ap=None, out_fn=None):
        ar.mark()
        sq = ar.alloc([8, 512], BF16)
        rstd = ar.alloc([512]); tmp = ar.alloc([2, 512])
        for tt in range(3):
            tsl = slice(tt * 512, (tt + 1) * 512)
            for kc in range(8):
                k.act(sq[:, kc, :], xT[:, kc, tsl], AF.Square)
            ss = ps()
            for kc in range(8):
                k.mm(ss, ones_bf, sq[:, kc, :], start=(kc == 0), stop=(kc == 7))
            k.act(rstd, ss, AF.Sqrt, scale=1.0 / D, bias=eps_c)
            k.recip(rstd, rstd)
            v = vsel(tt)
            for kc in range(8):
                if j < 3:
                    a_col = Amod[:, l, j, kc, v:v + 1]
                    b_col = modT[:, l, 3 * j * 8 + kc, v:v + 1]
                    tb_ = tmp[:, kc % 2, :]
                    k.stt(tb_, xT[:, kc, tsl], a_col, rstd, ALU.mult, ALU.mult)
                    k.act(nT[:, kc, tsl], tb_, AF.Identity, bias=b_col)
                else:
                    k.stt(out_fn(kc, tt), xT[:, kc, tsl], fingT[:, kc:kc + 1], rstd, ALU.mult, ALU.mult)
        ar.release()

    def wload(src_ap, shape, dt=BF16):
        buf = ar.alloc(shape, dt)
        k.dma("pool" if dt == BF16 else "sp", buf, src_ap)
        return buf

    def ffn(l, i):
        rmsnorm_mod(l, 0 if i == 0 else 2)
        j = 0 if i == 0 else 2
        ar.mark()
        gT = ar.alloc([22, NT], BF16)
        sa = ar.alloc([2, 512])
        wv = w_ffn_in.ap()[l, i].rearrange("(c p) f -> p c f", p=128)
        wbufs = [ar.alloc([2, 8, 256], BF16) for _ in range(2)]
        for c in range(11):
            wb = wbufs[c % 2]
            k.dma("pool", wb[:, 0], wv[:, :, c * 256:(c + 1) * 256])
            k.dma("pool", wb[:, 1], wv[:, :, DFF + c * 256:DFF + (c + 1) * 256])
            for s in range(2):
                fb = 2 * c + s
                for tt in range(3):
                    tsl = slice(tt * 512, (tt + 1) * 512)
                    pa = ps(); pb = ps()
                    for kc in range(8):
                        k.mm(pa, wb[:, 0, kc, s * 128:(s + 1) * 128], nT[:, kc, tsl], start=(kc == 0), stop=(kc == 7))
                    for kc in range(8):
                        k.mm(pb, wb[:, 1, kc, s * 128:(s + 1) * 128], nT[:, kc, tsl], start=(kc == 0), stop=(kc == 7))
                    sab = sa[:, (fb * 3 + tt) % 2, :]
                    k.act(sab, pa, AF.Silu)
                    k.tt("dve", gT[:, fb, tsl], sab, pb, ALU.mult)
        wov = w_ffn_out.ap()[l, i].rearrange("(f p) d -> p f d", p=128)
        wobufs = [ar.alloc([22, 128], BF16) for _ in range(2)]
        for dc in range(8):
            wo = wobufs[dc % 2]
            k.dma("pool", wo, wov[:, :, dc * 128:(dc + 1) * 128])
            for tt in range(3):
                tsl = slice(tt * 512, (tt + 1) * 512)
                py = ps()
                for fb in range(22):
                    k.mm(py, wo[:, fb, :], gT[:, fb, tsl], start=(fb == 0), stop=(fb == 21))
                v = vsel(tt)
                k.stt(xT[:, dc, tsl], py, Gmod[:, l, j, dc, v:v + 1], xT[:, dc, tsl], ALU.mult, ALU.add)
        ar.release()

    def bc_last(ap2, n):
        pat = ap2.ap
        return apm.AP(ap2.tensor, int(ap2.offset), [list(pat[0]), list(pat[1]), [0, n]])

    def s5_branch(l, uT, yaT):
        ar.mark()
        lr = ar.alloc([2, 16]); li = ar.alloc([2, 16]); ldt = ar.alloc([2, 16])
        h0 = ar.alloc([2, 2, 16])
        with nc.allow_non_contiguous_dma(reason="ssm params"):
            for d in range(2):
                k.dma("sp", lr[:, d, :], lam_re.ap()[l, d].rearrange("(W h) p -> (h p) W", h=2))
                k.dma("sp", li[:, d, :], lam_im.ap()[l, d].rearrange("(W h) p -> (h p) W", h=2))
                for h in range(2):
                    src = apm.AP(log_dt, (l * 2 + d) * 32 + h, [[0, 64], [2, 16]])
                    k.dma("sp", ldt[64 * h:64 * h + 64, d, :], src)
                for r in range(2):
                    k.dma("sp", h0[:, d, r, :], stt_in.ap()[l, d, r].rearrange("(W hp) -> hp W", hp=128))
        dt_ = ar.alloc([2, 16]); mcol = ar.alloc([2, 16]); th = ar.alloc([2, 16])
        k.act(dt_, ldt, AF.Exp)
        a_ = ar.alloc([2, 16])
        k.tt("dve", a_, lr, dt_, ALU.mult)
        k.act(mcol, a_, AF.Exp)
        k.tt("dve", th, li, dt_, ALU.mult)
        qi_ = ar.alloc([2, 16], I32); rr = ar.alloc([2, 16]); sn = ar.alloc([2, 16]); cs = ar.alloc([2, 16])
        k.ts("dve", qi_, th, 1.0 / TWO_PI)
        k.stt(rr, qi_, -TWO_PI, th, ALU.mult, ALU.add)
        k.ts("dve", rr, rr, PI_LO, -PI_LO, ALU.min, ALU.max)
        k.act(sn, rr, AF.Sin)
        rr2 = ar.alloc([2, 16])
        k.act(rr2, rr, AF.Abs)
        k.act(cs, rr2, AF.Sin, scale=-1.0, bias=hpi_x)
        abr = ar.alloc([2, 16]); abi = ar.alloc([2, 16]); den = ar.alloc([2, 16]); t1 = ar.alloc([2, 16])
        t2_ = ar.alloc([2, 16]); den2 = ar.alloc([2, 16]); rden = ar.alloc([2, 16]); nr = ar.alloc([2, 16])
        kr = ar.alloc([2, 16]); ki = ar.alloc([2, 16]); kr0 = ar.alloc([2, 16]); ki0 = ar.alloc([2, 16])
        k.tt("dve", abr, mcol, cs, ALU.mult)
        k.tt("dve", abi, mcol, sn, ALU.mult)
        k.tt("dve", den, lr, lr, ALU.mult)
        k.tt("dve", t1, li, li, ALU.mult)
        k.tt("dve", den2, den, t1, ALU.add)
        k.recip(rden, den2)
        k.ts("dve", nr, abr, -1.0, None, ALU.add)
        k.tt("dve", kr0, nr, lr, ALU.mult)
        k.tt("dve", t2_, abi, li, ALU.mult)
        k.tt("dve", kr, kr0, t2_, ALU.add)
        k.tt("dve", kr, kr, rden, ALU.mult)
        k.tt("dve", ki0, abi, lr, ALU.mult)
        k.tt("dve", t1, nr, li, ALU.mult)
        k.tt("dve", ki, ki0, t1, ALU.subtract)
        k.tt("dve", ki, ki, rden, ALU.mult)
        BL = ar.alloc([2, 2, 4, 128], BF16)
        cns = ar.alloc([2, 2, 4, 64])
        for d in range(2):
            for r, csrc in enumerate((c_re, c_im)):
                k.dma("sp", cns[:, d, r], csrc.ap()[l, d].rearrange("(q g) c p -> (g c) q p", g=8))
        ar.mark()
        for d in range(2):
            braw = ar.alloc([16, 16]); iraw = ar.alloc([16, 16])
            with nc.allow_non_contiguous_dma(reason="ssm params"):
                k.dma("sp", braw, b_re.ap()[l, d].rearrange("(W h) p c -> (h p) W c", h=2))
                k.dma("sp", iraw, b_im.ap()[l, d].rearrange("(W h) p c -> (h p) W c", h=2))
            krb = bc_last(kr[:, d, :], 16); kib = bc_last(ki[:, d, :], 16)
            bb = ar.alloc([2, 16, 16]); tq = ar.alloc([16, 16]); tq2 = ar.alloc([16, 16])
            k.tt("dve", tq2, braw, krb, ALU.mult)
            k.tt("dve", tq, iraw, kib, ALU.mult)
            k.tt("dve", bb[:, 0], tq2, tq, ALU.subtract)
            tq3 = ar.alloc([16, 16]); tq4 = ar.alloc([16, 16])
            k.tt("dve", tq3, iraw, krb, ALU.mult)
            k.tt("dve", tq4, braw, kib, ALU.mult)
            k.tt("dve", bb[:, 1], tq3, tq4, ALU.add)
            for r in range(2):
                xall = ar.alloc([16, 2, 16])
                k.memset("dve", xall, 0.0)
                k.copy("dve", xall[0:64, :, 0, :], bb[0:64, r])
                k.copy("dve", xall[64:128, :, 1, :], bb[64:128, r])
                xf = xall.rearrange("p W h c -> p (W h c)")
                pt = ps((3, 4, 5, 6, 7))
                for q in range(4):
                    k.transpose(pt[:, q * 128:(q + 1) * 128], xf[:, q * 128:(q + 1) * 128], ident)
                k.copy("act", BL[:, d, r].rearrange("p q c -> p (q c)"), pt)
        ar.release()
        ar.mark()
        nTw = nT.rearrange("p a b -> p (a b)").bitcast(F32).rearrange("p (a b) -> p a b", a=4)
        sets = []
        for s_ in range(2):
            st = {}
            st["tabC"] = ar.alloc([1024]); st["tabS"] = ar.alloc([1024])
            st["xr"] = nTw[:, 2 * s_, :]; st["xi"] = nTw[:, 2 * s_ + 1, :]
            st["tm"] = ar.alloc([4, 512]); st["pr"] = ar.alloc([4, NT], BF16)
            st["h32"] = ar.alloc([2, 512])
            sets.append(st)
        NS = ar.alloc([2, 2, 2, 16])
        ygf = ar.alloc([512])
        CLqs = [ar.alloc([2, 3, 4, 128], BF16) for _ in range(2)]
        zall = ar.alloc([2, 128])
        seqs = [(0, 256), (256, 256), (512, 1024)]

        def tabv(tab, d, tt):
            row = _row(tab); off = int(tab.offset)
            if d == 0:
                if tt == 0:
                    return apm.AP(tab.tensor, off, [[row, 128], [0, 2], [1, 256]])
                return tab[:, (tt - 1) * 512:tt * 512]
            if tt == 0:
                return apm.AP(tab.tensor, off + 255, [[row, 128], [0, 2], [-1, 256]])
            return apm.AP(tab.tensor, off + (1023 if tt == 1 else 511), [[row, 128], [-1, 512]])

        def v3(ap, tt):
            return ap.rearrange("p (s t) -> p s t", s=2) if tt == 0 else ap

        Ybanks = (0, 1, 2)
        nxt = l + 1 if l + 1 < depth else None
        XP = (3, 4, 5, 6, 7) if nxt is None else (3, 4, 5, 6)
        mps_n = psum[:, 7, :]
        wma = [ar.alloc([8, 128]) for _ in range(2)] if nxt is not None else None
        pairs = [(q, d, w) for q in range(4) for d in range(2) for w in range(4)]

        def build_CL(q):
            CLq = CLqs[q % 2]
            k.memset("pool", CLq.rearrange("p d r w c -> p (d r w c)"), 0.0)
            for d in range(2):
                for r in range(3):
                    zz = zall[:, (d * 3 + r) % 2]
                    sgn = 1.0 if r == 0 else -1.0
                    src = cns[:, d, 1 if r == 1 else 0, q, :]
                    k.ts("pool", zz[:, 0:64], src, maskc[:, 0:1], sgn, ALU.mult, ALU.mult)
                    k.ts("pool", zz[:, 64:128], src, maskc[:, 1:2], sgn, ALU.mult, ALU.mult)
                    pt = ps(XP)
                    k.transpose(pt[:, 0:128], zz, ident)
                    for w in range(4):
                        k.copy("act", CLq[:, d, r, w, 32 * w:32 * w + 32], pt[:, 32 * w:32 * w + 32])

        def stA(i):
            q, d, w = pairs[i]; st = sets[i % 2]; W = 4 * q + w
            thc = th[:, d, W:W + 1]
            ang = st["tm"][:, 0:2].rearrange("p a b -> p (a b)"); rb = st["tm"][:, 2:4].rearrange("p a b -> p (a b)")
            qi = st["h32"].rearrange("p a b -> p (a b)").bitcast(I32)
            k.act(ang, iota, AF.Identity, scale=thc)
            k.ts("dve", qi, ang, 1.0 / TWO_PI)
            k.stt(rb, qi, -TWO_PI, ang, ALU.mult, ALU.add)
            k.ts("dve", rb, rb, PI_LO, -PI_LO, ALU.min, ALU.max)
            k.act(st["tabS"], rb, AF.Sin)
            k.act(ang, rb, AF.Abs)
            k.act(st["tabC"], ang, AF.Sin, scale=-1.0, bias=hpi_x)

        def stB(i):
            q, d, w = pairs[i]; st = sets[i % 2]
            tm = st["tm"]
            for tt in range(3):
                tsl = slice(tt * 512, (tt + 1) * 512)
                XR = ps(XP); XI = ps(XP)
                k.mm(XR, BL[32 * w:32 * w + 32, d, 0, q, :], uT[32 * w:32 * w + 32, q, tsl], tile_position=(32 * w, 0))
                k.mm(XI, BL[32 * w:32 * w + 32, d, 1, q, :], uT[32 * w:32 * w + 32, q, tsl], tile_position=(32 * w, 0))
                C = tabv(st["tabC"], d, tt); S = tabv(st["tabS"], d, tt)
                k.tt("dve", v3(tm[:, 0], tt), v3(XR, tt), C, ALU.mult)
                k.tt("dve", v3(tm[:, 1], tt), v3(XI, tt), S, ALU.mult)
                k.tt("pool", st["xr"][:, tsl], tm[:, 0], tm[:, 1], ALU.add)
                k.tt("dve", v3(tm[:, 2], tt), v3(XI, tt), C, ALU.mult)
                k.tt("dve", v3(tm[:, 3], tt), v3(XR, tt), S, ALU.mult)
                k.tt("pool", st["xi"][:, tsl], tm[:, 2], tm[:, 3], ALU.subtract)

        def stC(i):
            q, d, w = pairs[i]; st = sets[i % 2]; W = 4 * q + w
            mc = mcol[:, d, W:W + 1]
            for si, (o, L) in enumerate(seqs):
                for (buf, r) in ((st["xr"], 0), (st["xi"], 1)):
                    init = h0[:, d, r, W:W + 1] if si == 2 else 0.0
                    a_ = buf[:, o:o + L]
                    if d == 1:
                        a_ = rev(a_)
                    k.scan(a_, mc.to_broadcast([128, L]), a_, init)

        def stD(i):
            q, d, w = pairs[i]; st = sets[i % 2]; W = 4 * q + w
            gr = st["xr"]; gi = st["xi"]; pr = st["pr"]
            CLq = CLqs[q % 2]
            col = 255 if d == 0 else 0
            grc = apm.AP(gr.tensor, int(gr.offset) + col, [[_row(gr), 128], [256, 2]])
            gic = apm.AP(gi.tensor, int(gi.offset) + col, [[_row(gi), 128], [256, 2]])
            c255 = st["tabC"][:, 255:256]; s255 = st["tabS"][:, 255:256]
            tn = st["h32"][:, 0, 0:4]
            k.ts("dve", tn[:, 0:2], gic, s255)
            k.stt(NS[:, :, d, 0, W], grc, c255, tn[:, 0:2], ALU.mult, ALU.subtract)
            k.ts("dve", tn[:, 2:4], gic, c255)
            k.stt(NS[:, :, d, 1, W], grc, s255, tn[:, 2:4], ALU.mult, ALU.add)
            for tt in range(3):
                tsl = slice(tt * 512, (tt + 1) * 512)
                C = tabv(st["tabC"], d, tt); S = tabv(st["tabS"], d, tt)
                k.tt("dve", v3(pr[:, 0, tsl], tt), v3(gr[:, tsl], tt), C, ALU.mult)
                k.tt("dve", v3(pr[:, 1, tsl], tt), v3(gi[:, tsl], tt), S, ALU.mult)
                k.tt("dve", v3(pr[:, 2, tsl], tt), v3(gr[:, tsl], tt), S, ALU.mult)
                k.tt("dve", v3(pr[:, 3, tsl], tt), v3(gi[:, tsl], tt), C, ALU.mult)
                Y = psum[:, Ybanks[tt], :]
                last = (d == 1 and w == 3)
                k.mm(Y, CLq[:, d, 0, w, :], pr[:, 0, tsl], start=(d == 0 and w == 0), stop=False)
                k.mm(Y, CLq[:, d, 2, w, :], pr[:, 1, tsl], start=False, stop=False)
                k.mm(Y, CLq[:, d, 1, w, :], pr[:, 2, tsl], start=False, stop=False)
                k.mm(Y, CLq[:, d, 1, w, :], pr[:, 3, tsl], start=False, stop=last)
            if d == 1 and w == 3:
                for tt in range(3):
                    tsl = slice(tt * 512, (tt + 1) * 512)
                    Y = psum[:, Ybanks[tt], :]
                    k.stt(ygf, uT[:, q, tsl], dskipT[:, l * 4 + q:l * 4 + q + 1], Y, ALU.mult, ALU.add)
                    k.act(yaT[:, q, tsl], ygf, AF.Gelu_apprx_tanh)

        build_CL(0)
        stA(0); stB(0)
        if nxt is not None:
            adaln_dma(nxt, 0, wma[0]); adaln_dma(nxt, 1, wma[1])
        for i in range(32):
            if i + 1 < 32:
                if pairs[i + 1][1] == 0 and pairs[i + 1][2] == 0:
                    build_CL(pairs[i + 1][0])
                stA(i + 1)
            stC(i)
            if i + 1 < 32:
                stB(i + 1)
            if nxt is not None:
                for fb in range(i * 72 // 32, (i + 1) * 72 // 32):
                    adaln_mm(nxt, fb, wma[fb % 2], mps_n)
                    if fb + 2 < 72:
                        adaln_dma(nxt, fb + 2, wma[fb % 2])
            stD(i)
        if nxt is not None:
            adaln_fin(nxt, mps_n)
        pt = ps(XP)
        k.transpose(pt[:, 0:128], NS.rearrange("p s d r W -> p (s d r W)"), ident)
        nst = ar.alloc([128])
        k.copy("dve", nst, pt[:, 0:128])
        for s in range(2):
            k.dma("pool", ns_out.ap()[s, l].rearrange("d r (W hp) -> (d r W) hp", hp=128), nst[64 * s:64 * s + 64, :],
                  is_output=True)
        ar.release()
        wg = ar.alloc([4, 512], BF16)
        k.dma("pool", wg, w_glu.ap()[l].rearrange("(c p) f -> p c f", p=128))
        sg = ar.alloc([4, 512])
        for tt in range(3):
            tsl = slice(tt * 512, (tt + 1) * 512)
            for fb in range(4):
                pg = ps()
                for kc in range(4):
                    k.mm(pg, wg[:, kc, fb * 128:(fb + 1) * 128], yaT[:, kc, tsl], start=(kc == 0), stop=(kc == 3))
                k.act(sg[:, fb], pg, AF.Sigmoid)
            for fb in range(4):
                k.tt("dve", yaT[:, fb, tsl], yaT[:, fb, tsl], sg[:, fb], ALU.mult)
        ar.release()

    def attn_branch(l, qT, kT, Vt, ybT):
        ar.mark()
        so = ar.alloc([2, 512]); sz = ar.alloc([2, 512])
        rz = ar.alloc([2, 512]); o = ar.alloc([512]); t0 = ar.alloc([512]); sq = ar.alloc([512], BF16)
        rstd = ar.alloc([512])
        jobs = []
        for s in range(2):
            jobs.append((s * 256, 256, [(s * 256 + j * 128, s * 2 + j) for j in range(2)]))
        for qt in range(2):
            keys = [(512 + j * 128, 4 + j) for j in range(8)] + [(1536 + j * 128, 12 + j) for j in range(4)]
            jobs.append((512 + qt * 512, 512, keys))
        E = ar.alloc([4, 512], BF16)
        pending = []
        for h in range(4):
            for (qo, nq, keys) in jobs:
                O = [psum[:, 0, 0:nq], psum[:, 1, 0:nq]]
                Z = [psum[:, 2, 0:nq], psum[:, 3, 0:nq]]
                items = [(m, ki_, ko, vb) for m in range(2) for ki_, (ko, vb) in enumerate(keys)]
                n_it = len(items); nk_ = len(keys)
                Sl = {}

                def emitS(j):
                    m, ki_, ko, vb = items[j]
                    S = ps((4, 5, 6, 7))[:, 0:nq]
                    k.mm(S, kT[64 * m:64 * m + 64, h, ko:ko + 128], qT[64 * m:64 * m + 64, h, qo:qo + nq])
                    Sl[j] = S

                emitS(0)
                if n_it > 1:
                    emitS(1)
                for j in range(n_it):
                    m, ki_, ko, vb = items[j]
                    Eb = E[:, j % 4, 0:nq]
                    if pending and j == min(10, n_it - 1):
                        pending.pop(0)()
                    k.act(Eb, Sl.pop(j), AF.Exp, scale=0.125)
                    if j + 2 < n_it:
                        emitS(j + 2)
                    k.mm(O[m], Vt[:, vb, h * 128:(h + 1) * 128], Eb, start=(ki_ == 0), stop=(ki_ == nk_ - 1))
                    k.mm(Z[m], ones_bf, Eb, start=(ki_ == 0), stop=(ki_ == nk_ - 1))
                for m in range(2):
                    k.copy("act", so[:, m, 0:nq], O[m])
                    k.copy("act", sz[:, m, 0:nq], Z[m])

                def epi(h=h, qo=qo, nq=nq):
                    k.recip(rz[:, 0, 0:nq], sz[:, 0, 0:nq])
                    k.recip(rz[:, 1, 0:nq], sz[:, 1, 0:nq])
                    k.tt("dve", t0[:, 0:nq], so[:, 0, 0:nq], rz[:, 0, 0:nq], ALU.mult)
                    k.tt("dve", o[:, 0:nq], so[:, 1, 0:nq], rz[:, 1, 0:nq], ALU.mult)
                    k.stt(o[:, 0:nq], o[:, 0:nq], lamc[:, l, 1:2], t0[:, 0:nq], ALU.mult, ALU.add)
                    k.tt("pool", sq[:, 0:nq], o[:, 0:nq], o[:, 0:nq], ALU.mult)
                    ssq = ps((4, 5, 6, 7))[:, 0:nq]
                    k.mm(ssq, ones_bf, sq[:, 0:nq])
                    k.act(rstd[:, 0:nq], ssq, AF.Sqrt, scale=1.0 / 128, bias=eps_c)
                    k.recip(rstd[:, 0:nq], rstd[:, 0:nq])
                    k.stt(ybT[:, h, qo:qo + nq], o[:, 0:nq], gfac[:, l:l + 1], rstd[:, 0:nq], ALU.mult, ALU.mult)
                pending.append(epi)
        while pending:
            pending.pop(0)()
        ar.release()

    def pool_branch(l, zp, ycT):
        ar.mark()
        ZW = 1600
        offs = [16, 288, 560]
        wa = ar.alloc([ZW]); wb_ = ar.alloc([ZW]); pooled = ar.alloc([4, NT], BF16); pf = ar.alloc([ZW])
        k.memset("dve", wa, 0.0); k.memset("dve", wb_, 0.0)
        lo, hi_ = 8, ZW - 8
        for g, wdw in enumerate((2, 4, 8, 16)):
            z = zp[:, g, :]
            k.tt("dve", wa[:, lo:hi_], z[:, lo - 1:hi_ - 1], z[:, lo:hi_], ALU.add)
            cur, oth = wa, wb_
            sh = 1
            ww = 2
            while ww < wdw:
                k.tt("dve", oth[:, lo:hi_], cur[:, lo - sh:hi_ - sh], cur[:, lo + sh:hi_ + sh], ALU.add)
                cur, oth = oth, cur
                sh *= 2; ww *= 2
            k.stt(pf[:, lo:hi_], cur[:, lo:hi_], 1.0 / wdw, z[:, lo:hi_], ALU.mult, ALU.subtract)
            for si, (o, L) in enumerate(((16, 256), (288, 256), (560, 1024))):
                for e in range(2):
                    c0 = o if e == 0 else o + L - 8
                    k.tt("dve", pf[:, c0:c0 + 8], cur[:, c0:c0 + 8], poolc[:, g, 0 if si < 2 else 1, e, :], ALU.mult)
                    k.tt("dve", pf[:, c0:c0 + 8], pf[:, c0:c0 + 8], z[:, c0:c0 + 8], ALU.subtract)
            for si, (o, L) in enumerate(((16, 256), (288, 256), (560, 1024))):
                to = (0, 256, 512)[si]
                k.copy("act", pooled[:, g, to:to + L], pf[:, o:o + L])
        wp = ar.alloc([4, 128], BF16)
        k.dma("pool", wp, w_pool.ap()[l].rearrange("g c d -> c g d"))
        for g in range(4):
            for tt in range(3):
                tsl = slice(tt * 512, (tt + 1) * 512)
                pp = ps()
                k.mm(pp, wp[:, g, :], pooled[:, g, tsl])
                k.act(ycT[:, g, tsl], pp, AF.Identity, scale=pscT[:, l * 4 + g:l * 4 + g + 1])
        ar.release()

    def mixer(l):
        rmsnorm_mod(l, 1)
        ar.mark()
        yaT = ar.alloc([4, NT], BF16)
        wv = w_in.ap()[l].rearrange("(c p) f -> p c f", p=128)

        def wchunk(c):
            return wload(wv[:, :, c * 512:(c + 1) * 512], [8, 512])

        def proj_fm(wc, fb, tt):
            p = ps()
            for kc in range(8):
                k.mm(p, wc[:, kc, fb * 128:(fb + 1) * 128], nT[:, kc, tt * 512:(tt + 1) * 512],
                     start=(kc == 0), stop=(kc == 7))
            return p

        def proj_tm(wc, tb):
            p = ps()
            for kc in range(8):
                k.mm(p, nT[:, kc, tb * 128:(tb + 1) * 128], wc[:, kc, :], start=(kc == 0), stop=(kc == 7))
            return p

        ar.mark()
        uT = yaT
        ar.mark()
        wc = wchunk(0)
        for fb in range(4):
            for tt in range(3):
                p = proj_fm(wc, fb, tt)
                k.copy("act" if (fb + tt) % 2 else "dve", uT[:, fb, tt * 512:(tt + 1) * 512], p)
        ar.release()
        s5_branch(l, uT, yaT)
        ar.release()
        rmsnorm_mod(l, 1)
        stage('s5', [yaT])
        ybT = ar.alloc([4, NT], BF16); ycT = ar.alloc([4, NT], BF16)
        ar.mark()
        qT = ar.alloc([4, NT], BF16); kT = ar.alloc([4, 2048], BF16); Vt = ar.alloc([16, 512], BF16)
        qraw = ar.alloc([2, 512], BF16); tr = ar.alloc([2, 512]); stg_o = ar.alloc([2, 512])
        for which, dst in ((1, qT), (2, kT)):
            ar.mark()
            wc = wchunk(which)
            for fb in range(4):
                for tt in range(3):
                    tsl = slice(tt * 512, (tt + 1) * 512)
                    p = proj_fm(wc, fb, tt)
                    if tt == 0:
                        k.copy("act", dst[:, fb, tsl], p)
                    else:
                        qb = qraw[:, (fb + tt) % 2]
                        k.copy("act", qb, p)
                        pq = ps()
                        k.mm(pq, pt_bf, qb)
                        pos = slice((tt - 1) * 512, tt * 512)
                        k.tt("dve", tr[:, 0], qb, rope[:, 0, pos], ALU.mult)
                        k.tt("dve", tr[:, 1], pq, rope[:, 1, pos], ALU.mult)
                        k.tt("pool", dst[:, fb, tsl], tr[:, 0], tr[:, 1], ALU.add)
            if which == 2:
                for tb in range(4):
                    p = proj_tm(wc, tb)
                    st_ = stg_o[:, tb % 2]
                    k.copy("dve", st_, p)
                    k.dma("pool", nk_out.ap()[tb // 2, l, (tb % 2) * 128:(tb % 2) * 128 + 128, :], st_, is_output=True)
            ar.release()
        stage('qk', [qT])
        ar.mark()
        wc = wchunk(3)
        for tb in ((4, 5, 6, 7, 8, 9, 10, 11, 0, 1, 2, 3) if 'e' in _VD else range(4 if 'a' in _VD else 12)):
            if 'c' in _VD and tb < 4:
                continue
            p = proj_tm(wc, tb)
            if 'b' in _VD:
                k.copy("act", Vt[:, tb, :], p)
                continue
            if tb >= 4:
                k.copy("act", Vt[:, tb, :], p)
            else:
                st_ = stg_o[:, tb % 2]
                k.copy("dve", st_, p)
                k.copy("act", Vt[:, tb, :], st_)
                if 'd' in _VD:
                    continue
                k.dma("pool", nv_out.ap()[tb // 2, l, (tb % 2) * 128:(tb % 2) * 128 + 128, :], st_, is_output=True)
        stage('v', [ybT])
        k.dma("pool", Vt[:, 12:16, :], cv.ap()[l].rearrange("(b p) f -> p b f", p=128))
        ckb = ar.alloc([4, 512], BF16)
        k.dma("pool", ckb, ck.ap()[l].rearrange("(b p) f -> p b f", p=128))
        for h in range(4):
            ptb = ps().bitcast(BF16)
            for b in range(4):
                k.transpose(ptb[:, b * 128:(b + 1) * 128], ckb[:, b, h * 128:(h + 1) * 128], ident_bf)
            k.copy("dve", kT[:, h, 1536:2048], ptb[:, 0:512])
        ar.release()
        stage('cache', [ybT])
        attn_branch(l, qT, kT, Vt, ybT)
        ar.release()
        stage('attn', [ybT])
        ar.mark()
        zp = ar.alloc([4, 1600])
        k.memset("pool", zp, 0.0)
        ar.mark()
        wc = wchunk(4)
        for fb in range(4):
            p = proj_fm(wc, fb, 0)
            k.copy("act", zp[:, fb, 16:272], p[:, 0:256])
            k.copy("act", zp[:, fb, 288:544], p[:, 256:512])
            for tt in (1, 2):
                p = proj_fm(wc, fb, tt)
                k.copy("act", zp[:, fb, 560 + (tt - 1) * 512:560 + tt * 512], p)
        ar.release()
        pool_branch(l, zp, ycT)
        ar.release()
        stage('pool', [ycT])
        ar.mark()
        mT = ar.alloc([8, NT], BF16)
        ys = (yaT, ybT, ycT)
        gs = ar.alloc([2, 512]); acc = ar.alloc([512]); t2 = ar.alloc([512])
        wbv = w_branch.ap()[l].rearrange("n (c p) d -> p n c d", p=128)
        mw = [(ar.alloc([3, 8, 128], BF16), ar.alloc([3, 4, 128], BF16)) for _ in range(2)]
        for dc in range(8):
            ar.mark()
            wg_, wb3 = mw[dc % 2]
            for n in range(3):
                k.dma("pool", wg_[:, n], wv[:, :, 2560 + n * 1024 + dc * 128:2560 + n * 1024 + (dc + 1) * 128])
            k.dma("pool", wb3, wbv[:, :, :, dc * 128:(dc + 1) * 128])
            for tt in range(3):
                tsl = slice(tt * 512, (tt + 1) * 512)
                for n in range(3):
                    pg = ps()
                    for kc in range(8):
                        k.mm(pg, wg_[:, n, kc, :], nT[:, kc, tsl], start=(kc == 0), stop=(kc == 7))
                    pb = ps()
                    for kc in range(4):
                        k.mm(pb, wb3[:, n, kc, :], ys[n][:, kc, tsl], start=(kc == 0), stop=(kc == 3))
                    gb = gs[:, n % 2]
                    k.act(gb, pg, AF.Sigmoid)
                    if n == 0:
                        k.tt("dve", acc, gb, pb, ALU.mult)
                    elif n == 1:
                        k.tt("dve", t2, gb, pb, ALU.mult)
                        k.tt("pool", acc, acc, t2, ALU.add)
                    else:
                        k.tt("dve", t2, gb, pb, ALU.mult)
                        k.tt("pool", mT[:, dc, tsl], acc, t2, ALU.add)
            ar.release()
        wov = w_out.ap()[l].rearrange("(c p) d -> p c d", p=128)
        for half in range(2):
            ar.mark()
            wo = wload(wov[:, :, half * 512:(half + 1) * 512], [8, 512])
            for s in range(4):
                dc = half * 4 + s
                for tt in range(3):
                    tsl = slice(tt * 512, (tt + 1) * 512)
                    py = ps()
                    for kc in range(8):
                        k.mm(py, wo[:, kc, s * 128:(s + 1) * 128], mT[:, kc, tsl], start=(kc == 0), stop=(kc == 7))
                    v = vsel(tt)
                    k.stt(xT[:, dc, tsl], py, Gmod[:, l, 1, dc, v:v + 1], xT[:, dc, tsl], ALU.mult, ALU.add)
            ar.release()
        ar.release()
        ar.release()

    for l in range(depth):
        ffn(l, 0)
        stage('ffn0', [xT])
        mixer(l)
        stage('mixer', [xT])
        ffn(l, 1)

    ar.mark()
    yT = ar.alloc([8, NT])
    rmsnorm_mod(0, 3, out_fn=lambda kc, tt: yT[:, kc, tt * 512:(tt + 1) * 512])
    obufs = [ar.alloc([D]) for _ in range(2)]
    for tb in range(12):
        ob = obufs[tb % 2]
        for g4 in range(2):
            pt = ps()
            for c in range(4):
                kc = g4 * 4 + c
                k.transpose(pt[:, c * 128:(c + 1) * 128], yT[:, kc, tb * 128:(tb + 1) * 128], ident)
            k.copy("act" if g4 else "dve", ob[:, g4 * 512:(g4 + 1) * 512], pt)
        k.dma("sp", y_out.ap()[tb * 128:(tb + 1) * 128, :], ob, is_output=True)
    ar.release()
    k.finish()
    return k


def _consts():
    c = {}
    c["c_ident"] = np.eye(128, dtype=np.float32)
    inv = (10000.0 ** (-np.arange(16, dtype=np.float32) / 16)).astype(np.float32)
    t = np.arange(1024)
    row = (t // 64).astype(np.float32); col = (t % 64).astype(np.float32)
    rope = np.zeros((128, 2, 1024), np.float32)
    pt = np.zeros((128, 128), np.float32)
    for m in range(2):
        for d in range(64):
            p = m * 64 + d
            pos = row if d < 32 else col
            ang = (pos * inv[d % 16]).astype(np.float32)
            rope[p, 0] = np.cos(ang); rope[p, 1] = np.sin(ang)
            dd = d % 32
            if dd < 16:
                pt[p + 16, p] = -1.0
            else:
                pt[p - 16, p] = 1.0
    c["c_rope"] = rope; c["c_pt"] = pt
    c["c_iota"] = np.broadcast_to(np.arange(1, 1025, dtype=np.float32), (128, 1024)).copy()
    mk = np.zeros((128, 2), np.float32)
    for p in range(128):
        g = p // 16
        mk[p, g % 2] = 1.0
    c["c_mask"] = mk
    pc = np.zeros((128, 4, 2, 2, 8), np.float32)
    for g, w in enumerate((2, 4, 8, 16)):
        for lt, L in enumerate((256, 1024)):
            for e in range(2):
                for j in range(8):
                    t_ = j if e == 0 else L - 8 + j
                    lo = min(max(t_ - w // 2, 0), L); hi = min(max(t_ - w // 2 + w, 0), L)
                    pc[:, g, lt, e, j] = 1.0 / (hi - lo)
    c["c_pool"] = pc
    return c


_W_NAMES = ["norm_g", "w_mod", "b_mod", "w_ffn_in", "w_ffn_out", "w_in", "ssm_lam_re", "ssm_lam_im", "ssm_log_dt",
            "ssm_b_re", "ssm_b_im", "ssm_c_re", "ssm_c_im", "ssm_d", "w_glu", "lam_q1", "lam_k1", "lam_q2", "lam_k2",
            "attn_norm_g", "w_pool", "pool_scale", "w_branch", "w_out", "final_norm_g"]


def kernel(**inp):
    nc = bass.Bass("TRN2", target_bir_lowering=False)
    build(nc)
    consts = _consts()
    f = lambda a: np.ascontiguousarray(np.asarray(a, dtype=np.float32))
    xp = f(inp["x_prompt"]); xs = f(inp["x_sample"])
    ck = f(inp["cache_k"]); cv = f(inp["cache_v"]); st = f(inp["state_ssm"]); c = f(inp["c"]); cctx = f(inp["c_ctx"])
    wts = {n: f(inp[n]) for n in _W_NAMES}
    in_maps = []
    for i in range(8):
        m = dict(wts); m.update(consts)
        m["xin"] = np.concatenate([xp[2 * i], xp[2 * i + 1], xs[i]], axis=0)
        m["ck"] = ck[i].reshape(DEPTH, 512, 512)
        m["cv"] = cv[i].reshape(DEPTH, 512, 512)
        m["st"] = st[i].reshape(DEPTH, 2, 2, 2048)
        m["cvec"] = np.stack([cctx, c[i]], axis=0)
        in_maps.append(m)
    res = run_bass_kernel_spmd(nc, in_maps, core_ids=list(range(8)))
    R = res.results
    y_prompt = np.stack([R[i // 2]["y"][(i % 2) * 256:(i % 2) * 256 + 256] for i in range(16)], axis=0)
    y_sample = np.stack([R[i]["y"][512:1536] for i in range(8)], axis=0)
    nk = np.concatenate([R[i]["nk"] for i in range(8)], axis=0).reshape(16, DEPTH, 256, 4, 2, 64)
    nv = np.concatenate([R[i]["nv"] for i in range(8)], axis=0).reshape(16, DEPTH, 256, 4, 128)
    ns = np.concatenate([R[i]["ns"] for i in range(8)], axis=0).reshape(16, DEPTH, 2, 2, 32, 64)
    return (y_prompt.astype(np.float32), y_sample.astype(np.float32), nk.astype(np.float32), nv.astype(np.float32),
            ns.astype(np.float32))
```
